# Optimizing a Trainium2 kernel written in Bass

```python
import jax
import jax.numpy as jnp
from jax import lax
import numpy as np

D_MODEL = 1024
BATCH = 2
SEQ = 8192
DEPTH = 2

GRID_W = 64
CTX_LEN = 256
EPS = 1e-6

POOL_WINDOWS = (2, 4, 8, 16)
POOL_GROUPS = 4
POOL_GROUP_DIM = D_MODEL // 16
POOL_WIDTH = POOL_GROUPS * POOL_GROUP_DIM

SGU_CHUNK = 128
SGU_HEADS = 4
SGU_HEAD_DIM = D_MODEL // 16
SGU_WIDTH = SGU_HEADS * SGU_HEAD_DIM

NA_HEAD_DIM = 64
NA_HEADS = D_MODEL // 128
NA_WIDTH = NA_HEADS * NA_HEAD_DIM
NA_KH = 8
NA_KW = 16
ROPE_THETA = 10000.0

N_BRANCH = 3
D_FF = 4 * D_MODEL

P_END = POOL_WIDTH
U_END = P_END + SGU_WIDTH
VS_END = U_END + SGU_WIDTH
Q_END = VS_END + NA_WIDTH
K_END = Q_END + NA_WIDTH
V_END = K_END + NA_WIDTH
PROJ_COLS = V_END + N_BRANCH * D_MODEL
SPLIT_POINTS = (P_END, U_END, VS_END, Q_END, K_END, V_END)

kernel_name = 'hybrid_pool_sgu_natten_diffusion_block'


def rms_norm(x, g):
    x32 = x.astype(jnp.float32)
    y = x32 * lax.rsqrt(jnp.mean(x32 * x32, axis=-1, keepdims=True) + EPS)
    return (y * g.astype(jnp.float32)).astype(x.dtype)


def modulate(h, shift, scale):
    return h * (1 + scale) + shift


def to_heads(t):
    return t.reshape(t.shape[0], t.shape[1], NA_HEADS, NA_HEAD_DIM)


def pool_mix(p, w_pool, pool_scale):
    B, L, _ = p.shape
    pg = p.reshape(B, L, POOL_GROUPS, POOL_GROUP_DIM).astype(jnp.float32)
    csum = jnp.concatenate([jnp.zeros((B, 1, POOL_GROUPS, POOL_GROUP_DIM), jnp.float32),
                            jnp.cumsum(pg, axis=1)], axis=1)
    t = jnp.arange(L)
    outs = []
    for g, w in enumerate(POOL_WINDOWS):
        lo = jnp.clip(t - w // 2, 0, L)
        hi = jnp.clip(t + (w - w // 2), 0, L)
        cg = csum[:, :, g]
        win_sum = jnp.take(cg, hi, axis=1) - jnp.take(cg, lo, axis=1)
        cnt = (hi - lo).astype(jnp.float32)[None, :, None]
        outs.append(win_sum / cnt - pg[:, :, g])
    pooled = jnp.stack(outs, axis=2).astype(p.dtype)
    y = jnp.einsum('blgd,gde->blge', pooled, w_pool)
    return y.reshape(B, L, POOL_WIDTH) * pool_scale


def sgu_mix(u, v, sgu_w, sgu_b):
    B, L, _ = v.shape
    n_chunks = L // SGU_CHUNK
    v32 = v.astype(jnp.float32)
    mu = jnp.mean(v32, axis=-1, keepdims=True)
    var = jnp.mean(jnp.square(v32 - mu), axis=-1, keepdims=True)
    vn = ((v32 - mu) * lax.rsqrt(var + EPS)).astype(v.dtype)
    vc = vn.reshape(B, n_chunks, SGU_CHUNK, SGU_HEADS, SGU_HEAD_DIM)
    mixed = jnp.einsum('hpq,bcqhd->bcphd', sgu_w, vc) + sgu_b.T[None, None, :, :, None]
    return u * mixed.reshape(B, L, SGU_WIDTH)


def axial_rope(L):
    t = jnp.arange(L)
    row = (t // GRID_W).astype(jnp.float32)
    col = (t % GRID_W).astype(jnp.float32)
    n_freq = NA_HEAD_DIM // 4
    inv_freq = ROPE_THETA ** (-jnp.arange(n_freq, dtype=jnp.float32) / n_freq)
    ang = jnp.concatenate([row[:, None] * inv_freq, col[:, None] * inv_freq], axis=-1)
    return jnp.cos(ang), jnp.sin(ang)


def apply_rope(x, cos, sin):
    x32 = x.astype(jnp.float32)
    x1, x2 = x32[..., 0::2], x32[..., 1::2]
    c = cos[None, :, None, :]
    s = sin[None, :, None, :]
    y = jnp.stack([x1 * c - x2 * s, x1 * s + x2 * c], axis=-1)
    return y.reshape(x.shape).astype(x.dtype)


def context_attention(q, k, v):
    s = jnp.einsum('bqhd,bkhd->bhqk', q, k).astype(jnp.float32) * (NA_HEAD_DIM ** -0.5)
    p = jax.nn.softmax(s, axis=-1).astype(v.dtype)
    return jnp.einsum('bhqk,bkhd->bqhd', p, v)


def neighbourhood_attention(q, k, v, k_ctx, v_ctx, rpb, rows):
    B, L, H, Dh = q.shape
    kh = min(NA_KH, rows)
    kw = NA_KW
    scale = Dh ** -0.5
    kg = k.reshape(B, rows, GRID_W, H, Dh)
    vg = v.reshape(B, rows, GRID_W, H, Dh)
    q_rows = q.reshape(B, rows, GRID_W, H, Dh).transpose(1, 0, 2, 3, 4)
    col = np.arange(GRID_W)
    col_start = np.clip(col - kw // 2, 0, GRID_W - kw)
    col_idx = col_start[:, None] + np.arange(kw)[None, :]
    dc = col_idx - col[:, None] + (NA_KW - 1)
    n_win = kh * kw

    def row_block(args):
        r, q_row = args
        rs = jnp.clip(r - kh // 2, 0, rows - kh)
        k_rows = lax.dynamic_slice_in_dim(kg, rs, kh, axis=1)
        v_rows = lax.dynamic_slice_in_dim(vg, rs, kh, axis=1)
        k_win = k_rows[:, :, col_idx]
        v_win = v_rows[:, :, col_idx]
        dr = rs + jnp.arange(kh) - r + (NA_KH - 1)
        bias = rpb[:, dr][:, :, dc].astype(jnp.float32)
        s_win = jnp.einsum('bqhd,biqjhd->bhqij', q_row, k_win).astype(jnp.float32) * scale
        s_win = s_win + bias.transpose(0, 2, 1, 3)[None]
        s_ctx = jnp.einsum('bqhd,bchd->bhqc', q_row, k_ctx).astype(jnp.float32) * scale
        s = jnp.concatenate([s_win.reshape(B, H, GRID_W, n_win), s_ctx], axis=-1)
        p = jax.nn.softmax(s, axis=-1).astype(v.dtype)
        p_win = p[..., :n_win].reshape(B, H, GRID_W, kh, kw)
        p_ctx = p[..., n_win:]
        return (jnp.einsum('bhqij,biqjhd->bqhd', p_win, v_win)
                + jnp.einsum('bhqc,bchd->bqhd', p_ctx, v_ctx))

    out = lax.map(row_block, (jnp.arange(rows), q_rows))
    return out.transpose(1, 0, 2, 3, 4).reshape(B, L, H, Dh)


def merge_branches(pool_o, sgu_o, attn_o, gate_logits, w_po, w_so, w_ao, w_o):
    gp, gs, ga = jnp.split(jax.nn.sigmoid(gate_logits), N_BRANCH, axis=-1)
    y = gp * (pool_o @ w_po) + gs * (sgu_o @ w_so) + ga * (attn_o @ w_ao)
    return y @ w_o


def sq_relu_mlp(h, w1, w2):
    return jnp.square(jax.nn.relu(h @ w1)) @ w2


def setup_inputs(seed: int = 0) -> dict:
    key = jax.random.key(seed)
    ks = jax.random.split(key, 21)
    D = D_MODEL

    def nrm(k, shape, scale):
        return jax.random.normal(k, shape, jnp.float32) * scale

    return {
        'x': nrm(ks[0], (BATCH, SEQ, D), 1.0),
        'c': nrm(ks[1], (BATCH, D), 1.0),
        'ctx': nrm(ks[2], (BATCH, CTX_LEN, D), 1.0),
        'c_ctx': nrm(ks[3], (D,), 1.0),
        'norm1_g': 1.0 + nrm(ks[4], (DEPTH, D), 0.05),
        'norm2_g': 1.0 + nrm(ks[5], (DEPTH, D), 0.05),
        'w_mod': nrm(ks[6], (DEPTH, D, 6 * D), 0.5 * D ** -0.5),
        'b_mod': nrm(ks[7], (DEPTH, 6 * D), 0.02),
        'w_in': nrm(ks[8], (DEPTH, D, PROJ_COLS), D ** -0.5),
        'w_pool': nrm(ks[9], (DEPTH, POOL_GROUPS, POOL_GROUP_DIM, POOL_GROUP_DIM), POOL_GROUP_DIM ** -0.5),
        'pool_scale': 1.0 + nrm(ks[10], (DEPTH, POOL_WIDTH), 0.1),
        'sgu_w': nrm(ks[11], (DEPTH, SGU_HEADS, SGU_CHUNK, SGU_CHUNK), SGU_CHUNK ** -0.5),
        'sgu_b': 1.0 + nrm(ks[12], (DEPTH, SGU_HEADS, SGU_CHUNK), 0.1),
        'na_rpb': nrm(ks[13], (DEPTH, NA_HEADS, 2 * NA_KH - 1, 2 * NA_KW - 1), 0.5),
        'w_pool_out': nrm(ks[14], (DEPTH, POOL_WIDTH, D), POOL_WIDTH ** -0.5),
        'w_sgu_out': nrm(ks[15], (DEPTH, SGU_WIDTH, D), SGU_WIDTH ** -0.5),
        'w_attn_out': nrm(ks[16], (DEPTH, NA_WIDTH, D), NA_WIDTH ** -0.5),
        'w_o': nrm(ks[17], (DEPTH, D, D), D ** -0.5),
        'w_ff1': nrm(ks[18], (DEPTH, D, D_FF), D ** -0.5),
        'w_ff2': nrm(ks[19], (DEPTH, D_FF, D), D_FF ** -0.5),
        'final_g': 1.0 + nrm(ks[20], (D,), 0.05),
    }


def reference(x, c, ctx, c_ctx, norm1_g, norm2_g, w_mod, b_mod, w_in, w_pool, pool_scale,
              sgu_w, sgu_b, na_rpb, w_pool_out, w_sgu_out, w_attn_out, w_o, w_ff1, w_ff2, final_g):
    n_lat = x.shape[1]
    rows = n_lat // GRID_W
    cos, sin = axial_rope(n_lat)
    silu_c = jax.nn.silu(c)
    silu_cc = jax.nn.silu(c_ctx)
    h, hc = x, ctx
    for l in range(DEPTH):
        mod = silu_c @ w_mod[l] + b_mod[l]
        mod_c = silu_cc @ w_mod[l] + b_mod[l]
        sh1, sc1, g1, sh2, sc2, g2 = jnp.split(mod[:, None, :], 6, axis=-1)
        csh1, csc1, cg1, csh2, csc2, cg2 = jnp.split(mod_c, 6, axis=-1)
        has_ctx_out = l < DEPTH - 1

        ac = modulate(rms_norm(hc, norm1_g[l]), csh1, csc1)
        if has_ctx_out:
            p, u, vs, q, k, v, gl = jnp.split(ac @ w_in[l], SPLIT_POINTS, axis=-1)
            k_ctx, v_ctx = to_heads(k), to_heads(v)
            attn_c = context_attention(to_heads(q), k_ctx, v_ctx).reshape(hc.shape[0], hc.shape[1], NA_WIDTH)
            mix_c = merge_branches(pool_mix(p, w_pool[l], pool_scale[l]), sgu_mix(u, vs, sgu_w[l], sgu_b[l]),
                                   attn_c, gl, w_pool_out[l], w_sgu_out[l], w_attn_out[l], w_o[l])
        else:
            k, v = jnp.split(ac @ w_in[l][:, Q_END:V_END], 2, axis=-1)
            k_ctx, v_ctx = to_heads(k), to_heads(v)

        a = modulate(rms_norm(h, norm1_g[l]), sh1, sc1)
        p, u, vs, q, k, v, gl = jnp.split(a @ w_in[l], SPLIT_POINTS, axis=-1)
        attn = neighbourhood_attention(apply_rope(to_heads(q), cos, sin), apply_rope(to_heads(k), cos, sin),
                                       to_heads(v), k_ctx, v_ctx, na_rpb[l], rows)
        mix = merge_branches(pool_mix(p, w_pool[l], pool_scale[l]), sgu_mix(u, vs, sgu_w[l], sgu_b[l]),
                             attn.reshape(h.shape[0], n_lat, NA_WIDTH), gl,
                             w_pool_out[l], w_sgu_out[l], w_attn_out[l], w_o[l])
        h = h + g1 * mix
        h = h + g2 * sq_relu_mlp(modulate(rms_norm(h, norm2_g[l]), sh2, sc2), w_ff1[l], w_ff2[l])

        if has_ctx_out:
            hc = hc + cg1 * mix_c
            hc = hc + cg2 * sq_relu_mlp(modulate(rms_norm(hc, norm2_g[l]), csh2, csc2), w_ff1[l], w_ff2[l])
    return rms_norm(h, final_g)
```

```python
import numpy as np
import ml_dtypes
import concourse.bass as bass
import concourse.mybir as mybir
from concourse.bass_utils import run_bass_kernel_spmd

F32 = mybir.dt.float32
F32R = mybir.dt.float32r
BF16 = mybir.dt.bfloat16
AF = mybir.ActivationFunctionType
ALU = mybir.AluOpType
AX = mybir.AxisListType

D = 1024
NLS = 24
NS = 26
NLAT = NLS * 128
TOK = NS * 128
EPS = 1e-6
NEG = -30000.0
DBG_SKIP_LN = False


class Buf:
    __slots__ = ("name", "last_writer", "readers")

    def __init__(self, name):
        self.name = name
        self.last_writer = None
        self.readers = []


class Op:
    __slots__ = ("eng", "fn", "deps", "is_dma", "key", "idx", "signal", "sem", "val", "waits")

    def __init__(self, eng, fn, is_dma, key, idx):
        self.eng = eng
        self.fn = fn
        self.is_dma = is_dma
        self.key = key
        self.idx = idx
        self.deps = []
        self.signal = False
        self.sem = None
        self.val = 0
        self.waits = []


ENGS = ("pe", "act", "dve", "pool", "sp")


class Prog:
    def __init__(self, nc):
        self.nc = nc
        self.ops = []
        self.final_waits = []
        self.phase_buf = Buf("phase")
        self.bar_ap = None
        self.halted = False
        self.dummy = Op("dve", None, False, None, -1)

    def add(self, eng, fn, reads=(), writes=(), dma_key=None):
        if self.halted:
            return self.dummy
        is_dma = dma_key is not None
        op = Op(eng, fn, is_dma, dma_key, len(self.ops))
        deps = {}
        for b in reads:
            w = b.last_writer
            if w is not None:
                deps[w.idx] = [w, True]
        for b in writes:
            w = b.last_writer
            if w is not None and w.idx not in deps:
                deps[w.idx] = [w, False]
            for r in b.readers:
                if r.idx not in deps:
                    deps[r.idx] = [r, False]
        pb = self.phase_buf
        if pb.last_writer is not None:
            deps[pb.last_writer.idx] = [pb.last_writer, True]
        pb.readers.append(op)
        for b in reads:
            b.readers.append(op)
        for b in writes:
            b.last_writer = op
            b.readers = []
        deps.pop(op.idx, None)
        op.deps = list(deps.values())
        self.ops.append(op)
        return op

    def barrier(self):
        if self.halted:
            return self.dummy
        pb = self.phase_buf
        op = Op("dve", lambda e: e.memset(self.bar_ap, 0.0), False, None, len(self.ops))
        deps = {}
        for r in pb.readers:
            deps[r.idx] = [r, True]
        if pb.last_writer is not None:
            deps[pb.last_writer.idx] = [pb.last_writer, True]
        op.deps = list(deps.values())
        pb.last_writer = op
        pb.readers = []
        self.ops.append(op)
        return op

    def dma(self, eng, out, in_, R=(), W=(), key=None, **kw):
        return self.add(eng, lambda e: e.dma_start(out=out, in_=in_, **kw), R, W, dma_key=key)

    def mm(self, out, lhsT, rhs, start, stop, R, W):
        return self.add("pe", lambda e: e.matmul(out, lhsT=lhsT, rhs=rhs, start=start, stop=stop), R, W)

    def tr(self, out, in_, ident, R, W):
        return self.add("pe", lambda e: e.transpose(out=out, in_=in_, identity=ident), R, W)

    def act(self, out, in_, func, R, W, **kw):
        return self.add("act", lambda e: e.activation(out=out, in_=in_, func=func, **kw), R, W)

    def copy(self, eng, out, in_, R, W):
        if eng == "act":
            return self.add("act", lambda e: e.copy(out=out, in_=in_), R, W)
        return self.add(eng, lambda e: e.tensor_copy(out=out, in_=in_), R, W)

    def tt(self, eng, out, in0, in1, op, R, W):
        return self.add(eng, lambda e: e.tensor_tensor(out=out, in0=in0, in1=in1, op=op), R, W)

    def ts(self, eng, out, in0, s1, s2, op0, op1, R, W):
        return self.add(eng, lambda e: e.tensor_scalar(out=out, in0=in0, scalar1=s1, scalar2=s2, op0=op0, op1=op1), R, W)

    def stt(self, eng, out, in0, scalar, in1, op0, op1, R, W):
        return self.add(eng, lambda e: e.scalar_tensor_tensor(out=out, in0=in0, scalar=scalar, in1=in1,
                                                              op0=op0, op1=op1), R, W)

    def emit(self):
        nc = self.nc
        ops = self.ops
        for op in ops:
            need = []
            for d, raw in op.deps:
                if d.is_dma or op.is_dma or d.eng != op.eng or (raw and op.eng != "pe"):
                    need.append(d)
            op.waits = need
            for d in need:
                d.signal = True
        for op in self.final_waits:
            op.signal = True
        for op in ops:
            if op.is_dma:
                op.signal = True
        sems = {}
        counters = {}
        for op in ops:
            if not op.signal:
                continue
            k = ("dma", op.key) if op.is_dma else ("eng", op.eng)
            if k not in sems:
                sems[k] = nc.alloc_semaphore("s%d" % len(sems))
                counters[k] = 0
            op.sem = sems[k]
            counters[k] += 16 if op.is_dma else 1
            op.val = counters[k]
            op.key = k
        self.n_sems = len(sems)
        per_eng = {e: [] for e in ENGS}
        for op in ops:
            per_eng[op.eng].append(op)
        finals = list(self.final_waits)

        def run(eng_name, eng):
            waited = {}
            for op in per_eng[eng_name]:
                req = {}
                for d in op.waits:
                    if req.get(d.key, 0) < d.val:
                        req[d.key] = d.val
                for k, v in req.items():
                    if waited.get(k, 0) >= v:
                        continue
                    waited[k] = v
                    eng.wait_ge(sems[k], v)
                inst = op.fn(eng)
                if op.signal:
                    inst.then_inc(op.sem, 16 if op.is_dma else 1)
            if eng_name == "sp":
                for d in finals:
                    if waited.get(d.key, 0) < d.val:
                        waited[d.key] = d.val
                        eng.wait_ge(sems[d.key], d.val)

        with nc.Block() as block:
            @block.tensor
            def _(e):
                run("pe", e)

            @block.scalar
            def _(e):
                run("act", e)

            @block.vector
            def _(e):
                run("dve", e)

            @block.gpsimd
            def _(e):
                run("pool", e)

            @block.sync
            def _(e):
                run("sp", e)


class Arena:
    def __init__(self, nc, nbytes):
        self.t = nc.alloc_sbuf_tensor("arena", [128, nbytes // 4], F32)
        self.n = nbytes
        self.off = 0
        self.peak = 0

    def alloc(self, shape, dtype):
        esz = 2 if dtype == BF16 else 4
        n = int(np.prod(shape)) * esz
        n4 = (n + 31) // 32 * 32
        assert self.off + n4 <= self.n, ("SBUF arena overflow", self.off, n4, self.n)
        a = self.t[:, self.off // 4:(self.off + n) // 4]
        self.off += n4
        self.peak = max(self.peak, self.off)
        if dtype != F32:
            a = a.bitcast(dtype)
        if len(shape) == 2:
            return a.rearrange("p (a b) -> p a b", a=shape[0])
        if len(shape) == 3:
            return a.rearrange("p (a b c) -> p a b c", a=shape[0], b=shape[1])
        return a

    def mark(self):
        return self.off

    def release(self, m):
        self.off = m


class RR:
    def __init__(self, items):
        self.items = items
        self.i = 0

    def next(self):
        it = self.items[self.i % len(self.items)]
        self.i += 1
        return it


def segs_for(kind, layer):
    if kind == "A":
        if layer == 0:
            s = [(4 * b, 4 * b + 4, 0) for b in range(6)]
        else:
            s = [(2, 4, 0)] + [(4 * b, 4 * b + 4, 0) for b in range(1, 5)] + [(20, 22, 0)]
        return s + [(24, 26, 1)]
    if kind == "DE":
        s = [(4 * b, 4 * b + 4, 0) for b in range(1, 5)]
        if layer == 0:
            s = [(2, 4, 0)] + s + [(20, 22, 0), (24, 26, 1)]
        return s
    if kind == "F":
        g = [[(4, 8, 0), (8, 12, 0)], [(12, 16, 0), (16, 20, 0)]]
        if layer == 0:
            g.append([(2, 4, 0), (20, 22, 0), (24, 26, 1)])
        return g
    raise ValueError(kind)


def build_program(n_layers=2, stop_after=None, debug=False):
    nc = bass.Bass("TRN2", target_bir_lowering=False)
    P = Prog(nc)

    def din(name, shape, dt=F32):
        return nc.dram_tensor(name, list(shape), dt, kind="ExternalInput").ap()

    xT = din("xT", [8, 128, NLAT])
    ctxT = din("ctxT", [8, 128, 256])
    vecs = din("vecs", [128, 156])
    consts = din("consts", [128, 3, 128])
    rope = din("rope", [2, 128, NLAT])
    bands = din("bands", [5, 128, 12, 128])
    biasd = din("bias", [2, 6, 8, 128, 1024], BF16)
    w_mod = din("w_mod", [2, 1024, 6144])
    wA_tm = din("wA_tm", [2, 2, 128, 8, 512])
    wA_fm = din("wA_fm", [2, 34, 128, 8, 128])
    sguw = din("sguw", [128, 2, 4, 128])
    sgub_d = din("sgub", [128, 2, 2, 128])
    wblk_d = din("wblk", [128, 2, 2, 128])
    w_po = din("w_pool_out", [2, 256, 1024])
    w_so = din("w_sgu_out", [2, 256, 1024])
    w_ao = din("w_attn_out", [2, 512, 1024])
    w_o = din("w_o", [2, 1024, 1024])
    w1r = din("w1r", [2, 32, 128, 8, 128])
    w2r = din("w2r", [2, 8, 128, 32, 128])
    outT = nc.dram_tensor("outT", [8, 128, 2048], F32, kind="ExternalOutput").ap()

    skind = "ExternalOutput" if debug else "Internal"

    def dscr(name, shape, dt=F32):
        return nc.dram_tensor(name, list(shape), dt, kind=skind).ap()

    h_scr = dscr("h_scr", [8, 128, TOK])
    q_scr = dscr("q_scr", [NS, 128, 4, 128], BF16)
    u_scr = dscr("u_scr", [NS, 128, 2, 128])
    vn_scr = dscr("vn_scr", [NS, 128, 256], BF16)
    g_scr = dscr("g_scr", [NS, 8, 128, 3, 128])
    if debug:
        dbg_k = dscr("dbg_k", [128, 4, TOK], BF16)
        dbg_v = dscr("dbg_v", [128, NS, 512], BF16)
        dbg_p = dscr("dbg_p", [128, NS, 256], BF16)
        dbg_a = dscr("dbg_a", [128, 8, TOK], BF16)
        dbg_mod = dscr("dbg_mod", [128, 2, 2, 48])
        dbg_o = dscr("dbg_o", [NS, 128, 8, 128], BF16)

    Bh = [Buf("h%d" % t) for t in range(NS)]
    Bq = [Buf("q%d" % t) for t in range(NS)]
    Bu = [Buf("u%d" % t) for t in range(NS)]
    Bvn = [Buf("vn%d" % t) for t in range(NS)]
    Bg = [Buf("g%d" % t) for t in range(NS)]
    Bout = Buf("out")
    Bdbg = Buf("dbg")

    psd = [nc.alloc_psum_tensor("psd%d" % i, [128, 1024], F32) for i in range(4)]
    Bk = [Buf("bank%d" % i) for i in range(8)]

    def bank(i):
        return psd[i // 2][:, (i % 2) * 512:(i % 2) * 512 + 512]

    AR = Arena(nc, 194 * 1024)
    bar_t = AR.alloc([8], F32)
    P.bar_ap = bar_t
    cst = AR.alloc([3, 128], F32)
    identb = AR.alloc([128], BF16)
    perm_r = nc.alloc_sbuf_tensor("perm_r", [128, 128], F32R)[:]
    ones_r = nc.alloc_sbuf_tensor("ones_r", [128, 128], F32R)[:]
    sq_rr = RR([(nc.alloc_sbuf_tensor("sq_r%d" % i, [128, 512], F32R)[:], Buf("sq_r%d" % i)) for i in range(2)])
    qf_rr = RR([(nc.alloc_sbuf_tensor("qf_r%d" % i, [128, 512], F32R)[:], Buf("qf_r%d" % i)) for i in range(2)])
    vec = AR.alloc([156], F32)
    modt = AR.alloc([24, 8], F32)
    silu_b = AR.alloc([2, 8], BF16)
    B_c = Buf("consts")
    B_vec = Buf("vec")
    B_modt = Buf("modt")
    B_silu = Buf("silu")

    def V_c(s):
        return vec[:, s * 8:(s + 1) * 8]

    def V_bmod(l):
        return vec[:, 16 + l * 48:16 + (l + 1) * 48]

    def V_n1(l):
        return vec[:, 112 + l * 8:112 + (l + 1) * 8]

    def V_n2(l):
        return vec[:, 128 + l * 8:128 + (l + 1) * 8]

    V_fg = vec[:, 144:152]

    def V_ps(l):
        return vec[:, 152 + l * 2:152 + (l + 1) * 2]

    def MT(l, s, kind):
        i = (l * 2 + s) * 6 + kind
        return modt[:, i, :]

    P.dma("sp", cst, consts, W=[B_c], key="cst")
    P.dma("sp", vec, vecs, W=[B_vec], key="vec")
    B_id = Buf("ident")
    P.copy("dve", identb, cst[:, 0, :], [B_c], [B_id])
    P.copy("dve", perm_r, cst[:, 1, :], [B_c], [B_id])
    P.copy("dve", ones_r, cst[:, 2, :], [B_c], [B_id])
    P.act(silu_b.rearrange("p s k -> p (s k)"), vec[:, 0:16], AF.Silu, [B_vec], [B_silu])

    def stop(name, l=0):
        if stop_after is not None and tuple(stop_after) == (name, l):
            P.halted = True

    stop("S")

    m0 = AR.mark()
    wm = [(AR.alloc([8, 512], BF16), Buf("wm%d" % i)) for i in range(2)]
    wm_rr = RR(wm)
    modraw = AR.alloc([48, 2], F32)
    tmpm = AR.alloc([8, 2], F32)
    B_modraw = Buf("modraw")
    for l in range(n_layers):
        psm = bank(0).rearrange("p (a b) -> p a b", b=2)[:, 0:48, :]
        for pc in range(12):
            wt, wb = wm_rr.next()
            P.dma("pool", wt, w_mod[l, :, pc * 512:(pc + 1) * 512].rearrange("(k p) n -> p k n", p=128),
                  W=[wb], key="wm%d" % (wm_rr.i % 2), max_dma_last_dim=4096)
            for cc in range(4):
                ch = pc * 4 + cc
                for k in range(8):
                    P.mm(psm[:, ch, :], wt[:, k, cc * 128:(cc + 1) * 128], silu_b[:, :, k], k == 0, k == 7,
                         [wb, B_silu], [Bk[0]])
        P.tt("dve", modraw, psm, V_bmod(l).unsqueeze(2).broadcast_to([128, 48, 2]), ALU.add,
             [Bk[0], B_vec], [B_modraw])
        for s in range(2):
            def chunk(i):
                return modraw[:, i * 8:(i + 1) * 8, s]
            P.stt("dve", MT(l, s, 0), chunk(1), 1.0, V_n1(l), ALU.add, ALU.mult, [B_modraw, B_vec], [B_modt])
            P.copy("dve", MT(l, s, 1), chunk(0), [B_modraw], [B_modt])
            P.copy("dve", MT(l, s, 2), chunk(2), [B_modraw], [B_modt])
            P.stt("dve", MT(l, s, 3), chunk(4), 1.0, V_n2(l), ALU.add, ALU.mult, [B_modraw, B_vec], [B_modt])
            P.copy("dve", MT(l, s, 4), chunk(3), [B_modraw], [B_modt])
            P.copy("dve", MT(l, s, 5), chunk(5), [B_modraw], [B_modt])
    if debug:
        P.dma("sp", dbg_mod.rearrange("p l s c -> p (l s c)"), modt.rearrange("p a b -> p (a b)")[:, 0:192],
              R=[B_modt], W=[Bdbg], key="dbgm")
    AR.release(m0)
    stop("M")
    P.barrier()

    kT_all = AR.alloc([4, TOK], BF16)
    v_all = AR.alloc([NS, 512], BF16)
    p_all = AR.alloc([NS, 256], BF16)
    Bkt = [Buf("kT%d" % t) for t in range(NS)]
    Bv = [Buf("v%d" % t) for t in range(NS)]
    Bp = [Buf("p%d" % t) for t in range(NS)]
    res_mark = AR.mark()

    def rng(bufs, lo, hi):
        return [bufs[t] for t in range(lo, hi)]

    def h_src(l, c, lo, hi):
        if l == 0:
            if lo >= 24:
                return ctxT[c, :, (lo - 24) * 128:(hi - 24) * 128]
            return xT[c, :, lo * 128:hi * 128]
        return h_scr[c, :, lo * 128:hi * 128]

    def load_h(dst, l, lo, hi, Bdst, key, first_layer_input):
        n = (hi - lo) * 128
        if first_layer_input:
            src = (ctxT[:, :, (lo - 24) * 128:(hi - 24) * 128] if lo >= 24 else xT[:, :, lo * 128:hi * 128])
            R = []
        else:
            src = h_scr[:, :, lo * 128:hi * 128]
            R = rng(Bh, lo, hi)
        P.dma("sp", dst[:, :, 0:n], src.rearrange("c p n -> p c n"), R=R, W=[Bdst], key=key)

    def rms_rstd(hb, Bhb, n, rsb, psb):
        rs, Brs = rsb
        pb, Bpb = psb
        for c in range(8):
            sq, Bsq = sq_rr.next()
            P.act(sq[:, 0:n], hb[:, c, 0:n], AF.Square, [Bhb], [Bsq])
            P.mm(pb[:, 0:n], ones_r, sq[:, 0:n], c == 0, c == 7, [Bsq, B_id], [Bpb])
        P.ts("dve", rs[:, 0:n], pb[:, 0:n], 1.0 / D, EPS, ALU.mult, ALU.add, [Bpb], [Brs])
        P.act(rs[:, 0:n], rs[:, 0:n], AF.Sqrt, [Brs], [Brs])
        P.add("dve", lambda e: e.reciprocal(out=rs[:, 0:n], in_=rs[:, 0:n]), [Brs], [Brs])

    def rms_to_aT(hb, Bhb, n, gs, sh, dst, Wdst, rsb, tmp_rr, psb):
        rs, Brs = rsb
        rms_rstd(hb, Bhb, n, rsb, psb)
        for c in range(8):
            tmp, Btmp = tmp_rr.next()
            P.stt("dve", tmp[:, 0:n], hb[:, c, 0:n], gs[:, c:c + 1], rs[:, 0:n], ALU.mult, ALU.mult,
                  [Bhb, Brs, B_modt], [Btmp])
            P.act(dst[:, c, 0:n], tmp[:, 0:n], AF.Identity, [Btmp, B_modt], Wdst, bias=sh[:, c:c + 1], scale=1.0)

    out_ops = []

    for l in range(n_layers):
        last = (l == n_layers - 1)
        if l > 0:
            P.barrier()
        AR.release(res_mark)
        aT_all = AR.alloc([8, TOK], BF16)
        Ba = [Buf("aT%d" % t) for t in range(NS)]
        a_mark = AR.mark()
        hst = [(AR.alloc([8, 512], F32), Buf("hst%d" % i)) for i in range(2)]
        hst_rr = RR(hst)
        rsb = (AR.alloc([512], F32), Buf("rs"))
        tmp_rr = RR([(AR.alloc([512], F32), Buf("tmp%d" % i)) for i in range(2)])
        segsA = segs_for("A", l)
        for si, (lo, hi, st) in enumerate(segsA):
            n = (hi - lo) * 128
            hb, Bhb = hst_rr.next()
            load_h(hb, l, lo, hi, Bhb, "hst%d" % (hst_rr.i % 2), l == 0)
            bi = si % 2
            rms_to_aT(hb, Bhb, n, MT(l, st, 0), MT(l, st, 1), aT_all[:, :, lo * 128:hi * 128], rng(Ba, lo, hi), rsb,
                      tmp_rr, (bank(bi), Bk[bi]))
        if debug and l == 0:
            P.dma("sp", dbg_a, aT_all, R=Ba, W=[Bdbg], key="dbga")
        stop("A1", l)
        P.barrier()
        AR.release(a_mark)

        wbuf = [(AR.alloc([8192], BF16), Buf("wbuf%d" % i)) for i in range(2)]
        wb_rr = RR(wbuf)
        ropeb = [(AR.alloc([2, 512], F32), Buf("rope%d" % i)) for i in range(2)]
        rope_rr = RR(ropeb)
        t1_rr = RR([(AR.alloc([512], F32), Buf("t1%d" % i)) for i in range(2)])
        t2_rr = RR([(AR.alloc([512], F32), Buf("t2%d" % i)) for i in range(2)])
        qo_rr = RR([(AR.alloc([512], BF16), Buf("qo%d" % i)) for i in range(2)])
        us_rr = RR([(AR.alloc([512], F32), Buf("us%d" % i)) for i in range(3)])
        vns_rr = RR([(AR.alloc([256], BF16), Buf("vns%d" % i)) for i in range(2)])
        st_rr = RR([(AR.alloc([8], F32), Buf("st%d" % i)) for i in range(2)])
        vsf_rr = RR([(AR.alloc([256], F32), Buf("vsf%d" % i)) for i in range(4)])
        pj_rr = RR([(bank(i), Bk[i]) for i in range(4)])
        pm_rr = RR([(bank(i), Bk[i]) for i in (4, 5)])
        tm_rr = RR([(bank(i), Bk[i]) for i in (6, 7)])

        def load_w(src, shape):
            wt, wb = wb_rr.next()
            nel = int(np.prod(shape))
            view = wt[:, 0:nel]
            if len(shape) == 2:
                view = view.rearrange("p (a b) -> p a b", a=shape[0])
            else:
                view = view.rearrange("p (a b c) -> p a b c", a=shape[0], b=shape[1])
            P.dma("pool", view, src, W=[wb], key="wbuf%d" % (wb_rr.i % 2), max_dma_last_dim=4096)
            return view, wb

        for piece in range(2):
            wv, wb = load_w(wA_tm[l, piece], [8, 512])
            for (lo, hi, st) in segsA:
                for t in range(lo, hi):
                    pt, Bpt = tm_rr.next()
                    for k in range(8):
                        P.mm(pt, aT_all[:, k, t * 128:(t + 1) * 128], wv[:, k, :], k == 0, k == 7, [Ba[t], wb], [Bpt])
                    if piece == 0:
                        P.copy("act", p_all[:, t, :], pt[:, 0:256], [Bpt], [Bp[t]])
                        if DBG_SKIP_LN:
                            continue
                        stt_, Bst = st_rr.next()
                        vf, Bvf = vsf_rr.next()
                        P.act(vf, pt[:, 256:512], AF.Identity, [Bpt], [Bvf, Bst], accum_out=stt_[:, 0:1])
                        P.ts("dve", stt_[:, 1:2], stt_[:, 0:1], -1.0 / 256, None, ALU.mult, ALU.bypass, [Bst], [Bst])
                        jk, Bjk = vsf_rr.next()
                        P.act(jk, vf, AF.Square, [Bvf, Bst], [Bjk, Bst], bias=stt_[:, 1:2], scale=1.0, accum_out=stt_[:, 2:3])
                        P.ts("dve", stt_[:, 3:4], stt_[:, 2:3], 1.0 / 256, EPS, ALU.mult, ALU.add, [Bst], [Bst])
                        P.act(stt_[:, 3:4], stt_[:, 3:4], AF.Sqrt, [Bst], [Bst])
                        P.add("dve", lambda e, o=stt_[:, 3:4]: e.reciprocal(out=o, in_=o), [Bst], [Bst])
                        vs_, Bvs = vns_rr.next()
                        P.ts("dve", vs_, vf, stt_[:, 1:2], stt_[:, 3:4], ALU.add, ALU.mult, [Bvf, Bst], [Bvs])
                        P.dma("sp", vn_scr[t], vs_, R=[Bvs], W=[Bvn[t]], key="vns%d" % (vns_rr.i % 2))
                    else:
                        P.copy("act", v_all[:, t, :], pt, [Bpt], [Bv[t]])

        stop("A2", l)
        def proj_chunk(wv, wb, lo, hi):
            n = (hi - lo) * 128
            pj, Bpj = pj_rr.next()
            for k in range(8):
                P.mm(pj[:, 0:n], wv[:, k, :], aT_all[:, k, lo * 128:hi * 128], k == 0, k == 7,
                     rng(Ba, lo, hi) + [wb], [Bpj])
            return pj, Bpj, n

        wv, wb = load_w(wA_fm[l, 0:2].rearrange("c p k m -> p c k m"), [2, 8, 128])
        for (lo, hi, st) in segsA:
            for c in range(2):
                pj, Bpj, n = proj_chunk(wv[:, c], wb, lo, hi)
                us, Bus = us_rr.next()
                P.copy("act", us[:, 0:n], pj[:, 0:n], [Bpj], [Bus])
                P.dma("sp", u_scr[lo:hi, :, c, :].rearrange("t p n -> p t n"),
                      us[:, 0:n].rearrange("p (t n) -> p t n", n=128), R=[Bus], W=rng(Bu, lo, hi),
                      key="us%d" % (us_rr.i % 3))
        stop("A3", l)
        wv, wb = load_w(wA_fm[l, 2:10].rearrange("c p k m -> p c k m"), [8, 8, 128])
        for (lo, hi, st) in segsA:
            n = (hi - lo) * 128
            if st == 0:
                rt, Brt = rope_rr.next()
                P.dma("sp", rt[:, :, 0:n], rope[:, :, lo * 128:hi * 128].rearrange("a p n -> p a n"), W=[Brt],
                      key="rope%d" % (rope_rr.i % 2))
            for c in range(8):
                pj, Bpj, n = proj_chunk(wv[:, c], wb, lo, hi)
                isq = c < 4
                if st == 1:
                    if isq:
                        qo, Bqo = qo_rr.next()
                        P.copy("act", qo[:, 0:n], pj[:, 0:n], [Bpj], [Bqo])
                    else:
                        P.copy("act", kT_all[:, c - 4, lo * 128:hi * 128], pj[:, 0:n], [Bpj], rng(Bkt, lo, hi))
                else:
                    qf, Bqf = qf_rr.next()
                    P.copy("act", qf[:, 0:n], pj[:, 0:n], [Bpj], [Bqf])
                    pm, Bpm = pm_rr.next()
                    P.mm(pm[:, 0:n], perm_r, qf[:, 0:n], True, True, [Bqf, B_id], [Bpm])
                    t1, Bt1 = t1_rr.next()
                    t2, Bt2 = t2_rr.next()
                    P.tt("pool", t1[:, 0:n], qf[:, 0:n].bitcast(F32), rt[:, 0, 0:n], ALU.mult, [Bqf, Brt], [Bt1])
                    P.tt("dve", t2[:, 0:n], pm[:, 0:n], rt[:, 1, 0:n], ALU.mult, [Bpm, Brt], [Bt2])
                    if isq:
                        qo, Bqo = qo_rr.next()
                        P.tt("dve", qo[:, 0:n], t1[:, 0:n], t2[:, 0:n], ALU.add, [Bt1, Bt2], [Bqo])
                    else:
                        P.tt("dve", kT_all[:, c - 4, lo * 128:hi * 128], t1[:, 0:n], t2[:, 0:n], ALU.add,
                             [Bt1, Bt2], rng(Bkt, lo, hi))
                if isq:
                    P.dma("sp", q_scr[lo:hi, :, c, :].rearrange("t p n -> p t n"),
                          qo[:, 0:n].rearrange("p (t n) -> p t n", n=128), R=[Bqo], W=rng(Bq, lo, hi),
                          key="qo%d" % (qo_rr.i % 2))
        stop("A4", l)
        for gp in range(6):
            wv, wb = load_w(wA_fm[l, 10 + gp * 4:10 + gp * 4 + 4].rearrange("c p k m -> p c k m"), [4, 8, 128])
            for (lo, hi, st) in segsA:
                for cc in range(4):
                    gch = gp * 4 + cc
                    br, c = gch // 8, gch % 8
                    pj, Bpj, n = proj_chunk(wv[:, cc], wb, lo, hi)
                    us, Bus = us_rr.next()
                    P.act(us[:, 0:n], pj[:, 0:n], AF.Sigmoid, [Bpj], [Bus])
                    P.dma("sp", g_scr[lo:hi, c, :, br, :].rearrange("t p n -> p t n"),
                          us[:, 0:n].rearrange("p (t n) -> p t n", n=128), R=[Bus], W=rng(Bg, lo, hi),
                          key="us%d" % (us_rr.i % 3))
        if debug and l == 0:
            hh_ = P.halted
            P.halted = False
            P.dma("sp", dbg_k, kT_all, R=Bkt, W=[Bdbg], key="dbgk")
            P.dma("sp", dbg_v, v_all, R=Bv, W=[Bdbg], key="dbgv")
            P.dma("sp", dbg_p, p_all, R=Bp, W=[Bdbg], key="dbgp")
            P.halted = hh_
        stop("A", l)

        P.barrier()
        AR.release(res_mark)
        wsT = AR.alloc([4, 128], BF16)
        sgub = AR.alloc([2, 128], F32)
        wblk = AR.alloc([2, 128], BF16)
        bnd = AR.alloc([2 * 12, 128], BF16)
        B_bsp = Buf("band_special")
        wpo = AR.alloc([2, 1024], BF16)
        wso = AR.alloc([2, 1024], BF16)
        wao = AR.alloc([4, 1024], BF16)
        wo = AR.alloc([8, 1024], BF16)
        B_wD = Buf("wD")
        P.dma("pool", wsT, sguw[:, l], W=[B_wD], key="wD0")
        P.dma("sp", sgub, sgub_d[:, l], W=[B_wD], key="wD1")
        P.dma("pool", wblk, wblk_d[:, l], W=[B_wD], key="wD2")
        P.dma("pool", bnd[:, 0:12, :], bands[0], W=[B_wD], key="wD3")
        P.dma("pool", wpo, w_po[l].rearrange("(k p) n -> p k n", p=128), W=[B_wD], key="wD4", max_dma_last_dim=4096)
        P.dma("pool", wso, w_so[l].rearrange("(k p) n -> p k n", p=128), W=[B_wD], key="wD5", max_dma_last_dim=4096)
        P.dma("pool", wao, w_ao[l].rearrange("(k p) n -> p k n", p=128), W=[B_wD], key="wD6", max_dma_last_dim=4096)
        P.dma("pool", wo, w_o[l].rearrange("(k p) n -> p k n", p=128), W=[B_wD], key="wD7", max_dma_last_dim=4096)

        bias_rr = RR([(AR.alloc([1024], BF16), Buf("bias%d" % i)) for i in range(2)])
        qt_rr = RR([(AR.alloc([4, 128], BF16), Buf("qt%d" % i)) for i in range(2)])
        ut_rr = RR([(AR.alloc([2, 128], F32), Buf("ut%d" % i)) for i in range(2)])
        vt_rr = RR([(AR.alloc([256], BF16), Buf("vt%d" % i)) for i in range(2)])
        ssb_rr = RR([(AR.alloc([1024], F32), Buf("ssb%d" % i)) for i in range(2)])
        pe_rr = RR([(AR.alloc([1024], BF16), Buf("pexp%d" % i)) for i in range(2)])
        ptb_rr = RR([(AR.alloc([8, 128], BF16), Buf("ptsb%d" % i)) for i in range(2)])
        smx = AR.alloc([16], F32)
        B_smx = Buf("smx")
        rinv = AR.alloc([8], F32)
        B_rinv = Buf("rinv")
        ao = AR.alloc([512], BF16)
        B_ao = Buf("ao")
        pooledT = AR.alloc([2, 128], BF16)
        B_pooled = Buf("pooled")
        sgt = AR.alloc([2, 128], F32)
        B_sgt = Buf("sgt")
        oT = AR.alloc([8, 512], BF16)
        B_oT = Buf("oT")
        gt_rr = RR([(AR.alloc([4, 3, 128], F32), Buf("gt%d" % i)) for i in range(2)])
        hblk = AR.alloc([8, 512], F32)
        B_hblk = Buf("hblk")
        e_t = [(AR.alloc([512], F32), Buf("et%d" % i)) for i in range(4)]
        yT = AR.alloc([8, 512], BF16)
        B_yT = Buf("yT")

        def tile_type(t):
            if t >= 24:
                return 5
            j = t - 4
            return {0: 1, 1: 2, 14: 3, 15: 4}.get(j, 0)

        def band_type(t):
            if t == 24:
                return 3
            if t == 25:
                return 4
            j = t - 4
            return {0: 1, 15: 2}.get(j, 0)

        for (lo, hi, st) in segs_for("DE", l):
            n = (hi - lo) * 128
            for t in range(lo, hi):
                tc0 = (t - lo) * 128
                qt, Bqt = qt_rr.next()
                P.dma("sp", qt, q_scr[t], R=[Bq[t]], W=[Bqt], key="qt%d" % (qt_rr.i % 2))
                ut, But = ut_rr.next()
                P.dma("sp", ut, u_scr[t], R=[Bu[t]], W=[But], key="ut%d" % (ut_rr.i % 2))
                vt, Bvt = vt_rr.next()
                P.dma("sp", vt, vn_scr[t], R=[Bvn[t]], W=[Bvt], key="vt%d" % (vt_rr.i % 2))
                if st == 1:
                    kranges = [(24, 26)]
                else:
                    j = t - 4
                    klo, khi = t - 2, t + 3
                    if j == 0:
                        khi = t + 4
                    if j == 15:
                        klo = t - 3
                    kranges = [(klo, khi), (24, 26)]
                nk = sum((b - a) for a, b in kranges) * 128
                kslots = [s_ for a, b in kranges for s_ in range(a, b)]
                ty = tile_type(t)
                for h in range(8):
                    ch, pb = h // 2, (h % 2) * 64
                    S = psd[h % 2]
                    BS = [Bk[2 * (h % 2)], Bk[2 * (h % 2) + 1]]
                    bt, Bbt = bias_rr.next()
                    P.dma("sp", bt[:, 0:nk], biasd[l, ty, h, :, 0:nk], W=[Bbt], key="bias%d" % (bias_rr.i % 2))
                    col = 0
                    for (a, b) in kranges:
                        c0 = a * 128
                        rem = (b - a) * 128
                        while rem > 0:
                            w_ = min(rem, 512 - (col % 512))
                            P.mm(S[:, col:col + w_], qt[pb:pb + 64, ch, :], kT_all[pb:pb + 64, ch, c0:c0 + w_],
                                 True, True, [Bqt] + rng(Bkt, a, b), [BS[col // 512]])
                            col += w_
                            c0 += w_
                            rem -= w_
                    ssb, Bssb = ssb_rr.next()
                    P.stt("dve", ssb[:, 0:nk], S[:, 0:nk], 0.125, bt[:, 0:nk], ALU.mult, ALU.add, BS + [Bbt], [Bssb])
                    P.add("dve", lambda e, o=smx[:, h:h + 1], i=ssb[:, 0:nk]: e.tensor_reduce(
                        out=o, in_=i, axis=AX.X, op=ALU.max, negate=True), [Bssb], [B_smx])
                    pex, Bpex = pe_rr.next()
                    P.act(pex[:, 0:nk], ssb[:, 0:nk], AF.Exp, [Bssb, B_smx], [Bpex, B_smx], bias=smx[:, h:h + 1],
                          scale=1.0, accum_out=smx[:, 8 + h:9 + h])
                    PT = bank(4).bitcast(BF16).rearrange("p (a b) -> p a b", b=128)
                    nkt = nk // 128
                    for kt in range(nkt):
                        P.tr(PT[:, kt, :], pex[:, kt * 128:(kt + 1) * 128], identb, [Bpex, B_id], [Bk[4]])
                    ptb, Bptb = ptb_rr.next()
                    if h % 2 == 0:
                        P.copy("act", ptb[:, 0:nkt, :], PT[:, 0:nkt, :], [Bk[4]], [Bptb])
                    else:
                        P.copy("dve", ptb[:, 0:nkt, :], PT[:, 0:nkt, :], [Bk[4]], [Bptb])
                    for kt in range(nkt):
                        P.mm(bank(5)[:, h * 64:(h + 1) * 64], ptb[:, kt, :], v_all[:, kslots[kt], h * 64:(h + 1) * 64],
                             kt == 0, kt == nkt - 1, [Bptb, Bv[kslots[kt]]], [Bk[5]])
                P.add("dve", lambda e: e.reciprocal(out=rinv, in_=smx[:, 8:16]), [B_smx], [B_rinv])
                P.tt("dve", ao.rearrange("p (h d) -> p h d", d=64), bank(5).rearrange("p (h d) -> p h d", d=64),
                     rinv.unsqueeze(2).broadcast_to([128, 8, 64]), ALU.mult, [Bk[5], B_rinv], [B_ao])
                AT = bank(6).bitcast(BF16).rearrange("p (a b) -> p a b", b=128)
                for c in range(4):
                    P.tr(AT[:, c, :], ao[:, c * 128:(c + 1) * 128], identb, [B_ao, B_id], [Bk[6]])
                P.copy("act", oT[:, 4:8, tc0:tc0 + 128], AT[:, 0:4, :], [Bk[6]], [B_oT])
                bty = band_type(t)
                boff = 0
                if bty != 0:
                    P.dma("pool", bnd[:, 12:24, :], bands[bty], W=[B_bsp], key="bsp")
                    boff = 12
                PP = bank(7).rearrange("p (a b) -> p a b", b=128)
                for g in range(4):
                    c = g // 2
                    srcs = [d_ for d_ in (-1, 0, 1) if not ((t == 24 and d_ == -1) or (t == 25 and d_ == 1))]
                    for ii, d_ in enumerate(srcs):
                        P.mm(PP[:, g, :], p_all[:, t + d_, c * 128:(c + 1) * 128], bnd[:, boff + g * 3 + (d_ + 1), :],
                             ii == 0, ii == len(srcs) - 1, [Bp[t + d_], B_wD, B_bsp], [Bk[7]])
                for g in range(4):
                    gp_ = (g % 2) * 64
                    P.copy("act" if g % 2 == 0 else "dve", pooledT[gp_:gp_ + 64, g // 2, :], PP[gp_:gp_ + 64, g, :],
                           [Bk[7]], [B_pooled])
                PY = bank(6)[:, 256:512].rearrange("p (a b) -> p a b", b=128)
                for c in range(2):
                    P.mm(PY[:, c, :], wblk[:, c, :], pooledT[:, c, :], True, True, [B_pooled, B_wD], [Bk[6]])
                for c in range(2):
                    P.act(oT[:, c, tc0:tc0 + 128], PY[:, c, :], AF.Identity, [Bk[6], B_vec], [B_oT],
                          scale=V_ps(l)[:, c:c + 1], bias=0.0)
                PS_ = bank(7).rearrange("p (a b) -> p a b", b=128)
                for hh in range(4):
                    P.mm(PS_[:, hh, :], vt[:, (hh // 2) * 128:(hh // 2 + 1) * 128], wsT[:, hh, :], True, True,
                         [Bvt, B_wD], [Bk[7]])
                for hh in range(4):
                    hp = (hh % 2) * 64
                    P.tt("dve", sgt[hp:hp + 64, hh // 2, :], PS_[hp:hp + 64, hh, :], sgub[hp:hp + 64, hh // 2, :], ALU.add,
                         [Bk[7], B_wD], [B_sgt])
                P.tt("pool", oT[:, 2:4, tc0:tc0 + 128], sgt, ut, ALU.mult, [B_sgt, But], [B_oT])
            if debug and l == 0:
                for t in range(lo, hi):
                    P.dma("sp", dbg_o[t], oT[:, :, (t - lo) * 128:(t - lo + 1) * 128], R=[B_oT], W=[Bdbg], key="dbgo")
            load_h(hblk, l, lo, hi, B_hblk, "hblk", l == 0)
            for c in range(8):
                gt, Bgt = gt_rr.next()
                P.dma("sp", gt[:, 0:hi - lo], g_scr[lo:hi, c].rearrange("t p b n -> p t b n"), R=rng(Bg, lo, hi), W=[Bgt],
                      key="gt%d" % (gt_rr.i % 2))
                b0 = (c % 2) * 3
                brs = [(wpo, 2, 0), (wso, 2, 2), (wao, 4, 4)]
                for bi_, (wt_, nkk, o0) in enumerate(brs):
                    for k in range(nkk):
                        P.mm(bank(b0 + bi_)[:, 0:n], wt_[:, k, c * 128:(c + 1) * 128], oT[:, o0 + k, 0:n], k == 0, k == nkk - 1,
                             [B_wD, B_oT], [Bk[b0 + bi_]])
                for bi_ in range(3):
                    et, Bet = e_t[bi_]
                    P.tt("dve", et[:, 0:n].rearrange("p (t n) -> p t n", n=128),
                         bank(b0 + bi_)[:, 0:n].rearrange("p (t n) -> p t n", n=128), gt[:, 0:hi - lo, bi_, :],
                         ALU.mult, [Bk[b0 + bi_], Bgt], [Bet])
                et3, Bet3 = e_t[3]
                P.tt("pool", et3[:, 0:n], e_t[0][0][:, 0:n], e_t[1][0][:, 0:n], ALU.add, [e_t[0][1], e_t[1][1]], [Bet3])
                P.tt("pool", yT[:, c, 0:n], et3[:, 0:n], e_t[2][0][:, 0:n], ALU.add, [Bet3, e_t[2][1]], [B_yT])
            for c2 in range(8):
                ob = 6 + (c2 % 2)
                for c in range(8):
                    P.mm(bank(ob)[:, 0:n], wo[:, c, c2 * 128:(c2 + 1) * 128], yT[:, c, 0:n], c == 0, c == 7,
                         [B_wD, B_yT], [Bk[ob]])
                P.stt("dve", hblk[:, c2, 0:n], bank(ob)[:, 0:n], MT(l, st, 2)[:, c2:c2 + 1], hblk[:, c2, 0:n],
                      ALU.mult, ALU.add, [Bk[ob], B_hblk, B_modt], [B_hblk])
            P.dma("sp", h_scr[:, :, lo * 128:hi * 128].rearrange("c p n -> p c n"), hblk[:, :, 0:n], R=[B_hblk],
                  W=rng(Bh, lo, hi), key="hst_out")
        stop("E", l)

        P.barrier()
        AR.release(m0)
        hF = AR.alloc([8, 1024], F32)
        B_hF = Buf("hF")
        aF = AR.alloc([8, 1024], BF16)
        B_aF = Buf("aF")
        hid = AR.alloc([32, 1024], BF16)
        B_hid = Buf("hid")
        ostg = (AR.alloc([8, 512], F32), Buf("ostg"))
        rsbF = (AR.alloc([512], F32), Buf("rsF"))
        tmpF_rr = RR([(AR.alloc([512], F32), Buf("tmpF%d" % i)) for i in range(2)])
        w1_rr = RR([(AR.alloc([4, 8, 128], BF16), Buf("w1b%d" % i)) for i in range(2)])
        w2_rr = RR([(AR.alloc([32, 128], BF16), Buf("w2b%d" % i)) for i in range(2)])
        rl_rr = RR([(AR.alloc([512], F32), Buf("rl%d" % i)) for i in range(2)])
        f1_rr = RR([(bank(i), Bk[i]) for i in (1, 2, 3, 4)])
        f2_rr = RR([(bank(i), Bk[i]) for i in (5, 6, 7)])
        for grp in segs_for("F", l):
            offs = []
            o_ = 0
            for (lo, hi, st) in grp:
                offs.append(o_)
                o_ += (hi - lo) * 128
            for (lo, hi, st), o0 in zip(grp, offs):
                n = (hi - lo) * 128
                P.dma("sp", hF[:, :, o0:o0 + n], h_scr[:, :, lo * 128:hi * 128].rearrange("c p n -> p c n"),
                      R=rng(Bh, lo, hi), W=[B_hF], key="hF")
            for (lo, hi, st), o0 in zip(grp, offs):
                n = (hi - lo) * 128
                rms_to_aT(hF[:, :, o0:o0 + n], B_hF, n, MT(l, st, 3), MT(l, st, 4), aF[:, :, o0:o0 + n], [B_aF],
                          rsbF, tmpF_rr, (bank(0), Bk[0]))
            for jp in range(8):
                w1t, Bw1 = w1_rr.next()
                P.dma("pool", w1t, w1r[l, jp * 4:jp * 4 + 4].rearrange("j p k m -> p j k m"), W=[Bw1],
                      key="w1b%d" % (w1_rr.i % 2), max_dma_last_dim=4096)
                for jj in range(4):
                    j = jp * 4 + jj
                    for (lo, hi, st), o0 in zip(grp, offs):
                        n = (hi - lo) * 128
                        pf, Bpf = f1_rr.next()
                        for k in range(8):
                            P.mm(pf[:, 0:n], w1t[:, jj, k, :], aF[:, k, o0:o0 + n], k == 0, k == 7, [Bw1, B_aF], [Bpf])
                        rl, Brl = rl_rr.next()
                        P.act(rl[:, 0:n], pf[:, 0:n], AF.Relu, [Bpf], [Brl])
                        P.stt("dve", hid[:, j, o0:o0 + n], pf[:, 0:n], 0.0, rl[:, 0:n], ALU.max, ALU.mult,
                              [Bpf, Brl], [B_hid])
            for c2 in range(8):
                w2t, Bw2 = w2_rr.next()
                P.dma("pool", w2t, w2r[l, c2], W=[Bw2], key="w2b%d" % (w2_rr.i % 2), max_dma_last_dim=4096)
                for (lo, hi, st), o0 in zip(grp, offs):
                    n = (hi - lo) * 128
                    pf, Bpf = f2_rr.next()
                    for j in range(32):
                        P.mm(pf[:, 0:n], w2t[:, j, :], hid[:, j, o0:o0 + n], j == 0, j == 31, [Bw2, B_hid], [Bpf])
                    P.stt("dve", hF[:, c2, o0:o0 + n], pf[:, 0:n], MT(l, st, 5)[:, c2:c2 + 1], hF[:, c2, o0:o0 + n],
                          ALU.mult, ALU.add, [Bpf, B_hF, B_modt], [B_hF])
            for (lo, hi, st), o0 in zip(grp, offs):
                n = (hi - lo) * 128
                if not last:
                    P.dma("sp", h_scr[:, :, lo * 128:hi * 128].rearrange("c p n -> p c n"), hF[:, :, o0:o0 + n],
                          R=[B_hF], W=rng(Bh, lo, hi), key="hF_out")
                else:
                    og, Bog = ostg
                    rs, Brs = rsbF
                    rms_rstd(hF[:, :, o0:o0 + n], B_hF, n, rsbF, (bank(0), Bk[0]))
                    for c in range(8):
                        P.stt("dve", og[:, c, 0:n], hF[:, c, o0:o0 + n], V_fg[:, c:c + 1], rs[:, 0:n], ALU.mult, ALU.mult,
                              [B_hF, Brs, B_vec], [Bog])
                    op = P.dma("sp", outT[:, :, (lo - 4) * 128:(hi - 4) * 128].rearrange("c p n -> p c n"), og[:, :, 0:n],
                               R=[Bog], W=[Bout], key="outst")
                    out_ops.append(op)
        stop("F", l)

    P.halted = False
    if not out_ops:
        z = AR.t[:, 0:2048]
        out_ops.append(P.dma("sp", outT[0], z, R=[], W=[Bout], key="outst"))
    if debug:
        out_ops.append(P.dma("sp", outT[1, :, 0:8], vec[:, 0:8], R=[Bdbg], W=[Bout], key="dbgfin"))
    P.final_waits = out_ops
    P.emit()
    return nc, P, AR


def _fm(v):
    v = np.asarray(v, np.float32)
    return np.ascontiguousarray(v.reshape(-1, 128).T)


def _rope_perm():
    idx = []
    for h in range(8):
        idx += [h * 64 + 2 * i for i in range(32)] + [h * 64 + 2 * i + 1 for i in range(32)]
    return np.array(idx)


def _shared_inputs(inp):
    w_in = np.asarray(inp["w_in"], np.float32)
    perm = _rope_perm()
    wp = w_in.copy()
    wp[:, :, 768:1280] = w_in[:, :, 768:1280][:, :, perm]
    wp[:, :, 1280:1792] = w_in[:, :, 1280:1792][:, :, perm]
    tm_cols = [np.r_[0:256, 512:768], np.r_[1792:2304]]
    wA_tm = np.stack([np.stack([wp[l][:, cols].reshape(8, 128, 512).transpose(1, 0, 2) for cols in tm_cols])
                      for l in range(2)])
    fm_starts = [256, 384] + [768 + 128 * i for i in range(8)] + [2304 + 128 * i for i in range(24)]
    wA_fm = np.stack([np.stack([wp[l][:, s:s + 128].reshape(8, 128, 128).transpose(1, 0, 2) for s in fm_starts])
                      for l in range(2)])
    w1 = np.asarray(inp["w_ff1"], np.float32)
    w1r = np.stack([np.stack([w1[l][:, j * 128:(j + 1) * 128].reshape(8, 128, 128).transpose(1, 0, 2)
                              for j in range(32)]) for l in range(2)])
    w2 = np.asarray(inp["w_ff2"], np.float32)
    w2r = np.stack([np.stack([w2[l][:, c * 128:(c + 1) * 128].reshape(32, 128, 128).transpose(1, 0, 2)
                              for c in range(8)]) for l in range(2)])
    sgu_w = np.asarray(inp["sgu_w"], np.float32)
    sguw = np.ascontiguousarray(sgu_w.transpose(3, 0, 1, 2))
    sgu_b = np.asarray(inp["sgu_b"], np.float32)
    sgub = np.zeros((128, 2, 2, 128), np.float32)
    for part in range(128):
        for c in range(2):
            sgub[part, :, c, :] = sgu_b[:, 2 * c + part // 64, :]
    w_pool = np.asarray(inp["w_pool"], np.float32)
    wblk = np.zeros((128, 2, 2, 128), np.float32)
    for c in range(2):
        for gl in range(2):
            wblk[gl * 64:(gl + 1) * 64, :, c, gl * 64:(gl + 1) * 64] = w_pool[:, 2 * c + gl].transpose(1, 0, 2)
    consts = np.zeros((128, 3, 128), np.float32)
    consts[:, 0, :] = np.eye(128)
    for pp in range(128):
        partner = pp + 32 if (pp % 64) < 32 else pp - 32
        consts[partner, 1, pp] = 1.0
    consts[:, 2, :] = 1.0
    return dict(w_mod=np.ascontiguousarray(inp["w_mod"], np.float32), wA_tm=np.ascontiguousarray(wA_tm),
                wA_fm=np.ascontiguousarray(wA_fm), sguw=sguw, sgub=sgub, wblk=wblk,
                w_pool_out=np.ascontiguousarray(inp["w_pool_out"], np.float32),
                w_sgu_out=np.ascontiguousarray(inp["w_sgu_out"], np.float32),
                w_attn_out=np.ascontiguousarray(inp["w_attn_out"], np.float32),
                w_o=np.ascontiguousarray(inp["w_o"], np.float32), w1r=np.ascontiguousarray(w1r),
                w2r=np.ascontiguousarray(w2r), consts=consts)


def _band_set(kind):
    out = np.zeros((128, 12, 128), np.float32)
    L = 384
    base = 128
    for g, w in enumerate((2, 4, 8, 16)):
        for tt in range(128):
            pos = base + tt
            lo_b = base if kind == 1 else 0
            hi_b = base + 128 if kind == 2 else L
            lo = min(max(pos - w // 2, lo_b), hi_b)
            hi = min(max(pos + (w - w // 2), lo_b), hi_b)
            cnt = hi - lo
            for s in range(lo, hi):
                d = s // 128
                out[s % 128, g * 3 + d, tt] += 1.0 / cnt
            out[tt, g * 3 + 1, tt] -= 1.0
    return out


def _bias_tables(rpb, core_rows0, n_rows_total=128):
    out = np.full((2, 6, 8, 128, 1024), NEG, np.float32)
    q_i = np.arange(128)
    for ty, j in ((0, 4), (1, 0), (2, 1), (3, 14), (4, 15)):
        klo, khi = j - 2, j + 3
        if j == 0:
            khi = j + 4
        if j == 15:
            klo = j - 3
        nkl = (khi - klo) * 128
        key = np.arange(nkl)
        k_row = core_rows0 + 2 * klo + key // 64
        k_col = key % 64
        q_row = core_rows0 + 2 * j + q_i // 64
        q_col = q_i % 64
        rs = np.clip(q_row - 4, 0, n_rows_total - 8)
        cs = np.clip(q_col - 8, 0, 64 - 16)
        valid = ((k_row[None, :] >= rs[:, None]) & (k_row[None, :] < rs[:, None] + 8) &
                 (k_col[None, :] >= cs[:, None]) & (k_col[None, :] < cs[:, None] + 16) &
                 (k_row[None, :] >= 0) & (k_row[None, :] < n_rows_total))
        dr = np.clip(k_row[None, :] - q_row[:, None] + 7, 0, 14)
        dc = np.clip(k_col[None, :] - q_col[:, None] + 15, 0, 30)
        for l in range(2):
            for h in range(8):
                g = rpb[l, h][dr, dc]
                out[l, ty, h, :, 0:nkl] = np.where(valid, g, NEG)
                out[l, ty, h, :, nkl:nkl + 256] = 0.0
    out[:, 5, :, :, 0:256] = 0.0
    return out.astype(ml_dtypes.bfloat16)


def _core_inputs(inp, core):
    b, blk = core // 4, core % 4
    row0 = 32 * blk
    x = np.asarray(inp["x"], np.float32)[b]
    t0 = (row0 - 8) * 64
    xs = np.zeros((NLAT, D), np.float32)
    lo, hi = max(t0, 0), min(t0 + NLAT, 8192)
    xs[lo - t0:hi - t0] = x[lo:hi]
    xT = np.ascontiguousarray(xs.T.reshape(8, 128, NLAT))
    ctxT = np.ascontiguousarray(np.asarray(inp["ctx"], np.float32)[b].T.reshape(8, 128, 256))
    vecs = np.zeros((128, 156), np.float32)
    vecs[:, 0:8] = _fm(inp["c"][b])
    vecs[:, 8:16] = _fm(inp["c_ctx"])
    for l in range(2):
        vecs[:, 16 + l * 48:16 + (l + 1) * 48] = _fm(inp["b_mod"][l])
        vecs[:, 112 + l * 8:112 + (l + 1) * 8] = _fm(inp["norm1_g"][l])
        vecs[:, 128 + l * 8:128 + (l + 1) * 8] = _fm(inp["norm2_g"][l])
        vecs[:, 152 + l * 2:152 + (l + 1) * 2] = _fm(inp["pool_scale"][l])
    vecs[:, 144:152] = _fm(inp["final_g"])
    tok = np.arange(NLAT)
    row = (row0 - 8 + tok // 64).astype(np.float32)
    col = (tok % 64).astype(np.float32)
    inv_freq = (10000.0 ** (-np.arange(16, dtype=np.float32) / 16)).astype(np.float32)
    ang = np.concatenate([row[:, None] * inv_freq, col[:, None] * inv_freq], axis=-1).astype(np.float32)
    cos, sin = np.cos(ang), np.sin(ang)
    rope = np.zeros((2, 128, NLAT), np.float32)
    for pp in range(128):
        i = pp % 64
        e = i % 32
        rope[0, pp] = cos[:, e]
        rope[1, pp] = -sin[:, e] if i < 32 else sin[:, e]
    first = (blk == 0)
    lastb = (blk == 3)
    gen = _band_set(0)
    bands = np.stack([gen, _band_set(1) if first else gen, _band_set(2) if lastb else gen, _band_set(1), _band_set(2)])
    if first:
        bands[1][:, [0, 3, 6, 9], :] = 0.0
    bias = _bias_tables(np.asarray(inp["na_rpb"], np.float32), row0)
    return dict(xT=xT, ctxT=ctxT, vecs=vecs, rope=rope, bands=np.ascontiguousarray(bands), bias=bias)


_PROG = {}


def kernel(**inputs):
    if "nc" not in _PROG:
        _PROG["nc"] = build_program()[0]
    nc = _PROG["nc"]
    shared = _shared_inputs(inputs)
    in_maps = []
    for core in range(8):
        m = dict(shared)
        m.update(_core_inputs(inputs, core))
        in_maps.append(m)
    res = run_bass_kernel_spmd(nc, in_maps, core_ids=list(range(8)))
    out = np.zeros((2, 8192, D), np.float32)
    for core in range(8):
        b, blk = core // 4, core % 4
        oT = np.asarray(res.results[core]["outT"], np.float32)
        out[b, blk * 2048:(blk + 1) * 2048, :] = oT.reshape(D, 2048).T
    return out
```

```python
import numpy as np
import ml_dtypes
import concourse.bass as bass
import concourse.mybir as mybir
from concourse.bass_utils import run_bass_kernel_spmd

F32 = mybir.dt.float32
F32R = mybir.dt.float32r
BF16 = mybir.dt.bfloat16
AF = mybir.ActivationFunctionType
ALU = mybir.AluOpType
AX = mybir.AxisListType

D = 1024
NLS = 24
NS = 26
NLAT = NLS * 128
TOK = NS * 128
EPS = 1e-6
NEG = -30000.0
DBG_SKIP_LN = False


class Buf:
    __slots__ = ("name", "last_writer", "readers")

    def __init__(self, name):
        self.name = name
        self.last_writer = None
        self.readers = []


class Op:
    __slots__ = ("eng", "fn", "deps", "is_dma", "key", "idx", "signal", "sem", "val", "waits")

    def __init__(self, eng, fn, is_dma, key, idx):
        self.eng = eng
        self.fn = fn
        self.is_dma = is_dma
        self.key = key
        self.idx = idx
        self.deps = []
        self.signal = False
        self.sem = None
        self.val = 0
        self.waits = []


ENGS = ("pe", "act", "dve", "pool", "sp")


class Prog:
    def __init__(self, nc):
        self.nc = nc
        self.ops = []
        self.final_waits = []
        self.phase_buf = Buf("phase")
        self.bar_ap = None
        self.halted = False
        self.dummy = Op("dve", None, False, None, -1)

    def add(self, eng, fn, reads=(), writes=(), dma_key=None):
        if self.halted:
            return self.dummy
        is_dma = dma_key is not None
        op = Op(eng, fn, is_dma, dma_key, len(self.ops))
        deps = {}
        for b in reads:
            w = b.last_writer
            if w is not None:
                deps[w.idx] = [w, True]
        for b in writes:
            w = b.last_writer
            if w is not None and w.idx not in deps:
                deps[w.idx] = [w, False]
            for r in b.readers:
                if r.idx not in deps:
                    deps[r.idx] = [r, False]
        pb = self.phase_buf
        if pb.last_writer is not None:
            deps[pb.last_writer.idx] = [pb.last_writer, True]
        pb.readers.append(op)
        for b in reads:
            b.readers.append(op)
        for b in writes:
            b.last_writer = op
            b.readers = []
        deps.pop(op.idx, None)
        op.deps = list(deps.values())
        self.ops.append(op)
        return op

    def barrier(self):
        if self.halted:
            return self.dummy
        pb = self.phase_buf
        op = Op("dve", lambda e: e.memset(self.bar_ap, 0.0), False, None, len(self.ops))
        deps = {}
        for r in pb.readers:
            deps[r.idx] = [r, True]
        if pb.last_writer is not None:
            deps[pb.last_writer.idx] = [pb.last_writer, True]
        op.deps = list(deps.values())
        pb.last_writer = op
        pb.readers = []
        self.ops.append(op)
        return op

    def dma(self, eng, out, in_, R=(), W=(), key=None, **kw):
        return self.add(eng, lambda e: e.dma_start(out=out, in_=in_, **kw), R, W, dma_key=key)

    def mm(self, out, lhsT, rhs, start, stop, R, W):
        return self.add("pe", lambda e: e.matmul(out, lhsT=lhsT, rhs=rhs, start=start, stop=stop), R, W)

    def tr(self, out, in_, ident, R, W):
        return self.add("pe", lambda e: e.transpose(out=out, in_=in_, identity=ident), R, W)

    def act(self, out, in_, func, R, W, **kw):
        return self.add("act", lambda e: e.activation(out=out, in_=in_, func=func, **kw), R, W)

    def copy(self, eng, out, in_, R, W):
        if eng == "act":
            return self.add("act", lambda e: e.copy(out=out, in_=in_), R, W)
        return self.add(eng, lambda e: e.tensor_copy(out=out, in_=in_), R, W)

    def tt(self, eng, out, in0, in1, op, R, W):
        return self.add(eng, lambda e: e.tensor_tensor(out=out, in0=in0, in1=in1, op=op), R, W)

    def ts(self, eng, out, in0, s1, s2, op0, op1, R, W):
        return self.add(eng, lambda e: e.tensor_scalar(out=out, in0=in0, scalar1=s1, scalar2=s2, op0=op0, op1=op1), R, W)

    def stt(self, eng, out, in0, scalar, in1, op0, op1, R, W):
        return self.add(eng, lambda e: e.scalar_tensor_tensor(out=out, in0=in0, scalar=scalar, in1=in1,
                                                              op0=op0, op1=op1), R, W)

    def emit(self):
        nc = self.nc
        ops = self.ops
        for op in ops:
            need = []
            for d, raw in op.deps:
                if d.is_dma or op.is_dma or d.eng != op.eng or (raw and op.eng != "pe"):
                    need.append(d)
            op.waits = need
            for d in need:
                d.signal = True
        for op in self.final_waits:
            op.signal = True
        for op in ops:
            if op.is_dma:
                op.signal = True
        sems = {}
        counters = {}
        for op in ops:
            if not op.signal:
                continue
            k = ("dma", op.key) if op.is_dma else ("eng", op.eng)
            if k not in sems:
                sems[k] = nc.alloc_semaphore("s%d" % len(sems))
                counters[k] = 0
            op.sem = sems[k]
            counters[k] += 16 if op.is_dma else 1
            op.val = counters[k]
            op.key = k
        self.n_sems = len(sems)
        per_eng = {e: [] for e in ENGS}
        for op in ops:
            per_eng[op.eng].append(op)
        finals = list(self.final_waits)

        def run(eng_name, eng):
            waited = {}
            for op in per_eng[eng_name]:
                req = {}
                for d in op.waits:
                    if req.get(d.key, 0) < d.val:
                        req[d.key] = d.val
                for k, v in req.items():
                    if waited.get(k, 0) >= v:
                        continue
                    waited[k] = v
                    eng.wait_ge(sems[k], v)
                inst = op.fn(eng)
                if op.signal:
                    inst.then_inc(op.sem, 16 if op.is_dma else 1)
            if eng_name == "sp":
                for d in finals:
                    if waited.get(d.key, 0) < d.val:
                        waited[d.key] = d.val
                        eng.wait_ge(sems[d.key], d.val)

        with nc.Block() as block:
            @block.tensor
            def _(e):
                run("pe", e)

            @block.scalar
            def _(e):
                run("act", e)

            @block.vector
            def _(e):
                run("dve", e)

            @block.gpsimd
            def _(e):
                run("pool", e)

            @block.sync
            def _(e):
                run("sp", e)


class Arena:
    def __init__(self, nc, nbytes):
        self.t = nc.alloc_sbuf_tensor("arena", [128, nbytes // 4], F32)
        self.n = nbytes
        self.off = 0
        self.peak = 0

    def alloc(self, shape, dtype):
        esz = 2 if dtype == BF16 else 4
        n = int(np.prod(shape)) * esz
        n4 = (n + 31) // 32 * 32
        assert self.off + n4 <= self.n, ("SBUF arena overflow", self.off, n4, self.n)
        a = self.t[:, self.off // 4:(self.off + n) // 4]
        self.off += n4
        self.peak = max(self.peak, self.off)
        if dtype != F32:
            a = a.bitcast(dtype)
        if len(shape) == 2:
            return a.rearrange("p (a b) -> p a b", a=shape[0])
        if len(shape) == 3:
            return a.rearrange("p (a b c) -> p a b c", a=shape[0], b=shape[1])
        return a

    def mark(self):
        return self.off

    def release(self, m):
        self.off = m


class RR:
    def __init__(self, items):
        self.items = items
        self.i = 0

    def next(self):
        it = self.items[self.i % len(self.items)]
        self.i += 1
        return it


def segs_for(kind, layer):
    if kind == "A":
        if layer == 0:
            s = [(4 * b, 4 * b + 4, 0) for b in range(6)]
        else:
            s = [(2, 4, 0)] + [(4 * b, 4 * b + 4, 0) for b in range(1, 5)] + [(20, 22, 0)]
        return s + [(24, 26, 1)]
    if kind == "DE":
        s = [(4 * b, 4 * b + 4, 0) for b in range(1, 5)]
        if layer == 0:
            s = [(2, 4, 0)] + s + [(20, 22, 0), (24, 26, 1)]
        return s
    if kind == "F":
        g = [[(4, 8, 0), (8, 12, 0)], [(12, 16, 0), (16, 20, 0)]]
        if layer == 0:
            g.append([(2, 4, 0), (20, 22, 0), (24, 26, 1)])
        return g
    raise ValueError(kind)


def build_program(n_layers=2, stop_after=None, debug=False):
    nc = bass.Bass("TRN2", target_bir_lowering=False)
    P = Prog(nc)

    def din(name, shape, dt=F32):
        return nc.dram_tensor(name, list(shape), dt, kind="ExternalInput").ap()

    xT = din("xT", [8, 128, NLAT])
    ctxT = din("ctxT", [8, 128, 256])
    vecs = din("vecs", [128, 156])
    consts = din("consts", [128, 3, 128])
    rope = din("rope", [2, 128, NLAT])
    bands = din("bands", [5, 128, 12, 128])
    biasd = din("bias", [2, 6, 8, 128, 1024], BF16)
    w_mod = din("w_mod", [2, 1024, 6144])
    wA_tm = din("wA_tm", [2, 2, 128, 8, 512])
    wA_fm = din("wA_fm", [2, 34, 128, 8, 128])
    sguw = din("sguw", [128, 2, 4, 128])
    sgub_d = din("sgub", [128, 2, 2, 128])
    wblk_d = din("wblk", [128, 2, 2, 128])
    w_po = din("w_pool_out", [2, 256, 1024])
    w_so = din("w_sgu_out", [2, 256, 1024])
    w_ao = din("w_attn_out", [2, 512, 1024])
    w_o = din("w_o", [2, 1024, 1024])
    w1r = din("w1r", [2, 32, 128, 8, 128])
    w2r = din("w2r", [2, 8, 128, 32, 128])
    outT = nc.dram_tensor("outT", [8, 128, 2048], F32, kind="ExternalOutput").ap()

    skind = "ExternalOutput" if debug else "Internal"

    def dscr(name, shape, dt=F32):
        return nc.dram_tensor(name, list(shape), dt, kind=skind).ap()

    h_scr = dscr("h_scr", [8, 128, TOK])
    q_scr = dscr("q_scr", [NS, 128, 4, 128], BF16)
    u_scr = dscr("u_scr", [NS, 128, 2, 128])
    vn_scr = dscr("vn_scr", [NS, 128, 256], BF16)
    g_scr = dscr("g_scr", [NS, 8, 128, 3, 128], BF16)
    if debug:
        dbg_k = dscr("dbg_k", [128, 4, TOK], BF16)
        dbg_v = dscr("dbg_v", [128, NS, 512], BF16)
        dbg_p = dscr("dbg_p", [128, NS, 256], BF16)
        dbg_a = dscr("dbg_a", [128, 8, TOK], BF16)
        dbg_mod = dscr("dbg_mod", [128, 2, 2, 48])
        dbg_o = dscr("dbg_o", [NS, 128, 8, 128], BF16)

    Bh = [[Buf("h%d_%d" % (t, c)) for c in range(8)] for t in range(NS)]
    Bq = [Buf("q%d" % t) for t in range(NS)]
    Bu = [Buf("u%d" % t) for t in range(NS)]
    Bvn = [Buf("vn%d" % t) for t in range(NS)]
    Bg = [Buf("g%d" % t) for t in range(NS)]
    Bout = Buf("out")
    Bdbg = Buf("dbg")

    psd = [nc.alloc_psum_tensor("psd%d" % i, [128, 1024], F32) for i in range(4)]
    Bk = [Buf("bank%d" % i) for i in range(8)]

    def bank(i):
        return psd[i // 2][:, (i % 2) * 512:(i % 2) * 512 + 512]

    AR = Arena(nc, 197 * 1024)
    bar_t = AR.alloc([8], F32)
    P.bar_ap = bar_t
    cst = AR.alloc([3, 128], F32)
    identb = AR.alloc([128], BF16)
    perm_r = nc.alloc_sbuf_tensor("perm_r", [128, 128], F32R)[:]
    ones_r = nc.alloc_sbuf_tensor("ones_r", [128, 128], F32R)[:]
    sq_rr = RR([(nc.alloc_sbuf_tensor("sq_r%d" % i, [128, 512], F32R)[:], Buf("sq_r%d" % i)) for i in range(2)])
    qf_rr = RR([(nc.alloc_sbuf_tensor("qf_r%d" % i, [128, 512], F32R)[:], Buf("qf_r%d" % i)) for i in range(2)])
    vec = AR.alloc([156], F32)
    modt = AR.alloc([24, 8], F32)
    silu_b = AR.alloc([2, 8], BF16)
    B_c = Buf("consts")
    B_vec = Buf("vec")
    B_modt = Buf("modt")
    B_silu = Buf("silu")

    def V_c(s):
        return vec[:, s * 8:(s + 1) * 8]

    def V_bmod(l):
        return vec[:, 16 + l * 48:16 + (l + 1) * 48]

    def V_n1(l):
        return vec[:, 112 + l * 8:112 + (l + 1) * 8]

    def V_n2(l):
        return vec[:, 128 + l * 8:128 + (l + 1) * 8]

    V_fg = vec[:, 144:152]

    def V_ps(l):
        return vec[:, 152 + l * 2:152 + (l + 1) * 2]

    def MT(l, s, kind):
        i = (l * 2 + s) * 6 + kind
        return modt[:, i, :]

    P.dma("sp", cst, consts, W=[B_c], key="cst")
    P.dma("sp", vec, vecs, W=[B_vec], key="vec")
    B_id = Buf("ident")
    P.copy("dve", identb, cst[:, 0, :], [B_c], [B_id])
    P.copy("dve", perm_r, cst[:, 1, :], [B_c], [B_id])
    P.copy("dve", ones_r, cst[:, 2, :], [B_c], [B_id])
    P.act(silu_b.rearrange("p s k -> p (s k)"), vec[:, 0:16], AF.Silu, [B_vec], [B_silu])

    def stop(name, l=0):
        if stop_after is not None and tuple(stop_after) == (name, l):
            P.halted = True

    stop("S")

    m0 = AR.mark()
    wm = [(AR.alloc([8, 512], BF16), Buf("wm%d" % i)) for i in range(2)]
    wm_rr = RR(wm)
    modraw = AR.alloc([48, 2], F32)
    tmpm = AR.alloc([8, 2], F32)
    B_modraw = Buf("modraw")
    for l in range(n_layers):
        psm = bank(0).rearrange("p (a b) -> p a b", b=2)[:, 0:48, :]
        for pc in range(12):
            wt, wb = wm_rr.next()
            P.dma("pool", wt, w_mod[l, :, pc * 512:(pc + 1) * 512].rearrange("(k p) n -> p k n", p=128),
                  W=[wb], key="wm%d" % (wm_rr.i % 2), max_dma_last_dim=4096)
            for cc in range(4):
                ch = pc * 4 + cc
                for k in range(8):
                    P.mm(psm[:, ch, :], wt[:, k, cc * 128:(cc + 1) * 128], silu_b[:, :, k], k == 0, k == 7,
                         [wb, B_silu], [Bk[0]])
        P.tt("dve", modraw, psm, V_bmod(l).unsqueeze(2).broadcast_to([128, 48, 2]), ALU.add,
             [Bk[0], B_vec], [B_modraw])
        for s in range(2):
            def chunk(i):
                return modraw[:, i * 8:(i + 1) * 8, s]
            P.stt("dve", MT(l, s, 0), chunk(1), 1.0, V_n1(l), ALU.add, ALU.mult, [B_modraw, B_vec], [B_modt])
            P.copy("dve", MT(l, s, 1), chunk(0), [B_modraw], [B_modt])
            P.copy("dve", MT(l, s, 2), chunk(2), [B_modraw], [B_modt])
            P.stt("dve", MT(l, s, 3), chunk(4), 1.0, V_n2(l), ALU.add, ALU.mult, [B_modraw, B_vec], [B_modt])
            P.copy("dve", MT(l, s, 4), chunk(3), [B_modraw], [B_modt])
            P.copy("dve", MT(l, s, 5), chunk(5), [B_modraw], [B_modt])
    if debug:
        P.dma("sp", dbg_mod.rearrange("p l s c -> p (l s c)"), modt.rearrange("p a b -> p (a b)")[:, 0:192],
              R=[B_modt], W=[Bdbg], key="dbgm")
    AR.release(m0)
    stop("M")
    P.barrier()

    kT_all = AR.alloc([4, TOK], BF16)
    v_all = AR.alloc([NS, 512], BF16)
    p_all = AR.alloc([NS, 256], BF16)
    Bkt = [Buf("kT%d" % t) for t in range(NS)]
    Bv = [Buf("v%d" % t) for t in range(NS)]
    Bp = [Buf("p%d" % t) for t in range(NS)]
    res_mark = AR.mark()

    def rng(bufs, lo, hi):
        return [bufs[t] for t in range(lo, hi)]

    def h_src(l, c, lo, hi):
        if l == 0:
            if lo >= 24:
                return ctxT[c, :, (lo - 24) * 128:(hi - 24) * 128]
            return xT[c, :, lo * 128:hi * 128]
        return h_scr[c, :, lo * 128:hi * 128]

    def load_h(dst, l, lo, hi, Bdst, key, first_layer_input):
        n = (hi - lo) * 128
        if first_layer_input:
            src = (ctxT[:, :, (lo - 24) * 128:(hi - 24) * 128] if lo >= 24 else xT[:, :, lo * 128:hi * 128])
            R = []
        else:
            src = h_scr[:, :, lo * 128:hi * 128]
            R = [b_ for t in range(lo, hi) for b_ in Bh[t]]
        P.dma("sp", dst[:, :, 0:n], src.rearrange("c p n -> p c n"), R=R, W=[Bdst], key=key)

    def rms_rstd(hb, Bhb, n, rsb, psb):
        rs, Brs = rsb
        pb, Bpb = psb
        for c in range(8):
            sq, Bsq = sq_rr.next()
            P.act(sq[:, 0:n], hb[:, c, 0:n], AF.Square, [Bhb], [Bsq])
            P.mm(pb[:, 0:n], ones_r, sq[:, 0:n], c == 0, c == 7, [Bsq, B_id], [Bpb])
        P.ts("dve", rs[:, 0:n], pb[:, 0:n], 1.0 / D, EPS, ALU.mult, ALU.add, [Bpb], [Brs])
        P.act(rs[:, 0:n], rs[:, 0:n], AF.Sqrt, [Brs], [Brs])
        P.add("dve", lambda e: e.reciprocal(out=rs[:, 0:n], in_=rs[:, 0:n]), [Brs], [Brs])

    def rms_to_aT(hb, Bhb, n, gs, sh, dst, Wdst, rsb, tmp_rr, psb):
        rs, Brs = rsb
        rms_rstd(hb, Bhb, n, rsb, psb)
        for c in range(8):
            tmp, Btmp = tmp_rr.next()
            P.stt("dve", tmp[:, 0:n], hb[:, c, 0:n], gs[:, c:c + 1], rs[:, 0:n], ALU.mult, ALU.mult,
                  [Bhb, Brs, B_modt], [Btmp])
            P.act(dst[:, c, 0:n], tmp[:, 0:n], AF.Identity, [Btmp, B_modt], Wdst, bias=sh[:, c:c + 1], scale=1.0)

    out_ops = []

    for l in range(n_layers):
        last = (l == n_layers - 1)
        if l > 0:
            P.barrier()
        AR.release(res_mark)
        aT_all = AR.alloc([8, TOK], BF16)
        Ba = [Buf("aT%d" % t) for t in range(NS)]
        a_mark = AR.mark()
        hst = [(AR.alloc([8, 512], F32), Buf("hst%d" % i)) for i in range(2)]
        hst_rr = RR(hst)
        rsb = (AR.alloc([512], F32), Buf("rs"))
        tmp_rr = RR([(AR.alloc([512], F32), Buf("tmp%d" % i)) for i in range(2)])
        segsA = segs_for("A", l)
        for si, (lo, hi, st) in enumerate(segsA):
            n = (hi - lo) * 128
            hb, Bhb = hst_rr.next()
            load_h(hb, l, lo, hi, Bhb, "hst%d" % (hst_rr.i % 2), l == 0)
            bi = si % 2
            rms_to_aT(hb, Bhb, n, MT(l, st, 0), MT(l, st, 1), aT_all[:, :, lo * 128:hi * 128], rng(Ba, lo, hi), rsb,
                      tmp_rr, (bank(bi), Bk[bi]))
        if debug and l == 0:
            P.dma("sp", dbg_a, aT_all, R=Ba, W=[Bdbg], key="dbga")
        stop("A1", l)
        P.barrier()
        AR.release(a_mark)

        wbuf = [(AR.alloc([8192], BF16), Buf("wbuf%d" % i)) for i in range(2)]
        wb_rr = RR(wbuf)
        ropeb = [(AR.alloc([2, 512], F32), Buf("rope%d" % i)) for i in range(2)]
        rope_rr = RR(ropeb)
        t1_rr = RR([(AR.alloc([512], F32), Buf("t1%d" % i)) for i in range(2)])
        t2_rr = RR([(AR.alloc([512], F32), Buf("t2%d" % i)) for i in range(2)])
        qo_rr = RR([(AR.alloc([512], BF16), Buf("qo%d" % i)) for i in range(2)])
        us_rr = RR([(AR.alloc([512], F32), Buf("us%d" % i)) for i in range(3)])
        gs_rr = RR([(AR.alloc([512], BF16), Buf("gs%d" % i)) for i in range(3)])
        vns_rr = RR([(AR.alloc([256], BF16), Buf("vns%d" % i)) for i in range(2)])
        st_rr = RR([(AR.alloc([8], F32), Buf("st%d" % i)) for i in range(2)])
        vsf_rr = RR([(AR.alloc([256], F32), Buf("vsf%d" % i)) for i in range(4)])
        pj_rr = RR([(bank(i), Bk[i]) for i in range(4)])
        pm_rr = RR([(bank(i), Bk[i]) for i in (4, 5)])
        tm_rr = RR([(bank(i), Bk[i]) for i in (6, 7)])

        def load_w(src, shape):
            wt, wb = wb_rr.next()
            nel = int(np.prod(shape))
            view = wt[:, 0:nel]
            if len(shape) == 2:
                view = view.rearrange("p (a b) -> p a b", a=shape[0])
            else:
                view = view.rearrange("p (a b c) -> p a b c", a=shape[0], b=shape[1])
            P.dma("pool", view, src, W=[wb], key="wbuf%d" % (wb_rr.i % 2), max_dma_last_dim=4096)
            return view, wb

        for piece in range(2):
            wv, wb = load_w(wA_tm[l, piece], [8, 512])
            for (lo, hi, st) in segsA:
                for t in range(lo, hi):
                    pt, Bpt = tm_rr.next()
                    for k in range(8):
                        P.mm(pt, aT_all[:, k, t * 128:(t + 1) * 128], wv[:, k, :], k == 0, k == 7, [Ba[t], wb], [Bpt])
                    if piece == 0:
                        P.copy("act", p_all[:, t, :], pt[:, 0:256], [Bpt], [Bp[t]])
                        if DBG_SKIP_LN:
                            continue
                        stt_, Bst = st_rr.next()
                        vf, Bvf = vsf_rr.next()
                        P.act(vf, pt[:, 256:512], AF.Identity, [Bpt], [Bvf, Bst], accum_out=stt_[:, 0:1])
                        P.ts("dve", stt_[:, 1:2], stt_[:, 0:1], -1.0 / 256, None, ALU.mult, ALU.bypass, [Bst], [Bst])
                        jk, Bjk = vsf_rr.next()
                        P.act(jk, vf, AF.Square, [Bvf, Bst], [Bjk, Bst], bias=stt_[:, 1:2], scale=1.0, accum_out=stt_[:, 2:3])
                        P.ts("dve", stt_[:, 3:4], stt_[:, 2:3], 1.0 / 256, EPS, ALU.mult, ALU.add, [Bst], [Bst])
                        P.act(stt_[:, 3:4], stt_[:, 3:4], AF.Sqrt, [Bst], [Bst])
                        P.add("dve", lambda e, o=stt_[:, 3:4]: e.reciprocal(out=o, in_=o), [Bst], [Bst])
                        vs_, Bvs = vns_rr.next()
                        P.ts("dve", vs_, vf, stt_[:, 1:2], stt_[:, 3:4], ALU.add, ALU.mult, [Bvf, Bst], [Bvs])
                        P.dma("sp", vn_scr[t], vs_, R=[Bvs], W=[Bvn[t]], key="vns%d" % (vns_rr.i % 2))
                    else:
                        P.copy("act", v_all[:, t, :], pt, [Bpt], [Bv[t]])

        stop("A2", l)
        def proj_chunk(wv, wb, lo, hi):
            n = (hi - lo) * 128
            pj, Bpj = pj_rr.next()
            for k in range(8):
                P.mm(pj[:, 0:n], wv[:, k, :], aT_all[:, k, lo * 128:hi * 128], k == 0, k == 7,
                     rng(Ba, lo, hi) + [wb], [Bpj])
            return pj, Bpj, n

        wv, wb = load_w(wA_fm[l, 0:2].rearrange("c p k m -> p c k m"), [2, 8, 128])
        for (lo, hi, st) in segsA:
            for c in range(2):
                pj, Bpj, n = proj_chunk(wv[:, c], wb, lo, hi)
                us, Bus = us_rr.next()
                P.copy("act", us[:, 0:n], pj[:, 0:n], [Bpj], [Bus])
                P.dma("sp", u_scr[lo:hi, :, c, :].rearrange("t p n -> p t n"),
                      us[:, 0:n].rearrange("p (t n) -> p t n", n=128), R=[Bus], W=rng(Bu, lo, hi),
                      key="us%d" % (us_rr.i % 3))
        stop("A3", l)
        wv, wb = load_w(wA_fm[l, 2:10].rearrange("c p k m -> p c k m"), [8, 8, 128])
        for (lo, hi, st) in segsA:
            n = (hi - lo) * 128
            if st == 0:
                rt, Brt = rope_rr.next()
                P.dma("sp", rt[:, :, 0:n], rope[:, :, lo * 128:hi * 128].rearrange("a p n -> p a n"), W=[Brt],
                      key="rope%d" % (rope_rr.i % 2))
            for c in range(8):
                pj, Bpj, n = proj_chunk(wv[:, c], wb, lo, hi)
                isq = c < 4
                if st == 1:
                    if isq:
                        qo, Bqo = qo_rr.next()
                        P.copy("act", qo[:, 0:n], pj[:, 0:n], [Bpj], [Bqo])
                    else:
                        P.copy("act", kT_all[:, c - 4, lo * 128:hi * 128], pj[:, 0:n], [Bpj], rng(Bkt, lo, hi))
                else:
                    qf, Bqf = qf_rr.next()
                    P.copy("act", qf[:, 0:n], pj[:, 0:n], [Bpj], [Bqf])
                    pm, Bpm = pm_rr.next()
                    P.mm(pm[:, 0:n], perm_r, qf[:, 0:n], True, True, [Bqf, B_id], [Bpm])
                    t1, Bt1 = t1_rr.next()
                    t2, Bt2 = t2_rr.next()
                    P.tt("pool", t1[:, 0:n], qf[:, 0:n].bitcast(F32), rt[:, 0, 0:n], ALU.mult, [Bqf, Brt], [Bt1])
                    P.tt("dve", t2[:, 0:n], pm[:, 0:n], rt[:, 1, 0:n], ALU.mult, [Bpm, Brt], [Bt2])
                    if isq:
                        qo, Bqo = qo_rr.next()
                        P.tt("dve", qo[:, 0:n], t1[:, 0:n], t2[:, 0:n], ALU.add, [Bt1, Bt2], [Bqo])
                    else:
                        P.tt("dve", kT_all[:, c - 4, lo * 128:hi * 128], t1[:, 0:n], t2[:, 0:n], ALU.add,
                             [Bt1, Bt2], rng(Bkt, lo, hi))
                if isq:
                    P.dma("sp", q_scr[lo:hi, :, c, :].rearrange("t p n -> p t n"),
                          qo[:, 0:n].rearrange("p (t n) -> p t n", n=128), R=[Bqo], W=rng(Bq, lo, hi),
                          key="qo%d" % (qo_rr.i % 2))
        stop("A4", l)
        for gp in range(6):
            wv, wb = load_w(wA_fm[l, 10 + gp * 4:10 + gp * 4 + 4].rearrange("c p k m -> p c k m"), [4, 8, 128])
            for (lo, hi, st) in segsA:
                for cc in range(4):
                    gch = gp * 4 + cc
                    br, c = gch // 8, gch % 8
                    pj, Bpj, n = proj_chunk(wv[:, cc], wb, lo, hi)
                    gs_, Bgs = gs_rr.next()
                    P.act(gs_[:, 0:n], pj[:, 0:n], AF.Sigmoid, [Bpj], [Bgs])
                    P.dma("sp", g_scr[lo:hi, c, :, br, :].rearrange("t p n -> p t n"),
                          gs_[:, 0:n].rearrange("p (t n) -> p t n", n=128), R=[Bgs], W=rng(Bg, lo, hi),
                          key="gs%d" % (gs_rr.i % 3))
        if debug and l == 0:
            hh_ = P.halted
            P.halted = False
            P.dma("sp", dbg_k, kT_all, R=Bkt, W=[Bdbg], key="dbgk")
            P.dma("sp", dbg_v, v_all, R=Bv, W=[Bdbg], key="dbgv")
            P.dma("sp", dbg_p, p_all, R=Bp, W=[Bdbg], key="dbgp")
            P.halted = hh_
        stop("A", l)

        P.barrier()
        AR.release(res_mark)
        wsT = AR.alloc([4, 128], BF16)
        sgub = AR.alloc([2, 128], F32)
        wblk = AR.alloc([2, 128], BF16)
        bnd = AR.alloc([2 * 12, 128], BF16)
        B_bsp = Buf("band_special")
        wpo = AR.alloc([2, 1024], BF16)
        wso = AR.alloc([2, 1024], BF16)
        wao = AR.alloc([4, 1024], BF16)
        wo = AR.alloc([8, 1024], BF16)
        B_wD = Buf("wD")
        P.dma("pool", wsT, sguw[:, l], W=[B_wD], key="wD0")
        P.dma("sp", sgub, sgub_d[:, l], W=[B_wD], key="wD1")
        P.dma("pool", wblk, wblk_d[:, l], W=[B_wD], key="wD2")
        P.dma("pool", bnd[:, 0:12, :], bands[0], W=[B_wD], key="wD3")
        P.dma("pool", wpo, w_po[l].rearrange("(k p) n -> p k n", p=128), W=[B_wD], key="wD4", max_dma_last_dim=4096)
        P.dma("pool", wso, w_so[l].rearrange("(k p) n -> p k n", p=128), W=[B_wD], key="wD5", max_dma_last_dim=4096)
        P.dma("pool", wao, w_ao[l].rearrange("(k p) n -> p k n", p=128), W=[B_wD], key="wD6", max_dma_last_dim=4096)
        P.dma("pool", wo, w_o[l].rearrange("(k p) n -> p k n", p=128), W=[B_wD], key="wD7", max_dma_last_dim=4096)

        bias_rr = RR([(AR.alloc([1024], BF16), Buf("bias%d" % i)) for i in range(3)])
        qt_rr = RR([(AR.alloc([4, 128], BF16), Buf("qt%d" % i)) for i in range(2)])
        ut_rr = RR([(AR.alloc([2, 128], F32), Buf("ut%d" % i)) for i in range(2)])
        vt_rr = RR([(AR.alloc([256], BF16), Buf("vt%d" % i)) for i in range(2)])
        ssb_rr = RR([(AR.alloc([1024], F32), Buf("ssb%d" % i)) for i in range(2)])
        pe_rr = RR([(AR.alloc([1024], BF16), Buf("pexp%d" % i)) for i in range(3)])
        ptb_rr = RR([(AR.alloc([8, 128], BF16), Buf("ptsb%d" % i)) for i in range(3)])
        smx2 = [(AR.alloc([16], F32), Buf("smx%d" % i)) for i in range(2)]
        rinv = AR.alloc([8], F32)
        B_rinv = Buf("rinv")
        ao = AR.alloc([512], BF16)
        B_ao = Buf("ao")
        pooledT = AR.alloc([2, 128], BF16)
        B_pooled = Buf("pooled")
        sgt = AR.alloc([2, 128], F32)
        B_sgt = Buf("sgt")
        oT_rr = RR([(AR.alloc([8, 512], BF16), Buf("oT%d" % i)) for i in range(2)])
        gt_rr = RR([(AR.alloc([4, 3, 128], BF16), Buf("gt%d" % i)) for i in range(2)])
        hc_rr = RR([(AR.alloc([512], F32), Buf("hc%d" % i)) for i in range(4)])
        e_t = [(AR.alloc([512], F32), Buf("et%d" % i)) for i in range(4)]
        yT = AR.alloc([8, 512], BF16)
        B_yT = Buf("yT")

        def tile_type(t):
            if t >= 24:
                return 5
            j = t - 4
            return {0: 1, 1: 2, 14: 3, 15: 4}.get(j, 0)

        def band_type(t):
            if t == 24:
                return 3
            if t == 25:
                return 4
            j = t - 4
            return {0: 1, 15: 2}.get(j, 0)

        for (lo, hi, st) in segs_for("DE", l):
            n = (hi - lo) * 128
            oT, B_oT = oT_rr.next()
            tiles = list(range(lo, hi))
            TI = {}
            for t in tiles:
                if st == 1:
                    kranges = [(24, 26)]
                else:
                    j = t - 4
                    klo, khi = t - 2, t + 3
                    if j == 0:
                        khi = t + 4
                    if j == 15:
                        klo = t - 3
                    kranges = [(klo, khi), (24, 26)]
                TI[t] = dict(kr=kranges, nk=sum((b - a) for a, b in kranges) * 128,
                             ks=[s_ for a, b in kranges for s_ in range(a, b)], ty=tile_type(t),
                             tc0=(t - lo) * 128, smx=smx2[t % 2])

            def loads(t):
                ti = TI[t]
                ti["qt"] = qt_rr.next()
                P.dma("sp", ti["qt"][0], q_scr[t], R=[Bq[t]], W=[ti["qt"][1]], key="qt%d" % (qt_rr.i % 2))
                ti["ut"] = ut_rr.next()
                P.dma("sp", ti["ut"][0], u_scr[t], R=[Bu[t]], W=[ti["ut"][1]], key="ut%d" % (ut_rr.i % 2))
                ti["vt"] = vt_rr.next()
                P.dma("sp", ti["vt"][0], vn_scr[t], R=[Bvn[t]], W=[ti["vt"][1]], key="vt%d" % (vt_rr.i % 2))

            def prologue(t):
                ti = TI[t]
                tc0 = ti["tc0"]
                ut, But = ti["ut"]
                vt, Bvt = ti["vt"]
                bty = band_type(t)
                boff = 0
                if bty != 0:
                    P.dma("pool", bnd[:, 12:24, :], bands[bty], W=[B_bsp], key="bsp")
                    boff = 12
                PP = bank(7).rearrange("p (a b) -> p a b", b=128)
                for g in range(4):
                    c = g // 2
                    srcs = [d_ for d_ in (-1, 0, 1) if not ((t == 24 and d_ == -1) or (t == 25 and d_ == 1))]
                    for ii, d_ in enumerate(srcs):
                        P.mm(PP[:, g, :], p_all[:, t + d_, c * 128:(c + 1) * 128], bnd[:, boff + g * 3 + (d_ + 1), :],
                             ii == 0, ii == len(srcs) - 1, [Bp[t + d_], B_wD, B_bsp], [Bk[7]])
                for g in range(4):
                    gp_ = (g % 2) * 64
                    P.copy("act", pooledT[gp_:gp_ + 64, g // 2, :], PP[gp_:gp_ + 64, g, :], [Bk[7]], [B_pooled])
                PY = bank(7)[:, 0:256].rearrange("p (a b) -> p a b", b=128)
                for c in range(2):
                    P.mm(PY[:, c, :], wblk[:, c, :], pooledT[:, c, :], True, True, [B_pooled, B_wD], [Bk[7]])
                for c in range(2):
                    P.act(oT[:, c, tc0:tc0 + 128], PY[:, c, :], AF.Identity, [Bk[7], B_vec], [B_oT],
                          scale=V_ps(l)[:, c:c + 1], bias=0.0)
                PS_ = bank(7).rearrange("p (a b) -> p a b", b=128)
                for hh in range(4):
                    P.mm(PS_[:, hh, :], vt[:, (hh // 2) * 128:(hh // 2 + 1) * 128], wsT[:, hh, :], True, True,
                         [Bvt, B_wD], [Bk[7]])
                for hh in range(4):
                    hp = (hh % 2) * 64
                    P.tt("pool" if False else "dve", sgt[hp:hp + 64, hh // 2, :], PS_[hp:hp + 64, hh, :],
                         sgub[hp:hp + 64, hh // 2, :], ALU.add, [Bk[7], B_wD], [B_sgt])
                P.tt("pool", oT[:, 2:4, tc0:tc0 + 128], sgt, ut, ALU.mult, [B_sgt, But], [B_oT])

            units = [(t, h) for t in tiles for h in range(8)]
            US = [dict() for _ in units]

            def s1(k):
                t, h = units[k]
                ti = TI[t]
                qt, Bqt = ti["qt"]
                nk = ti["nk"]
                ch, pb = h // 2, (h % 2) * 64
                S = psd[h % 2]
                BS = [Bk[2 * (h % 2)], Bk[2 * (h % 2) + 1]]
                bt, Bbt = bias_rr.next()
                P.dma("sp", bt[:, 0:nk], biasd[l, ti["ty"], h, :, 0:nk], W=[Bbt], key="bias%d" % (bias_rr.i % 3))
                col = 0
                for (a, b) in ti["kr"]:
                    c0 = a * 128
                    rem = (b - a) * 128
                    while rem > 0:
                        w_ = min(rem, 512 - (col % 512))
                        P.mm(S[:, col:col + w_], qt[pb:pb + 64, ch, :], kT_all[pb:pb + 64, ch, c0:c0 + w_],
                             True, True, [Bqt] + rng(Bkt, a, b), [BS[col // 512]])
                        col += w_
                        c0 += w_
                        rem -= w_
                US[k].update(S=S, BS=BS, bt=bt, Bbt=Bbt)

            def s2(k):
                t, h = units[k]
                ti = TI[t]
                nk = ti["nk"]
                smx, B_smx = ti["smx"]
                u = US[k]
                ssb, Bssb = ssb_rr.next()
                P.stt("dve", ssb[:, 0:nk], u["S"][:, 0:nk], 0.125, u["bt"][:, 0:nk], ALU.mult, ALU.add,
                      u["BS"] + [u["Bbt"]], [Bssb])
                P.add("dve", lambda e, o=smx[:, h:h + 1], i=ssb[:, 0:nk]: e.tensor_reduce(
                    out=o, in_=i, axis=AX.X, op=ALU.max, negate=True), [Bssb], [B_smx])
                pex, Bpex = pe_rr.next()
                P.act(pex[:, 0:nk], ssb[:, 0:nk], AF.Exp, [Bssb, B_smx], [Bpex, B_smx], bias=smx[:, h:h + 1],
                      scale=1.0, accum_out=smx[:, 8 + h:9 + h])
                u.update(pex=pex, Bpex=Bpex)

            def s3(k):
                t, h = units[k]
                nkt = TI[t]["nk"] // 128
                u = US[k]
                pb_ = 4 + (k % 2)
                PT = bank(pb_).bitcast(BF16).rearrange("p (a b) -> p a b", b=128)
                for kt in range(nkt):
                    P.tr(PT[:, kt, :], u["pex"][:, kt * 128:(kt + 1) * 128], identb, [u["Bpex"], B_id], [Bk[pb_]])
                ptb, Bptb = ptb_rr.next()
                P.copy("act", ptb[:, 0:nkt, :], PT[:, 0:nkt, :], [Bk[pb_]], [Bptb])
                u.update(ptb=ptb, Bptb=Bptb)

            def s4(k):
                t, h = units[k]
                ti = TI[t]
                nkt = ti["nk"] // 128
                ks = ti["ks"]
                u = US[k]
                for kt in range(nkt):
                    P.mm(bank(6)[:, h * 64:(h + 1) * 64], u["ptb"][:, kt, :], v_all[:, ks[kt], h * 64:(h + 1) * 64],
                         kt == 0, kt == nkt - 1, [u["Bptb"], Bv[ks[kt]]], [Bk[6]])
                if h == 7:
                    smx, B_smx = ti["smx"]
                    tc0 = ti["tc0"]
                    P.add("dve", lambda e, s_=smx: e.reciprocal(out=rinv, in_=s_[:, 8:16]), [B_smx], [B_rinv])
                    P.tt("dve", ao.rearrange("p (h d) -> p h d", d=64), bank(6).rearrange("p (h d) -> p h d", d=64),
                         rinv.unsqueeze(2).broadcast_to([128, 8, 64]), ALU.mult, [Bk[6], B_rinv], [B_ao])
                    AT = bank(7).bitcast(BF16).rearrange("p (a b) -> p a b", b=128)
                    for c in range(4):
                        P.tr(AT[:, c, :], ao[:, c * 128:(c + 1) * 128], identb, [B_ao, B_id], [Bk[7]])
                    P.copy("act", oT[:, 4:8, tc0:tc0 + 128], AT[:, 0:4, :], [Bk[7]], [B_oT])

            NU = len(units)
            loads(tiles[0])
            for k in range(NU + 3):
                if k < NU:
                    t, h = units[k]
                    if h == 0:
                        if t + 1 < hi:
                            loads(t + 1)
                        prologue(t)
                    s1(k)
                if 0 <= k - 1 < NU:
                    s2(k - 1)
                if 0 <= k - 2 < NU:
                    s3(k - 2)
                if 0 <= k - 3 < NU:
                    s4(k - 3)
            if debug and l == 0:
                for t in range(lo, hi):
                    P.dma("sp", dbg_o[t], oT[:, :, (t - lo) * 128:(t - lo + 1) * 128], R=[B_oT], W=[Bdbg], key="dbgo")
            for c in range(8):
                gt, Bgt = gt_rr.next()
                P.dma("sp", gt[:, 0:hi - lo], g_scr[lo:hi, c].rearrange("t p b n -> p t b n"), R=rng(Bg, lo, hi), W=[Bgt],
                      key="gt%d" % (gt_rr.i % 2))
                b0 = (c % 2) * 3
                brs = [(wpo, 2, 0), (wso, 2, 2), (wao, 4, 4)]
                for bi_, (wt_, nkk, o0) in enumerate(brs):
                    for k in range(nkk):
                        P.mm(bank(b0 + bi_)[:, 0:n], wt_[:, k, c * 128:(c + 1) * 128], oT[:, o0 + k, 0:n], k == 0, k == nkk - 1,
                             [B_wD, B_oT], [Bk[b0 + bi_]])
                for bi_ in range(3):
                    et, Bet = e_t[bi_]
                    P.tt("dve", et[:, 0:n].rearrange("p (t n) -> p t n", n=128),
                         bank(b0 + bi_)[:, 0:n].rearrange("p (t n) -> p t n", n=128), gt[:, 0:hi - lo, bi_, :],
                         ALU.mult, [Bk[b0 + bi_], Bgt], [Bet])
                et3, Bet3 = e_t[3]
                P.tt("pool", et3[:, 0:n], e_t[0][0][:, 0:n], e_t[1][0][:, 0:n], ALU.add, [e_t[0][1], e_t[1][1]], [Bet3])
                P.tt("pool", yT[:, c, 0:n], et3[:, 0:n], e_t[2][0][:, 0:n], ALU.add, [Bet3, e_t[2][1]], [B_yT])
            for c2 in range(8):
                ob = 6 + (c2 % 2)
                hc, Bhc = hc_rr.next()
                P.dma("sp", hc[:, 0:n], h_src(l, c2, lo, hi), R=([Bh[t][c2] for t in range(lo, hi)] if l > 0 else []),
                      W=[Bhc], key="hc%d" % (hc_rr.i % 4))
                for c in range(8):
                    P.mm(bank(ob)[:, 0:n], wo[:, c, c2 * 128:(c2 + 1) * 128], yT[:, c, 0:n], c == 0, c == 7,
                         [B_wD, B_yT], [Bk[ob]])
                P.stt("dve", hc[:, 0:n], bank(ob)[:, 0:n], MT(l, st, 2)[:, c2:c2 + 1], hc[:, 0:n],
                      ALU.mult, ALU.add, [Bk[ob], Bhc, B_modt], [Bhc])
                P.dma("sp", h_scr[c2, :, lo * 128:hi * 128], hc[:, 0:n], R=[Bhc], W=[Bh[t][c2] for t in range(lo, hi)],
                      key="hc%d" % (hc_rr.i % 4))
        stop("E", l)

        P.barrier()
        AR.release(m0)
        hF = AR.alloc([8, 1024], F32)
        B_hF = Buf("hF")
        aF = AR.alloc([8, 1024], BF16)
        B_aF = Buf("aF")
        hid = AR.alloc([32, 1024], BF16)
        B_hid = Buf("hid")
        ostg = (AR.alloc([8, 512], F32), Buf("ostg"))
        rsbF = (AR.alloc([512], F32), Buf("rsF"))
        tmpF_rr = RR([(AR.alloc([512], F32), Buf("tmpF%d" % i)) for i in range(2)])
        w1_rr = RR([(AR.alloc([4, 8, 128], BF16), Buf("w1b%d" % i)) for i in range(2)])
        w2_rr = RR([(AR.alloc([32, 128], BF16), Buf("w2b%d" % i)) for i in range(2)])
        rl_rr = RR([(AR.alloc([512], F32), Buf("rl%d" % i)) for i in range(2)])
        f1_rr = RR([(bank(i), Bk[i]) for i in (1, 2, 3, 4)])
        f2_rr = RR([(bank(i), Bk[i]) for i in (5, 6, 7)])
        for grp in segs_for("F", l):
            offs = []
            o_ = 0
            for (lo, hi, st) in grp:
                offs.append(o_)
                o_ += (hi - lo) * 128
            for (lo, hi, st), o0 in zip(grp, offs):
                n = (hi - lo) * 128
                P.dma("sp", hF[:, :, o0:o0 + n], h_scr[:, :, lo * 128:hi * 128].rearrange("c p n -> p c n"),
                      R=[b_ for t in range(lo, hi) for b_ in Bh[t]], W=[B_hF], key="hF")
            for (lo, hi, st), o0 in zip(grp, offs):
                n = (hi - lo) * 128
                rms_to_aT(hF[:, :, o0:o0 + n], B_hF, n, MT(l, st, 3), MT(l, st, 4), aF[:, :, o0:o0 + n], [B_aF],
                          rsbF, tmpF_rr, (bank(0), Bk[0]))
            for jp in range(8):
                w1t, Bw1 = w1_rr.next()
                P.dma("pool", w1t, w1r[l, jp * 4:jp * 4 + 4].rearrange("j p k m -> p j k m"), W=[Bw1],
                      key="w1b%d" % (w1_rr.i % 2), max_dma_last_dim=4096)
                for jj in range(4):
                    j = jp * 4 + jj
                    for (lo, hi, st), o0 in zip(grp, offs):
                        n = (hi - lo) * 128
                        pf, Bpf = f1_rr.next()
                        for k in range(8):
                            P.mm(pf[:, 0:n], w1t[:, jj, k, :], aF[:, k, o0:o0 + n], k == 0, k == 7, [Bw1, B_aF], [Bpf])
                        rl, Brl = rl_rr.next()
                        P.act(rl[:, 0:n], pf[:, 0:n], AF.Relu, [Bpf], [Brl])
                        P.stt("dve", hid[:, j, o0:o0 + n], pf[:, 0:n], 0.0, rl[:, 0:n], ALU.max, ALU.mult,
                              [Bpf, Brl], [B_hid])
            for c2 in range(8):
                w2t, Bw2 = w2_rr.next()
                P.dma("pool", w2t, w2r[l, c2], W=[Bw2], key="w2b%d" % (w2_rr.i % 2), max_dma_last_dim=4096)
                for (lo, hi, st), o0 in zip(grp, offs):
                    n = (hi - lo) * 128
                    pf, Bpf = f2_rr.next()
                    for j in range(32):
                        P.mm(pf[:, 0:n], w2t[:, j, :], hid[:, j, o0:o0 + n], j == 0, j == 31, [Bw2, B_hid], [Bpf])
                    P.stt("dve", hF[:, c2, o0:o0 + n], pf[:, 0:n], MT(l, st, 5)[:, c2:c2 + 1], hF[:, c2, o0:o0 + n],
                          ALU.mult, ALU.add, [Bpf, B_hF, B_modt], [B_hF])
            for (lo, hi, st), o0 in zip(grp, offs):
                n = (hi - lo) * 128
                if not last:
                    P.dma("sp", h_scr[:, :, lo * 128:hi * 128].rearrange("c p n -> p c n"), hF[:, :, o0:o0 + n],
                          R=[B_hF], W=[b_ for t in range(lo, hi) for b_ in Bh[t]], key="hF_out")
                else:
                    og, Bog = ostg
                    rs, Brs = rsbF
                    rms_rstd(hF[:, :, o0:o0 + n], B_hF, n, rsbF, (bank(0), Bk[0]))
                    for c in range(8):
                        P.stt("dve", og[:, c, 0:n], hF[:, c, o0:o0 + n], V_fg[:, c:c + 1], rs[:, 0:n], ALU.mult, ALU.mult,
                              [B_hF, Brs, B_vec], [Bog])
                    op = P.dma("sp", outT[:, :, (lo - 4) * 128:(hi - 4) * 128].rearrange("c p n -> p c n"), og[:, :, 0:n],
                               R=[Bog], W=[Bout], key="outst")
                    out_ops.append(op)
        stop("F", l)

    P.halted = False
    if not out_ops:
        z = AR.t[:, 0:2048]
        out_ops.append(P.dma("sp", outT[0], z, R=[], W=[Bout], key="outst"))
    if debug:
        out_ops.append(P.dma("sp", outT[1, :, 0:8], vec[:, 0:8], R=[Bdbg], W=[Bout], key="dbgfin"))
    P.final_waits = out_ops
    P.emit()
    return nc, P, AR


def _fm(v):
    v = np.asarray(v, np.float32)
    return np.ascontiguousarray(v.reshape(-1, 128).T)


def _rope_perm():
    idx = []
    for h in range(8):
        idx += [h * 64 + 2 * i for i in range(32)] + [h * 64 + 2 * i + 1 for i in range(32)]
    return np.array(idx)


def _shared_inputs(inp):
    w_in = np.asarray(inp["w_in"], np.float32)
    perm = _rope_perm()
    wp = w_in.copy()
    wp[:, :, 768:1280] = w_in[:, :, 768:1280][:, :, perm]
    wp[:, :, 1280:1792] = w_in[:, :, 1280:1792][:, :, perm]
    tm_cols = [np.r_[0:256, 512:768], np.r_[1792:2304]]
    wA_tm = np.stack([np.stack([wp[l][:, cols].reshape(8, 128, 512).transpose(1, 0, 2) for cols in tm_cols])
                      for l in range(2)])
    fm_starts = [256, 384] + [768 + 128 * i for i in range(8)] + [2304 + 128 * i for i in range(24)]
    wA_fm = np.stack([np.stack([wp[l][:, s:s + 128].reshape(8, 128, 128).transpose(1, 0, 2) for s in fm_starts])
                      for l in range(2)])
    w1 = np.asarray(inp["w_ff1"], np.float32)
    w1r = np.stack([np.stack([w1[l][:, j * 128:(j + 1) * 128].reshape(8, 128, 128).transpose(1, 0, 2)
                              for j in range(32)]) for l in range(2)])
    w2 = np.asarray(inp["w_ff2"], np.float32)
    w2r = np.stack([np.stack([w2[l][:, c * 128:(c + 1) * 128].reshape(32, 128, 128).transpose(1, 0, 2)
                              for c in range(8)]) for l in range(2)])
    sgu_w = np.asarray(inp["sgu_w"], np.float32)
    sguw = np.ascontiguousarray(sgu_w.transpose(3, 0, 1, 2))
    sgu_b = np.asarray(inp["sgu_b"], np.float32)
    sgub = np.zeros((128, 2, 2, 128), np.float32)
    for part in range(128):
        for c in range(2):
            sgub[part, :, c, :] = sgu_b[:, 2 * c + part // 64, :]
    w_pool = np.asarray(inp["w_pool"], np.float32)
    wblk = np.zeros((128, 2, 2, 128), np.float32)
    for c in range(2):
        for gl in range(2):
            wblk[gl * 64:(gl + 1) * 64, :, c, gl * 64:(gl + 1) * 64] = w_pool[:, 2 * c + gl].transpose(1, 0, 2)
    consts = np.zeros((128, 3, 128), np.float32)
    consts[:, 0, :] = np.eye(128)
    for pp in range(128):
        partner = pp + 32 if (pp % 64) < 32 else pp - 32
        consts[partner, 1, pp] = 1.0
    consts[:, 2, :] = 1.0
    return dict(w_mod=np.ascontiguousarray(inp["w_mod"], np.float32), wA_tm=np.ascontiguousarray(wA_tm),
                wA_fm=np.ascontiguousarray(wA_fm), sguw=sguw, sgub=sgub, wblk=wblk,
                w_pool_out=np.ascontiguousarray(inp["w_pool_out"], np.float32),
                w_sgu_out=np.ascontiguousarray(inp["w_sgu_out"], np.float32),
                w_attn_out=np.ascontiguousarray(inp["w_attn_out"], np.float32),
                w_o=np.ascontiguousarray(inp["w_o"], np.float32), w1r=np.ascontiguousarray(w1r),
                w2r=np.ascontiguousarray(w2r), consts=consts)


def _band_set(kind):
    out = np.zeros((128, 12, 128), np.float32)
    L = 384
    base = 128
    for g, w in enumerate((2, 4, 8, 16)):
        for tt in range(128):
            pos = base + tt
            lo_b = base if kind == 1 else 0
            hi_b = base + 128 if kind == 2 else L
            lo = min(max(pos - w // 2, lo_b), hi_b)
            hi = min(max(pos + (w - w // 2), lo_b), hi_b)
            cnt = hi - lo
            for s in range(lo, hi):
                d = s // 128
                out[s % 128, g * 3 + d, tt] += 1.0 / cnt
            out[tt, g * 3 + 1, tt] -= 1.0
    return out


def _bias_tables(rpb, core_rows0, n_rows_total=128):
    out = np.full((2, 6, 8, 128, 1024), NEG, np.float32)
    q_i = np.arange(128)
    for ty, j in ((0, 4), (1, 0), (2, 1), (3, 14), (4, 15)):
        klo, khi = j - 2, j + 3
        if j == 0:
            khi = j + 4
        if j == 15:
            klo = j - 3
        nkl = (khi - klo) * 128
        key = np.arange(nkl)
        k_row = core_rows0 + 2 * klo + key // 64
        k_col = key % 64
        q_row = core_rows0 + 2 * j + q_i // 64
        q_col = q_i % 64
        rs = np.clip(q_row - 4, 0, n_rows_total - 8)
        cs = np.clip(q_col - 8, 0, 64 - 16)
        valid = ((k_row[None, :] >= rs[:, None]) & (k_row[None, :] < rs[:, None] + 8) &
                 (k_col[None, :] >= cs[:, None]) & (k_col[None, :] < cs[:, None] + 16) &
                 (k_row[None, :] >= 0) & (k_row[None, :] < n_rows_total))
        dr = np.clip(k_row[None, :] - q_row[:, None] + 7, 0, 14)
        dc = np.clip(k_col[None, :] - q_col[:, None] + 15, 0, 30)
        for l in range(2):
            for h in range(8):
                g = rpb[l, h][dr, dc]
                out[l, ty, h, :, 0:nkl] = np.where(valid, g, NEG)
                out[l, ty, h, :, nkl:nkl + 256] = 0.0
    out[:, 5, :, :, 0:256] = 0.0
    return out.astype(ml_dtypes.bfloat16)


def _core_inputs(inp, core):
    b, blk = core // 4, core % 4
    row0 = 32 * blk
    x = np.asarray(inp["x"], np.float32)[b]
    t0 = (row0 - 8) * 64
    xs = np.zeros((NLAT, D), np.float32)
    lo, hi = max(t0, 0), min(t0 + NLAT, 8192)
    xs[lo - t0:hi - t0] = x[lo:hi]
    xT = np.ascontiguousarray(xs.T.reshape(8, 128, NLAT))
    ctxT = np.ascontiguousarray(np.asarray(inp["ctx"], np.float32)[b].T.reshape(8, 128, 256))
    vecs = np.zeros((128, 156), np.float32)
    vecs[:, 0:8] = _fm(inp["c"][b])
    vecs[:, 8:16] = _fm(inp["c_ctx"])
    for l in range(2):
        vecs[:, 16 + l * 48:16 + (l + 1) * 48] = _fm(inp["b_mod"][l])
        vecs[:, 112 + l * 8:112 + (l + 1) * 8] = _fm(inp["norm1_g"][l])
        vecs[:, 128 + l * 8:128 + (l + 1) * 8] = _fm(inp["norm2_g"][l])
        vecs[:, 152 + l * 2:152 + (l + 1) * 2] = _fm(inp["pool_scale"][l])
    vecs[:, 144:152] = _fm(inp["final_g"])
    tok = np.arange(NLAT)
    row = (row0 - 8 + tok // 64).astype(np.float32)
    col = (tok % 64).astype(np.float32)
    inv_freq = (10000.0 ** (-np.arange(16, dtype=np.float32) / 16)).astype(np.float32)
    ang = np.concatenate([row[:, None] * inv_freq, col[:, None] * inv_freq], axis=-1).astype(np.float32)
    cos, sin = np.cos(ang), np.sin(ang)
    rope = np.zeros((2, 128, NLAT), np.float32)
    for pp in range(128):
        i = pp % 64
        e = i % 32
        rope[0, pp] = cos[:, e]
        rope[1, pp] = -sin[:, e] if i < 32 else sin[:, e]
    first = (blk == 0)
    lastb = (blk == 3)
    gen = _band_set(0)
    bands = np.stack([gen, _band_set(1) if first else gen, _band_set(2) if lastb else gen, _band_set(1), _band_set(2)])
    if first:
        bands[1][:, [0, 3, 6, 9], :] = 0.0
    bias = _bias_tables(np.asarray(inp["na_rpb"], np.float32), row0)
    return dict(xT=xT, ctxT=ctxT, vecs=vecs, rope=rope, bands=np.ascontiguousarray(bands), bias=bias)


_PROG = {}


def kernel(**inputs):
    if "nc" not in _PROG:
        _PROG["nc"] = build_program()[0]
    nc = _PROG["nc"]
    shared = _shared_inputs(inputs)
    in_maps = []
    for core in range(8):
        m = dict(shared)
        m.update(_core_inputs(inputs, core))
        in_maps.append(m)
    res = run_bass_kernel_spmd(nc, in_maps, core_ids=list(range(8)))
    out = np.zeros((2, 8192, D), np.float32)
    for core in range(8):
        b, blk = core // 4, core % 4
        oT = np.asarray(res.results[core]["outT"], np.float32)
        out[b, blk * 2048:(blk + 1) * 2048, :] = oT.reshape(D, 2048).T
    return out
```

```python
import numpy as np
import ml_dtypes
import concourse.bass as bass
import concourse.mybir as mybir
from concourse.bass_utils import run_bass_kernel_spmd

F32 = mybir.dt.float32
F32R = mybir.dt.float32r
BF16 = mybir.dt.bfloat16
AF = mybir.ActivationFunctionType
ALU = mybir.AluOpType
AX = mybir.AxisListType

D = 1024
NLS = 24
NS = 26
NLAT = NLS * 128
TOK = NS * 128
EPS = 1e-6
NEG = -30000.0
DBG_SKIP_LN = False


class Buf:
    __slots__ = ("name", "last_writer", "readers")

    def __init__(self, name):
        self.name = name
        self.last_writer = None
        self.readers = []


class Op:
    __slots__ = ("eng", "fn", "deps", "is_dma", "key", "idx", "signal", "sem", "val", "waits")

    def __init__(self, eng, fn, is_dma, key, idx):
        self.eng = eng
        self.fn = fn
        self.is_dma = is_dma
        self.key = key
        self.idx = idx
        self.deps = []
        self.signal = False
        self.sem = None
        self.val = 0
        self.waits = []


ENGS = ("pe", "act", "dve", "pool", "sp")


class Prog:
    def __init__(self, nc):
        self.nc = nc
        self.ops = []
        self.final_waits = []
        self.phase_buf = Buf("phase")
        self.bar_ap = None
        self.halted = False
        self.dummy = Op("dve", None, False, None, -1)

    def add(self, eng, fn, reads=(), writes=(), dma_key=None):
        if self.halted:
            return self.dummy
        is_dma = dma_key is not None
        op = Op(eng, fn, is_dma, dma_key, len(self.ops))
        deps = {}
        for b in reads:
            w = b.last_writer
            if w is not None:
                deps[w.idx] = [w, True]
        for b in writes:
            w = b.last_writer
            if w is not None and w.idx not in deps:
                deps[w.idx] = [w, False]
            for r in b.readers:
                if r.idx not in deps:
                    deps[r.idx] = [r, False]
        pb = self.phase_buf
        if pb.last_writer is not None:
            deps[pb.last_writer.idx] = [pb.last_writer, True]
        pb.readers.append(op)
        for b in reads:
            b.readers.append(op)
        for b in writes:
            b.last_writer = op
            b.readers = []
        deps.pop(op.idx, None)
        op.deps = list(deps.values())
        self.ops.append(op)
        return op

    def barrier(self):
        if self.halted:
            return self.dummy
        pb = self.phase_buf
        op = Op("dve", lambda e: e.memset(self.bar_ap, 0.0), False, None, len(self.ops))
        deps = {}
        for r in pb.readers:
            deps[r.idx] = [r, True]
        if pb.last_writer is not None:
            deps[pb.last_writer.idx] = [pb.last_writer, True]
        op.deps = list(deps.values())
        pb.last_writer = op
        pb.readers = []
        self.ops.append(op)
        return op

    def dma(self, eng, out, in_, R=(), W=(), key=None, **kw):
        return self.add(eng, lambda e: e.dma_start(out=out, in_=in_, **kw), R, W, dma_key=key)

    def mm(self, out, lhsT, rhs, start, stop, R, W):
        return self.add("pe", lambda e: e.matmul(out, lhsT=lhsT, rhs=rhs, start=start, stop=stop), R, W)

    def tr(self, out, in_, ident, R, W):
        return self.add("pe", lambda e: e.transpose(out=out, in_=in_, identity=ident), R, W)

    def act(self, out, in_, func, R, W, **kw):
        return self.add("act", lambda e: e.activation(out=out, in_=in_, func=func, **kw), R, W)

    def copy(self, eng, out, in_, R, W):
        if eng == "act":
            return self.add("act", lambda e: e.copy(out=out, in_=in_), R, W)
        return self.add(eng, lambda e: e.tensor_copy(out=out, in_=in_), R, W)

    def tt(self, eng, out, in0, in1, op, R, W):
        return self.add(eng, lambda e: e.tensor_tensor(out=out, in0=in0, in1=in1, op=op), R, W)

    def ts(self, eng, out, in0, s1, s2, op0, op1, R, W):
        return self.add(eng, lambda e: e.tensor_scalar(out=out, in0=in0, scalar1=s1, scalar2=s2, op0=op0, op1=op1), R, W)

    def stt(self, eng, out, in0, scalar, in1, op0, op1, R, W):
        return self.add(eng, lambda e: e.scalar_tensor_tensor(out=out, in0=in0, scalar=scalar, in1=in1,
                                                              op0=op0, op1=op1), R, W)

    def emit(self):
        nc = self.nc
        ops = self.ops
        for op in ops:
            need = []
            for d, raw in op.deps:
                if d.is_dma or op.is_dma or d.eng != op.eng or (raw and op.eng != "pe"):
                    need.append(d)
            op.waits = need
            for d in need:
                d.signal = True
        for op in self.final_waits:
            op.signal = True
        for op in ops:
            if op.is_dma:
                op.signal = True
        sems = {}
        counters = {}
        for op in ops:
            if not op.signal:
                continue
            k = ("dma", op.key) if op.is_dma else ("eng", op.eng)
            if k not in sems:
                sems[k] = nc.alloc_semaphore("s%d" % len(sems))
                counters[k] = 0
            op.sem = sems[k]
            counters[k] += 16 if op.is_dma else 1
            op.val = counters[k]
            op.key = k
        self.n_sems = len(sems)
        per_eng = {e: [] for e in ENGS}
        for op in ops:
            per_eng[op.eng].append(op)
        finals = list(self.final_waits)

        def run(eng_name, eng):
            waited = {}
            for op in per_eng[eng_name]:
                req = {}
                for d in op.waits:
                    if req.get(d.key, 0) < d.val:
                        req[d.key] = d.val
                for k, v in req.items():
                    if waited.get(k, 0) >= v:
                        continue
                    waited[k] = v
                    eng.wait_ge(sems[k], v)
                inst = op.fn(eng)
                if op.signal:
                    inst.then_inc(op.sem, 16 if op.is_dma else 1)
            if eng_name == "sp":
                for d in finals:
                    if waited.get(d.key, 0) < d.val:
                        waited[d.key] = d.val
                        eng.wait_ge(sems[d.key], d.val)

        with nc.Block() as block:
            @block.tensor
            def _(e):
                run("pe", e)

            @block.scalar
            def _(e):
                run("act", e)

            @block.vector
            def _(e):
                run("dve", e)

            @block.gpsimd
            def _(e):
                run("pool", e)

            @block.sync
            def _(e):
                run("sp", e)


class Arena:
    def __init__(self, nc, nbytes):
        self.t = nc.alloc_sbuf_tensor("arena", [128, nbytes // 4], F32)
        self.n = nbytes
        self.off = 0
        self.peak = 0

    def alloc(self, shape, dtype):
        esz = 2 if dtype == BF16 else 4
        n = int(np.prod(shape)) * esz
        n4 = (n + 31) // 32 * 32
        assert self.off + n4 <= self.n, ("SBUF arena overflow", self.off, n4, self.n)
        a = self.t[:, self.off // 4:(self.off + n) // 4]
        self.off += n4
        self.peak = max(self.peak, self.off)
        if dtype != F32:
            a = a.bitcast(dtype)
        if len(shape) == 2:
            return a.rearrange("p (a b) -> p a b", a=shape[0])
        if len(shape) == 3:
            return a.rearrange("p (a b c) -> p a b c", a=shape[0], b=shape[1])
        return a

    def mark(self):
        return self.off

    def release(self, m):
        self.off = m


class RR:
    def __init__(self, items):
        self.items = items
        self.i = 0

    def next(self):
        it = self.items[self.i % len(self.items)]
        self.i += 1
        return it


def segs_for(kind, layer):
    if kind == "A":
        if layer == 0:
            s = [(4 * b, 4 * b + 4, 0) for b in range(6)]
        else:
            s = [(2, 4, 0)] + [(4 * b, 4 * b + 4, 0) for b in range(1, 5)] + [(20, 22, 0)]
        return s + [(24, 26, 1)]
    if kind == "DE":
        s = [(4 * b, 4 * b + 4, 0) for b in range(1, 5)]
        if layer == 0:
            s = [(2, 4, 0)] + s + [(20, 22, 0), (24, 26, 1)]
        return s
    if kind == "F":
        g = [[(4, 8, 0), (8, 12, 0)], [(12, 16, 0), (16, 20, 0)]]
        if layer == 0:
            g.append([(2, 4, 0), (20, 22, 0), (24, 26, 1)])
        return g
    raise ValueError(kind)


def build_program(n_layers=2, stop_after=None, debug=False):
    nc = bass.Bass("TRN2", target_bir_lowering=False)
    P = Prog(nc)

    def din(name, shape, dt=F32):
        return nc.dram_tensor(name, list(shape), dt, kind="ExternalInput").ap()

    xT = din("xT", [8, 128, NLAT])
    ctxT = din("ctxT", [8, 128, 256])
    vecs = din("vecs", [128, 156])
    consts = din("consts", [128, 3, 128])
    rope = din("rope", [4, 128, NLAT])
    bands = din("bands", [5, 128, 12, 128])
    biasd = din("bias", [2, 6, 8, 128, 1024], BF16)
    w_mod = din("w_mod", [2, 1024, 6144])
    wA_tm = din("wA_tm", [2, 2, 128, 8, 512])
    wA_fm = din("wA_fm", [2, 34, 128, 8, 128])
    sguw = din("sguw", [128, 2, 4, 128])
    sgub_d = din("sgub", [128, 2, 2, 128])
    wblk_d = din("wblk", [128, 2, 2, 128])
    w_po = din("w_pool_out", [2, 256, 1024])
    w_so = din("w_sgu_out", [2, 256, 1024])
    w_ao = din("w_attn_out", [2, 512, 1024])
    w_o = din("w_o", [2, 1024, 1024])
    w1r = din("w1r", [2, 32, 128, 8, 128])
    w2r = din("w2r", [2, 8, 128, 32, 128])
    outT = nc.dram_tensor("outT", [8, 128, 2048], F32, kind="ExternalOutput").ap()

    skind = "ExternalOutput" if debug else "Internal"

    def dscr(name, shape, dt=F32):
        return nc.dram_tensor(name, list(shape), dt, kind=skind).ap()

    h_scr = dscr("h_scr", [8, 128, TOK])
    q_scr = dscr("q_scr", [NS, 128, 4, 128], BF16)
    u_scr = dscr("u_scr", [NS, 128, 2, 128])
    vn_scr = dscr("vn_scr", [NS, 128, 256], BF16)
    g_scr = dscr("g_scr", [NS, 8, 128, 3, 128], BF16)
    if debug:
        dbg_k = dscr("dbg_k", [128, 4, TOK], BF16)
        dbg_v = dscr("dbg_v", [128, NS, 512], BF16)
        dbg_p = dscr("dbg_p", [128, NS, 256], BF16)
        dbg_a = dscr("dbg_a", [128, 8, TOK], BF16)
        dbg_mod = dscr("dbg_mod", [128, 2, 2, 48])
        dbg_o = dscr("dbg_o", [NS, 128, 8, 128], BF16)

    Bh = [[Buf("h%d_%d" % (t, c)) for c in range(8)] for t in range(NS)]
    Bq = [Buf("q%d" % t) for t in range(NS)]
    Bu = [Buf("u%d" % t) for t in range(NS)]
    Bvn = [Buf("vn%d" % t) for t in range(NS)]
    Bg = [Buf("g%d" % t) for t in range(NS)]
    Bout = Buf("out")
    Bdbg = Buf("dbg")

    psd = [nc.alloc_psum_tensor("psd%d" % i, [128, 1024], F32) for i in range(4)]
    Bk = [Buf("bank%d" % i) for i in range(8)]

    def bank(i):
        return psd[i // 2][:, (i % 2) * 512:(i % 2) * 512 + 512]

    AR = Arena(nc, 197 * 1024)
    bar_t = AR.alloc([8], F32)
    P.bar_ap = bar_t
    cst = AR.alloc([3, 128], F32)
    identb = AR.alloc([128], BF16)
    perm_r = nc.alloc_sbuf_tensor("perm_r", [128, 128], F32R)[:]
    ones_r = nc.alloc_sbuf_tensor("ones_r", [128, 128], F32R)[:]
    sq_rr = RR([(nc.alloc_sbuf_tensor("sq_r%d" % i, [128, 512], F32R)[:], Buf("sq_r%d" % i)) for i in range(2)])
    qf_rr = RR([(nc.alloc_sbuf_tensor("qf_r%d" % i, [128, 512], F32R)[:], Buf("qf_r%d" % i)) for i in range(2)])
    vec = AR.alloc([156], F32)
    modt = AR.alloc([24, 8], F32)
    silu_b = AR.alloc([2, 8], BF16)
    B_c = Buf("consts")
    B_vec = Buf("vec")
    B_modt = Buf("modt")
    B_silu = Buf("silu")

    def V_c(s):
        return vec[:, s * 8:(s + 1) * 8]

    def V_bmod(l):
        return vec[:, 16 + l * 48:16 + (l + 1) * 48]

    def V_n1(l):
        return vec[:, 112 + l * 8:112 + (l + 1) * 8]

    def V_n2(l):
        return vec[:, 128 + l * 8:128 + (l + 1) * 8]

    V_fg = vec[:, 144:152]

    def V_ps(l):
        return vec[:, 152 + l * 2:152 + (l + 1) * 2]

    def MT(l, s, kind):
        i = (l * 2 + s) * 6 + kind
        return modt[:, i, :]

    P.dma("sp", cst, consts, W=[B_c], key="cst")
    P.dma("sp", vec, vecs, W=[B_vec], key="vec")
    B_id = Buf("ident")
    P.copy("dve", identb, cst[:, 0, :], [B_c], [B_id])
    P.copy("dve", perm_r, cst[:, 1, :], [B_c], [B_id])
    P.copy("dve", ones_r, cst[:, 2, :], [B_c], [B_id])
    P.act(silu_b.rearrange("p s k -> p (s k)"), vec[:, 0:16], AF.Silu, [B_vec], [B_silu])

    def stop(name, l=0):
        if stop_after is not None and tuple(stop_after) == (name, l):
            P.halted = True

    stop("S")

    m0 = AR.mark()
    wm = [(AR.alloc([8, 512], BF16), Buf("wm%d" % i)) for i in range(2)]
    wm_rr = RR(wm)
    modraw = AR.alloc([48, 2], F32)
    tmpm = AR.alloc([8, 2], F32)
    B_modraw = Buf("modraw")
    for l in range(n_layers):
        psm = bank(0).rearrange("p (a b) -> p a b", b=2)[:, 0:48, :]
        for pc in range(12):
            wt, wb = wm_rr.next()
            P.dma("pool", wt, w_mod[l, :, pc * 512:(pc + 1) * 512].rearrange("(k p) n -> p k n", p=128),
                  W=[wb], key="wm%d" % (wm_rr.i % 2), max_dma_last_dim=4096)
            for cc in range(4):
                ch = pc * 4 + cc
                for k in range(8):
                    P.mm(psm[:, ch, :], wt[:, k, cc * 128:(cc + 1) * 128], silu_b[:, :, k], k == 0, k == 7,
                         [wb, B_silu], [Bk[0]])
        P.tt("dve", modraw, psm, V_bmod(l).unsqueeze(2).broadcast_to([128, 48, 2]), ALU.add,
             [Bk[0], B_vec], [B_modraw])
        for s in range(2):
            def chunk(i):
                return modraw[:, i * 8:(i + 1) * 8, s]
            P.stt("dve", MT(l, s, 0), chunk(1), 1.0, V_n1(l), ALU.add, ALU.mult, [B_modraw, B_vec], [B_modt])
            P.copy("dve", MT(l, s, 1), chunk(0), [B_modraw], [B_modt])
            P.copy("dve", MT(l, s, 2), chunk(2), [B_modraw], [B_modt])
            P.stt("dve", MT(l, s, 3), chunk(4), 1.0, V_n2(l), ALU.add, ALU.mult, [B_modraw, B_vec], [B_modt])
            P.copy("dve", MT(l, s, 4), chunk(3), [B_modraw], [B_modt])
            P.copy("dve", MT(l, s, 5), chunk(5), [B_modraw], [B_modt])
    if debug:
        P.dma("sp", dbg_mod.rearrange("p l s c -> p (l s c)"), modt.rearrange("p a b -> p (a b)")[:, 0:192],
              R=[B_modt], W=[Bdbg], key="dbgm")
    AR.release(m0)
    stop("M")
    P.barrier()

    kT_all = AR.alloc([4, TOK], BF16)
    v_all = AR.alloc([NS, 512], BF16)
    p_all = AR.alloc([NS, 256], BF16)
    Bkt = [Buf("kT%d" % t) for t in range(NS)]
    Bv = [Buf("v%d" % t) for t in range(NS)]
    Bp = [Buf("p%d" % t) for t in range(NS)]
    res_mark = AR.mark()

    def rng(bufs, lo, hi):
        return [bufs[t] for t in range(lo, hi)]

    def h_src(l, c, lo, hi):
        if l == 0:
            if lo >= 24:
                return ctxT[c, :, (lo - 24) * 128:(hi - 24) * 128]
            return xT[c, :, lo * 128:hi * 128]
        return h_scr[c, :, lo * 128:hi * 128]

    def load_h(dst, l, lo, hi, Bdst, key, first_layer_input):
        n = (hi - lo) * 128
        if first_layer_input:
            src = (ctxT[:, :, (lo - 24) * 128:(hi - 24) * 128] if lo >= 24 else xT[:, :, lo * 128:hi * 128])
            R = []
        else:
            src = h_scr[:, :, lo * 128:hi * 128]
            R = [b_ for t in range(lo, hi) for b_ in Bh[t]]
        P.dma("sp", dst[:, :, 0:n], src.rearrange("c p n -> p c n"), R=R, W=[Bdst], key=key)

    def rms_rstd(hb, Bhb, n, rsb, psb):
        rs, Brs = rsb
        pb, Bpb = psb
        for c in range(8):
            sq, Bsq = sq_rr.next()
            P.act(sq[:, 0:n], hb[:, c, 0:n], AF.Square, [Bhb], [Bsq])
            P.mm(pb[:, 0:n], ones_r, sq[:, 0:n], c == 0, c == 7, [Bsq, B_id], [Bpb])
        P.ts("dve", rs[:, 0:n], pb[:, 0:n], 1.0 / D, EPS, ALU.mult, ALU.add, [Bpb], [Brs])
        P.act(rs[:, 0:n], rs[:, 0:n], AF.Sqrt, [Brs], [Brs])
        P.add("dve", lambda e: e.reciprocal(out=rs[:, 0:n], in_=rs[:, 0:n]), [Brs], [Brs])

    def rms_to_aT(hb, Bhb, n, gs, sh, dst, Wdst, rsb, tmp_rr, psb):
        rs, Brs = rsb
        rms_rstd(hb, Bhb, n, rsb, psb)
        for c in range(8):
            tmp, Btmp = tmp_rr.next()
            P.stt("dve", tmp[:, 0:n], hb[:, c, 0:n], gs[:, c:c + 1], rs[:, 0:n], ALU.mult, ALU.mult,
                  [Bhb, Brs, B_modt], [Btmp])
            P.act(dst[:, c, 0:n], tmp[:, 0:n], AF.Identity, [Btmp, B_modt], Wdst, bias=sh[:, c:c + 1], scale=1.0)

    out_ops = []

    for l in range(n_layers):
        last = (l == n_layers - 1)
        if l > 0:
            P.barrier()
        AR.release(res_mark)
        aT_all = AR.alloc([8, TOK], BF16)
        Ba = [Buf("aT%d" % t) for t in range(NS)]
        a_mark = AR.mark()
        hst = [(AR.alloc([8, 512], F32), Buf("hst%d" % i)) for i in range(2)]
        hst_rr = RR(hst)
        rsb = (AR.alloc([512], F32), Buf("rs"))
        tmp_rr = RR([(AR.alloc([512], F32), Buf("tmp%d" % i)) for i in range(2)])
        segsA = segs_for("A", l)
        for si, (lo, hi, st) in enumerate(segsA):
            n = (hi - lo) * 128
            hb, Bhb = hst_rr.next()
            load_h(hb, l, lo, hi, Bhb, "hst%d" % (hst_rr.i % 2), l == 0)
            bi = si % 2
            rms_to_aT(hb, Bhb, n, MT(l, st, 0), MT(l, st, 1), aT_all[:, :, lo * 128:hi * 128], rng(Ba, lo, hi), rsb,
                      tmp_rr, (bank(bi), Bk[bi]))
        if debug and l == 0:
            P.dma("sp", dbg_a, aT_all, R=Ba, W=[Bdbg], key="dbga")
        stop("A1", l)
        P.barrier()
        AR.release(a_mark)

        wbuf = [(AR.alloc([8192], BF16), Buf("wbuf%d" % i)) for i in range(2)]
        wb_rr = RR(wbuf)
        ropeb = [(AR.alloc([4, 512], F32), Buf("rope%d" % i)) for i in range(2)]
        rope_rr = RR(ropeb)
        t1_rr = RR([(AR.alloc([512], F32), Buf("t1%d" % i)) for i in range(2)])
        t2_rr = RR([(AR.alloc([512], F32), Buf("t2%d" % i)) for i in range(2)])
        qo_rr = RR([(AR.alloc([512], BF16), Buf("qo%d" % i)) for i in range(2)])
        us_rr = RR([(AR.alloc([512], F32), Buf("us%d" % i)) for i in range(3)])
        gs_rr = RR([(AR.alloc([512], BF16), Buf("gs%d" % i)) for i in range(3)])
        vns_rr = RR([(AR.alloc([256], BF16), Buf("vns%d" % i)) for i in range(2)])
        st_rr = RR([(AR.alloc([8], F32), Buf("st%d" % i)) for i in range(2)])
        vsf_rr = RR([(AR.alloc([256], F32), Buf("vsf%d" % i)) for i in range(4)])
        pj_rr = RR([(bank(i), Bk[i]) for i in range(4)])
        pm_rr = RR([(bank(i), Bk[i]) for i in (4, 5)])
        tm_rr = RR([(bank(i), Bk[i]) for i in (4, 5, 6, 7)])

        def load_w(src, shape):
            wt, wb = wb_rr.next()
            nel = int(np.prod(shape))
            view = wt[:, 0:nel]
            if len(shape) == 2:
                view = view.rearrange("p (a b) -> p a b", a=shape[0])
            else:
                view = view.rearrange("p (a b c) -> p a b c", a=shape[0], b=shape[1])
            P.dma("pool", view, src, W=[wb], key="wbuf%d" % (wb_rr.i % 2), max_dma_last_dim=4096)
            return view, wb

        for piece in range(2):
            wv, wb = load_w(wA_tm[l, piece], [8, 512])
            for (lo, hi, st) in segsA:
                for t in range(lo, hi):
                    pt, Bpt = tm_rr.next()
                    for k in range(8):
                        P.mm(pt, aT_all[:, k, t * 128:(t + 1) * 128], wv[:, k, :], k == 0, k == 7, [Ba[t], wb], [Bpt])
                    if piece == 0:
                        P.copy("act", p_all[:, t, :], pt[:, 0:256], [Bpt], [Bp[t]])
                        if DBG_SKIP_LN:
                            continue
                        stt_, Bst = st_rr.next()
                        vf, Bvf = vsf_rr.next()
                        P.act(vf, pt[:, 256:512], AF.Identity, [Bpt], [Bvf, Bst], accum_out=stt_[:, 0:1])
                        P.ts("dve", stt_[:, 1:2], stt_[:, 0:1], -1.0 / 256, None, ALU.mult, ALU.bypass, [Bst], [Bst])
                        jk, Bjk = vsf_rr.next()
                        P.act(jk, vf, AF.Square, [Bvf, Bst], [Bjk, Bst], bias=stt_[:, 1:2], scale=1.0, accum_out=stt_[:, 2:3])
                        P.ts("dve", stt_[:, 3:4], stt_[:, 2:3], 1.0 / 256, EPS, ALU.mult, ALU.add, [Bst], [Bst])
                        P.act(stt_[:, 3:4], stt_[:, 3:4], AF.Sqrt, [Bst], [Bst])
                        P.add("dve", lambda e, o=stt_[:, 3:4]: e.reciprocal(out=o, in_=o), [Bst], [Bst])
                        vs_, Bvs = vns_rr.next()
                        P.ts("dve", vs_, vf, stt_[:, 1:2], stt_[:, 3:4], ALU.add, ALU.mult, [Bvf, Bst], [Bvs])
                        P.dma("sp", vn_scr[t], vs_, R=[Bvs], W=[Bvn[t]], key="vns%d" % (vns_rr.i % 2))
                    else:
                        P.copy("act", v_all[:, t, :], pt, [Bpt], [Bv[t]])

        stop("A2", l)
        def proj_chunk(wv, wb, lo, hi):
            n = (hi - lo) * 128
            pj, Bpj = pj_rr.next()
            for k in range(8):
                P.mm(pj[:, 0:n], wv[:, k, :], aT_all[:, k, lo * 128:hi * 128], k == 0, k == 7,
                     rng(Ba, lo, hi) + [wb], [Bpj])
            return pj, Bpj, n

        wv, wb = load_w(wA_fm[l, 0:2].rearrange("c p k m -> p c k m"), [2, 8, 128])
        for (lo, hi, st) in segsA:
            for c in range(2):
                pj, Bpj, n = proj_chunk(wv[:, c], wb, lo, hi)
                us, Bus = us_rr.next()
                P.copy("act", us[:, 0:n], pj[:, 0:n], [Bpj], [Bus])
                P.dma("sp", u_scr[lo:hi, :, c, :].rearrange("t p n -> p t n"),
                      us[:, 0:n].rearrange("p (t n) -> p t n", n=128), R=[Bus], W=rng(Bu, lo, hi),
                      key="us%d" % (us_rr.i % 3))
        stop("A3", l)
        wv, wb = load_w(wA_fm[l, 2:10].rearrange("c p k m -> p c k m"), [8, 8, 128])
        for (lo, hi, st) in segsA:
            n = (hi - lo) * 128
            if st == 0:
                rt, Brt = rope_rr.next()
                P.dma("sp", rt[:, :, 0:n], rope[:, :, lo * 128:hi * 128].rearrange("a p n -> p a n"), W=[Brt],
                      key="rope%d" % (rope_rr.i % 2))
            for c in range(8):
                pj, Bpj, n = proj_chunk(wv[:, c], wb, lo, hi)
                isq = c < 4
                if st == 1:
                    if isq:
                        qo, Bqo = qo_rr.next()
                        P.act(qo[:, 0:n], pj[:, 0:n], AF.Identity, [Bpj], [Bqo], scale=0.125, bias=0.0)
                    else:
                        P.copy("act", kT_all[:, c - 4, lo * 128:hi * 128], pj[:, 0:n], [Bpj], rng(Bkt, lo, hi))
                else:
                    qf, Bqf = qf_rr.next()
                    P.copy("act", qf[:, 0:n], pj[:, 0:n], [Bpj], [Bqf])
                    pm, Bpm = pm_rr.next()
                    P.mm(pm[:, 0:n], perm_r, qf[:, 0:n], True, True, [Bqf, B_id], [Bpm])
                    t1, Bt1 = t1_rr.next()
                    t2, Bt2 = t2_rr.next()
                    ro = 2 if isq else 0
                    P.tt("pool", t1[:, 0:n], qf[:, 0:n].bitcast(F32), rt[:, ro, 0:n], ALU.mult, [Bqf, Brt], [Bt1])
                    P.tt("dve", t2[:, 0:n], pm[:, 0:n], rt[:, ro + 1, 0:n], ALU.mult, [Bpm, Brt], [Bt2])
                    if isq:
                        qo, Bqo = qo_rr.next()
                        P.tt("dve", qo[:, 0:n], t1[:, 0:n], t2[:, 0:n], ALU.add, [Bt1, Bt2], [Bqo])
                    else:
                        P.tt("dve", kT_all[:, c - 4, lo * 128:hi * 128], t1[:, 0:n], t2[:, 0:n], ALU.add,
                             [Bt1, Bt2], rng(Bkt, lo, hi))
                if isq:
                    P.dma("sp", q_scr[lo:hi, :, c, :].rearrange("t p n -> p t n"),
                          qo[:, 0:n].rearrange("p (t n) -> p t n", n=128), R=[Bqo], W=rng(Bq, lo, hi),
                          key="qo%d" % (qo_rr.i % 2))
        stop("A4", l)
        for gp in range(6):
            wv, wb = load_w(wA_fm[l, 10 + gp * 4:10 + gp * 4 + 4].rearrange("c p k m -> p c k m"), [4, 8, 128])
            for (lo, hi, st) in segsA:
                for cc in range(4):
                    gch = gp * 4 + cc
                    br, c = gch // 8, gch % 8
                    pj, Bpj, n = proj_chunk(wv[:, cc], wb, lo, hi)
                    gs_, Bgs = gs_rr.next()
                    P.act(gs_[:, 0:n], pj[:, 0:n], AF.Sigmoid, [Bpj], [Bgs])
                    P.dma("sp", g_scr[lo:hi, c, :, br, :].rearrange("t p n -> p t n"),
                          gs_[:, 0:n].rearrange("p (t n) -> p t n", n=128), R=[Bgs], W=rng(Bg, lo, hi),
                          key="gs%d" % (gs_rr.i % 3))
        if debug and l == 0:
            hh_ = P.halted
            P.halted = False
            P.dma("sp", dbg_k, kT_all, R=Bkt, W=[Bdbg], key="dbgk")
            P.dma("sp", dbg_v, v_all, R=Bv, W=[Bdbg], key="dbgv")
            P.dma("sp", dbg_p, p_all, R=Bp, W=[Bdbg], key="dbgp")
            P.halted = hh_
        stop("A", l)

        P.barrier()
        AR.release(res_mark)
        wsT = AR.alloc([4, 128], BF16)
        sgub = AR.alloc([2, 128], F32)
        wblk = AR.alloc([2, 128], BF16)
        bnd = AR.alloc([2 * 12, 128], BF16)
        B_bsp = Buf("band_special")
        wpo = AR.alloc([2, 1024], BF16)
        wso = AR.alloc([2, 1024], BF16)
        wao = AR.alloc([4, 1024], BF16)
        wo = AR.alloc([8, 1024], BF16)
        B_wD = Buf("wD")
        P.dma("pool", wsT, sguw[:, l], W=[B_wD], key="wD0")
        P.dma("sp", sgub, sgub_d[:, l], W=[B_wD], key="wD1")
        P.dma("pool", wblk, wblk_d[:, l], W=[B_wD], key="wD2")
        P.dma("pool", bnd[:, 0:12, :], bands[0], W=[B_wD], key="wD3")
        P.dma("pool", wpo, w_po[l].rearrange("(k p) n -> p k n", p=128), W=[B_wD], key="wD4", max_dma_last_dim=4096)
        P.dma("pool", wso, w_so[l].rearrange("(k p) n -> p k n", p=128), W=[B_wD], key="wD5", max_dma_last_dim=4096)
        P.dma("pool", wao, w_ao[l].rearrange("(k p) n -> p k n", p=128), W=[B_wD], key="wD6", max_dma_last_dim=4096)
        P.dma("pool", wo, w_o[l].rearrange("(k p) n -> p k n", p=128), W=[B_wD], key="wD7", max_dma_last_dim=4096)

        bias_rr = RR([(AR.alloc([1024], BF16), Buf("bias%d" % i)) for i in range(3)])
        qt_rr = RR([(AR.alloc([8, 128], BF16), Buf("qt%d" % i)) for i in range(2)])
        for qz_, Bqz_ in qt_rr.items:
            P.add("pool", lambda e, o=qz_: e.memset(o, 0.0), [], [Bqz_])
        ut_rr = RR([(AR.alloc([2, 128], F32), Buf("ut%d" % i)) for i in range(2)])
        vt_rr = RR([(AR.alloc([256], BF16), Buf("vt%d" % i)) for i in range(2)])
        pe_rr = RR([(AR.alloc([1024], BF16), Buf("pexp%d" % i)) for i in range(3)])
        ptb_rr = RR([(AR.alloc([8, 128], BF16), Buf("ptsb%d" % i)) for i in range(3)])
        smx2 = [(AR.alloc([16], F32), Buf("smx%d" % i)) for i in range(2)]
        rinv = AR.alloc([8], F32)
        B_rinv = Buf("rinv")
        ao = AR.alloc([512], BF16)
        B_ao = Buf("ao")
        pooledT = AR.alloc([2, 128], BF16)
        B_pooled = Buf("pooled")
        sgt = AR.alloc([2, 128], F32)
        B_sgt = Buf("sgt")
        oT_rr = RR([(AR.alloc([8, 512], BF16), Buf("oT%d" % i)) for i in range(2)])
        gt_rr = RR([(AR.alloc([4, 3, 128], BF16), Buf("gt%d" % i)) for i in range(2)])
        hc_rr = RR([(AR.alloc([512], F32), Buf("hc%d" % i)) for i in range(4)])
        e_sets = [[(AR.alloc([512], F32), Buf("et%d_%d" % (j_, i))) for i in range(3)] for j_ in range(2)]
        yT = AR.alloc([8, 512], BF16)
        B_yT = Buf("yT")

        def tile_type(t):
            if t >= 24:
                return 5
            j = t - 4
            return {0: 1, 1: 2, 14: 3, 15: 4}.get(j, 0)

        def band_type(t):
            if t == 24:
                return 3
            if t == 25:
                return 4
            j = t - 4
            return {0: 1, 15: 2}.get(j, 0)

        for (lo, hi, st) in segs_for("DE", l):
            n = (hi - lo) * 128
            oT, B_oT = oT_rr.next()
            tiles = list(range(lo, hi))
            TI = {}
            for t in tiles:
                if st == 1:
                    kranges = [(24, 26)]
                else:
                    j = t - 4
                    klo, khi = t - 2, t + 3
                    if j == 0:
                        khi = t + 4
                    if j == 15:
                        klo = t - 3
                    kranges = [(klo, khi), (24, 26)]
                TI[t] = dict(kr=kranges, nk=sum((b - a) for a, b in kranges) * 128,
                             ks=[s_ for a, b in kranges for s_ in range(a, b)], ty=tile_type(t),
                             tc0=(t - lo) * 128, smx=smx2[t % 2])

            def loads(t):
                ti = TI[t]
                ti["qt"] = qt_rr.next()
                qz4 = ti["qt"][0].rearrange("p (c two) n -> p c two n", two=2)
                P.dma("sp", qz4[0:64, :, 0, :], q_scr[t, 0:64], R=[Bq[t]], W=[ti["qt"][1]], key="qta%d" % (qt_rr.i % 2))
                P.dma("sp", qz4[64:128, :, 1, :], q_scr[t, 64:128], R=[Bq[t]], W=[ti["qt"][1]], key="qtb%d" % (qt_rr.i % 2))
                ti["ut"] = ut_rr.next()
                P.dma("sp", ti["ut"][0], u_scr[t], R=[Bu[t]], W=[ti["ut"][1]], key="ut%d" % (ut_rr.i % 2))
                ti["vt"] = vt_rr.next()
                P.dma("sp", ti["vt"][0], vn_scr[t], R=[Bvn[t]], W=[ti["vt"][1]], key="vt%d" % (vt_rr.i % 2))

            def prologue(t):
                ti = TI[t]
                tc0 = ti["tc0"]
                ut, But = ti["ut"]
                vt, Bvt = ti["vt"]
                bty = band_type(t)
                boff = 0
                if bty != 0:
                    P.dma("pool", bnd[:, 12:24, :], bands[bty], W=[B_bsp], key="bsp")
                    boff = 12
                PP = bank(7).rearrange("p (a b) -> p a b", b=128)
                for g in range(4):
                    c = g // 2
                    srcs = [d_ for d_ in (-1, 0, 1) if not ((t == 24 and d_ == -1) or (t == 25 and d_ == 1))]
                    for ii, d_ in enumerate(srcs):
                        P.mm(PP[:, g, :], p_all[:, t + d_, c * 128:(c + 1) * 128], bnd[:, boff + g * 3 + (d_ + 1), :],
                             ii == 0, ii == len(srcs) - 1, [Bp[t + d_], B_wD, B_bsp], [Bk[7]])
                for g in range(4):
                    gp_ = (g % 2) * 64
                    P.copy("act", pooledT[gp_:gp_ + 64, g // 2, :], PP[gp_:gp_ + 64, g, :], [Bk[7]], [B_pooled])
                PY = bank(7)[:, 0:256].rearrange("p (a b) -> p a b", b=128)
                for c in range(2):
                    P.mm(PY[:, c, :], wblk[:, c, :], pooledT[:, c, :], True, True, [B_pooled, B_wD], [Bk[7]])
                for c in range(2):
                    P.act(oT[:, c, tc0:tc0 + 128], PY[:, c, :], AF.Identity, [Bk[7], B_vec], [B_oT],
                          scale=V_ps(l)[:, c:c + 1], bias=0.0)
                PS_ = bank(7).rearrange("p (a b) -> p a b", b=128)
                for hh in range(4):
                    P.mm(PS_[:, hh, :], vt[:, (hh // 2) * 128:(hh // 2 + 1) * 128], wsT[:, hh, :], True, True,
                         [Bvt, B_wD], [Bk[7]])
                for hh in range(4):
                    hp = (hh % 2) * 64
                    P.tt("pool" if False else "dve", sgt[hp:hp + 64, hh // 2, :], PS_[hp:hp + 64, hh, :],
                         sgub[hp:hp + 64, hh // 2, :], ALU.add, [Bk[7], B_wD], [B_sgt])
                P.tt("pool", oT[:, 2:4, tc0:tc0 + 128], sgt, ut, ALU.mult, [B_sgt, But], [B_oT])

            units = [(t, h) for t in tiles for h in range(8)]
            US = [dict() for _ in units]

            def s1(k):
                t, h = units[k]
                ti = TI[t]
                qt, Bqt = ti["qt"]
                nk = ti["nk"]
                ch, pb = h // 2, (h % 2) * 64
                S = psd[h % 2]
                BS = [Bk[2 * (h % 2)], Bk[2 * (h % 2) + 1]]
                bt, Bbt = bias_rr.next()
                P.dma("sp", bt[:, 0:nk], biasd[l, ti["ty"], h, :, 0:nk], W=[Bbt], key="bias%d" % (bias_rr.i % 3))
                col = 0
                for (a, b) in ti["kr"]:
                    c0 = a * 128
                    rem = (b - a) * 128
                    while rem > 0:
                        w_ = min(rem, 512 - (col % 512))
                        P.mm(S[:, col:col + w_], qt[:, h, :], kT_all[:, ch, c0:c0 + w_],
                             (col % 512) == 0, False, [Bqt] + rng(Bkt, a, b), [BS[col // 512]])
                        col += w_
                        c0 += w_
                        rem -= w_
                for b0_ in range(0, nk, 512):
                    w_ = min(512, nk - b0_)
                    P.mm(S[:, b0_:b0_ + w_], identb, bt[:, b0_:b0_ + w_], False, True, [Bbt, B_id], [BS[b0_ // 512]])
                US[k].update(S=S, BS=BS, bt=bt, Bbt=Bbt)

            def s2(k):
                t, h = units[k]
                ti = TI[t]
                nk = ti["nk"]
                smx, B_smx = ti["smx"]
                u = US[k]
                P.add("dve", lambda e, o=smx[:, h:h + 1], i=u["S"][:, 0:nk]: e.tensor_reduce(
                    out=o, in_=i, axis=AX.X, op=ALU.max, negate=True), u["BS"], [B_smx])
                pex, Bpex = pe_rr.next()
                P.act(pex[:, 0:nk], u["S"][:, 0:nk], AF.Exp, u["BS"] + [B_smx], [Bpex, B_smx], bias=smx[:, h:h + 1],
                      scale=1.0, accum_out=smx[:, 8 + h:9 + h])
                u.update(pex=pex, Bpex=Bpex)

            def s3(k):
                t, h = units[k]
                nkt = TI[t]["nk"] // 128
                u = US[k]
                pb_ = 4 + (k % 2)
                PT = bank(pb_).bitcast(BF16).rearrange("p (a b) -> p a b", b=128)
                for kt in range(nkt):
                    P.tr(PT[:, kt, :], u["pex"][:, kt * 128:(kt + 1) * 128], identb, [u["Bpex"], B_id], [Bk[pb_]])
                ptb, Bptb = ptb_rr.next()
                P.copy("act", ptb[:, 0:nkt, :], PT[:, 0:nkt, :], [Bk[pb_]], [Bptb])
                u.update(ptb=ptb, Bptb=Bptb)

            def s4(k):
                t, h = units[k]
                ti = TI[t]
                nkt = ti["nk"] // 128
                ks = ti["ks"]
                u = US[k]
                for kt in range(nkt):
                    P.mm(bank(6)[:, h * 64:(h + 1) * 64], u["ptb"][:, kt, :], v_all[:, ks[kt], h * 64:(h + 1) * 64],
                         kt == 0, kt == nkt - 1, [u["Bptb"], Bv[ks[kt]]], [Bk[6]])
                if h == 7:
                    smx, B_smx = ti["smx"]
                    tc0 = ti["tc0"]
                    P.add("dve", lambda e, s_=smx: e.reciprocal(out=rinv, in_=s_[:, 8:16]), [B_smx], [B_rinv])
                    P.tt("dve", ao.rearrange("p (h d) -> p h d", d=64), bank(6).rearrange("p (h d) -> p h d", d=64),
                         rinv.unsqueeze(2).broadcast_to([128, 8, 64]), ALU.mult, [Bk[6], B_rinv], [B_ao])
                    AT = bank(7).bitcast(BF16).rearrange("p (a b) -> p a b", b=128)
                    for c in range(4):
                        P.tr(AT[:, c, :], ao[:, c * 128:(c + 1) * 128], identb, [B_ao, B_id], [Bk[7]])
                    P.copy("act", oT[:, 4:8, tc0:tc0 + 128], AT[:, 0:4, :], [Bk[7]], [B_oT])

            NU = len(units)
            loads(tiles[0])
            for k in range(NU + 3):
                if k < NU:
                    t, h = units[k]
                    if h == 0:
                        if t + 1 < hi:
                            loads(t + 1)
                        prologue(t)
                    s1(k)
                if 0 <= k - 1 < NU:
                    s2(k - 1)
                if 0 <= k - 2 < NU:
                    s3(k - 2)
                if 0 <= k - 3 < NU:
                    s4(k - 3)
            if debug and l == 0:
                for t in range(lo, hi):
                    P.dma("sp", dbg_o[t], oT[:, :, (t - lo) * 128:(t - lo + 1) * 128], R=[B_oT], W=[Bdbg], key="dbgo")
            for c in range(8):
                gt, Bgt = gt_rr.next()
                P.dma("sp", gt[:, 0:hi - lo], g_scr[lo:hi, c].rearrange("t p b n -> p t b n"), R=rng(Bg, lo, hi), W=[Bgt],
                      key="gt%d" % (gt_rr.i % 2))
                b0 = (c % 2) * 3
                brs = [(wpo, 2, 0), (wso, 2, 2), (wao, 4, 4)]
                for bi_, (wt_, nkk, o0) in enumerate(brs):
                    for k in range(nkk):
                        P.mm(bank(b0 + bi_)[:, 0:n], wt_[:, k, c * 128:(c + 1) * 128], oT[:, o0 + k, 0:n], k == 0, k == nkk - 1,
                             [B_wD, B_oT], [Bk[b0 + bi_]])
                e_t = e_sets[c % 2]
                for bi_ in range(3):
                    et, Bet = e_t[bi_]
                    P.tt("dve", et[:, 0:n].rearrange("p (t n) -> p t n", n=128),
                         bank(b0 + bi_)[:, 0:n].rearrange("p (t n) -> p t n", n=128), gt[:, 0:hi - lo, bi_, :],
                         ALU.mult, [Bk[b0 + bi_], Bgt], [Bet])
                P.tt("pool", e_t[0][0][:, 0:n], e_t[0][0][:, 0:n], e_t[1][0][:, 0:n], ALU.add, [e_t[0][1], e_t[1][1]], [e_t[0][1]])
                P.tt("pool", yT[:, c, 0:n], e_t[0][0][:, 0:n], e_t[2][0][:, 0:n], ALU.add, [e_t[0][1], e_t[2][1]], [B_yT])
            for c2 in range(8):
                ob = 6 + (c2 % 2)
                hc, Bhc = hc_rr.next()
                P.dma("sp", hc[:, 0:n], h_src(l, c2, lo, hi), R=([Bh[t][c2] for t in range(lo, hi)] if l > 0 else []),
                      W=[Bhc], key="hc%d" % (hc_rr.i % 4))
                for c in range(8):
                    P.mm(bank(ob)[:, 0:n], wo[:, c, c2 * 128:(c2 + 1) * 128], yT[:, c, 0:n], c == 0, c == 7,
                         [B_wD, B_yT], [Bk[ob]])
                P.stt("dve", hc[:, 0:n], bank(ob)[:, 0:n], MT(l, st, 2)[:, c2:c2 + 1], hc[:, 0:n],
                      ALU.mult, ALU.add, [Bk[ob], Bhc, B_modt], [Bhc])
                P.dma("sp", h_scr[c2, :, lo * 128:hi * 128], hc[:, 0:n], R=[Bhc], W=[Bh[t][c2] for t in range(lo, hi)],
                      key="hc%d" % (hc_rr.i % 4))
        stop("E", l)

        P.barrier()
        AR.release(m0)
        hF = AR.alloc([8, 1024], F32)
        B_hF = Buf("hF")
        aF = AR.alloc([8, 1024], BF16)
        B_aF = Buf("aF")
        hid = AR.alloc([32, 1024], BF16)
        B_hid = Buf("hid")
        ostg = (AR.alloc([8, 512], F32), Buf("ostg"))
        rsbF = (AR.alloc([512], F32), Buf("rsF"))
        tmpF_rr = RR([(AR.alloc([512], F32), Buf("tmpF%d" % i)) for i in range(2)])
        w1_rr = RR([(AR.alloc([4, 8, 128], BF16), Buf("w1b%d" % i)) for i in range(2)])
        w2_rr = RR([(AR.alloc([32, 128], BF16), Buf("w2b%d" % i)) for i in range(2)])
        rl_rr = RR([(AR.alloc([512], F32), Buf("rl%d" % i)) for i in range(2)])
        f1_rr = RR([(bank(i), Bk[i]) for i in (1, 2, 3, 4)])
        f2_rr = RR([(bank(i), Bk[i]) for i in (5, 6, 7)])
        for grp in segs_for("F", l):
            offs = []
            o_ = 0
            for (lo, hi, st) in grp:
                offs.append(o_)
                o_ += (hi - lo) * 128
            for (lo, hi, st), o0 in zip(grp, offs):
                n = (hi - lo) * 128
                P.dma("sp", hF[:, :, o0:o0 + n], h_scr[:, :, lo * 128:hi * 128].rearrange("c p n -> p c n"),
                      R=[b_ for t in range(lo, hi) for b_ in Bh[t]], W=[B_hF], key="hF")
            for (lo, hi, st), o0 in zip(grp, offs):
                n = (hi - lo) * 128
                rms_to_aT(hF[:, :, o0:o0 + n], B_hF, n, MT(l, st, 3), MT(l, st, 4), aF[:, :, o0:o0 + n], [B_aF],
                          rsbF, tmpF_rr, (bank(0), Bk[0]))
            for jp in range(8):
                w1t, Bw1 = w1_rr.next()
                P.dma("pool", w1t, w1r[l, jp * 4:jp * 4 + 4].rearrange("j p k m -> p j k m"), W=[Bw1],
                      key="w1b%d" % (w1_rr.i % 2), max_dma_last_dim=4096)
                for jj in range(4):
                    j = jp * 4 + jj
                    for (lo, hi, st), o0 in zip(grp, offs):
                        n = (hi - lo) * 128
                        pf, Bpf = f1_rr.next()
                        for k in range(8):
                            P.mm(pf[:, 0:n], w1t[:, jj, k, :], aF[:, k, o0:o0 + n], k == 0, k == 7, [Bw1, B_aF], [Bpf])
                        rl, Brl = rl_rr.next()
                        P.act(rl[:, 0:n], pf[:, 0:n], AF.Relu, [Bpf], [Brl])
                        P.stt("dve", hid[:, j, o0:o0 + n], pf[:, 0:n], 0.0, rl[:, 0:n], ALU.max, ALU.mult,
                              [Bpf, Brl], [B_hid])
            for c2 in range(8):
                w2t, Bw2 = w2_rr.next()
                P.dma("pool", w2t, w2r[l, c2], W=[Bw2], key="w2b%d" % (w2_rr.i % 2), max_dma_last_dim=4096)
                for (lo, hi, st), o0 in zip(grp, offs):
                    n = (hi - lo) * 128
                    pf, Bpf = f2_rr.next()
                    for j in range(32):
                        P.mm(pf[:, 0:n], w2t[:, j, :], hid[:, j, o0:o0 + n], j == 0, j == 31, [Bw2, B_hid], [Bpf])
                    P.stt("dve", hF[:, c2, o0:o0 + n], pf[:, 0:n], MT(l, st, 5)[:, c2:c2 + 1], hF[:, c2, o0:o0 + n],
                          ALU.mult, ALU.add, [Bpf, B_hF, B_modt], [B_hF])
            for (lo, hi, st), o0 in zip(grp, offs):
                n = (hi - lo) * 128
                if not last:
                    P.dma("sp", h_scr[:, :, lo * 128:hi * 128].rearrange("c p n -> p c n"), hF[:, :, o0:o0 + n],
                          R=[B_hF], W=[b_ for t in range(lo, hi) for b_ in Bh[t]], key="hF_out")
                else:
                    og, Bog = ostg
                    rs, Brs = rsbF
                    rms_rstd(hF[:, :, o0:o0 + n], B_hF, n, rsbF, (bank(0), Bk[0]))
                    for c in range(8):
                        P.stt("dve", og[:, c, 0:n], hF[:, c, o0:o0 + n], V_fg[:, c:c + 1], rs[:, 0:n], ALU.mult, ALU.mult,
                              [B_hF, Brs, B_vec], [Bog])
                    op = P.dma("sp", outT[:, :, (lo - 4) * 128:(hi - 4) * 128].rearrange("c p n -> p c n"), og[:, :, 0:n],
                               R=[Bog], W=[Bout], key="outst")
                    out_ops.append(op)
        stop("F", l)

    P.halted = False
    if not out_ops:
        z = AR.t[:, 0:2048]
        out_ops.append(P.dma("sp", outT[0], z, R=[], W=[Bout], key="outst"))
    if debug:
        out_ops.append(P.dma("sp", outT[1, :, 0:8], vec[:, 0:8], R=[Bdbg], W=[Bout], key="dbgfin"))
    P.final_waits = out_ops
    P.emit()
    return nc, P, AR


def _fm(v):
    v = np.asarray(v, np.float32)
    return np.ascontiguousarray(v.reshape(-1, 128).T)


def _rope_perm():
    idx = []
    for h in range(8):
        idx += [h * 64 + 2 * i for i in range(32)] + [h * 64 + 2 * i + 1 for i in range(32)]
    return np.array(idx)


def _shared_inputs(inp):
    w_in = np.asarray(inp["w_in"], np.float32)
    perm = _rope_perm()
    wp = w_in.copy()
    wp[:, :, 768:1280] = w_in[:, :, 768:1280][:, :, perm]
    wp[:, :, 1280:1792] = w_in[:, :, 1280:1792][:, :, perm]
    tm_cols = [np.r_[0:256, 512:768], np.r_[1792:2304]]
    wA_tm = np.stack([np.stack([wp[l][:, cols].reshape(8, 128, 512).transpose(1, 0, 2) for cols in tm_cols])
                      for l in range(2)])
    fm_starts = [256, 384] + [768 + 128 * i for i in range(8)] + [2304 + 128 * i for i in range(24)]
    wA_fm = np.stack([np.stack([wp[l][:, s:s + 128].reshape(8, 128, 128).transpose(1, 0, 2) for s in fm_starts])
                      for l in range(2)])
    w1 = np.asarray(inp["w_ff1"], np.float32)
    w1r = np.stack([np.stack([w1[l][:, j * 128:(j + 1) * 128].reshape(8, 128, 128).transpose(1, 0, 2)
                              for j in range(32)]) for l in range(2)])
    w2 = np.asarray(inp["w_ff2"], np.float32)
    w2r = np.stack([np.stack([w2[l][:, c * 128:(c + 1) * 128].reshape(32, 128, 128).transpose(1, 0, 2)
                              for c in range(8)]) for l in range(2)])
    sgu_w = np.asarray(inp["sgu_w"], np.float32)
    sguw = np.ascontiguousarray(sgu_w.transpose(3, 0, 1, 2))
    sgu_b = np.asarray(inp["sgu_b"], np.float32)
    sgub = np.zeros((128, 2, 2, 128), np.float32)
    for part in range(128):
        for c in range(2):
            sgub[part, :, c, :] = sgu_b[:, 2 * c + part // 64, :]
    w_pool = np.asarray(inp["w_pool"], np.float32)
    wblk = np.zeros((128, 2, 2, 128), np.float32)
    for c in range(2):
        for gl in range(2):
            wblk[gl * 64:(gl + 1) * 64, :, c, gl * 64:(gl + 1) * 64] = w_pool[:, 2 * c + gl].transpose(1, 0, 2)
    consts = np.zeros((128, 3, 128), np.float32)
    consts[:, 0, :] = np.eye(128)
    for pp in range(128):
        partner = pp + 32 if (pp % 64) < 32 else pp - 32
        consts[partner, 1, pp] = 1.0
    consts[:, 2, :] = 1.0
    return dict(w_mod=np.ascontiguousarray(inp["w_mod"], np.float32), wA_tm=np.ascontiguousarray(wA_tm),
                wA_fm=np.ascontiguousarray(wA_fm), sguw=sguw, sgub=sgub, wblk=wblk,
                w_pool_out=np.ascontiguousarray(inp["w_pool_out"], np.float32),
                w_sgu_out=np.ascontiguousarray(inp["w_sgu_out"], np.float32),
                w_attn_out=np.ascontiguousarray(inp["w_attn_out"], np.float32),
                w_o=np.ascontiguousarray(inp["w_o"], np.float32), w1r=np.ascontiguousarray(w1r),
                w2r=np.ascontiguousarray(w2r), consts=consts)


def _band_set(kind):
    out = np.zeros((128, 12, 128), np.float32)
    L = 384
    base = 128
    for g, w in enumerate((2, 4, 8, 16)):
        for tt in range(128):
            pos = base + tt
            lo_b = base if kind == 1 else 0
            hi_b = base + 128 if kind == 2 else L
            lo = min(max(pos - w // 2, lo_b), hi_b)
            hi = min(max(pos + (w - w // 2), lo_b), hi_b)
            cnt = hi - lo
            for s in range(lo, hi):
                d = s // 128
                out[s % 128, g * 3 + d, tt] += 1.0 / cnt
            out[tt, g * 3 + 1, tt] -= 1.0
    return out


def _bias_tables(rpb, core_rows0, n_rows_total=128):
    out = np.full((2, 6, 8, 128, 1024), NEG, np.float32)
    q_i = np.arange(128)
    for ty, j in ((0, 4), (1, 0), (2, 1), (3, 14), (4, 15)):
        klo, khi = j - 2, j + 3
        if j == 0:
            khi = j + 4
        if j == 15:
            klo = j - 3
        nkl = (khi - klo) * 128
        key = np.arange(nkl)
        k_row = core_rows0 + 2 * klo + key // 64
        k_col = key % 64
        q_row = core_rows0 + 2 * j + q_i // 64
        q_col = q_i % 64
        rs = np.clip(q_row - 4, 0, n_rows_total - 8)
        cs = np.clip(q_col - 8, 0, 64 - 16)
        valid = ((k_row[None, :] >= rs[:, None]) & (k_row[None, :] < rs[:, None] + 8) &
                 (k_col[None, :] >= cs[:, None]) & (k_col[None, :] < cs[:, None] + 16) &
                 (k_row[None, :] >= 0) & (k_row[None, :] < n_rows_total))
        dr = np.clip(k_row[None, :] - q_row[:, None] + 7, 0, 14)
        dc = np.clip(k_col[None, :] - q_col[:, None] + 15, 0, 30)
        for l in range(2):
            for h in range(8):
                g = rpb[l, h][dr, dc]
                out[l, ty, h, :, 0:nkl] = np.where(valid, g, NEG)
                out[l, ty, h, :, nkl:nkl + 256] = 0.0
    out[:, 5, :, :, 0:256] = 0.0
    return out.astype(ml_dtypes.bfloat16)


def _core_inputs(inp, core):
    b, blk = core // 4, core % 4
    row0 = 32 * blk
    x = np.asarray(inp["x"], np.float32)[b]
    t0 = (row0 - 8) * 64
    xs = np.zeros((NLAT, D), np.float32)
    lo, hi = max(t0, 0), min(t0 + NLAT, 8192)
    xs[lo - t0:hi - t0] = x[lo:hi]
    xT = np.ascontiguousarray(xs.T.reshape(8, 128, NLAT))
    ctxT = np.ascontiguousarray(np.asarray(inp["ctx"], np.float32)[b].T.reshape(8, 128, 256))
    vecs = np.zeros((128, 156), np.float32)
    vecs[:, 0:8] = _fm(inp["c"][b])
    vecs[:, 8:16] = _fm(inp["c_ctx"])
    for l in range(2):
        vecs[:, 16 + l * 48:16 + (l + 1) * 48] = _fm(inp["b_mod"][l])
        vecs[:, 112 + l * 8:112 + (l + 1) * 8] = _fm(inp["norm1_g"][l])
        vecs[:, 128 + l * 8:128 + (l + 1) * 8] = _fm(inp["norm2_g"][l])
        vecs[:, 152 + l * 2:152 + (l + 1) * 2] = _fm(inp["pool_scale"][l])
    vecs[:, 144:152] = _fm(inp["final_g"])
    tok = np.arange(NLAT)
    row = (row0 - 8 + tok // 64).astype(np.float32)
    col = (tok % 64).astype(np.float32)
    inv_freq = (10000.0 ** (-np.arange(16, dtype=np.float32) / 16)).astype(np.float32)
    ang = np.concatenate([row[:, None] * inv_freq, col[:, None] * inv_freq], axis=-1).astype(np.float32)
    cos, sin = np.cos(ang), np.sin(ang)
    rope = np.zeros((4, 128, NLAT), np.float32)
    for pp in range(128):
        i = pp % 64
        e = i % 32
        rope[0, pp] = cos[:, e]
        rope[1, pp] = -sin[:, e] if i < 32 else sin[:, e]
    rope[2:4] = rope[0:2] * np.float32(0.125)
    first = (blk == 0)
    lastb = (blk == 3)
    gen = _band_set(0)
    bands = np.stack([gen, _band_set(1) if first else gen, _band_set(2) if lastb else gen, _band_set(1), _band_set(2)])
    if first:
        bands[1][:, [0, 3, 6, 9], :] = 0.0
    bias = _bias_tables(np.asarray(inp["na_rpb"], np.float32), row0)
    return dict(xT=xT, ctxT=ctxT, vecs=vecs, rope=rope, bands=np.ascontiguousarray(bands), bias=bias)


_PROG = {}


def kernel(**inputs):
    if "nc" not in _PROG:
        _PROG["nc"] = build_program()[0]
    nc = _PROG["nc"]
    shared = _shared_inputs(inputs)
    in_maps = []
    for core in range(8):
        m = dict(shared)
        m.update(_core_inputs(inputs, core))
        in_maps.append(m)
    res = run_bass_kernel_spmd(nc, in_maps, core_ids=list(range(8)))
    out = np.zeros((2, 8192, D), np.float32)
    for core in range(8):
        b, blk = core // 4, core % 4
        oT = np.asarray(res.results[core]["outT"], np.float32)
        out[b, blk * 2048:(blk + 1) * 2048, :] = oT.reshape(D, 2048).T
    return out
```

```python
import numpy as np
import ml_dtypes
import concourse.bass as bass
import concourse.mybir as mybir
from concourse.bass_utils import run_bass_kernel_spmd

F32 = mybir.dt.float32
F32R = mybir.dt.float32r
BF16 = mybir.dt.bfloat16
AF = mybir.ActivationFunctionType
ALU = mybir.AluOpType
AX = mybir.AxisListType

D = 1024
NLS = 24
NS = 26
NLAT = NLS * 128
TOK = NS * 128
EPS = 1e-6
NEG = -30000.0
DBG_SKIP_LN = False


class Buf:
    __slots__ = ("name", "last_writer", "readers")

    def __init__(self, name):
        self.name = name
        self.last_writer = None
        self.readers = []


class Op:
    __slots__ = ("eng", "fn", "deps", "is_dma", "key", "idx", "signal", "sem", "val", "waits")

    def __init__(self, eng, fn, is_dma, key, idx):
        self.eng = eng
        self.fn = fn
        self.is_dma = is_dma
        self.key = key
        self.idx = idx
        self.deps = []
        self.signal = False
        self.sem = None
        self.val = 0
        self.waits = []


ENGS = ("pe", "act", "dve", "pool", "sp")


class Prog:
    def __init__(self, nc):
        self.nc = nc
        self.ops = []
        self.final_waits = []
        self.phase_buf = Buf("phase")
        self.bar_ap = None
        self.halted = False
        self.dummy = Op("dve", None, False, None, -1)

    def add(self, eng, fn, reads=(), writes=(), dma_key=None):
        if self.halted:
            return self.dummy
        is_dma = dma_key is not None
        op = Op(eng, fn, is_dma, dma_key, len(self.ops))
        deps = {}
        for b in reads:
            w = b.last_writer
            if w is not None:
                deps[w.idx] = [w, True]
        for b in writes:
            w = b.last_writer
            if w is not None and w.idx not in deps:
                deps[w.idx] = [w, False]
            for r in b.readers:
                if r.idx not in deps:
                    deps[r.idx] = [r, False]
        pb = self.phase_buf
        if pb.last_writer is not None:
            deps[pb.last_writer.idx] = [pb.last_writer, True]
        pb.readers.append(op)
        for b in reads:
            b.readers.append(op)
        for b in writes:
            b.last_writer = op
            b.readers = []
        deps.pop(op.idx, None)
        op.deps = list(deps.values())
        self.ops.append(op)
        return op

    def barrier(self):
        if self.halted:
            return self.dummy
        pb = self.phase_buf
        op = Op("dve", lambda e: e.memset(self.bar_ap, 0.0), False, None, len(self.ops))
        deps = {}
        for r in pb.readers:
            deps[r.idx] = [r, True]
        if pb.last_writer is not None:
            deps[pb.last_writer.idx] = [pb.last_writer, True]
        op.deps = list(deps.values())
        pb.last_writer = op
        pb.readers = []
        self.ops.append(op)
        return op

    def dma(self, eng, out, in_, R=(), W=(), key=None, **kw):
        return self.add(eng, lambda e: e.dma_start(out=out, in_=in_, **kw), R, W, dma_key=key)

    def mm(self, out, lhsT, rhs, start, stop, R, W):
        return self.add("pe", lambda e: e.matmul(out, lhsT=lhsT, rhs=rhs, start=start, stop=stop), R, W)

    def tr(self, out, in_, ident, R, W):
        return self.add("pe", lambda e: e.transpose(out=out, in_=in_, identity=ident), R, W)

    def act(self, out, in_, func, R, W, **kw):
        return self.add("act", lambda e: e.activation(out=out, in_=in_, func=func, **kw), R, W)

    def copy(self, eng, out, in_, R, W):
        if eng == "act":
            return self.add("act", lambda e: e.copy(out=out, in_=in_), R, W)
        return self.add(eng, lambda e: e.tensor_copy(out=out, in_=in_), R, W)

    def tt(self, eng, out, in0, in1, op, R, W):
        return self.add(eng, lambda e: e.tensor_tensor(out=out, in0=in0, in1=in1, op=op), R, W)

    def ts(self, eng, out, in0, s1, s2, op0, op1, R, W):
        return self.add(eng, lambda e: e.tensor_scalar(out=out, in0=in0, scalar1=s1, scalar2=s2, op0=op0, op1=op1), R, W)

    def stt(self, eng, out, in0, scalar, in1, op0, op1, R, W):
        return self.add(eng, lambda e: e.scalar_tensor_tensor(out=out, in0=in0, scalar=scalar, in1=in1,
                                                              op0=op0, op1=op1), R, W)

    def emit(self):
        nc = self.nc
        ops = self.ops
        for op in ops:
            need = []
            for d, raw in op.deps:
                if d.is_dma or op.is_dma or d.eng != op.eng or (raw and op.eng != "pe"):
                    need.append(d)
            op.waits = need
            for d in need:
                d.signal = True
        for op in self.final_waits:
            op.signal = True
        for op in ops:
            if op.is_dma:
                op.signal = True
        sems = {}
        counters = {}
        for op in ops:
            if not op.signal:
                continue
            k = ("dma", op.key) if op.is_dma else ("eng", op.eng)
            if k not in sems:
                sems[k] = nc.alloc_semaphore("s%d" % len(sems))
                counters[k] = 0
            op.sem = sems[k]
            counters[k] += 16 if op.is_dma else 1
            op.val = counters[k]
            op.key = k
        self.n_sems = len(sems)
        per_eng = {e: [] for e in ENGS}
        for op in ops:
            per_eng[op.eng].append(op)
        finals = list(self.final_waits)

        def run(eng_name, eng):
            waited = {}
            for op in per_eng[eng_name]:
                req = {}
                for d in op.waits:
                    if req.get(d.key, 0) < d.val:
                        req[d.key] = d.val
                for k, v in req.items():
                    if waited.get(k, 0) >= v:
                        continue
                    waited[k] = v
                    eng.wait_ge(sems[k], v)
                inst = op.fn(eng)
                if op.signal:
                    inst.then_inc(op.sem, 16 if op.is_dma else 1)
            if eng_name == "sp":
                for d in finals:
                    if waited.get(d.key, 0) < d.val:
                        waited[d.key] = d.val
                        eng.wait_ge(sems[d.key], d.val)

        with nc.Block() as block:
            @block.tensor
            def _(e):
                run("pe", e)

            @block.scalar
            def _(e):
                run("act", e)

            @block.vector
            def _(e):
                run("dve", e)

            @block.gpsimd
            def _(e):
                run("pool", e)

            @block.sync
            def _(e):
                run("sp", e)


class Arena:
    def __init__(self, nc, nbytes):
        self.t = nc.alloc_sbuf_tensor("arena", [128, nbytes // 4], F32)
        self.n = nbytes
        self.off = 0
        self.peak = 0

    def alloc(self, shape, dtype):
        esz = 2 if dtype == BF16 else 4
        n = int(np.prod(shape)) * esz
        n4 = (n + 31) // 32 * 32
        assert self.off + n4 <= self.n, ("SBUF arena overflow", self.off, n4, self.n)
        a = self.t[:, self.off // 4:(self.off + n) // 4]
        self.off += n4
        self.peak = max(self.peak, self.off)
        if dtype != F32:
            a = a.bitcast(dtype)
        if len(shape) == 2:
            return a.rearrange("p (a b) -> p a b", a=shape[0])
        if len(shape) == 3:
            return a.rearrange("p (a b c) -> p a b c", a=shape[0], b=shape[1])
        return a

    def mark(self):
        return self.off

    def release(self, m):
        self.off = m


class RR:
    def __init__(self, items):
        self.items = items
        self.i = 0

    def next(self):
        it = self.items[self.i % len(self.items)]
        self.i += 1
        return it


def segs_for(kind, layer):
    if kind == "A":
        if layer == 0:
            s = [(4 * b, 4 * b + 4, 0) for b in range(6)]
        else:
            s = [(2, 4, 0)] + [(4 * b, 4 * b + 4, 0) for b in range(1, 5)] + [(20, 22, 0)]
        return s + [(24, 26, 1)]
    if kind == "DE":
        s = [(4 * b, 4 * b + 4, 0) for b in range(1, 5)]
        if layer == 0:
            s = [(2, 4, 0)] + s + [(20, 22, 0), (24, 26, 1)]
        return s
    if kind == "F":
        g = [[(4, 8, 0), (8, 12, 0)], [(12, 16, 0), (16, 20, 0)]]
        if layer == 0:
            g.append([(2, 4, 0), (20, 22, 0), (24, 26, 1)])
        return g
    raise ValueError(kind)


def build_program(n_layers=2, stop_after=None, debug=False):
    nc = bass.Bass("TRN2", target_bir_lowering=False)
    P = Prog(nc)

    def din(name, shape, dt=F32):
        return nc.dram_tensor(name, list(shape), dt, kind="ExternalInput").ap()

    xT = din("xT", [8, 128, NLAT])
    ctxT = din("ctxT", [8, 128, 256])
    vecs = din("vecs", [128, 156])
    consts = din("consts", [128, 3, 128])
    rope = din("rope", [4, 128, NLAT])
    bands = din("bands", [5, 128, 12, 128])
    biasd = din("bias", [2, 6, 8, 128, 1024], BF16)
    w_mod = din("w_mod", [2, 1024, 6144])
    wA_tm = din("wA_tm", [2, 2, 128, 8, 512])
    wA_fm = din("wA_fm", [2, 34, 128, 8, 128])
    sguw = din("sguw", [128, 2, 4, 128])
    sgub_d = din("sgub", [128, 2, 2, 128])
    wblk_d = din("wblk", [128, 2, 2, 128])
    w_po = din("w_pool_out", [2, 256, 1024])
    w_so = din("w_sgu_out", [2, 256, 1024])
    w_ao = din("w_attn_out", [2, 512, 1024])
    w_o = din("w_o", [2, 1024, 1024])
    w1r = din("w1r", [2, 32, 128, 8, 128])
    w2r = din("w2r", [2, 8, 128, 32, 128])
    outT = nc.dram_tensor("outT", [8, 128, 2048], F32, kind="ExternalOutput").ap()

    skind = "ExternalOutput" if debug else "Internal"

    def dscr(name, shape, dt=F32):
        return nc.dram_tensor(name, list(shape), dt, kind=skind).ap()

    h_scr = dscr("h_scr", [8, 128, TOK])
    q_scr = dscr("q_scr", [NS, 128, 4, 128], BF16)
    u_scr = dscr("u_scr", [NS, 128, 2, 128])
    vn_scr = dscr("vn_scr", [NS, 128, 256], BF16)
    g_scr = dscr("g_scr", [NS, 8, 128, 3, 128], BF16)
    if debug:
        dbg_k = dscr("dbg_k", [128, 4, TOK], BF16)
        dbg_v = dscr("dbg_v", [128, NS, 512], BF16)
        dbg_p = dscr("dbg_p", [128, NS, 256], BF16)
        dbg_a = dscr("dbg_a", [128, 8, TOK], BF16)
        dbg_mod = dscr("dbg_mod", [128, 2, 2, 48])
        dbg_o = dscr("dbg_o", [NS, 128, 8, 128], BF16)

    Bh = [[Buf("h%d_%d" % (t, c)) for c in range(8)] for t in range(NS)]
    Bq = [Buf("q%d" % t) for t in range(NS)]
    Bu = [Buf("u%d" % t) for t in range(NS)]
    Bvn = [Buf("vn%d" % t) for t in range(NS)]
    Bg = [Buf("g%d" % t) for t in range(NS)]
    Bout = Buf("out")
    Bdbg = Buf("dbg")

    psd = [nc.alloc_psum_tensor("psd%d" % i, [128, 1024], F32) for i in range(4)]
    Bk = [Buf("bank%d" % i) for i in range(8)]

    def bank(i):
        return psd[i // 2][:, (i % 2) * 512:(i % 2) * 512 + 512]

    AR = Arena(nc, 197 * 1024)
    bar_t = AR.alloc([8], F32)
    P.bar_ap = bar_t
    cst = AR.alloc([3, 128], F32)
    identb = AR.alloc([128], BF16)
    perm_r = nc.alloc_sbuf_tensor("perm_r", [128, 128], F32R)[:]
    ones_r = nc.alloc_sbuf_tensor("ones_r", [128, 128], F32R)[:]
    sq_rr = RR([(nc.alloc_sbuf_tensor("sq_r%d" % i, [128, 512], F32R)[:], Buf("sq_r%d" % i)) for i in range(2)])
    qf_rr = RR([(nc.alloc_sbuf_tensor("qf_r%d" % i, [128, 512], F32R)[:], Buf("qf_r%d" % i)) for i in range(2)])
    vec = AR.alloc([156], F32)
    modt = AR.alloc([24, 8], F32)
    silu_b = AR.alloc([2, 8], BF16)
    B_c = Buf("consts")
    B_vec = Buf("vec")
    B_modt = Buf("modt")
    B_silu = Buf("silu")

    def V_c(s):
        return vec[:, s * 8:(s + 1) * 8]

    def V_bmod(l):
        return vec[:, 16 + l * 48:16 + (l + 1) * 48]

    def V_n1(l):
        return vec[:, 112 + l * 8:112 + (l + 1) * 8]

    def V_n2(l):
        return vec[:, 128 + l * 8:128 + (l + 1) * 8]

    V_fg = vec[:, 144:152]

    def V_ps(l):
        return vec[:, 152 + l * 2:152 + (l + 1) * 2]

    def MT(l, s, kind):
        i = (l * 2 + s) * 6 + kind
        return modt[:, i, :]

    P.dma("sp", cst, consts, W=[B_c], key="cst")
    P.dma("sp", vec, vecs, W=[B_vec], key="vec")
    B_id = Buf("ident")
    P.copy("dve", identb, cst[:, 0, :], [B_c], [B_id])
    P.copy("dve", perm_r, cst[:, 1, :], [B_c], [B_id])
    P.copy("dve", ones_r, cst[:, 2, :], [B_c], [B_id])
    P.act(silu_b.rearrange("p s k -> p (s k)"), vec[:, 0:16], AF.Silu, [B_vec], [B_silu])

    def stop(name, l=0):
        if stop_after is not None and tuple(stop_after) == (name, l):
            P.halted = True

    stop("S")

    m0 = AR.mark()
    wm = [(AR.alloc([8, 512], BF16), Buf("wm%d" % i)) for i in range(2)]
    wm_rr = RR(wm)
    modraw = AR.alloc([48, 2], F32)
    tmpm = AR.alloc([8, 2], F32)
    B_modraw = Buf("modraw")
    for l in range(n_layers):
        psm = bank(0).rearrange("p (a b) -> p a b", b=2)[:, 0:48, :]
        for pc in range(12):
            wt, wb = wm_rr.next()
            P.dma("pool", wt, w_mod[l, :, pc * 512:(pc + 1) * 512].rearrange("(k p) n -> p k n", p=128),
                  W=[wb], key="wm%d" % (wm_rr.i % 2), max_dma_last_dim=4096)
            for cc in range(4):
                ch = pc * 4 + cc
                for k in range(8):
                    P.mm(psm[:, ch, :], wt[:, k, cc * 128:(cc + 1) * 128], silu_b[:, :, k], k == 0, k == 7,
                         [wb, B_silu], [Bk[0]])
        P.tt("dve", modraw, psm, V_bmod(l).unsqueeze(2).broadcast_to([128, 48, 2]), ALU.add,
             [Bk[0], B_vec], [B_modraw])
        for s in range(2):
            def chunk(i):
                return modraw[:, i * 8:(i + 1) * 8, s]
            P.stt("dve", MT(l, s, 0), chunk(1), 1.0, V_n1(l), ALU.add, ALU.mult, [B_modraw, B_vec], [B_modt])
            P.copy("dve", MT(l, s, 1), chunk(0), [B_modraw], [B_modt])
            P.copy("dve", MT(l, s, 2), chunk(2), [B_modraw], [B_modt])
            P.stt("dve", MT(l, s, 3), chunk(4), 1.0, V_n2(l), ALU.add, ALU.mult, [B_modraw, B_vec], [B_modt])
            P.copy("dve", MT(l, s, 4), chunk(3), [B_modraw], [B_modt])
            P.copy("dve", MT(l, s, 5), chunk(5), [B_modraw], [B_modt])
    if debug:
        P.dma("sp", dbg_mod.rearrange("p l s c -> p (l s c)"), modt.rearrange("p a b -> p (a b)")[:, 0:192],
              R=[B_modt], W=[Bdbg], key="dbgm")
    AR.release(m0)
    stop("M")
    P.barrier()

    kT_all = AR.alloc([4, TOK], BF16)
    v_all = AR.alloc([NS, 512], BF16)
    p_all = AR.alloc([NS, 256], BF16)
    Bkt = [Buf("kT%d" % t) for t in range(NS)]
    Bv = [Buf("v%d" % t) for t in range(NS)]
    Bp = [Buf("p%d" % t) for t in range(NS)]
    res_mark = AR.mark()

    def rng(bufs, lo, hi):
        return [bufs[t] for t in range(lo, hi)]

    def h_src(l, c, lo, hi):
        if l == 0:
            if lo >= 24:
                return ctxT[c, :, (lo - 24) * 128:(hi - 24) * 128]
            return xT[c, :, lo * 128:hi * 128]
        return h_scr[c, :, lo * 128:hi * 128]

    def load_h(dst, l, lo, hi, Bdst, key, first_layer_input):
        n = (hi - lo) * 128
        if first_layer_input:
            src = (ctxT[:, :, (lo - 24) * 128:(hi - 24) * 128] if lo >= 24 else xT[:, :, lo * 128:hi * 128])
            R = []
        else:
            src = h_scr[:, :, lo * 128:hi * 128]
            R = [b_ for t in range(lo, hi) for b_ in Bh[t]]
        P.dma("sp", dst[:, :, 0:n], src.rearrange("c p n -> p c n"), R=R, W=[Bdst], key=key)

    def rms_rstd(hb, Bhb, n, rsb, psb):
        rs, Brs = rsb
        pb, Bpb = psb
        for c in range(8):
            sq, Bsq = sq_rr.next()
            P.act(sq[:, 0:n], hb[:, c, 0:n], AF.Square, [Bhb], [Bsq])
            P.mm(pb[:, 0:n], ones_r, sq[:, 0:n], c == 0, c == 7, [Bsq, B_id], [Bpb])
        P.ts("dve", rs[:, 0:n], pb[:, 0:n], 1.0 / D, EPS, ALU.mult, ALU.add, [Bpb], [Brs])
        P.act(rs[:, 0:n], rs[:, 0:n], AF.Sqrt, [Brs], [Brs])
        P.add("dve", lambda e: e.reciprocal(out=rs[:, 0:n], in_=rs[:, 0:n]), [Brs], [Brs])

    def rms_to_aT(hb, Bhb, n, gs, sh, dst, Wdst, rsb, tmp_rr, psb):
        rs, Brs = rsb
        rms_rstd(hb, Bhb, n, rsb, psb)
        for c in range(8):
            tmp, Btmp = tmp_rr.next()
            P.stt("dve", tmp[:, 0:n], hb[:, c, 0:n], gs[:, c:c + 1], rs[:, 0:n], ALU.mult, ALU.mult,
                  [Bhb, Brs, B_modt], [Btmp])
            P.act(dst[:, c, 0:n], tmp[:, 0:n], AF.Identity, [Btmp, B_modt], Wdst, bias=sh[:, c:c + 1], scale=1.0)

    out_ops = []

    for l in range(n_layers):
        last = (l == n_layers - 1)
        if l > 0:
            P.barrier()
        AR.release(res_mark)
        aT_all = AR.alloc([8, TOK], BF16)
        Ba = [Buf("aT%d" % t) for t in range(NS)]
        a_mark = AR.mark()
        hst = [(AR.alloc([8, 512], F32), Buf("hst%d" % i)) for i in range(2)]
        hst_rr = RR(hst)
        rsb = (AR.alloc([512], F32), Buf("rs"))
        tmp_rr = RR([(AR.alloc([512], F32), Buf("tmp%d" % i)) for i in range(2)])
        segsA = segs_for("A", l)
        for si, (lo, hi, st) in enumerate(segsA):
            n = (hi - lo) * 128
            hb, Bhb = hst_rr.next()
            load_h(hb, l, lo, hi, Bhb, "hst%d" % (hst_rr.i % 2), l == 0)
            bi = si % 2
            rms_to_aT(hb, Bhb, n, MT(l, st, 0), MT(l, st, 1), aT_all[:, :, lo * 128:hi * 128], rng(Ba, lo, hi), rsb,
                      tmp_rr, (bank(bi), Bk[bi]))
        if debug and l == 0:
            P.dma("sp", dbg_a, aT_all, R=Ba, W=[Bdbg], key="dbga")
        stop("A1", l)
        P.barrier()
        AR.release(a_mark)

        wbuf = [(AR.alloc([8192], BF16), Buf("wbuf%d" % i)) for i in range(2)]
        wb_rr = RR(wbuf)
        ropeb = [(AR.alloc([4, 512], F32), Buf("rope%d" % i)) for i in range(2)]
        rope_rr = RR(ropeb)
        t1_rr = RR([(AR.alloc([512], F32), Buf("t1%d" % i)) for i in range(2)])
        t2_rr = RR([(AR.alloc([512], F32), Buf("t2%d" % i)) for i in range(2)])
        qo_rr = RR([(AR.alloc([512], BF16), Buf("qo%d" % i)) for i in range(2)])
        us_rr = RR([(AR.alloc([512], F32), Buf("us%d" % i)) for i in range(3)])
        gs_rr = RR([(AR.alloc([512], BF16), Buf("gs%d" % i)) for i in range(3)])
        vns_rr = RR([(AR.alloc([256], BF16), Buf("vns%d" % i)) for i in range(2)])
        st_rr = RR([(AR.alloc([8], F32), Buf("st%d" % i)) for i in range(2)])
        vsf_rr = RR([(AR.alloc([256], F32), Buf("vsf%d" % i)) for i in range(4)])
        pj_rr = RR([(bank(i), Bk[i]) for i in range(4)])
        pm_rr = RR([(bank(i), Bk[i]) for i in (4, 5)])
        tm_rr = RR([(bank(i), Bk[i]) for i in (4, 5, 6, 7)])

        def load_w(src, shape):
            wt, wb = wb_rr.next()
            nel = int(np.prod(shape))
            view = wt[:, 0:nel]
            if len(shape) == 2:
                view = view.rearrange("p (a b) -> p a b", a=shape[0])
            else:
                view = view.rearrange("p (a b c) -> p a b c", a=shape[0], b=shape[1])
            P.dma("pool", view, src, W=[wb], key="wbuf%d" % (wb_rr.i % 2), max_dma_last_dim=4096)
            return view, wb

        for piece in range(2):
            wv, wb = load_w(wA_tm[l, piece], [8, 512])
            for (lo, hi, st) in segsA:
                for t in range(lo, hi):
                    pt, Bpt = tm_rr.next()
                    for k in range(8):
                        P.mm(pt, aT_all[:, k, t * 128:(t + 1) * 128], wv[:, k, :], k == 0, k == 7, [Ba[t], wb], [Bpt])
                    if piece == 0:
                        P.copy("act", p_all[:, t, :], pt[:, 0:256], [Bpt], [Bp[t]])
                        if DBG_SKIP_LN:
                            continue
                        stt_, Bst = st_rr.next()
                        vf, Bvf = vsf_rr.next()
                        P.act(vf, pt[:, 256:512], AF.Identity, [Bpt], [Bvf, Bst], accum_out=stt_[:, 0:1])
                        P.ts("dve", stt_[:, 1:2], stt_[:, 0:1], -1.0 / 256, None, ALU.mult, ALU.bypass, [Bst], [Bst])
                        jk, Bjk = vsf_rr.next()
                        P.act(jk, vf, AF.Square, [Bvf, Bst], [Bjk, Bst], bias=stt_[:, 1:2], scale=1.0, accum_out=stt_[:, 2:3])
                        P.ts("dve", stt_[:, 3:4], stt_[:, 2:3], 1.0 / 256, EPS, ALU.mult, ALU.add, [Bst], [Bst])
                        P.act(stt_[:, 3:4], stt_[:, 3:4], AF.Sqrt, [Bst], [Bst])
                        P.add("dve", lambda e, o=stt_[:, 3:4]: e.reciprocal(out=o, in_=o), [Bst], [Bst])
                        vs_, Bvs = vns_rr.next()
                        P.ts("dve", vs_, vf, stt_[:, 1:2], stt_[:, 3:4], ALU.add, ALU.mult, [Bvf, Bst], [Bvs])
                        P.dma("sp", vn_scr[t], vs_, R=[Bvs], W=[Bvn[t]], key="vns%d" % (vns_rr.i % 2))
                    else:
                        P.copy("act", v_all[:, t, :], pt, [Bpt], [Bv[t]])

        stop("A2", l)
        def proj_chunk(wv, wb, lo, hi):
            n = (hi - lo) * 128
            pj, Bpj = pj_rr.next()
            for k in range(8):
                P.mm(pj[:, 0:n], wv[:, k, :], aT_all[:, k, lo * 128:hi * 128], k == 0, k == 7,
                     rng(Ba, lo, hi) + [wb], [Bpj])
            return pj, Bpj, n

        wv, wb = load_w(wA_fm[l, 0:2].rearrange("c p k m -> p c k m"), [2, 8, 128])
        for (lo, hi, st) in segsA:
            for c in range(2):
                pj, Bpj, n = proj_chunk(wv[:, c], wb, lo, hi)
                us, Bus = us_rr.next()
                P.copy("act", us[:, 0:n], pj[:, 0:n], [Bpj], [Bus])
                P.dma("sp", u_scr[lo:hi, :, c, :].rearrange("t p n -> p t n"),
                      us[:, 0:n].rearrange("p (t n) -> p t n", n=128), R=[Bus], W=rng(Bu, lo, hi),
                      key="us%d" % (us_rr.i % 3))
        stop("A3", l)
        wv, wb = load_w(wA_fm[l, 2:10].rearrange("c p k m -> p c k m"), [8, 8, 128])
        for (lo, hi, st) in segsA:
            n = (hi - lo) * 128
            if st == 0:
                rt, Brt = rope_rr.next()
                P.dma("sp", rt[:, :, 0:n], rope[:, :, lo * 128:hi * 128].rearrange("a p n -> p a n"), W=[Brt],
                      key="rope%d" % (rope_rr.i % 2))
            for c in range(8):
                pj, Bpj, n = proj_chunk(wv[:, c], wb, lo, hi)
                isq = c < 4
                if st == 1:
                    if isq:
                        qo, Bqo = qo_rr.next()
                        P.act(qo[:, 0:n], pj[:, 0:n], AF.Identity, [Bpj], [Bqo], scale=0.125, bias=0.0)
                    else:
                        P.copy("act", kT_all[:, c - 4, lo * 128:hi * 128], pj[:, 0:n], [Bpj], rng(Bkt, lo, hi))
                else:
                    qf, Bqf = qf_rr.next()
                    P.copy("act", qf[:, 0:n], pj[:, 0:n], [Bpj], [Bqf])
                    pm, Bpm = pm_rr.next()
                    P.mm(pm[:, 0:n], perm_r, qf[:, 0:n], True, True, [Bqf, B_id], [Bpm])
                    t1, Bt1 = t1_rr.next()
                    t2, Bt2 = t2_rr.next()
                    ro = 2 if isq else 0
                    P.tt("pool", t1[:, 0:n], qf[:, 0:n].bitcast(F32), rt[:, ro, 0:n], ALU.mult, [Bqf, Brt], [Bt1])
                    P.tt("dve", t2[:, 0:n], pm[:, 0:n], rt[:, ro + 1, 0:n], ALU.mult, [Bpm, Brt], [Bt2])
                    if isq:
                        qo, Bqo = qo_rr.next()
                        P.tt("dve", qo[:, 0:n], t1[:, 0:n], t2[:, 0:n], ALU.add, [Bt1, Bt2], [Bqo])
                    else:
                        P.tt("dve", kT_all[:, c - 4, lo * 128:hi * 128], t1[:, 0:n], t2[:, 0:n], ALU.add,
                             [Bt1, Bt2], rng(Bkt, lo, hi))
                if isq:
                    P.dma("sp", q_scr[lo:hi, :, c, :].rearrange("t p n -> p t n"),
                          qo[:, 0:n].rearrange("p (t n) -> p t n", n=128), R=[Bqo], W=rng(Bq, lo, hi),
                          key="qo%d" % (qo_rr.i % 2))
        stop("A4", l)
        for gp in range(6):
            wv, wb = load_w(wA_fm[l, 10 + gp * 4:10 + gp * 4 + 4].rearrange("c p k m -> p c k m"), [4, 8, 128])
            for (lo, hi, st) in segsA:
                for cc in range(4):
                    gch = gp * 4 + cc
                    br, c = gch // 8, gch % 8
                    pj, Bpj, n = proj_chunk(wv[:, cc], wb, lo, hi)
                    gs_, Bgs = gs_rr.next()
                    P.act(gs_[:, 0:n], pj[:, 0:n], AF.Sigmoid, [Bpj], [Bgs])
                    P.dma("sp", g_scr[lo:hi, c, :, br, :].rearrange("t p n -> p t n"),
                          gs_[:, 0:n].rearrange("p (t n) -> p t n", n=128), R=[Bgs], W=rng(Bg, lo, hi),
                          key="gs%d" % (gs_rr.i % 3))
        if debug and l == 0:
            hh_ = P.halted
            P.halted = False
            P.dma("sp", dbg_k, kT_all, R=Bkt, W=[Bdbg], key="dbgk")
            P.dma("sp", dbg_v, v_all, R=Bv, W=[Bdbg], key="dbgv")
            P.dma("sp", dbg_p, p_all, R=Bp, W=[Bdbg], key="dbgp")
            P.halted = hh_
        stop("A", l)

        P.barrier()
        AR.release(res_mark)
        wsT = AR.alloc([4, 128], BF16)
        sgub = AR.alloc([2, 128], F32)
        wblk = AR.alloc([2, 128], BF16)
        bnd = AR.alloc([2 * 12, 128], BF16)
        B_bsp = Buf("band_special")
        wpo = AR.alloc([2, 1024], BF16)
        wso = AR.alloc([2, 1024], BF16)
        wao = AR.alloc([4, 1024], BF16)
        wo = AR.alloc([8, 1024], BF16)
        B_wD = Buf("wD")
        B_wE = Buf("wE")
        P.dma("pool", wsT, sguw[:, l], W=[B_wD], key="wD0")
        P.dma("sp", sgub, sgub_d[:, l], W=[B_wD], key="wD1")
        P.dma("pool", wblk, wblk_d[:, l], W=[B_wD], key="wD2")
        P.dma("pool", bnd[:, 0:12, :], bands[0], W=[B_wD], key="wD3")
        P.dma("pool", wpo, w_po[l].rearrange("(k p) n -> p k n", p=128), W=[B_wE], key="wD4", max_dma_last_dim=4096)
        P.dma("pool", wso, w_so[l].rearrange("(k p) n -> p k n", p=128), W=[B_wE], key="wD5", max_dma_last_dim=4096)
        P.dma("pool", wao, w_ao[l].rearrange("(k p) n -> p k n", p=128), W=[B_wE], key="wD6", max_dma_last_dim=4096)
        P.dma("pool", wo, w_o[l].rearrange("(k p) n -> p k n", p=128), W=[B_wE], key="wD7", max_dma_last_dim=4096)

        bias_rr = RR([(AR.alloc([1024], BF16), Buf("bias%d" % i)) for i in range(3)])
        qt_rr = RR([(AR.alloc([8, 128], BF16), Buf("qt%d" % i)) for i in range(2)])
        for qz_, Bqz_ in qt_rr.items:
            P.add("pool", lambda e, o=qz_: e.memset(o, 0.0), [], [Bqz_])
        ut_rr = RR([(AR.alloc([2, 128], F32), Buf("ut%d" % i)) for i in range(2)])
        vt_rr = RR([(AR.alloc([256], BF16), Buf("vt%d" % i)) for i in range(2)])
        pe_rr = RR([(AR.alloc([1024], BF16), Buf("pexp%d" % i)) for i in range(3)])
        ptb_rr = RR([(AR.alloc([8, 128], BF16), Buf("ptsb%d" % i)) for i in range(3)])
        smx2 = [(AR.alloc([16], F32), Buf("smx%d" % i)) for i in range(2)]
        rinv = AR.alloc([8], F32)
        B_rinv = Buf("rinv")
        ao = AR.alloc([512], BF16)
        B_ao = Buf("ao")
        pooledT = AR.alloc([2, 128], BF16)
        B_pooled = Buf("pooled")
        sgt = AR.alloc([2, 128], F32)
        B_sgt = Buf("sgt")
        oT_rr = RR([(AR.alloc([8, 512], BF16), Buf("oT%d" % i)) for i in range(2)])
        gt_rr = RR([(AR.alloc([4, 3, 128], BF16), Buf("gt%d" % i)) for i in range(2)])
        hc_rr = RR([(AR.alloc([512], F32), Buf("hc%d" % i)) for i in range(4)])
        e_sets = [[(AR.alloc([512], F32), Buf("et%d_%d" % (j_, i))) for i in range(3)] for j_ in range(2)]
        yT = AR.alloc([8, 512], BF16)
        B_yT = Buf("yT")

        def tile_type(t):
            if t >= 24:
                return 5
            j = t - 4
            return {0: 1, 1: 2, 14: 3, 15: 4}.get(j, 0)

        def band_type(t):
            if t == 24:
                return 3
            if t == 25:
                return 4
            j = t - 4
            return {0: 1, 15: 2}.get(j, 0)

        for (lo, hi, st) in segs_for("DE", l):
            n = (hi - lo) * 128
            oT, B_oT = oT_rr.next()
            tiles = list(range(lo, hi))
            TI = {}
            for t in tiles:
                if st == 1:
                    kranges = [(24, 26)]
                else:
                    j = t - 4
                    klo, khi = t - 2, t + 3
                    if j == 0:
                        khi = t + 4
                    if j == 15:
                        klo = t - 3
                    kranges = [(klo, khi), (24, 26)]
                TI[t] = dict(kr=kranges, nk=sum((b - a) for a, b in kranges) * 128,
                             ks=[s_ for a, b in kranges for s_ in range(a, b)], ty=tile_type(t),
                             tc0=(t - lo) * 128, smx=smx2[t % 2])

            def loads(t):
                ti = TI[t]
                ti["qt"] = qt_rr.next()
                qz4 = ti["qt"][0].rearrange("p (c two) n -> p c two n", two=2)
                P.dma("sp", qz4[0:64, :, 0, :], q_scr[t, 0:64], R=[Bq[t]], W=[ti["qt"][1]], key="qta%d" % (qt_rr.i % 2))
                P.dma("sp", qz4[64:128, :, 1, :], q_scr[t, 64:128], R=[Bq[t]], W=[ti["qt"][1]], key="qtb%d" % (qt_rr.i % 2))
                ti["ut"] = ut_rr.next()
                P.dma("sp", ti["ut"][0], u_scr[t], R=[Bu[t]], W=[ti["ut"][1]], key="ut%d" % (ut_rr.i % 2))
                ti["vt"] = vt_rr.next()
                P.dma("sp", ti["vt"][0], vn_scr[t], R=[Bvn[t]], W=[ti["vt"][1]], key="vt%d" % (vt_rr.i % 2))

            def prologue(t):
                ti = TI[t]
                tc0 = ti["tc0"]
                ut, But = ti["ut"]
                vt, Bvt = ti["vt"]
                bty = band_type(t)
                boff = 0
                if bty != 0:
                    P.dma("pool", bnd[:, 12:24, :], bands[bty], W=[B_bsp], key="bsp")
                    boff = 12
                PP = bank(7).rearrange("p (a b) -> p a b", b=128)
                for g in range(4):
                    c = g // 2
                    srcs = [d_ for d_ in (-1, 0, 1) if not ((t == 24 and d_ == -1) or (t == 25 and d_ == 1))]
                    for ii, d_ in enumerate(srcs):
                        P.mm(PP[:, g, :], p_all[:, t + d_, c * 128:(c + 1) * 128], bnd[:, boff + g * 3 + (d_ + 1), :],
                             ii == 0, ii == len(srcs) - 1, [Bp[t + d_], B_wD, B_bsp], [Bk[7]])
                for g in range(4):
                    gp_ = (g % 2) * 64
                    P.copy("act", pooledT[gp_:gp_ + 64, g // 2, :], PP[gp_:gp_ + 64, g, :], [Bk[7]], [B_pooled])
                PY = bank(7)[:, 0:256].rearrange("p (a b) -> p a b", b=128)
                for c in range(2):
                    P.mm(PY[:, c, :], wblk[:, c, :], pooledT[:, c, :], True, True, [B_pooled, B_wD], [Bk[7]])
                for c in range(2):
                    P.act(oT[:, c, tc0:tc0 + 128], PY[:, c, :], AF.Identity, [Bk[7], B_vec], [B_oT],
                          scale=V_ps(l)[:, c:c + 1], bias=0.0)
                PS_ = bank(7).rearrange("p (a b) -> p a b", b=128)
                for hh in range(4):
                    P.mm(PS_[:, hh, :], vt[:, (hh // 2) * 128:(hh // 2 + 1) * 128], wsT[:, hh, :], True, True,
                         [Bvt, B_wD], [Bk[7]])
                for hh in range(4):
                    hp = (hh % 2) * 64
                    P.tt("pool" if False else "dve", sgt[hp:hp + 64, hh // 2, :], PS_[hp:hp + 64, hh, :],
                         sgub[hp:hp + 64, hh // 2, :], ALU.add, [Bk[7], B_wD], [B_sgt])
                P.tt("pool", oT[:, 2:4, tc0:tc0 + 128], sgt, ut, ALU.mult, [B_sgt, But], [B_oT])

            units = [(t, h) for t in tiles for h in range(8)]
            US = [dict() for _ in units]

            def s1(k):
                t, h = units[k]
                ti = TI[t]
                qt, Bqt = ti["qt"]
                nk = ti["nk"]
                ch, pb = h // 2, (h % 2) * 64
                S = psd[h % 2]
                BS = [Bk[2 * (h % 2)], Bk[2 * (h % 2) + 1]]
                bt, Bbt = bias_rr.next()
                P.dma("sp", bt[:, 0:nk], biasd[l, ti["ty"], h, :, 0:nk], W=[Bbt], key="bias%d" % (bias_rr.i % 3))
                col = 0
                for (a, b) in ti["kr"]:
                    c0 = a * 128
                    rem = (b - a) * 128
                    while rem > 0:
                        w_ = min(rem, 512 - (col % 512))
                        P.mm(S[:, col:col + w_], qt[:, h, :], kT_all[:, ch, c0:c0 + w_],
                             (col % 512) == 0, False, [Bqt] + rng(Bkt, a, b), [BS[col // 512]])
                        col += w_
                        c0 += w_
                        rem -= w_
                for b0_ in range(0, nk, 512):
                    w_ = min(512, nk - b0_)
                    P.mm(S[:, b0_:b0_ + w_], identb, bt[:, b0_:b0_ + w_], False, True, [Bbt, B_id], [BS[b0_ // 512]])
                US[k].update(S=S, BS=BS, bt=bt, Bbt=Bbt)

            def s2(k):
                t, h = units[k]
                ti = TI[t]
                nk = ti["nk"]
                smx, B_smx = ti["smx"]
                u = US[k]
                P.add("dve", lambda e, o=smx[:, h:h + 1], i=u["S"][:, 0:nk]: e.tensor_reduce(
                    out=o, in_=i, axis=AX.X, op=ALU.max, negate=True), u["BS"], [B_smx])
                pex, Bpex = pe_rr.next()
                P.act(pex[:, 0:nk], u["S"][:, 0:nk], AF.Exp, u["BS"] + [B_smx], [Bpex, B_smx], bias=smx[:, h:h + 1],
                      scale=1.0, accum_out=smx[:, 8 + h:9 + h])
                u.update(pex=pex, Bpex=Bpex)

            def s3(k):
                t, h = units[k]
                nkt = TI[t]["nk"] // 128
                u = US[k]
                pb_ = 4 + (k % 2)
                PT = bank(pb_).bitcast(BF16).rearrange("p (a b) -> p a b", b=128)
                for kt in range(nkt):
                    P.tr(PT[:, kt, :], u["pex"][:, kt * 128:(kt + 1) * 128], identb, [u["Bpex"], B_id], [Bk[pb_]])
                ptb, Bptb = ptb_rr.next()
                P.copy("act" if k % 2 == 0 else "dve", ptb[:, 0:nkt, :], PT[:, 0:nkt, :], [Bk[pb_]], [Bptb])
                u.update(ptb=ptb, Bptb=Bptb)

            def s4(k):
                t, h = units[k]
                ti = TI[t]
                nkt = ti["nk"] // 128
                ks = ti["ks"]
                u = US[k]
                for kt in range(nkt):
                    P.mm(bank(6)[:, h * 64:(h + 1) * 64], u["ptb"][:, kt, :], v_all[:, ks[kt], h * 64:(h + 1) * 64],
                         kt == 0, kt == nkt - 1, [u["Bptb"], Bv[ks[kt]]], [Bk[6]])
                if h == 7:
                    smx, B_smx = ti["smx"]
                    tc0 = ti["tc0"]
                    P.add("dve", lambda e, s_=smx: e.reciprocal(out=rinv, in_=s_[:, 8:16]), [B_smx], [B_rinv])
                    P.tt("dve", ao.rearrange("p (h d) -> p h d", d=64), bank(6).rearrange("p (h d) -> p h d", d=64),
                         rinv.unsqueeze(2).broadcast_to([128, 8, 64]), ALU.mult, [Bk[6], B_rinv], [B_ao])
                    AT = bank(7).bitcast(BF16).rearrange("p (a b) -> p a b", b=128)
                    for c in range(4):
                        P.tr(AT[:, c, :], ao[:, c * 128:(c + 1) * 128], identb, [B_ao, B_id], [Bk[7]])
                    P.copy("act", oT[:, 4:8, tc0:tc0 + 128], AT[:, 0:4, :], [Bk[7]], [B_oT])

            NU = len(units)
            loads(tiles[0])
            for k in range(NU + 3):
                if k < NU:
                    t, h = units[k]
                    if h == 0:
                        if t + 1 < hi:
                            loads(t + 1)
                        prologue(t)
                    s1(k)
                if 0 <= k - 1 < NU:
                    s2(k - 1)
                if 0 <= k - 2 < NU:
                    s3(k - 2)
                if 0 <= k - 3 < NU:
                    s4(k - 3)
            if debug and l == 0:
                for t in range(lo, hi):
                    P.dma("sp", dbg_o[t], oT[:, :, (t - lo) * 128:(t - lo + 1) * 128], R=[B_oT], W=[Bdbg], key="dbgo")
            for c in range(8):
                gt, Bgt = gt_rr.next()
                P.dma("sp", gt[:, 0:hi - lo], g_scr[lo:hi, c].rearrange("t p b n -> p t b n"), R=rng(Bg, lo, hi), W=[Bgt],
                      key="gt%d" % (gt_rr.i % 2))
                b0 = (c % 2) * 3
                brs = [(wpo, 2, 0), (wso, 2, 2), (wao, 4, 4)]
                for bi_, (wt_, nkk, o0) in enumerate(brs):
                    for k in range(nkk):
                        P.mm(bank(b0 + bi_)[:, 0:n], wt_[:, k, c * 128:(c + 1) * 128], oT[:, o0 + k, 0:n], k == 0, k == nkk - 1,
                             [B_wE, B_oT], [Bk[b0 + bi_]])
                e_t = e_sets[c % 2]
                for bi_ in range(3):
                    et, Bet = e_t[bi_]
                    P.tt("dve", et[:, 0:n].rearrange("p (t n) -> p t n", n=128),
                         bank(b0 + bi_)[:, 0:n].rearrange("p (t n) -> p t n", n=128), gt[:, 0:hi - lo, bi_, :],
                         ALU.mult, [Bk[b0 + bi_], Bgt], [Bet])
                P.tt("pool", e_t[0][0][:, 0:n], e_t[0][0][:, 0:n], e_t[1][0][:, 0:n], ALU.add, [e_t[0][1], e_t[1][1]], [e_t[0][1]])
                P.tt("pool", yT[:, c, 0:n], e_t[0][0][:, 0:n], e_t[2][0][:, 0:n], ALU.add, [e_t[0][1], e_t[2][1]], [B_yT])
            for c2 in range(8):
                ob = 6 + (c2 % 2)
                hc, Bhc = hc_rr.next()
                P.dma("sp", hc[:, 0:n], h_src(l, c2, lo, hi), R=([Bh[t][c2] for t in range(lo, hi)] if l > 0 else []),
                      W=[Bhc], key="hc%d" % (hc_rr.i % 4))
                for c in range(8):
                    P.mm(bank(ob)[:, 0:n], wo[:, c, c2 * 128:(c2 + 1) * 128], yT[:, c, 0:n], c == 0, c == 7,
                         [B_wE, B_yT], [Bk[ob]])
                P.stt("dve", hc[:, 0:n], bank(ob)[:, 0:n], MT(l, st, 2)[:, c2:c2 + 1], hc[:, 0:n],
                      ALU.mult, ALU.add, [Bk[ob], Bhc, B_modt], [Bhc])
                P.dma("sp", h_scr[c2, :, lo * 128:hi * 128], hc[:, 0:n], R=[Bhc], W=[Bh[t][c2] for t in range(lo, hi)],
                      key="hc%d" % (hc_rr.i % 4))
        stop("E", l)

        P.barrier()
        AR.release(m0)
        hF2 = [(AR.alloc([8, 1024], F32), Buf("hF%d" % i)) for i in range(2)]
        aF = AR.alloc([8, 1024], BF16)
        B_aF = Buf("aF")
        hid = AR.alloc([32, 1024], BF16)
        B_hid = Buf("hid")
        rsbF = (AR.alloc([512], F32), Buf("rsF"))
        tmpF_rr = RR([(AR.alloc([512], F32), Buf("tmpF%d" % i)) for i in range(2)])
        w1_rr = RR([(AR.alloc([4, 8, 128], BF16), Buf("w1b%d" % i)) for i in range(2)])
        w2_rr = RR([(AR.alloc([32, 128], BF16), Buf("w2b%d" % i)) for i in range(2)])
        rl_rr = RR([(AR.alloc([512], F32), Buf("rl%d" % i)) for i in range(2)])
        f1_rr = RR([(bank(i), Bk[i]) for i in (1, 2, 3, 4)])
        f2_rr = RR([(bank(i), Bk[i]) for i in (5, 6, 7)])
        groupsF = segs_for("F", l)

        def goffs(grp):
            offs = []
            o_ = 0
            for (lo, hi, st) in grp:
                offs.append(o_)
                o_ += (hi - lo) * 128
            return offs

        def load_group(gi):
            hF, B_hF = hF2[gi % 2]
            grp = groupsF[gi]
            for si_, ((lo, hi, st), o0) in enumerate(zip(grp, goffs(grp))):
                n = (hi - lo) * 128
                P.dma("sp", hF[:, :, o0:o0 + n], h_scr[:, :, lo * 128:hi * 128].rearrange("c p n -> p c n"),
                      R=[b_ for t in range(lo, hi) for b_ in Bh[t]], W=[B_hF], key="hF%d_%d" % (gi % 2, si_))

        load_group(0)
        for gi, grp in enumerate(groupsF):
            hF, B_hF = hF2[gi % 2]
            offs = goffs(grp)
            for (lo, hi, st), o0 in zip(grp, offs):
                n = (hi - lo) * 128
                rms_to_aT(hF[:, :, o0:o0 + n], B_hF, n, MT(l, st, 3), MT(l, st, 4), aF[:, :, o0:o0 + n], [B_aF],
                          rsbF, tmpF_rr, (bank(0), Bk[0]))
            if gi + 1 < len(groupsF):
                load_group(gi + 1)
            for jp in range(8):
                w1t, Bw1 = w1_rr.next()
                P.dma("pool", w1t, w1r[l, jp * 4:jp * 4 + 4].rearrange("j p k m -> p j k m"), W=[Bw1],
                      key="w1b%d" % (w1_rr.i % 2), max_dma_last_dim=4096)
                for jj in range(4):
                    j = jp * 4 + jj
                    for (lo, hi, st), o0 in zip(grp, offs):
                        n = (hi - lo) * 128
                        pf, Bpf = f1_rr.next()
                        for k in range(8):
                            P.mm(pf[:, 0:n], w1t[:, jj, k, :], aF[:, k, o0:o0 + n], k == 0, k == 7, [Bw1, B_aF], [Bpf])
                        rl, Brl = rl_rr.next()
                        P.act(rl[:, 0:n], pf[:, 0:n], AF.Relu, [Bpf], [Brl])
                        P.stt("dve", hid[:, j, o0:o0 + n], pf[:, 0:n], 0.0, rl[:, 0:n], ALU.max, ALU.mult,
                              [Bpf, Brl], [B_hid])
            for c2 in range(8):
                w2t, Bw2 = w2_rr.next()
                P.dma("pool", w2t, w2r[l, c2], W=[Bw2], key="w2b%d" % (w2_rr.i % 2), max_dma_last_dim=4096)
                for (lo, hi, st), o0 in zip(grp, offs):
                    n = (hi - lo) * 128
                    pf, Bpf = f2_rr.next()
                    for j in range(32):
                        P.mm(pf[:, 0:n], w2t[:, j, :], hid[:, j, o0:o0 + n], j == 0, j == 31, [Bw2, B_hid], [Bpf])
                    P.stt("dve", hF[:, c2, o0:o0 + n], pf[:, 0:n], MT(l, st, 5)[:, c2:c2 + 1], hF[:, c2, o0:o0 + n],
                          ALU.mult, ALU.add, [Bpf, B_hF, B_modt], [B_hF])
            for si_, ((lo, hi, st), o0) in enumerate(zip(grp, offs)):
                n = (hi - lo) * 128
                if not last:
                    P.dma("sp", h_scr[:, :, lo * 128:hi * 128].rearrange("c p n -> p c n"), hF[:, :, o0:o0 + n],
                          R=[B_hF], W=[b_ for t in range(lo, hi) for b_ in Bh[t]], key="hFo%d_%d" % (gi % 2, si_))
                else:
                    rs, Brs = rsbF
                    rms_rstd(hF[:, :, o0:o0 + n], B_hF, n, rsbF, (bank(0), Bk[0]))
                    for c in range(8):
                        P.stt("dve", hF[:, c, o0:o0 + n], hF[:, c, o0:o0 + n], V_fg[:, c:c + 1], rs[:, 0:n], ALU.mult, ALU.mult,
                              [B_hF, Brs, B_vec], [B_hF])
                    op = P.dma("sp", outT[:, :, (lo - 4) * 128:(hi - 4) * 128].rearrange("c p n -> p c n"), hF[:, :, o0:o0 + n],
                               R=[B_hF], W=[Bout], key="outst%d_%d" % (gi % 2, si_))
                    out_ops.append(op)
        stop("F", l)

    P.halted = False
    if not out_ops:
        z = AR.t[:, 0:2048]
        out_ops.append(P.dma("sp", outT[0], z, R=[], W=[Bout], key="outst"))
    if debug:
        out_ops.append(P.dma("sp", outT[1, :, 0:8], vec[:, 0:8], R=[Bdbg], W=[Bout], key="dbgfin"))
    P.final_waits = out_ops
    P.emit()
    return nc, P, AR


def _fm(v):
    v = np.asarray(v, np.float32)
    return np.ascontiguousarray(v.reshape(-1, 128).T)


def _rope_perm():
    idx = []
    for h in range(8):
        idx += [h * 64 + 2 * i for i in range(32)] + [h * 64 + 2 * i + 1 for i in range(32)]
    return np.array(idx)


def _shared_inputs(inp):
    w_in = np.asarray(inp["w_in"], np.float32)
    perm = _rope_perm()
    wp = w_in.copy()
    wp[:, :, 768:1280] = w_in[:, :, 768:1280][:, :, perm]
    wp[:, :, 1280:1792] = w_in[:, :, 1280:1792][:, :, perm]
    tm_cols = [np.r_[0:256, 512:768], np.r_[1792:2304]]
    wA_tm = np.stack([np.stack([wp[l][:, cols].reshape(8, 128, 512).transpose(1, 0, 2) for cols in tm_cols])
                      for l in range(2)])
    fm_starts = [256, 384] + [768 + 128 * i for i in range(8)] + [2304 + 128 * i for i in range(24)]
    wA_fm = np.stack([np.stack([wp[l][:, s:s + 128].reshape(8, 128, 128).transpose(1, 0, 2) for s in fm_starts])
                      for l in range(2)])
    w1 = np.asarray(inp["w_ff1"], np.float32)
    w1r = np.stack([np.stack([w1[l][:, j * 128:(j + 1) * 128].reshape(8, 128, 128).transpose(1, 0, 2)
                              for j in range(32)]) for l in range(2)])
    w2 = np.asarray(inp["w_ff2"], np.float32)
    w2r = np.stack([np.stack([w2[l][:, c * 128:(c + 1) * 128].reshape(32, 128, 128).transpose(1, 0, 2)
                              for c in range(8)]) for l in range(2)])
    sgu_w = np.asarray(inp["sgu_w"], np.float32)
    sguw = np.ascontiguousarray(sgu_w.transpose(3, 0, 1, 2))
    sgu_b = np.asarray(inp["sgu_b"], np.float32)
    sgub = np.zeros((128, 2, 2, 128), np.float32)
    for part in range(128):
        for c in range(2):
            sgub[part, :, c, :] = sgu_b[:, 2 * c + part // 64, :]
    w_pool = np.asarray(inp["w_pool"], np.float32)
    wblk = np.zeros((128, 2, 2, 128), np.float32)
    for c in range(2):
        for gl in range(2):
            wblk[gl * 64:(gl + 1) * 64, :, c, gl * 64:(gl + 1) * 64] = w_pool[:, 2 * c + gl].transpose(1, 0, 2)
    consts = np.zeros((128, 3, 128), np.float32)
    consts[:, 0, :] = np.eye(128)
    for pp in range(128):
        partner = pp + 32 if (pp % 64) < 32 else pp - 32
        consts[partner, 1, pp] = 1.0
    consts[:, 2, :] = 1.0
    return dict(w_mod=np.ascontiguousarray(inp["w_mod"], np.float32), wA_tm=np.ascontiguousarray(wA_tm),
                wA_fm=np.ascontiguousarray(wA_fm), sguw=sguw, sgub=sgub, wblk=wblk,
                w_pool_out=np.ascontiguousarray(inp["w_pool_out"], np.float32),
                w_sgu_out=np.ascontiguousarray(inp["w_sgu_out"], np.float32),
                w_attn_out=np.ascontiguousarray(inp["w_attn_out"], np.float32),
                w_o=np.ascontiguousarray(inp["w_o"], np.float32), w1r=np.ascontiguousarray(w1r),
                w2r=np.ascontiguousarray(w2r), consts=consts)


def _band_set(kind):
    out = np.zeros((128, 12, 128), np.float32)
    L = 384
    base = 128
    for g, w in enumerate((2, 4, 8, 16)):
        for tt in range(128):
            pos = base + tt
            lo_b = base if kind == 1 else 0
            hi_b = base + 128 if kind == 2 else L
            lo = min(max(pos - w // 2, lo_b), hi_b)
            hi = min(max(pos + (w - w // 2), lo_b), hi_b)
            cnt = hi - lo
            for s in range(lo, hi):
                d = s // 128
                out[s % 128, g * 3 + d, tt] += 1.0 / cnt
            out[tt, g * 3 + 1, tt] -= 1.0
    return out


def _bias_tables(rpb, core_rows0, n_rows_total=128):
    out = np.full((2, 6, 8, 128, 1024), NEG, np.float32)
    q_i = np.arange(128)
    for ty, j in ((0, 4), (1, 0), (2, 1), (3, 14), (4, 15)):
        klo, khi = j - 2, j + 3
        if j == 0:
            khi = j + 4
        if j == 15:
            klo = j - 3
        nkl = (khi - klo) * 128
        key = np.arange(nkl)
        k_row = core_rows0 + 2 * klo + key // 64
        k_col = key % 64
        q_row = core_rows0 + 2 * j + q_i // 64
        q_col = q_i % 64
        rs = np.clip(q_row - 4, 0, n_rows_total - 8)
        cs = np.clip(q_col - 8, 0, 64 - 16)
        valid = ((k_row[None, :] >= rs[:, None]) & (k_row[None, :] < rs[:, None] + 8) &
                 (k_col[None, :] >= cs[:, None]) & (k_col[None, :] < cs[:, None] + 16) &
                 (k_row[None, :] >= 0) & (k_row[None, :] < n_rows_total))
        dr = np.clip(k_row[None, :] - q_row[:, None] + 7, 0, 14)
        dc = np.clip(k_col[None, :] - q_col[:, None] + 15, 0, 30)
        for l in range(2):
            for h in range(8):
                g = rpb[l, h][dr, dc]
                out[l, ty, h, :, 0:nkl] = np.where(valid, g, NEG)
                out[l, ty, h, :, nkl:nkl + 256] = 0.0
    out[:, 5, :, :, 0:256] = 0.0
    return out.astype(ml_dtypes.bfloat16)


def _core_inputs(inp, core):
    b, blk = core // 4, core % 4
    row0 = 32 * blk
    x = np.asarray(inp["x"], np.float32)[b]
    t0 = (row0 - 8) * 64
    xs = np.zeros((NLAT, D), np.float32)
    lo, hi = max(t0, 0), min(t0 + NLAT, 8192)
    xs[lo - t0:hi - t0] = x[lo:hi]
    xT = np.ascontiguousarray(xs.T.reshape(8, 128, NLAT))
    ctxT = np.ascontiguousarray(np.asarray(inp["ctx"], np.float32)[b].T.reshape(8, 128, 256))
    vecs = np.zeros((128, 156), np.float32)
    vecs[:, 0:8] = _fm(inp["c"][b])
    vecs[:, 8:16] = _fm(inp["c_ctx"])
    for l in range(2):
        vecs[:, 16 + l * 48:16 + (l + 1) * 48] = _fm(inp["b_mod"][l])
        vecs[:, 112 + l * 8:112 + (l + 1) * 8] = _fm(inp["norm1_g"][l])
        vecs[:, 128 + l * 8:128 + (l + 1) * 8] = _fm(inp["norm2_g"][l])
        vecs[:, 152 + l * 2:152 + (l + 1) * 2] = _fm(inp["pool_scale"][l])
    vecs[:, 144:152] = _fm(inp["final_g"])
    tok = np.arange(NLAT)
    row = (row0 - 8 + tok // 64).astype(np.float32)
    col = (tok % 64).astype(np.float32)
    inv_freq = (10000.0 ** (-np.arange(16, dtype=np.float32) / 16)).astype(np.float32)
    ang = np.concatenate([row[:, None] * inv_freq, col[:, None] * inv_freq], axis=-1).astype(np.float32)
    cos, sin = np.cos(ang), np.sin(ang)
    rope = np.zeros((4, 128, NLAT), np.float32)
    for pp in range(128):
        i = pp % 64
        e = i % 32
        rope[0, pp] = cos[:, e]
        rope[1, pp] = -sin[:, e] if i < 32 else sin[:, e]
    rope[2:4] = rope[0:2] * np.float32(0.125)
    first = (blk == 0)
    lastb = (blk == 3)
    gen = _band_set(0)
    bands = np.stack([gen, _band_set(1) if first else gen, _band_set(2) if lastb else gen, _band_set(1), _band_set(2)])
    if first:
        bands[1][:, [0, 3, 6, 9], :] = 0.0
    bias = _bias_tables(np.asarray(inp["na_rpb"], np.float32), row0)
    return dict(xT=xT, ctxT=ctxT, vecs=vecs, rope=rope, bands=np.ascontiguousarray(bands), bias=bias)


_PROG = {}


def kernel(**inputs):
    if "nc" not in _PROG:
        _PROG["nc"] = build_program()[0]
    nc = _PROG["nc"]
    shared = _shared_inputs(inputs)
    in_maps = []
    for core in range(8):
        m = dict(shared)
        m.update(_core_inputs(inputs, core))
        in_maps.append(m)
    res = run_bass_kernel_spmd(nc, in_maps, core_ids=list(range(8)))
    out = np.zeros((2, 8192, D), np.float32)
    for core in range(8):
        b, blk = core // 4, core % 4
        oT = np.asarray(res.results[core]["outT"], np.float32)
        out[b, blk * 2048:(blk + 1) * 2048, :] = oT.reshape(D, 2048).T
    return out
```

```python
import numpy as np
import ml_dtypes
import concourse.bass as bass
import concourse.mybir as mybir
from concourse.bass_utils import run_bass_kernel_spmd

F32 = mybir.dt.float32
F32R = mybir.dt.float32r
BF16 = mybir.dt.bfloat16
AF = mybir.ActivationFunctionType
ALU = mybir.AluOpType
AX = mybir.AxisListType

D = 1024
NLS = 24
NS = 26
NLAT = NLS * 128
TOK = NS * 128
EPS = 1e-6
NEG = -30000.0
DBG_SKIP_LN = False


class Buf:
    __slots__ = ("name", "last_writer", "readers")

    def __init__(self, name):
        self.name = name
        self.last_writer = None
        self.readers = []


class Op:
    __slots__ = ("eng", "fn", "deps", "is_dma", "key", "idx", "signal", "sem", "val", "waits")

    def __init__(self, eng, fn, is_dma, key, idx):
        self.eng = eng
        self.fn = fn
        self.is_dma = is_dma
        self.key = key
        self.idx = idx
        self.deps = []
        self.signal = False
        self.sem = None
        self.val = 0
        self.waits = []


ENGS = ("pe", "act", "dve", "pool", "sp")


class Prog:
    def __init__(self, nc):
        self.nc = nc
        self.ops = []
        self.final_waits = []
        self.phase_buf = Buf("phase")
        self.bar_ap = None
        self.halted = False
        self.dummy = Op("dve", None, False, None, -1)

    def add(self, eng, fn, reads=(), writes=(), dma_key=None):
        if self.halted:
            return self.dummy
        is_dma = dma_key is not None
        op = Op(eng, fn, is_dma, dma_key, len(self.ops))
        deps = {}
        for b in reads:
            w = b.last_writer
            if w is not None:
                deps[w.idx] = [w, True]
        for b in writes:
            w = b.last_writer
            if w is not None and w.idx not in deps:
                deps[w.idx] = [w, False]
            for r in b.readers:
                if r.idx not in deps:
                    deps[r.idx] = [r, False]
        pb = self.phase_buf
        if pb.last_writer is not None:
            deps[pb.last_writer.idx] = [pb.last_writer, True]
        pb.readers.append(op)
        for b in reads:
            b.readers.append(op)
        for b in writes:
            b.last_writer = op
            b.readers = []
        deps.pop(op.idx, None)
        op.deps = list(deps.values())
        self.ops.append(op)
        return op

    def barrier(self):
        if self.halted:
            return self.dummy
        pb = self.phase_buf
        op = Op("dve", lambda e: e.memset(self.bar_ap, 0.0), False, None, len(self.ops))
        deps = {}
        for r in pb.readers:
            deps[r.idx] = [r, True]
        if pb.last_writer is not None:
            deps[pb.last_writer.idx] = [pb.last_writer, True]
        op.deps = list(deps.values())
        pb.last_writer = op
        pb.readers = []
        self.ops.append(op)
        return op

    def dma(self, eng, out, in_, R=(), W=(), key=None, **kw):
        return self.add(eng, lambda e: e.dma_start(out=out, in_=in_, **kw), R, W, dma_key=key)

    def mm(self, out, lhsT, rhs, start, stop, R, W):
        return self.add("pe", lambda e: e.matmul(out, lhsT=lhsT, rhs=rhs, start=start, stop=stop), R, W)

    def tr(self, out, in_, ident, R, W):
        return self.add("pe", lambda e: e.transpose(out=out, in_=in_, identity=ident), R, W)

    def act(self, out, in_, func, R, W, **kw):
        return self.add("act", lambda e: e.activation(out=out, in_=in_, func=func, **kw), R, W)

    def copy(self, eng, out, in_, R, W):
        if eng == "act":
            return self.add("act", lambda e: e.copy(out=out, in_=in_), R, W)
        return self.add(eng, lambda e: e.tensor_copy(out=out, in_=in_), R, W)

    def tt(self, eng, out, in0, in1, op, R, W):
        return self.add(eng, lambda e: e.tensor_tensor(out=out, in0=in0, in1=in1, op=op), R, W)

    def ts(self, eng, out, in0, s1, s2, op0, op1, R, W):
        return self.add(eng, lambda e: e.tensor_scalar(out=out, in0=in0, scalar1=s1, scalar2=s2, op0=op0, op1=op1), R, W)

    def stt(self, eng, out, in0, scalar, in1, op0, op1, R, W):
        return self.add(eng, lambda e: e.scalar_tensor_tensor(out=out, in0=in0, scalar=scalar, in1=in1,
                                                              op0=op0, op1=op1), R, W)

    def emit(self):
        nc = self.nc
        ops = self.ops
        for op in ops:
            need = []
            for d, raw in op.deps:
                if d.is_dma or op.is_dma or d.eng != op.eng or (raw and op.eng != "pe"):
                    need.append(d)
            op.waits = need
            for d in need:
                d.signal = True
        for op in self.final_waits:
            op.signal = True
        for op in ops:
            if op.is_dma:
                op.signal = True
        sems = {}
        counters = {}
        for op in ops:
            if not op.signal:
                continue
            k = ("dma", op.key) if op.is_dma else ("eng", op.eng)
            if k not in sems:
                sems[k] = nc.alloc_semaphore("s%d" % len(sems))
                counters[k] = 0
            op.sem = sems[k]
            counters[k] += 16 if op.is_dma else 1
            op.val = counters[k]
            op.key = k
        self.n_sems = len(sems)
        per_eng = {e: [] for e in ENGS}
        for op in ops:
            per_eng[op.eng].append(op)
        finals = list(self.final_waits)

        def run(eng_name, eng):
            waited = {}
            for op in per_eng[eng_name]:
                req = {}
                for d in op.waits:
                    if req.get(d.key, 0) < d.val:
                        req[d.key] = d.val
                for k, v in req.items():
                    if waited.get(k, 0) >= v:
                        continue
                    waited[k] = v
                    eng.wait_ge(sems[k], v)
                inst = op.fn(eng)
                if op.signal:
                    inst.then_inc(op.sem, 16 if op.is_dma else 1)
            if eng_name == "sp":
                for d in finals:
                    if waited.get(d.key, 0) < d.val:
                        waited[d.key] = d.val
                        eng.wait_ge(sems[d.key], d.val)

        with nc.Block() as block:
            @block.tensor
            def _(e):
                run("pe", e)

            @block.scalar
            def _(e):
                run("act", e)

            @block.vector
            def _(e):
                run("dve", e)

            @block.gpsimd
            def _(e):
                run("pool", e)

            @block.sync
            def _(e):
                run("sp", e)


class Arena:
    def __init__(self, nc, nbytes):
        self.t = nc.alloc_sbuf_tensor("arena", [128, nbytes // 4], F32)
        self.n = nbytes
        self.off = 0
        self.peak = 0

    def alloc(self, shape, dtype):
        esz = 2 if dtype == BF16 else 4
        n = int(np.prod(shape)) * esz
        n4 = (n + 31) // 32 * 32
        assert self.off + n4 <= self.n, ("SBUF arena overflow", self.off, n4, self.n)
        a = self.t[:, self.off // 4:(self.off + n) // 4]
        self.off += n4
        self.peak = max(self.peak, self.off)
        if dtype != F32:
            a = a.bitcast(dtype)
        if len(shape) == 2:
            return a.rearrange("p (a b) -> p a b", a=shape[0])
        if len(shape) == 3:
            return a.rearrange("p (a b c) -> p a b c", a=shape[0], b=shape[1])
        return a

    def mark(self):
        return self.off

    def release(self, m):
        self.off = m


class RR:
    def __init__(self, items):
        self.items = items
        self.i = 0

    def next(self):
        it = self.items[self.i % len(self.items)]
        self.i += 1
        return it


def segs_for(kind, layer):
    if kind == "A":
        if layer == 0:
            s = [(4 * b, 4 * b + 4, 0) for b in range(6)]
        else:
            s = [(2, 4, 0)] + [(4 * b, 4 * b + 4, 0) for b in range(1, 5)] + [(20, 22, 0)]
        return s + [(24, 26, 1)]
    if kind == "DE":
        s = [(4 * b, 4 * b + 4, 0) for b in range(1, 5)]
        if layer == 0:
            s = [(2, 4, 0)] + s + [(20, 22, 0), (24, 26, 1)]
        return s
    if kind == "F":
        g = [[(4, 8, 0), (8, 12, 0)], [(12, 16, 0), (16, 20, 0)]]
        if layer == 0:
            g.append([(2, 4, 0), (20, 22, 0), (24, 26, 1)])
        return g
    raise ValueError(kind)


def build_program(n_layers=2, stop_after=None, debug=False):
    nc = bass.Bass("TRN2", target_bir_lowering=False)
    P = Prog(nc)

    def din(name, shape, dt=F32):
        return nc.dram_tensor(name, list(shape), dt, kind="ExternalInput").ap()

    xT = din("xT", [8, 128, NLAT])
    ctxT = din("ctxT", [8, 128, 256])
    vecs = din("vecs", [128, 156])
    consts = din("consts", [128, 3, 128])
    rope = din("rope", [4, 128, NLAT])
    bands = din("bands", [5, 128, 12, 128])
    biasd = din("bias", [2, 6, 8, 128, 1024], BF16)
    w_mod = din("w_mod", [2, 1024, 6144])
    wA_tm = din("wA_tm", [2, 2, 128, 8, 512])
    wA_fm = din("wA_fm", [2, 34, 128, 8, 128])
    sguw = din("sguw", [128, 2, 4, 128])
    sgub_d = din("sgub", [128, 2, 2, 128])
    wblk_d = din("wblk", [128, 2, 2, 128])
    w_po = din("w_pool_out", [2, 256, 1024])
    w_so = din("w_sgu_out", [2, 256, 1024])
    w_ao = din("w_attn_out", [2, 512, 1024])
    w_o = din("w_o", [2, 1024, 1024])
    w1r = din("w1r", [2, 32, 128, 8, 128])
    w2r = din("w2r", [2, 8, 128, 32, 128])
    outT = nc.dram_tensor("outT", [8, 128, 2048], F32, kind="ExternalOutput").ap()

    skind = "ExternalOutput" if debug else "Internal"

    def dscr(name, shape, dt=F32):
        return nc.dram_tensor(name, list(shape), dt, kind=skind).ap()

    h_scr = dscr("h_scr", [8, 128, TOK])
    q_scr = dscr("q_scr", [NS, 128, 4, 128], BF16)
    u_scr = dscr("u_scr", [NS, 128, 2, 128])
    vn_scr = dscr("vn_scr", [NS, 128, 256], BF16)
    g_scr = dscr("g_scr", [NS, 8, 128, 3, 128], BF16)
    if debug:
        dbg_k = dscr("dbg_k", [128, 4, TOK], BF16)
        dbg_v = dscr("dbg_v", [128, NS, 512], BF16)
        dbg_p = dscr("dbg_p", [128, NS, 256], BF16)
        dbg_a = dscr("dbg_a", [128, 8, TOK], BF16)
        dbg_mod = dscr("dbg_mod", [128, 2, 2, 48])
        dbg_o = dscr("dbg_o", [NS, 128, 8, 128], BF16)

    Bh = [[Buf("h%d_%d" % (t, c)) for c in range(8)] for t in range(NS)]
    Bq = [Buf("q%d" % t) for t in range(NS)]
    Bu = [Buf("u%d" % t) for t in range(NS)]
    Bvn = [Buf("vn%d" % t) for t in range(NS)]
    Bg = [Buf("g%d" % t) for t in range(NS)]
    Bout = Buf("out")
    Bdbg = Buf("dbg")

    psd = [nc.alloc_psum_tensor("psd%d" % i, [128, 1024], F32) for i in range(4)]
    Bk = [Buf("bank%d" % i) for i in range(8)]

    def bank(i):
        return psd[i // 2][:, (i % 2) * 512:(i % 2) * 512 + 512]

    AR = Arena(nc, 197 * 1024)
    bar_t = AR.alloc([8], F32)
    P.bar_ap = bar_t
    cst = AR.alloc([3, 128], F32)
    identb = AR.alloc([128], BF16)
    perm_r = nc.alloc_sbuf_tensor("perm_r", [128, 128], F32R)[:]
    ones_r = nc.alloc_sbuf_tensor("ones_r", [128, 128], F32R)[:]
    sq_rr = RR([(nc.alloc_sbuf_tensor("sq_r%d" % i, [128, 512], F32R)[:], Buf("sq_r%d" % i)) for i in range(2)])
    qf_rr = RR([(nc.alloc_sbuf_tensor("qf_r%d" % i, [128, 512], F32R)[:], Buf("qf_r%d" % i)) for i in range(2)])
    vec = AR.alloc([156], F32)
    modt = AR.alloc([24, 8], F32)
    silu_b = AR.alloc([2, 8], BF16)
    B_c = Buf("consts")
    B_vec = Buf("vec")
    B_modt = Buf("modt")
    B_silu = Buf("silu")

    def V_c(s):
        return vec[:, s * 8:(s + 1) * 8]

    def V_bmod(l):
        return vec[:, 16 + l * 48:16 + (l + 1) * 48]

    def V_n1(l):
        return vec[:, 112 + l * 8:112 + (l + 1) * 8]

    def V_n2(l):
        return vec[:, 128 + l * 8:128 + (l + 1) * 8]

    V_fg = vec[:, 144:152]

    def V_ps(l):
        return vec[:, 152 + l * 2:152 + (l + 1) * 2]

    def MT(l, s, kind):
        i = (l * 2 + s) * 6 + kind
        return modt[:, i, :]

    P.dma("sp", cst, consts, W=[B_c], key="cst")
    P.dma("sp", vec, vecs, W=[B_vec], key="vec")
    B_id = Buf("ident")
    P.copy("dve", identb, cst[:, 0, :], [B_c], [B_id])
    P.copy("dve", perm_r, cst[:, 1, :], [B_c], [B_id])
    P.copy("dve", ones_r, cst[:, 2, :], [B_c], [B_id])
    P.act(silu_b.rearrange("p s k -> p (s k)"), vec[:, 0:16], AF.Silu, [B_vec], [B_silu])

    def stop(name, l=0):
        if stop_after is not None and tuple(stop_after) == (name, l):
            P.halted = True

    stop("S")

    m0 = AR.mark()
    wm = [(AR.alloc([8, 512], BF16), Buf("wm%d" % i)) for i in range(2)]
    wm_rr = RR(wm)
    modraw = AR.alloc([48, 2], F32)
    tmpm = AR.alloc([8, 2], F32)
    B_modraw = Buf("modraw")
    for l in range(n_layers):
        psm = bank(0).rearrange("p (a b) -> p a b", b=2)[:, 0:48, :]
        for pc in range(12):
            wt, wb = wm_rr.next()
            P.dma("pool", wt, w_mod[l, :, pc * 512:(pc + 1) * 512].rearrange("(k p) n -> p k n", p=128),
                  W=[wb], key="wm%d" % (wm_rr.i % 2), max_dma_last_dim=4096)
            for cc in range(4):
                ch = pc * 4 + cc
                for k in range(8):
                    P.mm(psm[:, ch, :], wt[:, k, cc * 128:(cc + 1) * 128], silu_b[:, :, k], k == 0, k == 7,
                         [wb, B_silu], [Bk[0]])
        P.tt("dve", modraw, psm, V_bmod(l).unsqueeze(2).broadcast_to([128, 48, 2]), ALU.add,
             [Bk[0], B_vec], [B_modraw])
        for s in range(2):
            def chunk(i):
                return modraw[:, i * 8:(i + 1) * 8, s]
            P.stt("dve", MT(l, s, 0), chunk(1), 1.0, V_n1(l), ALU.add, ALU.mult, [B_modraw, B_vec], [B_modt])
            P.copy("dve", MT(l, s, 1), chunk(0), [B_modraw], [B_modt])
            P.copy("dve", MT(l, s, 2), chunk(2), [B_modraw], [B_modt])
            P.stt("dve", MT(l, s, 3), chunk(4), 1.0, V_n2(l), ALU.add, ALU.mult, [B_modraw, B_vec], [B_modt])
            P.copy("dve", MT(l, s, 4), chunk(3), [B_modraw], [B_modt])
            P.copy("dve", MT(l, s, 5), chunk(5), [B_modraw], [B_modt])
    if debug:
        P.dma("sp", dbg_mod.rearrange("p l s c -> p (l s c)"), modt.rearrange("p a b -> p (a b)")[:, 0:192],
              R=[B_modt], W=[Bdbg], key="dbgm")
    AR.release(m0)
    stop("M")
    P.barrier()

    kT_all = AR.alloc([4, TOK], BF16)
    v_all = AR.alloc([NS, 512], BF16)
    p_all = AR.alloc([NS, 256], BF16)
    Bkt = [Buf("kT%d" % t) for t in range(NS)]
    Bv = [Buf("v%d" % t) for t in range(NS)]
    Bp = [Buf("p%d" % t) for t in range(NS)]
    res_mark = AR.mark()

    def rng(bufs, lo, hi):
        return [bufs[t] for t in range(lo, hi)]

    def h_src(l, c, lo, hi):
        if l == 0:
            if lo >= 24:
                return ctxT[c, :, (lo - 24) * 128:(hi - 24) * 128]
            return xT[c, :, lo * 128:hi * 128]
        return h_scr[c, :, lo * 128:hi * 128]

    def load_h(dst, l, lo, hi, Bdst, key, first_layer_input):
        n = (hi - lo) * 128
        if first_layer_input:
            src = (ctxT[:, :, (lo - 24) * 128:(hi - 24) * 128] if lo >= 24 else xT[:, :, lo * 128:hi * 128])
            R = []
        else:
            src = h_scr[:, :, lo * 128:hi * 128]
            R = [b_ for t in range(lo, hi) for b_ in Bh[t]]
        P.dma("sp", dst[:, :, 0:n], src.rearrange("c p n -> p c n"), R=R, W=[Bdst], key=key)

    def rms_rstd(hb, Bhb, n, rsb, psb):
        rs, Brs = rsb
        pb, Bpb = psb
        for c in range(8):
            sq, Bsq = sq_rr.next()
            P.act(sq[:, 0:n], hb[:, c, 0:n], AF.Square, [Bhb], [Bsq])
            P.mm(pb[:, 0:n], ones_r, sq[:, 0:n], c == 0, c == 7, [Bsq, B_id], [Bpb])
        P.ts("dve", rs[:, 0:n], pb[:, 0:n], 1.0 / D, EPS, ALU.mult, ALU.add, [Bpb], [Brs])
        P.act(rs[:, 0:n], rs[:, 0:n], AF.Sqrt, [Brs], [Brs])
        P.add("dve", lambda e: e.reciprocal(out=rs[:, 0:n], in_=rs[:, 0:n]), [Brs], [Brs])

    def rms_to_aT(hb, Bhb, n, gs, sh, dst, Wdst, rsb, tmp_rr, psb):
        rms_rstd(hb, Bhb, n, rsb, psb)
        rms_mod(hb, Bhb, n, gs, sh, dst, Wdst, rsb, tmp_rr)

    def rms_mod(hb, Bhb, n, gs, sh, dst, Wdst, rsb, tmp_rr):
        rs, Brs = rsb
        for c in range(8):
            tmp, Btmp = tmp_rr.next()
            P.stt("dve", tmp[:, 0:n], hb[:, c, 0:n], gs[:, c:c + 1], rs[:, 0:n], ALU.mult, ALU.mult,
                  [Bhb, Brs, B_modt], [Btmp])
            P.act(dst[:, c, 0:n], tmp[:, 0:n], AF.Identity, [Btmp, B_modt], Wdst, bias=sh[:, c:c + 1], scale=1.0)

    out_ops = []

    for l in range(n_layers):
        last = (l == n_layers - 1)
        if l > 0:
            P.barrier()
        AR.release(res_mark)
        aT_all = AR.alloc([8, TOK], BF16)
        Ba = [Buf("aT%d" % t) for t in range(NS)]
        a_mark = AR.mark()
        hst = [(AR.alloc([8, 512], F32), Buf("hst%d" % i)) for i in range(3)]
        hst_rr = RR(hst)
        rsb2 = [(AR.alloc([512], F32), Buf("rs%d" % i)) for i in range(2)]
        tmp_rr = RR([(AR.alloc([512], F32), Buf("tmp%d" % i)) for i in range(2)])
        segsA = segs_for("A", l)
        pend = None
        for si, (lo, hi, st) in enumerate(segsA):
            n = (hi - lo) * 128
            hb, Bhb = hst_rr.next()
            load_h(hb, l, lo, hi, Bhb, "hst%d" % (hst_rr.i % 3), l == 0)
            bi = si % 2
            rms_rstd(hb, Bhb, n, rsb2[bi], (bank(bi), Bk[bi]))
            if pend is not None:
                rms_mod(*pend)
            pend = (hb, Bhb, n, MT(l, st, 0), MT(l, st, 1), aT_all[:, :, lo * 128:hi * 128], rng(Ba, lo, hi), rsb2[bi], tmp_rr)
        rms_mod(*pend)
        if debug and l == 0:
            P.dma("sp", dbg_a, aT_all, R=Ba, W=[Bdbg], key="dbga")
        stop("A1", l)
        P.barrier()
        AR.release(a_mark)

        wbuf = [(AR.alloc([8192], BF16), Buf("wbuf%d" % i)) for i in range(2)]
        wb_rr = RR(wbuf)
        ropeb = [(AR.alloc([4, 512], F32), Buf("rope%d" % i)) for i in range(2)]
        rope_rr = RR(ropeb)
        t1_rr = RR([(AR.alloc([512], F32), Buf("t1%d" % i)) for i in range(2)])
        t2_rr = RR([(AR.alloc([512], F32), Buf("t2%d" % i)) for i in range(2)])
        qo_rr = RR([(AR.alloc([512], BF16), Buf("qo%d" % i)) for i in range(2)])
        us_rr = RR([(AR.alloc([512], F32), Buf("us%d" % i)) for i in range(3)])
        gs_rr = RR([(AR.alloc([512], BF16), Buf("gs%d" % i)) for i in range(3)])
        vns_rr = RR([(AR.alloc([256], BF16), Buf("vns%d" % i)) for i in range(2)])
        st_rr = RR([(AR.alloc([8], F32), Buf("st%d" % i)) for i in range(2)])
        vsf_rr = RR([(AR.alloc([256], F32), Buf("vsf%d" % i)) for i in range(4)])
        pj_rr = RR([(bank(i), Bk[i]) for i in range(4)])
        pm_rr = RR([(bank(i), Bk[i]) for i in (4, 5)])
        tm_rr = RR([(bank(i), Bk[i]) for i in (4, 5, 6, 7)])

        def load_w(src, shape):
            wt, wb = wb_rr.next()
            nel = int(np.prod(shape))
            view = wt[:, 0:nel]
            if len(shape) == 2:
                view = view.rearrange("p (a b) -> p a b", a=shape[0])
            else:
                view = view.rearrange("p (a b c) -> p a b c", a=shape[0], b=shape[1])
            P.dma("pool", view, src, W=[wb], key="wbuf%d" % (wb_rr.i % 2), max_dma_last_dim=4096)
            return view, wb

        for piece in range(2):
            wv, wb = load_w(wA_tm[l, piece], [8, 512])
            for (lo, hi, st) in segsA:
                for t in range(lo, hi):
                    pt, Bpt = tm_rr.next()
                    for k in range(8):
                        P.mm(pt, aT_all[:, k, t * 128:(t + 1) * 128], wv[:, k, :], k == 0, k == 7, [Ba[t], wb], [Bpt])
                    if piece == 0:
                        P.copy("act", p_all[:, t, :], pt[:, 0:256], [Bpt], [Bp[t]])
                        if DBG_SKIP_LN:
                            continue
                        stt_, Bst = st_rr.next()
                        vf, Bvf = vsf_rr.next()
                        P.act(vf, pt[:, 256:512], AF.Identity, [Bpt], [Bvf, Bst], accum_out=stt_[:, 0:1])
                        P.ts("dve", stt_[:, 1:2], stt_[:, 0:1], -1.0 / 256, None, ALU.mult, ALU.bypass, [Bst], [Bst])
                        jk, Bjk = vsf_rr.next()
                        P.act(jk, vf, AF.Square, [Bvf, Bst], [Bjk, Bst], bias=stt_[:, 1:2], scale=1.0, accum_out=stt_[:, 2:3])
                        P.ts("dve", stt_[:, 3:4], stt_[:, 2:3], 1.0 / 256, EPS, ALU.mult, ALU.add, [Bst], [Bst])
                        P.act(stt_[:, 3:4], stt_[:, 3:4], AF.Sqrt, [Bst], [Bst])
                        P.add("dve", lambda e, o=stt_[:, 3:4]: e.reciprocal(out=o, in_=o), [Bst], [Bst])
                        vs_, Bvs = vns_rr.next()
                        P.ts("dve", vs_, vf, stt_[:, 1:2], stt_[:, 3:4], ALU.add, ALU.mult, [Bvf, Bst], [Bvs])
                        P.dma("sp", vn_scr[t], vs_, R=[Bvs], W=[Bvn[t]], key="vns%d" % (vns_rr.i % 2))
                    else:
                        P.copy("act", v_all[:, t, :], pt, [Bpt], [Bv[t]])

        stop("A2", l)
        def proj_chunk(wv, wb, lo, hi):
            n = (hi - lo) * 128
            pj, Bpj = pj_rr.next()
            for k in range(8):
                P.mm(pj[:, 0:n], wv[:, k, :], aT_all[:, k, lo * 128:hi * 128], k == 0, k == 7,
                     rng(Ba, lo, hi) + [wb], [Bpj])
            return pj, Bpj, n

        wv, wb = load_w(wA_fm[l, 0:2].rearrange("c p k m -> p c k m"), [2, 8, 128])
        for (lo, hi, st) in segsA:
            for c in range(2):
                pj, Bpj, n = proj_chunk(wv[:, c], wb, lo, hi)
                us, Bus = us_rr.next()
                P.copy("act", us[:, 0:n], pj[:, 0:n], [Bpj], [Bus])
                P.dma("sp", u_scr[lo:hi, :, c, :].rearrange("t p n -> p t n"),
                      us[:, 0:n].rearrange("p (t n) -> p t n", n=128), R=[Bus], W=rng(Bu, lo, hi),
                      key="us%d" % (us_rr.i % 3))
        stop("A3", l)
        wv, wb = load_w(wA_fm[l, 2:10].rearrange("c p k m -> p c k m"), [8, 8, 128])
        for (lo, hi, st) in segsA:
            n = (hi - lo) * 128
            if st == 0:
                rt, Brt = rope_rr.next()
                P.dma("sp", rt[:, :, 0:n], rope[:, :, lo * 128:hi * 128].rearrange("a p n -> p a n"), W=[Brt],
                      key="rope%d" % (rope_rr.i % 2))
            def qk_tail(c, qf, Bqf):
                isq = c < 4
                pm, Bpm = pm_rr.next()
                P.mm(pm[:, 0:n], perm_r, qf[:, 0:n], True, True, [Bqf, B_id], [Bpm])
                t1, Bt1 = t1_rr.next()
                t2, Bt2 = t2_rr.next()
                ro = 2 if isq else 0
                P.tt("pool", t1[:, 0:n], qf[:, 0:n].bitcast(F32), rt[:, ro, 0:n], ALU.mult, [Bqf, Brt], [Bt1])
                P.tt("dve", t2[:, 0:n], pm[:, 0:n], rt[:, ro + 1, 0:n], ALU.mult, [Bpm, Brt], [Bt2])
                if isq:
                    qo, Bqo = qo_rr.next()
                    P.tt("dve", qo[:, 0:n], t1[:, 0:n], t2[:, 0:n], ALU.add, [Bt1, Bt2], [Bqo])
                    P.dma("sp", q_scr[lo:hi, :, c, :].rearrange("t p n -> p t n"),
                          qo[:, 0:n].rearrange("p (t n) -> p t n", n=128), R=[Bqo], W=rng(Bq, lo, hi),
                          key="qo%d" % (qo_rr.i % 2))
                else:
                    P.tt("dve", kT_all[:, c - 4, lo * 128:hi * 128], t1[:, 0:n], t2[:, 0:n], ALU.add,
                         [Bt1, Bt2], rng(Bkt, lo, hi))

            pend = None
            for c in range(8):
                pj, Bpj, n = proj_chunk(wv[:, c], wb, lo, hi)
                isq = c < 4
                if st == 1:
                    if isq:
                        qo, Bqo = qo_rr.next()
                        P.act(qo[:, 0:n], pj[:, 0:n], AF.Identity, [Bpj], [Bqo], scale=0.125, bias=0.0)
                        P.dma("sp", q_scr[lo:hi, :, c, :].rearrange("t p n -> p t n"),
                              qo[:, 0:n].rearrange("p (t n) -> p t n", n=128), R=[Bqo], W=rng(Bq, lo, hi),
                              key="qo%d" % (qo_rr.i % 2))
                    else:
                        P.copy("act", kT_all[:, c - 4, lo * 128:hi * 128], pj[:, 0:n], [Bpj], rng(Bkt, lo, hi))
                else:
                    qf, Bqf = qf_rr.next()
                    P.copy("act", qf[:, 0:n], pj[:, 0:n], [Bpj], [Bqf])
                    if pend is not None:
                        qk_tail(*pend)
                    pend = (c, qf, Bqf)
            if pend is not None:
                qk_tail(*pend)
        stop("A4", l)
        for gp in range(6):
            wv, wb = load_w(wA_fm[l, 10 + gp * 4:10 + gp * 4 + 4].rearrange("c p k m -> p c k m"), [4, 8, 128])
            for (lo, hi, st) in segsA:
                for cc in range(4):
                    gch = gp * 4 + cc
                    br, c = gch // 8, gch % 8
                    pj, Bpj, n = proj_chunk(wv[:, cc], wb, lo, hi)
                    gs_, Bgs = gs_rr.next()
                    P.act(gs_[:, 0:n], pj[:, 0:n], AF.Sigmoid, [Bpj], [Bgs])
                    P.dma("sp", g_scr[lo:hi, c, :, br, :].rearrange("t p n -> p t n"),
                          gs_[:, 0:n].rearrange("p (t n) -> p t n", n=128), R=[Bgs], W=rng(Bg, lo, hi),
                          key="gs%d" % (gs_rr.i % 3))
        if debug and l == 0:
            hh_ = P.halted
            P.halted = False
            P.dma("sp", dbg_k, kT_all, R=Bkt, W=[Bdbg], key="dbgk")
            P.dma("sp", dbg_v, v_all, R=Bv, W=[Bdbg], key="dbgv")
            P.dma("sp", dbg_p, p_all, R=Bp, W=[Bdbg], key="dbgp")
            P.halted = hh_
        stop("A", l)

        P.barrier()
        AR.release(res_mark)
        wsT = AR.alloc([4, 128], BF16)
        sgub = AR.alloc([2, 128], F32)
        wblk = AR.alloc([2, 128], BF16)
        bnd = AR.alloc([2 * 12, 128], BF16)
        B_bsp = Buf("band_special")
        wpo = AR.alloc([2, 1024], BF16)
        wso = AR.alloc([2, 1024], BF16)
        wao = AR.alloc([4, 1024], BF16)
        wo = AR.alloc([8, 1024], BF16)
        B_wD = Buf("wD")
        B_wE = Buf("wE")
        P.dma("pool", wsT, sguw[:, l], W=[B_wD], key="wD0")
        P.dma("sp", sgub, sgub_d[:, l], W=[B_wD], key="wD1")
        P.dma("pool", wblk, wblk_d[:, l], W=[B_wD], key="wD2")
        P.dma("pool", bnd[:, 0:12, :], bands[0], W=[B_wD], key="wD3")
        P.dma("pool", wpo, w_po[l].rearrange("(k p) n -> p k n", p=128), W=[B_wE], key="wD4", max_dma_last_dim=4096)
        P.dma("pool", wso, w_so[l].rearrange("(k p) n -> p k n", p=128), W=[B_wE], key="wD5", max_dma_last_dim=4096)
        P.dma("pool", wao, w_ao[l].rearrange("(k p) n -> p k n", p=128), W=[B_wE], key="wD6", max_dma_last_dim=4096)
        P.dma("pool", wo, w_o[l].rearrange("(k p) n -> p k n", p=128), W=[B_wE], key="wD7", max_dma_last_dim=4096)

        bias_rr = RR([(AR.alloc([1024], BF16), Buf("bias%d" % i)) for i in range(3)])
        qt_rr = RR([(AR.alloc([8, 128], BF16), Buf("qt%d" % i)) for i in range(2)])
        for qz_, Bqz_ in qt_rr.items:
            P.add("pool", lambda e, o=qz_: e.memset(o, 0.0), [], [Bqz_])
        ut_rr = RR([(AR.alloc([2, 128], F32), Buf("ut%d" % i)) for i in range(2)])
        vt_rr = RR([(AR.alloc([256], BF16), Buf("vt%d" % i)) for i in range(2)])
        pe_rr = RR([(AR.alloc([1024], BF16), Buf("pexp%d" % i)) for i in range(3)])
        ptb_rr = RR([(AR.alloc([8, 128], BF16), Buf("ptsb%d" % i)) for i in range(3)])
        smx2 = [(AR.alloc([16], F32), Buf("smx%d" % i)) for i in range(2)]
        rinv = AR.alloc([8], F32)
        B_rinv = Buf("rinv")
        ao = AR.alloc([512], BF16)
        B_ao = Buf("ao")
        pooledT = AR.alloc([2, 128], BF16)
        B_pooled = Buf("pooled")
        sgt = AR.alloc([2, 128], F32)
        B_sgt = Buf("sgt")
        oT_rr = RR([(AR.alloc([8, 512], BF16), Buf("oT%d" % i)) for i in range(2)])
        gt_rr = RR([(AR.alloc([4, 3, 128], BF16), Buf("gt%d" % i)) for i in range(2)])
        hc_rr = RR([(AR.alloc([512], F32), Buf("hc%d" % i)) for i in range(5)])
        e_sets = [[(AR.alloc([512], F32), Buf("et%d_%d" % (j_, i))) for i in range(3)] for j_ in range(2)]
        yT = AR.alloc([8, 512], BF16)
        B_yT = Buf("yT")

        def tile_type(t):
            if t >= 24:
                return 5
            j = t - 4
            return {0: 1, 1: 2, 14: 3, 15: 4}.get(j, 0)

        def band_type(t):
            if t == 24:
                return 3
            if t == 25:
                return 4
            j = t - 4
            return {0: 1, 15: 2}.get(j, 0)

        for (lo, hi, st) in segs_for("DE", l):
            n = (hi - lo) * 128
            oT, B_oT = oT_rr.next()
            tiles = list(range(lo, hi))
            TI = {}
            for t in tiles:
                if st == 1:
                    kranges = [(24, 26)]
                else:
                    j = t - 4
                    klo, khi = t - 2, t + 3
                    if j == 0:
                        khi = t + 4
                    if j == 15:
                        klo = t - 3
                    kranges = [(klo, khi), (24, 26)]
                TI[t] = dict(kr=kranges, nk=sum((b - a) for a, b in kranges) * 128,
                             ks=[s_ for a, b in kranges for s_ in range(a, b)], ty=tile_type(t),
                             tc0=(t - lo) * 128, smx=smx2[t % 2])

            def loads(t):
                ti = TI[t]
                ti["qt"] = qt_rr.next()
                qz4 = ti["qt"][0].rearrange("p (c two) n -> p c two n", two=2)
                P.dma("sp", qz4[0:64, :, 0, :], q_scr[t, 0:64], R=[Bq[t]], W=[ti["qt"][1]], key="qta%d" % (qt_rr.i % 2))
                P.dma("sp", qz4[64:128, :, 1, :], q_scr[t, 64:128], R=[Bq[t]], W=[ti["qt"][1]], key="qtb%d" % (qt_rr.i % 2))
                ti["ut"] = ut_rr.next()
                P.dma("sp", ti["ut"][0], u_scr[t], R=[Bu[t]], W=[ti["ut"][1]], key="ut%d" % (ut_rr.i % 2))
                ti["vt"] = vt_rr.next()
                P.dma("sp", ti["vt"][0], vn_scr[t], R=[Bvn[t]], W=[ti["vt"][1]], key="vt%d" % (vt_rr.i % 2))

            def prologue(t):
                ti = TI[t]
                tc0 = ti["tc0"]
                ut, But = ti["ut"]
                vt, Bvt = ti["vt"]
                bty = band_type(t)
                boff = 0
                if bty != 0:
                    P.dma("pool", bnd[:, 12:24, :], bands[bty], W=[B_bsp], key="bsp")
                    boff = 12
                PP = bank(7).rearrange("p (a b) -> p a b", b=128)
                for g in range(4):
                    c = g // 2
                    srcs = [d_ for d_ in (-1, 0, 1) if not ((t == 24 and d_ == -1) or (t == 25 and d_ == 1))]
                    for ii, d_ in enumerate(srcs):
                        P.mm(PP[:, g, :], p_all[:, t + d_, c * 128:(c + 1) * 128], bnd[:, boff + g * 3 + (d_ + 1), :],
                             ii == 0, ii == len(srcs) - 1, [Bp[t + d_], B_wD, B_bsp], [Bk[7]])
                for g in range(4):
                    gp_ = (g % 2) * 64
                    P.copy("act", pooledT[gp_:gp_ + 64, g // 2, :], PP[gp_:gp_ + 64, g, :], [Bk[7]], [B_pooled])
                PY = bank(7)[:, 0:256].rearrange("p (a b) -> p a b", b=128)
                for c in range(2):
                    P.mm(PY[:, c, :], wblk[:, c, :], pooledT[:, c, :], True, True, [B_pooled, B_wD], [Bk[7]])
                for c in range(2):
                    P.act(oT[:, c, tc0:tc0 + 128], PY[:, c, :], AF.Identity, [Bk[7], B_vec], [B_oT],
                          scale=V_ps(l)[:, c:c + 1], bias=0.0)
                PS_ = bank(7).rearrange("p (a b) -> p a b", b=128)
                for hh in range(4):
                    P.mm(PS_[:, hh, :], vt[:, (hh // 2) * 128:(hh // 2 + 1) * 128], wsT[:, hh, :], True, True,
                         [Bvt, B_wD], [Bk[7]])
                for hh in range(4):
                    hp = (hh % 2) * 64
                    P.tt("pool" if False else "dve", sgt[hp:hp + 64, hh // 2, :], PS_[hp:hp + 64, hh, :],
                         sgub[hp:hp + 64, hh // 2, :], ALU.add, [Bk[7], B_wD], [B_sgt])
                P.tt("pool", oT[:, 2:4, tc0:tc0 + 128], sgt, ut, ALU.mult, [B_sgt, But], [B_oT])

            units = [(t, h) for t in tiles for h in range(8)]
            US = [dict() for _ in units]

            def s1(k):
                t, h = units[k]
                ti = TI[t]
                qt, Bqt = ti["qt"]
                nk = ti["nk"]
                ch, pb = h // 2, (h % 2) * 64
                S = psd[h % 2]
                BS = [Bk[2 * (h % 2)], Bk[2 * (h % 2) + 1]]
                bt, Bbt = bias_rr.next()
                P.dma("sp", bt[:, 0:nk], biasd[l, ti["ty"], h, :, 0:nk], W=[Bbt], key="bias%d" % (bias_rr.i % 3))
                col = 0
                for (a, b) in ti["kr"]:
                    c0 = a * 128
                    rem = (b - a) * 128
                    while rem > 0:
                        w_ = min(rem, 512 - (col % 512))
                        P.mm(S[:, col:col + w_], qt[:, h, :], kT_all[:, ch, c0:c0 + w_],
                             (col % 512) == 0, False, [Bqt] + rng(Bkt, a, b), [BS[col // 512]])
                        col += w_
                        c0 += w_
                        rem -= w_
                for b0_ in range(0, nk, 512):
                    w_ = min(512, nk - b0_)
                    P.mm(S[:, b0_:b0_ + w_], identb, bt[:, b0_:b0_ + w_], False, True, [Bbt, B_id], [BS[b0_ // 512]])
                US[k].update(S=S, BS=BS, bt=bt, Bbt=Bbt)

            def s2(k):
                t, h = units[k]
                ti = TI[t]
                nk = ti["nk"]
                smx, B_smx = ti["smx"]
                u = US[k]
                P.add("dve", lambda e, o=smx[:, h:h + 1], i=u["S"][:, 0:nk]: e.tensor_reduce(
                    out=o, in_=i, axis=AX.X, op=ALU.max, negate=True), u["BS"], [B_smx])
                pex, Bpex = pe_rr.next()
                P.act(pex[:, 0:nk], u["S"][:, 0:nk], AF.Exp, u["BS"] + [B_smx], [Bpex, B_smx], bias=smx[:, h:h + 1],
                      scale=1.0, accum_out=smx[:, 8 + h:9 + h])
                u.update(pex=pex, Bpex=Bpex)

            def s3(k):
                t, h = units[k]
                nkt = TI[t]["nk"] // 128
                u = US[k]
                pb_ = 4 + (k % 2)
                PT = bank(pb_).bitcast(BF16).rearrange("p (a b) -> p a b", b=128)
                for kt in range(nkt):
                    P.tr(PT[:, kt, :], u["pex"][:, kt * 128:(kt + 1) * 128], identb, [u["Bpex"], B_id], [Bk[pb_]])
                ptb, Bptb = ptb_rr.next()
                P.copy("act" if k % 2 == 0 else "dve", ptb[:, 0:nkt, :], PT[:, 0:nkt, :], [Bk[pb_]], [Bptb])
                u.update(ptb=ptb, Bptb=Bptb)

            def s4(k):
                t, h = units[k]
                ti = TI[t]
                nkt = ti["nk"] // 128
                ks = ti["ks"]
                u = US[k]
                for kt in range(nkt):
                    P.mm(bank(6)[:, h * 64:(h + 1) * 64], u["ptb"][:, kt, :], v_all[:, ks[kt], h * 64:(h + 1) * 64],
                         kt == 0, kt == nkt - 1, [u["Bptb"], Bv[ks[kt]]], [Bk[6]])
                if h == 7:
                    smx, B_smx = ti["smx"]
                    tc0 = ti["tc0"]
                    P.add("dve", lambda e, s_=smx: e.reciprocal(out=rinv, in_=s_[:, 8:16]), [B_smx], [B_rinv])
                    P.tt("dve", ao.rearrange("p (h d) -> p h d", d=64), bank(6).rearrange("p (h d) -> p h d", d=64),
                         rinv.unsqueeze(2).broadcast_to([128, 8, 64]), ALU.mult, [Bk[6], B_rinv], [B_ao])
                    AT = bank(7).bitcast(BF16).rearrange("p (a b) -> p a b", b=128)
                    for c in range(4):
                        P.tr(AT[:, c, :], ao[:, c * 128:(c + 1) * 128], identb, [B_ao, B_id], [Bk[7]])
                    P.copy("act", oT[:, 4:8, tc0:tc0 + 128], AT[:, 0:4, :], [Bk[7]], [B_oT])

            NU = len(units)
            loads(tiles[0])
            for k in range(NU + 3):
                if k < NU:
                    t, h = units[k]
                    if h == 0:
                        if t + 1 < hi:
                            loads(t + 1)
                        prologue(t)
                    s1(k)
                if 0 <= k - 1 < NU:
                    s2(k - 1)
                if 0 <= k - 2 < NU:
                    s3(k - 2)
                if 0 <= k - 3 < NU:
                    s4(k - 3)
            if debug and l == 0:
                for t in range(lo, hi):
                    P.dma("sp", dbg_o[t], oT[:, :, (t - lo) * 128:(t - lo + 1) * 128], R=[B_oT], W=[Bdbg], key="dbgo")
            for c in range(8):
                gt, Bgt = gt_rr.next()
                P.dma("sp", gt[:, 0:hi - lo], g_scr[lo:hi, c].rearrange("t p b n -> p t b n"), R=rng(Bg, lo, hi), W=[Bgt],
                      key="gt%d" % (gt_rr.i % 2))
                b0 = (c % 2) * 3
                brs = [(wpo, 2, 0), (wso, 2, 2), (wao, 4, 4)]
                for bi_, (wt_, nkk, o0) in enumerate(brs):
                    for k in range(nkk):
                        P.mm(bank(b0 + bi_)[:, 0:n], wt_[:, k, c * 128:(c + 1) * 128], oT[:, o0 + k, 0:n], k == 0, k == nkk - 1,
                             [B_wE, B_oT], [Bk[b0 + bi_]])
                e_t = e_sets[c % 2]
                for bi_ in range(3):
                    et, Bet = e_t[bi_]
                    P.tt("dve", et[:, 0:n].rearrange("p (t n) -> p t n", n=128),
                         bank(b0 + bi_)[:, 0:n].rearrange("p (t n) -> p t n", n=128), gt[:, 0:hi - lo, bi_, :],
                         ALU.mult, [Bk[b0 + bi_], Bgt], [Bet])
                P.tt("pool", e_t[0][0][:, 0:n], e_t[0][0][:, 0:n], e_t[1][0][:, 0:n], ALU.add, [e_t[0][1], e_t[1][1]], [e_t[0][1]])
                P.tt("pool", yT[:, c, 0:n], e_t[0][0][:, 0:n], e_t[2][0][:, 0:n], ALU.add, [e_t[0][1], e_t[2][1]], [B_yT])
            hcs = []

            def ld_hc(c2):
                hc, Bhc = hc_rr.next()
                P.dma("sp", hc[:, 0:n], h_src(l, c2, lo, hi), R=([Bh[t][c2] for t in range(lo, hi)] if l > 0 else []),
                      W=[Bhc], key="hc%d" % (hc_rr.i % 5))
                hcs.append((hc, Bhc, "hc%d" % (hc_rr.i % 5)))

            for c2 in range(3):
                ld_hc(c2)
            for c2 in range(8):
                ob = 6 + (c2 % 2)
                hc, Bhc, hkey = hcs[c2]
                for c in range(8):
                    P.mm(bank(ob)[:, 0:n], wo[:, c, c2 * 128:(c2 + 1) * 128], yT[:, c, 0:n], c == 0, c == 7,
                         [B_wE, B_yT], [Bk[ob]])
                P.stt("dve", hc[:, 0:n], bank(ob)[:, 0:n], MT(l, st, 2)[:, c2:c2 + 1], hc[:, 0:n],
                      ALU.mult, ALU.add, [Bk[ob], Bhc, B_modt], [Bhc])
                if c2 + 3 < 8:
                    ld_hc(c2 + 3)
                P.dma("sp", h_scr[c2, :, lo * 128:hi * 128], hc[:, 0:n], R=[Bhc], W=[Bh[t][c2] for t in range(lo, hi)],
                      key=hkey)
        stop("E", l)

        P.barrier()
        AR.release(m0)
        hF2 = [(AR.alloc([8, 1024], F32), Buf("hF%d" % i)) for i in range(2)]
        aF = AR.alloc([8, 1024], BF16)
        B_aF = Buf("aF")
        hid = AR.alloc([32, 1024], BF16)
        B_hid = Buf("hid")
        rsbF = (AR.alloc([512], F32), Buf("rsF"))
        tmpF_rr = RR([(AR.alloc([512], F32), Buf("tmpF%d" % i)) for i in range(2)])
        w1_rr = RR([(AR.alloc([4, 8, 128], BF16), Buf("w1b%d" % i)) for i in range(2)])
        w2_rr = RR([(AR.alloc([32, 128], BF16), Buf("w2b%d" % i)) for i in range(2)])
        rl_rr = RR([(AR.alloc([512], F32), Buf("rl%d" % i)) for i in range(2)])
        f1_rr = RR([(bank(i), Bk[i]) for i in (1, 2, 3, 4)])
        f2_rr = RR([(bank(i), Bk[i]) for i in (5, 6, 7)])
        groupsF = segs_for("F", l)

        def goffs(grp):
            offs = []
            o_ = 0
            for (lo, hi, st) in grp:
                offs.append(o_)
                o_ += (hi - lo) * 128
            return offs

        def load_group(gi):
            hF, B_hF = hF2[gi % 2]
            grp = groupsF[gi]
            for si_, ((lo, hi, st), o0) in enumerate(zip(grp, goffs(grp))):
                n = (hi - lo) * 128
                P.dma("sp", hF[:, :, o0:o0 + n], h_scr[:, :, lo * 128:hi * 128].rearrange("c p n -> p c n"),
                      R=[b_ for t in range(lo, hi) for b_ in Bh[t]], W=[B_hF], key="hF%d_%d" % (gi % 2, si_))

        load_group(0)
        for gi, grp in enumerate(groupsF):
            hF, B_hF = hF2[gi % 2]
            offs = goffs(grp)
            for (lo, hi, st), o0 in zip(grp, offs):
                n = (hi - lo) * 128
                rms_to_aT(hF[:, :, o0:o0 + n], B_hF, n, MT(l, st, 3), MT(l, st, 4), aF[:, :, o0:o0 + n], [B_aF],
                          rsbF, tmpF_rr, (bank(0), Bk[0]))
            if gi + 1 < len(groupsF):
                load_group(gi + 1)
            for jp in range(8):
                w1t, Bw1 = w1_rr.next()
                P.dma("pool", w1t, w1r[l, jp * 4:jp * 4 + 4].rearrange("j p k m -> p j k m"), W=[Bw1],
                      key="w1b%d" % (w1_rr.i % 2), max_dma_last_dim=4096)
                for jj in range(4):
                    j = jp * 4 + jj
                    for (lo, hi, st), o0 in zip(grp, offs):
                        n = (hi - lo) * 128
                        pf, Bpf = f1_rr.next()
                        for k in range(8):
                            P.mm(pf[:, 0:n], w1t[:, jj, k, :], aF[:, k, o0:o0 + n], k == 0, k == 7, [Bw1, B_aF], [Bpf])
                        rl, Brl = rl_rr.next()
                        P.act(rl[:, 0:n], pf[:, 0:n], AF.Relu, [Bpf], [Brl])
                        P.stt("dve", hid[:, j, o0:o0 + n], pf[:, 0:n], 0.0, rl[:, 0:n], ALU.max, ALU.mult,
                              [Bpf, Brl], [B_hid])
            for c2 in range(8):
                w2t, Bw2 = w2_rr.next()
                P.dma("pool", w2t, w2r[l, c2], W=[Bw2], key="w2b%d" % (w2_rr.i % 2), max_dma_last_dim=4096)
                for (lo, hi, st), o0 in zip(grp, offs):
                    n = (hi - lo) * 128
                    pf, Bpf = f2_rr.next()
                    for j in range(32):
                        P.mm(pf[:, 0:n], w2t[:, j, :], hid[:, j, o0:o0 + n], j == 0, j == 31, [Bw2, B_hid], [Bpf])
                    P.stt("dve", hF[:, c2, o0:o0 + n], pf[:, 0:n], MT(l, st, 5)[:, c2:c2 + 1], hF[:, c2, o0:o0 + n],
                          ALU.mult, ALU.add, [Bpf, B_hF, B_modt], [B_hF])
            for si_, ((lo, hi, st), o0) in enumerate(zip(grp, offs)):
                n = (hi - lo) * 128
                if not last:
                    P.dma("sp", h_scr[:, :, lo * 128:hi * 128].rearrange("c p n -> p c n"), hF[:, :, o0:o0 + n],
                          R=[B_hF], W=[b_ for t in range(lo, hi) for b_ in Bh[t]], key="hFo%d_%d" % (gi % 2, si_))
                else:
                    rs, Brs = rsbF
                    rms_rstd(hF[:, :, o0:o0 + n], B_hF, n, rsbF, (bank(0), Bk[0]))
                    for c in range(8):
                        P.stt("dve", hF[:, c, o0:o0 + n], hF[:, c, o0:o0 + n], V_fg[:, c:c + 1], rs[:, 0:n], ALU.mult, ALU.mult,
                              [B_hF, Brs, B_vec], [B_hF])
                    op = P.dma("sp", outT[:, :, (lo - 4) * 128:(hi - 4) * 128].rearrange("c p n -> p c n"), hF[:, :, o0:o0 + n],
                               R=[B_hF], W=[Bout], key="outst%d_%d" % (gi % 2, si_))
                    out_ops.append(op)
        stop("F", l)

    P.halted = False
    if not out_ops:
        z = AR.t[:, 0:2048]
        out_ops.append(P.dma("sp", outT[0], z, R=[], W=[Bout], key="outst"))
    if debug:
        out_ops.append(P.dma("sp", outT[1, :, 0:8], vec[:, 0:8], R=[Bdbg], W=[Bout], key="dbgfin"))
    P.final_waits = out_ops
    P.emit()
    return nc, P, AR


def _fm(v):
    v = np.asarray(v, np.float32)
    return np.ascontiguousarray(v.reshape(-1, 128).T)


def _rope_perm():
    idx = []
    for h in range(8):
        idx += [h * 64 + 2 * i for i in range(32)] + [h * 64 + 2 * i + 1 for i in range(32)]
    return np.array(idx)


def _shared_inputs(inp):
    w_in = np.asarray(inp["w_in"], np.float32)
    perm = _rope_perm()
    wp = w_in.copy()
    wp[:, :, 768:1280] = w_in[:, :, 768:1280][:, :, perm]
    wp[:, :, 1280:1792] = w_in[:, :, 1280:1792][:, :, perm]
    tm_cols = [np.r_[0:256, 512:768], np.r_[1792:2304]]
    wA_tm = np.stack([np.stack([wp[l][:, cols].reshape(8, 128, 512).transpose(1, 0, 2) for cols in tm_cols])
                      for l in range(2)])
    fm_starts = [256, 384] + [768 + 128 * i for i in range(8)] + [2304 + 128 * i for i in range(24)]
    wA_fm = np.stack([np.stack([wp[l][:, s:s + 128].reshape(8, 128, 128).transpose(1, 0, 2) for s in fm_starts])
                      for l in range(2)])
    w1 = np.asarray(inp["w_ff1"], np.float32)
    w1r = np.stack([np.stack([w1[l][:, j * 128:(j + 1) * 128].reshape(8, 128, 128).transpose(1, 0, 2)
                              for j in range(32)]) for l in range(2)])
    w2 = np.asarray(inp["w_ff2"], np.float32)
    w2r = np.stack([np.stack([w2[l][:, c * 128:(c + 1) * 128].reshape(32, 128, 128).transpose(1, 0, 2)
                              for c in range(8)]) for l in range(2)])
    sgu_w = np.asarray(inp["sgu_w"], np.float32)
    sguw = np.ascontiguousarray(sgu_w.transpose(3, 0, 1, 2))
    sgu_b = np.asarray(inp["sgu_b"], np.float32)
    sgub = np.zeros((128, 2, 2, 128), np.float32)
    for part in range(128):
        for c in range(2):
            sgub[part, :, c, :] = sgu_b[:, 2 * c + part // 64, :]
    w_pool = np.asarray(inp["w_pool"], np.float32)
    wblk = np.zeros((128, 2, 2, 128), np.float32)
    for c in range(2):
        for gl in range(2):
            wblk[gl * 64:(gl + 1) * 64, :, c, gl * 64:(gl + 1) * 64] = w_pool[:, 2 * c + gl].transpose(1, 0, 2)
    consts = np.zeros((128, 3, 128), np.float32)
    consts[:, 0, :] = np.eye(128)
    for pp in range(128):
        partner = pp + 32 if (pp % 64) < 32 else pp - 32
        consts[partner, 1, pp] = 1.0
    consts[:, 2, :] = 1.0
    return dict(w_mod=np.ascontiguousarray(inp["w_mod"], np.float32), wA_tm=np.ascontiguousarray(wA_tm),
                wA_fm=np.ascontiguousarray(wA_fm), sguw=sguw, sgub=sgub, wblk=wblk,
                w_pool_out=np.ascontiguousarray(inp["w_pool_out"], np.float32),
                w_sgu_out=np.ascontiguousarray(inp["w_sgu_out"], np.float32),
                w_attn_out=np.ascontiguousarray(inp["w_attn_out"], np.float32),
                w_o=np.ascontiguousarray(inp["w_o"], np.float32), w1r=np.ascontiguousarray(w1r),
                w2r=np.ascontiguousarray(w2r), consts=consts)


def _band_set(kind):
    out = np.zeros((128, 12, 128), np.float32)
    L = 384
    base = 128
    for g, w in enumerate((2, 4, 8, 16)):
        for tt in range(128):
            pos = base + tt
            lo_b = base if kind == 1 else 0
            hi_b = base + 128 if kind == 2 else L
            lo = min(max(pos - w // 2, lo_b), hi_b)
            hi = min(max(pos + (w - w // 2), lo_b), hi_b)
            cnt = hi - lo
            for s in range(lo, hi):
                d = s // 128
                out[s % 128, g * 3 + d, tt] += 1.0 / cnt
            out[tt, g * 3 + 1, tt] -= 1.0
    return out


def _bias_tables(rpb, core_rows0, n_rows_total=128):
    out = np.full((2, 6, 8, 128, 1024), NEG, np.float32)
    q_i = np.arange(128)
    for ty, j in ((0, 4), (1, 0), (2, 1), (3, 14), (4, 15)):
        klo, khi = j - 2, j + 3
        if j == 0:
            khi = j + 4
        if j == 15:
            klo = j - 3
        nkl = (khi - klo) * 128
        key = np.arange(nkl)
        k_row = core_rows0 + 2 * klo + key // 64
        k_col = key % 64
        q_row = core_rows0 + 2 * j + q_i // 64
        q_col = q_i % 64
        rs = np.clip(q_row - 4, 0, n_rows_total - 8)
        cs = np.clip(q_col - 8, 0, 64 - 16)
        valid = ((k_row[None, :] >= rs[:, None]) & (k_row[None, :] < rs[:, None] + 8) &
                 (k_col[None, :] >= cs[:, None]) & (k_col[None, :] < cs[:, None] + 16) &
                 (k_row[None, :] >= 0) & (k_row[None, :] < n_rows_total))
        dr = np.clip(k_row[None, :] - q_row[:, None] + 7, 0, 14)
        dc = np.clip(k_col[None, :] - q_col[:, None] + 15, 0, 30)
        for l in range(2):
            for h in range(8):
                g = rpb[l, h][dr, dc]
                out[l, ty, h, :, 0:nkl] = np.where(valid, g, NEG)
                out[l, ty, h, :, nkl:nkl + 256] = 0.0
    out[:, 5, :, :, 0:256] = 0.0
    return out.astype(ml_dtypes.bfloat16)


def _core_inputs(inp, core):
    b, blk = core // 4, core % 4
    row0 = 32 * blk
    x = np.asarray(inp["x"], np.float32)[b]
    t0 = (row0 - 8) * 64
    xs = np.zeros((NLAT, D), np.float32)
    lo, hi = max(t0, 0), min(t0 + NLAT, 8192)
    xs[lo - t0:hi - t0] = x[lo:hi]
    xT = np.ascontiguousarray(xs.T.reshape(8, 128, NLAT))
    ctxT = np.ascontiguousarray(np.asarray(inp["ctx"], np.float32)[b].T.reshape(8, 128, 256))
    vecs = np.zeros((128, 156), np.float32)
    vecs[:, 0:8] = _fm(inp["c"][b])
    vecs[:, 8:16] = _fm(inp["c_ctx"])
    for l in range(2):
        vecs[:, 16 + l * 48:16 + (l + 1) * 48] = _fm(inp["b_mod"][l])
        vecs[:, 112 + l * 8:112 + (l + 1) * 8] = _fm(inp["norm1_g"][l])
        vecs[:, 128 + l * 8:128 + (l + 1) * 8] = _fm(inp["norm2_g"][l])
        vecs[:, 152 + l * 2:152 + (l + 1) * 2] = _fm(inp["pool_scale"][l])
    vecs[:, 144:152] = _fm(inp["final_g"])
    tok = np.arange(NLAT)
    row = (row0 - 8 + tok // 64).astype(np.float32)
    col = (tok % 64).astype(np.float32)
    inv_freq = (10000.0 ** (-np.arange(16, dtype=np.float32) / 16)).astype(np.float32)
    ang = np.concatenate([row[:, None] * inv_freq, col[:, None] * inv_freq], axis=-1).astype(np.float32)
    cos, sin = np.cos(ang), np.sin(ang)
    rope = np.zeros((4, 128, NLAT), np.float32)
    for pp in range(128):
        i = pp % 64
        e = i % 32
        rope[0, pp] = cos[:, e]
        rope[1, pp] = -sin[:, e] if i < 32 else sin[:, e]
    rope[2:4] = rope[0:2] * np.float32(0.125)
    first = (blk == 0)
    lastb = (blk == 3)
    gen = _band_set(0)
    bands = np.stack([gen, _band_set(1) if first else gen, _band_set(2) if lastb else gen, _band_set(1), _band_set(2)])
    if first:
        bands[1][:, [0, 3, 6, 9], :] = 0.0
    bias = _bias_tables(np.asarray(inp["na_rpb"], np.float32), row0)
    return dict(xT=xT, ctxT=ctxT, vecs=vecs, rope=rope, bands=np.ascontiguousarray(bands), bias=bias)


_PROG = {}


def kernel(**inputs):
    if "nc" not in _PROG:
        _PROG["nc"] = build_program()[0]
    nc = _PROG["nc"]
    shared = _shared_inputs(inputs)
    in_maps = []
    for core in range(8):
        m = dict(shared)
        m.update(_core_inputs(inputs, core))
        in_maps.append(m)
    res = run_bass_kernel_spmd(nc, in_maps, core_ids=list(range(8)))
    out = np.zeros((2, 8192, D), np.float32)
    for core in range(8):
        b, blk = core // 4, core % 4
        oT = np.asarray(res.results[core]["outT"], np.float32)
        out[b, blk * 2048:(blk + 1) * 2048, :] = oT.reshape(D, 2048).T
    return out
```

```python
import numpy as np
import ml_dtypes
import concourse.bass as bass
import concourse.mybir as mybir
from concourse.bass_utils import run_bass_kernel_spmd

F32 = mybir.dt.float32
F32R = mybir.dt.float32r
BF16 = mybir.dt.bfloat16
AF = mybir.ActivationFunctionType
ALU = mybir.AluOpType
AX = mybir.AxisListType

D = 1024
NLS = 24
NS = 26
NLAT = NLS * 128
TOK = NS * 128
EPS = 1e-6
NEG = -30000.0
DBG_SKIP_LN = False


class Buf:
    __slots__ = ("name", "last_writer", "readers")

    def __init__(self, name):
        self.name = name
        self.last_writer = None
        self.readers = []


class Op:
    __slots__ = ("eng", "fn", "deps", "is_dma", "key", "idx", "signal", "sem", "val", "waits")

    def __init__(self, eng, fn, is_dma, key, idx):
        self.eng = eng
        self.fn = fn
        self.is_dma = is_dma
        self.key = key
        self.idx = idx
        self.deps = []
        self.signal = False
        self.sem = None
        self.val = 0
        self.waits = []


ENGS = ("pe", "act", "dve", "pool", "sp")


class Prog:
    def __init__(self, nc):
        self.nc = nc
        self.ops = []
        self.final_waits = []
        self.phase_buf = Buf("phase")
        self.bar_ap = None
        self.halted = False
        self.dummy = Op("dve", None, False, None, -1)

    def add(self, eng, fn, reads=(), writes=(), dma_key=None):
        if self.halted:
            return self.dummy
        is_dma = dma_key is not None
        op = Op(eng, fn, is_dma, dma_key, len(self.ops))
        deps = {}
        for b in reads:
            w = b.last_writer
            if w is not None:
                deps[w.idx] = [w, True]
        for b in writes:
            w = b.last_writer
            if w is not None and w.idx not in deps:
                deps[w.idx] = [w, False]
            for r in b.readers:
                if r.idx not in deps:
                    deps[r.idx] = [r, False]
        pb = self.phase_buf
        if pb.last_writer is not None:
            deps[pb.last_writer.idx] = [pb.last_writer, True]
        pb.readers.append(op)
        for b in reads:
            b.readers.append(op)
        for b in writes:
            b.last_writer = op
            b.readers = []
        deps.pop(op.idx, None)
        op.deps = list(deps.values())
        self.ops.append(op)
        return op

    def barrier(self):
        if self.halted:
            return self.dummy
        pb = self.phase_buf
        op = Op("dve", lambda e: e.memset(self.bar_ap, 0.0), False, None, len(self.ops))
        deps = {}
        for r in pb.readers:
            deps[r.idx] = [r, True]
        if pb.last_writer is not None:
            deps[pb.last_writer.idx] = [pb.last_writer, True]
        op.deps = list(deps.values())
        pb.last_writer = op
        pb.readers = []
        self.ops.append(op)
        return op

    def dma(self, eng, out, in_, R=(), W=(), key=None, **kw):
        return self.add(eng, lambda e: e.dma_start(out=out, in_=in_, **kw), R, W, dma_key=key)

    def mm(self, out, lhsT, rhs, start, stop, R, W):
        return self.add("pe", lambda e: e.matmul(out, lhsT=lhsT, rhs=rhs, start=start, stop=stop), R, W)

    def tr(self, out, in_, ident, R, W):
        return self.add("pe", lambda e: e.transpose(out=out, in_=in_, identity=ident), R, W)

    def act(self, out, in_, func, R, W, **kw):
        return self.add("act", lambda e: e.activation(out=out, in_=in_, func=func, **kw), R, W)

    def copy(self, eng, out, in_, R, W):
        if eng == "act":
            return self.add("act", lambda e: e.copy(out=out, in_=in_), R, W)
        return self.add(eng, lambda e: e.tensor_copy(out=out, in_=in_), R, W)

    def tt(self, eng, out, in0, in1, op, R, W):
        return self.add(eng, lambda e: e.tensor_tensor(out=out, in0=in0, in1=in1, op=op), R, W)

    def ts(self, eng, out, in0, s1, s2, op0, op1, R, W):
        return self.add(eng, lambda e: e.tensor_scalar(out=out, in0=in0, scalar1=s1, scalar2=s2, op0=op0, op1=op1), R, W)

    def stt(self, eng, out, in0, scalar, in1, op0, op1, R, W):
        return self.add(eng, lambda e: e.scalar_tensor_tensor(out=out, in0=in0, scalar=scalar, in1=in1,
                                                              op0=op0, op1=op1), R, W)

    def emit(self):
        nc = self.nc
        ops = self.ops
        for op in ops:
            need = []
            for d, raw in op.deps:
                if d.is_dma or op.is_dma or d.eng != op.eng or (raw and op.eng != "pe"):
                    need.append(d)
            op.waits = need
            for d in need:
                d.signal = True
        for op in self.final_waits:
            op.signal = True
        for op in ops:
            if op.is_dma:
                op.signal = True
        sems = {}
        counters = {}
        for op in ops:
            if not op.signal:
                continue
            k = ("dma", op.key) if op.is_dma else ("eng", op.eng)
            if k not in sems:
                sems[k] = nc.alloc_semaphore("s%d" % len(sems))
                counters[k] = 0
            op.sem = sems[k]
            counters[k] += 16 if op.is_dma else 1
            op.val = counters[k]
            op.key = k
        self.n_sems = len(sems)
        per_eng = {e: [] for e in ENGS}
        for op in ops:
            per_eng[op.eng].append(op)
        finals = list(self.final_waits)

        def run(eng_name, eng):
            waited = {}
            for op in per_eng[eng_name]:
                req = {}
                for d in op.waits:
                    if req.get(d.key, 0) < d.val:
                        req[d.key] = d.val
                for k, v in req.items():
                    if waited.get(k, 0) >= v:
                        continue
                    waited[k] = v
                    eng.wait_ge(sems[k], v)
                inst = op.fn(eng)
                if op.signal:
                    inst.then_inc(op.sem, 16 if op.is_dma else 1)
            if eng_name == "sp":
                for d in finals:
                    if waited.get(d.key, 0) < d.val:
                        waited[d.key] = d.val
                        eng.wait_ge(sems[d.key], d.val)

        with nc.Block() as block:
            @block.tensor
            def _(e):
                run("pe", e)

            @block.scalar
            def _(e):
                run("act", e)

            @block.vector
            def _(e):
                run("dve", e)

            @block.gpsimd
            def _(e):
                run("pool", e)

            @block.sync
            def _(e):
                run("sp", e)


class Arena:
    def __init__(self, nc, nbytes):
        self.t = nc.alloc_sbuf_tensor("arena", [128, nbytes // 4], F32)
        self.n = nbytes
        self.off = 0
        self.peak = 0

    def alloc(self, shape, dtype):
        esz = 2 if dtype == BF16 else 4
        n = int(np.prod(shape)) * esz
        n4 = (n + 31) // 32 * 32
        assert self.off + n4 <= self.n, ("SBUF arena overflow", self.off, n4, self.n)
        a = self.t[:, self.off // 4:(self.off + n) // 4]
        self.off += n4
        self.peak = max(self.peak, self.off)
        if dtype != F32:
            a = a.bitcast(dtype)
        if len(shape) == 2:
            return a.rearrange("p (a b) -> p a b", a=shape[0])
        if len(shape) == 3:
            return a.rearrange("p (a b c) -> p a b c", a=shape[0], b=shape[1])
        return a

    def mark(self):
        return self.off

    def release(self, m):
        self.off = m


class RR:
    def __init__(self, items):
        self.items = items
        self.i = 0

    def next(self):
        it = self.items[self.i % len(self.items)]
        self.i += 1
        return it


def segs_for(kind, layer):
    if kind == "A":
        if layer == 0:
            s = [(4 * b, 4 * b + 4, 0) for b in range(6)]
        else:
            s = [(2, 4, 0)] + [(4 * b, 4 * b + 4, 0) for b in range(1, 5)] + [(20, 22, 0)]
        return s + [(24, 26, 1)]
    if kind == "DE":
        s = [(4 * b, 4 * b + 4, 0) for b in range(1, 5)]
        if layer == 0:
            s = [(2, 4, 0)] + s + [(20, 22, 0), (24, 26, 1)]
        return s
    if kind == "F":
        g = [[(4, 8, 0), (8, 12, 0)], [(12, 16, 0), (16, 20, 0)]]
        if layer == 0:
            g.append([(2, 4, 0), (20, 22, 0), (24, 26, 1)])
        return g
    raise ValueError(kind)


def build_program(n_layers=2, stop_after=None, debug=False):
    nc = bass.Bass("TRN2", target_bir_lowering=False)
    P = Prog(nc)

    def din(name, shape, dt=F32):
        return nc.dram_tensor(name, list(shape), dt, kind="ExternalInput").ap()

    xT = din("xT", [8, 128, NLAT])
    ctxT = din("ctxT", [8, 128, 256])
    vecs = din("vecs", [128, 156])
    consts = din("consts", [128, 3, 128])
    rope = din("rope", [4, 128, NLAT])
    bands = din("bands", [5, 128, 12, 128])
    biasd = din("bias", [2, 6, 8, 128, 1024], BF16)
    w_mod = din("w_mod", [2, 1024, 6144])
    wA_tm = din("wA_tm", [2, 2, 128, 8, 512])
    wA_fm = din("wA_fm", [2, 34, 128, 8, 128])
    sguw = din("sguw", [128, 2, 4, 128])
    sgub_d = din("sgub", [128, 2, 2, 128])
    wblk_d = din("wblk", [128, 2, 2, 128])
    w_po = din("w_pool_out", [2, 256, 1024])
    w_so = din("w_sgu_out", [2, 256, 1024])
    w_ao = din("w_attn_out", [2, 512, 1024])
    w_o = din("w_o", [2, 1024, 1024])
    w1r = din("w1r", [2, 32, 128, 8, 128])
    w2r = din("w2r", [2, 8, 128, 32, 128])
    outT = nc.dram_tensor("outT", [8, 128, 2048], F32, kind="ExternalOutput").ap()

    skind = "ExternalOutput" if debug else "Internal"

    def dscr(name, shape, dt=F32):
        return nc.dram_tensor(name, list(shape), dt, kind=skind).ap()

    h_scr = dscr("h_scr", [8, 128, TOK])
    q_scr = dscr("q_scr", [NS, 128, 4, 128], BF16)
    u_scr = dscr("u_scr", [NS, 128, 2, 128])
    vn_scr = dscr("vn_scr", [NS, 128, 256], BF16)
    g_scr = dscr("g_scr", [NS, 8, 128, 3, 128], BF16)
    if debug:
        dbg_k = dscr("dbg_k", [128, 4, TOK], BF16)
        dbg_v = dscr("dbg_v", [128, NS, 512], BF16)
        dbg_p = dscr("dbg_p", [128, NS, 256], BF16)
        dbg_a = dscr("dbg_a", [128, 8, TOK], BF16)
        dbg_mod = dscr("dbg_mod", [128, 2, 2, 48])
        dbg_o = dscr("dbg_o", [NS, 128, 8, 128], BF16)

    Bh = [[Buf("h%d_%d" % (t, c)) for c in range(8)] for t in range(NS)]
    Bq = [Buf("q%d" % t) for t in range(NS)]
    Bu = [Buf("u%d" % t) for t in range(NS)]
    Bvn = [Buf("vn%d" % t) for t in range(NS)]
    Bg = [Buf("g%d" % t) for t in range(NS)]
    Bout = Buf("out")
    Bdbg = Buf("dbg")

    psd = [nc.alloc_psum_tensor("psd%d" % i, [128, 1024], F32) for i in range(4)]
    Bk = [Buf("bank%d" % i) for i in range(8)]

    def bank(i):
        return psd[i // 2][:, (i % 2) * 512:(i % 2) * 512 + 512]

    AR = Arena(nc, 197 * 1024)
    bar_t = AR.alloc([8], F32)
    P.bar_ap = bar_t
    cst = AR.alloc([3, 128], F32)
    identb = AR.alloc([128], BF16)
    perm_r = nc.alloc_sbuf_tensor("perm_r", [128, 128], F32R)[:]
    ones_r = nc.alloc_sbuf_tensor("ones_r", [128, 128], F32R)[:]
    sq_rr = RR([(nc.alloc_sbuf_tensor("sq_r%d" % i, [128, 512], F32R)[:], Buf("sq_r%d" % i)) for i in range(2)])
    qf_rr = RR([(nc.alloc_sbuf_tensor("qf_r%d" % i, [128, 512], F32R)[:], Buf("qf_r%d" % i)) for i in range(2)])
    vec = AR.alloc([156], F32)
    modt = AR.alloc([24, 8], F32)
    silu_b = AR.alloc([2, 8], BF16)
    B_c = Buf("consts")
    B_vec = Buf("vec")
    B_modt = Buf("modt")
    B_silu = Buf("silu")

    def V_c(s):
        return vec[:, s * 8:(s + 1) * 8]

    def V_bmod(l):
        return vec[:, 16 + l * 48:16 + (l + 1) * 48]

    def V_n1(l):
        return vec[:, 112 + l * 8:112 + (l + 1) * 8]

    def V_n2(l):
        return vec[:, 128 + l * 8:128 + (l + 1) * 8]

    V_fg = vec[:, 144:152]

    def V_ps(l):
        return vec[:, 152 + l * 2:152 + (l + 1) * 2]

    def MT(l, s, kind):
        i = (l * 2 + s) * 6 + kind
        return modt[:, i, :]

    P.dma("sp", cst, consts, W=[B_c], key="cst")
    P.dma("sp", vec, vecs, W=[B_vec], key="vec")
    B_id = Buf("ident")
    P.copy("dve", identb, cst[:, 0, :], [B_c], [B_id])
    P.copy("dve", perm_r, cst[:, 1, :], [B_c], [B_id])
    P.copy("dve", ones_r, cst[:, 2, :], [B_c], [B_id])
    P.act(silu_b.rearrange("p s k -> p (s k)"), vec[:, 0:16], AF.Silu, [B_vec], [B_silu])

    def stop(name, l=0):
        if stop_after is not None and tuple(stop_after) == (name, l):
            P.halted = True

    stop("S")

    m0 = AR.mark()
    wm = [(AR.alloc([8, 512], BF16), Buf("wm%d" % i)) for i in range(2)]
    wm_rr = RR(wm)
    modraw = AR.alloc([48, 2], F32)
    tmpm = AR.alloc([8, 2], F32)
    B_modraw = Buf("modraw")
    for l in range(n_layers):
        psm = bank(0).rearrange("p (a b) -> p a b", b=2)[:, 0:48, :]
        for pc in range(12):
            wt, wb = wm_rr.next()
            P.dma("pool", wt, w_mod[l, :, pc * 512:(pc + 1) * 512].rearrange("(k p) n -> p k n", p=128),
                  W=[wb], key="wm%d" % (wm_rr.i % 2), max_dma_last_dim=4096)
            for cc in range(4):
                ch = pc * 4 + cc
                for k in range(8):
                    P.mm(psm[:, ch, :], wt[:, k, cc * 128:(cc + 1) * 128], silu_b[:, :, k], k == 0, k == 7,
                         [wb, B_silu], [Bk[0]])
        P.tt("dve", modraw, psm, V_bmod(l).unsqueeze(2).broadcast_to([128, 48, 2]), ALU.add,
             [Bk[0], B_vec], [B_modraw])
        for s in range(2):
            def chunk(i):
                return modraw[:, i * 8:(i + 1) * 8, s]
            P.stt("dve", MT(l, s, 0), chunk(1), 1.0, V_n1(l), ALU.add, ALU.mult, [B_modraw, B_vec], [B_modt])
            P.copy("dve", MT(l, s, 1), chunk(0), [B_modraw], [B_modt])
            P.copy("dve", MT(l, s, 2), chunk(2), [B_modraw], [B_modt])
            P.stt("dve", MT(l, s, 3), chunk(4), 1.0, V_n2(l), ALU.add, ALU.mult, [B_modraw, B_vec], [B_modt])
            P.copy("dve", MT(l, s, 4), chunk(3), [B_modraw], [B_modt])
            P.copy("dve", MT(l, s, 5), chunk(5), [B_modraw], [B_modt])
    if debug:
        P.dma("sp", dbg_mod.rearrange("p l s c -> p (l s c)"), modt.rearrange("p a b -> p (a b)")[:, 0:192],
              R=[B_modt], W=[Bdbg], key="dbgm")
    AR.release(m0)
    stop("M")
    P.barrier()

    kT_all = AR.alloc([4, TOK], BF16)
    v_all = AR.alloc([NS, 512], BF16)
    p_all = AR.alloc([NS, 256], BF16)
    Bkt = [Buf("kT%d" % t) for t in range(NS)]
    Bv = [Buf("v%d" % t) for t in range(NS)]
    Bp = [Buf("p%d" % t) for t in range(NS)]
    res_mark = AR.mark()

    def rng(bufs, lo, hi):
        return [bufs[t] for t in range(lo, hi)]

    def h_src(l, c, lo, hi):
        if l == 0:
            if lo >= 24:
                return ctxT[c, :, (lo - 24) * 128:(hi - 24) * 128]
            return xT[c, :, lo * 128:hi * 128]
        return h_scr[c, :, lo * 128:hi * 128]

    def load_h(dst, l, lo, hi, Bdst, key, first_layer_input):
        n = (hi - lo) * 128
        if first_layer_input:
            src = (ctxT[:, :, (lo - 24) * 128:(hi - 24) * 128] if lo >= 24 else xT[:, :, lo * 128:hi * 128])
            R = []
        else:
            src = h_scr[:, :, lo * 128:hi * 128]
            R = [b_ for t in range(lo, hi) for b_ in Bh[t]]
        P.dma("sp", dst[:, :, 0:n], src.rearrange("c p n -> p c n"), R=R, W=[Bdst], key=key)

    def rms_rstd(hb, Bhb, n, rsb, psb):
        rs, Brs = rsb
        pb, Bpb = psb
        for c in range(8):
            sq, Bsq = sq_rr.next()
            P.act(sq[:, 0:n], hb[:, c, 0:n], AF.Square, [Bhb], [Bsq])
            P.mm(pb[:, 0:n], ones_r, sq[:, 0:n], c == 0, c == 7, [Bsq, B_id], [Bpb])
        P.ts("dve", rs[:, 0:n], pb[:, 0:n], 1.0 / D, EPS, ALU.mult, ALU.add, [Bpb], [Brs])
        P.act(rs[:, 0:n], rs[:, 0:n], AF.Sqrt, [Brs], [Brs])
        P.add("dve", lambda e: e.reciprocal(out=rs[:, 0:n], in_=rs[:, 0:n]), [Brs], [Brs])

    def rms_to_aT(hb, Bhb, n, gs, sh, dst, Wdst, rsb, tmp_rr, psb):
        rms_rstd(hb, Bhb, n, rsb, psb)
        rms_mod(hb, Bhb, n, gs, sh, dst, Wdst, rsb, tmp_rr)

    def rms_mod(hb, Bhb, n, gs, sh, dst, Wdst, rsb, tmp_rr):
        rs, Brs = rsb
        for c in range(8):
            tmp, Btmp = tmp_rr.next()
            P.stt("dve", tmp[:, 0:n], hb[:, c, 0:n], gs[:, c:c + 1], rs[:, 0:n], ALU.mult, ALU.mult,
                  [Bhb, Brs, B_modt], [Btmp])
            P.act(dst[:, c, 0:n], tmp[:, 0:n], AF.Identity, [Btmp, B_modt], Wdst, bias=sh[:, c:c + 1], scale=1.0)

    out_ops = []

    for l in range(n_layers):
        last = (l == n_layers - 1)
        if l > 0:
            P.barrier()
        AR.release(res_mark)
        aT_all = AR.alloc([8, TOK], BF16)
        Ba = [Buf("aT%d" % t) for t in range(NS)]
        a_mark = AR.mark()
        hst = [(AR.alloc([8, 512], F32), Buf("hst%d" % i)) for i in range(3)]
        hst_rr = RR(hst)
        rsb2 = [(AR.alloc([512], F32), Buf("rs%d" % i)) for i in range(2)]
        tmp_rr = RR([(AR.alloc([512], F32), Buf("tmp%d" % i)) for i in range(2)])
        segsA = segs_for("A", l)
        pend = None
        for si, (lo, hi, st) in enumerate(segsA):
            n = (hi - lo) * 128
            hb, Bhb = hst_rr.next()
            load_h(hb, l, lo, hi, Bhb, "hst%d" % (hst_rr.i % 3), l == 0)
            bi = si % 2
            rms_rstd(hb, Bhb, n, rsb2[bi], (bank(bi), Bk[bi]))
            if pend is not None:
                rms_mod(*pend)
            pend = (hb, Bhb, n, MT(l, st, 0), MT(l, st, 1), aT_all[:, :, lo * 128:hi * 128], rng(Ba, lo, hi), rsb2[bi], tmp_rr)
        rms_mod(*pend)
        if debug and l == 0:
            P.dma("sp", dbg_a, aT_all, R=Ba, W=[Bdbg], key="dbga")
        stop("A1", l)
        P.barrier()
        AR.release(a_mark)

        wbuf = [(AR.alloc([8192], BF16), Buf("wbuf%d" % i)) for i in range(2)]
        wb_rr = RR(wbuf)
        ropeb = [(AR.alloc([4, 512], F32), Buf("rope%d" % i)) for i in range(2)]
        rope_rr = RR(ropeb)
        t1_rr = RR([(AR.alloc([512], F32), Buf("t1%d" % i)) for i in range(2)])
        t2_rr = RR([(AR.alloc([512], F32), Buf("t2%d" % i)) for i in range(2)])
        qo_rr = RR([(AR.alloc([512], BF16), Buf("qo%d" % i)) for i in range(2)])
        us_rr = RR([(AR.alloc([512], F32), Buf("us%d" % i)) for i in range(3)])
        gs_rr = RR([(AR.alloc([512], BF16), Buf("gs%d" % i)) for i in range(3)])
        vns_rr = RR([(AR.alloc([256], BF16), Buf("vns%d" % i)) for i in range(2)])
        st_rr = RR([(AR.alloc([8], F32), Buf("st%d" % i)) for i in range(2)])
        vsf_rr = RR([(AR.alloc([256], F32), Buf("vsf%d" % i)) for i in range(4)])
        pj_rr = RR([(bank(i), Bk[i]) for i in range(4)])
        pm_rr = RR([(bank(i), Bk[i]) for i in (4, 5)])
        tm_rr = RR([(bank(i), Bk[i]) for i in (4, 5, 6, 7)])

        def load_w(src, shape):
            wt, wb = wb_rr.next()
            nel = int(np.prod(shape))
            view = wt[:, 0:nel]
            if len(shape) == 2:
                view = view.rearrange("p (a b) -> p a b", a=shape[0])
            else:
                view = view.rearrange("p (a b c) -> p a b c", a=shape[0], b=shape[1])
            P.dma("pool", view, src, W=[wb], key="wbuf%d" % (wb_rr.i % 2), max_dma_last_dim=4096)
            return view, wb

        for piece in range(2):
            wv, wb = load_w(wA_tm[l, piece], [8, 512])
            for (lo, hi, st) in segsA:
                for t in range(lo, hi):
                    pt, Bpt = tm_rr.next()
                    for k in range(8):
                        P.mm(pt, aT_all[:, k, t * 128:(t + 1) * 128], wv[:, k, :], k == 0, k == 7, [Ba[t], wb], [Bpt])
                    if piece == 0:
                        P.copy("act", p_all[:, t, :], pt[:, 0:256], [Bpt], [Bp[t]])
                        if DBG_SKIP_LN:
                            continue
                        stt_, Bst = st_rr.next()
                        vf, Bvf = vsf_rr.next()
                        P.act(vf, pt[:, 256:512], AF.Identity, [Bpt], [Bvf, Bst], accum_out=stt_[:, 0:1])
                        P.ts("dve", stt_[:, 1:2], stt_[:, 0:1], -1.0 / 256, None, ALU.mult, ALU.bypass, [Bst], [Bst])
                        jk, Bjk = vsf_rr.next()
                        P.act(jk, vf, AF.Square, [Bvf, Bst], [Bjk, Bst], bias=stt_[:, 1:2], scale=1.0, accum_out=stt_[:, 2:3])
                        P.ts("dve", stt_[:, 3:4], stt_[:, 2:3], 1.0 / 256, EPS, ALU.mult, ALU.add, [Bst], [Bst])
                        P.act(stt_[:, 3:4], stt_[:, 3:4], AF.Sqrt, [Bst], [Bst])
                        P.add("dve", lambda e, o=stt_[:, 3:4]: e.reciprocal(out=o, in_=o), [Bst], [Bst])
                        vs_, Bvs = vns_rr.next()
                        P.ts("dve", vs_, vf, stt_[:, 1:2], stt_[:, 3:4], ALU.add, ALU.mult, [Bvf, Bst], [Bvs])
                        P.dma("sp", vn_scr[t], vs_, R=[Bvs], W=[Bvn[t]], key="vns%d" % (vns_rr.i % 2))
                    else:
                        P.copy("act", v_all[:, t, :], pt, [Bpt], [Bv[t]])

        stop("A2", l)
        def proj_chunk(wv, wb, lo, hi):
            n = (hi - lo) * 128
            pj, Bpj = pj_rr.next()
            for k in range(8):
                P.mm(pj[:, 0:n], wv[:, k, :], aT_all[:, k, lo * 128:hi * 128], k == 0, k == 7,
                     rng(Ba, lo, hi) + [wb], [Bpj])
            return pj, Bpj, n

        wv, wb = load_w(wA_fm[l, 0:2].rearrange("c p k m -> p c k m"), [2, 8, 128])
        for (lo, hi, st) in segsA:
            for c in range(2):
                pj, Bpj, n = proj_chunk(wv[:, c], wb, lo, hi)
                us, Bus = us_rr.next()
                P.copy("act", us[:, 0:n], pj[:, 0:n], [Bpj], [Bus])
                P.dma("sp", u_scr[lo:hi, :, c, :].rearrange("t p n -> p t n"),
                      us[:, 0:n].rearrange("p (t n) -> p t n", n=128), R=[Bus], W=rng(Bu, lo, hi),
                      key="us%d" % (us_rr.i % 3))
        stop("A3", l)
        wv, wb = load_w(wA_fm[l, 2:10].rearrange("c p k m -> p c k m"), [8, 8, 128])
        for (lo, hi, st) in segsA:
            n = (hi - lo) * 128
            if st == 0:
                rt, Brt = rope_rr.next()
                P.dma("sp", rt[:, :, 0:n], rope[:, :, lo * 128:hi * 128].rearrange("a p n -> p a n"), W=[Brt],
                      key="rope%d" % (rope_rr.i % 2))
            def qk_tail(c, qf, Bqf):
                isq = c < 4
                pm, Bpm = pm_rr.next()
                P.mm(pm[:, 0:n], perm_r, qf[:, 0:n], True, True, [Bqf, B_id], [Bpm])
                t1, Bt1 = t1_rr.next()
                t2, Bt2 = t2_rr.next()
                ro = 2 if isq else 0
                P.tt("pool", t1[:, 0:n], qf[:, 0:n].bitcast(F32), rt[:, ro, 0:n], ALU.mult, [Bqf, Brt], [Bt1])
                P.tt("dve", t2[:, 0:n], pm[:, 0:n], rt[:, ro + 1, 0:n], ALU.mult, [Bpm, Brt], [Bt2])
                if isq:
                    qo, Bqo = qo_rr.next()
                    P.tt("dve", qo[:, 0:n], t1[:, 0:n], t2[:, 0:n], ALU.add, [Bt1, Bt2], [Bqo])
                    P.dma("sp", q_scr[lo:hi, :, c, :].rearrange("t p n -> p t n"),
                          qo[:, 0:n].rearrange("p (t n) -> p t n", n=128), R=[Bqo], W=rng(Bq, lo, hi),
                          key="qo%d" % (qo_rr.i % 2))
                else:
                    P.tt("dve", kT_all[:, c - 4, lo * 128:hi * 128], t1[:, 0:n], t2[:, 0:n], ALU.add,
                         [Bt1, Bt2], rng(Bkt, lo, hi))

            pend = None
            for c in range(8):
                pj, Bpj, n = proj_chunk(wv[:, c], wb, lo, hi)
                isq = c < 4
                if st == 1:
                    if isq:
                        qo, Bqo = qo_rr.next()
                        P.act(qo[:, 0:n], pj[:, 0:n], AF.Identity, [Bpj], [Bqo], scale=0.125, bias=0.0)
                        P.dma("sp", q_scr[lo:hi, :, c, :].rearrange("t p n -> p t n"),
                              qo[:, 0:n].rearrange("p (t n) -> p t n", n=128), R=[Bqo], W=rng(Bq, lo, hi),
                              key="qo%d" % (qo_rr.i % 2))
                    else:
                        P.copy("act", kT_all[:, c - 4, lo * 128:hi * 128], pj[:, 0:n], [Bpj], rng(Bkt, lo, hi))
                else:
                    qf, Bqf = qf_rr.next()
                    P.copy("act", qf[:, 0:n], pj[:, 0:n], [Bpj], [Bqf])
                    if pend is not None:
                        qk_tail(*pend)
                    pend = (c, qf, Bqf)
            if pend is not None:
                qk_tail(*pend)
        stop("A4", l)
        for gp in range(6):
            wv, wb = load_w(wA_fm[l, 10 + gp * 4:10 + gp * 4 + 4].rearrange("c p k m -> p c k m"), [4, 8, 128])
            for (lo, hi, st) in segsA:
                for cc in range(4):
                    gch = gp * 4 + cc
                    br, c = gch // 8, gch % 8
                    pj, Bpj, n = proj_chunk(wv[:, cc], wb, lo, hi)
                    gs_, Bgs = gs_rr.next()
                    P.act(gs_[:, 0:n], pj[:, 0:n], AF.Sigmoid, [Bpj], [Bgs])
                    P.dma("sp", g_scr[lo:hi, c, :, br, :].rearrange("t p n -> p t n"),
                          gs_[:, 0:n].rearrange("p (t n) -> p t n", n=128), R=[Bgs], W=rng(Bg, lo, hi),
                          key="gs%d" % (gs_rr.i % 3))
        if debug and l == 0:
            hh_ = P.halted
            P.halted = False
            P.dma("sp", dbg_k, kT_all, R=Bkt, W=[Bdbg], key="dbgk")
            P.dma("sp", dbg_v, v_all, R=Bv, W=[Bdbg], key="dbgv")
            P.dma("sp", dbg_p, p_all, R=Bp, W=[Bdbg], key="dbgp")
            P.halted = hh_
        stop("A", l)

        P.barrier()
        AR.release(res_mark)
        wsT = AR.alloc([4, 128], BF16)
        sgub = AR.alloc([2, 128], F32)
        wblk = AR.alloc([2, 128], BF16)
        bnd = AR.alloc([2 * 12, 128], BF16)
        B_bsp = Buf("band_special")
        wpo = AR.alloc([2, 1024], BF16)
        wso = AR.alloc([2, 1024], BF16)
        wao = AR.alloc([4, 1024], BF16)
        wo = AR.alloc([8, 1024], BF16)
        B_wD = Buf("wD")
        B_wE = Buf("wE")
        P.dma("pool", wsT, sguw[:, l], W=[B_wD], key="wD0")
        P.dma("sp", sgub, sgub_d[:, l], W=[B_wD], key="wD1")
        P.dma("pool", wblk, wblk_d[:, l], W=[B_wD], key="wD2")
        P.dma("pool", bnd[:, 0:12, :], bands[0], W=[B_wD], key="wD3")
        P.dma("pool", wpo, w_po[l].rearrange("(k p) n -> p k n", p=128), W=[B_wE], key="wD4", max_dma_last_dim=4096)
        P.dma("pool", wso, w_so[l].rearrange("(k p) n -> p k n", p=128), W=[B_wE], key="wD5", max_dma_last_dim=4096)
        P.dma("pool", wao, w_ao[l].rearrange("(k p) n -> p k n", p=128), W=[B_wE], key="wD6", max_dma_last_dim=4096)
        P.dma("pool", wo, w_o[l].rearrange("(k p) n -> p k n", p=128), W=[B_wE], key="wD7", max_dma_last_dim=4096)

        bias_rr = RR([(AR.alloc([1024], BF16), Buf("bias%d" % i)) for i in range(3)])
        qt_rr = RR([(AR.alloc([8, 128], BF16), Buf("qt%d" % i)) for i in range(2)])
        for qz_, Bqz_ in qt_rr.items:
            P.add("pool", lambda e, o=qz_: e.memset(o, 0.0), [], [Bqz_])
        ut_rr = RR([(AR.alloc([2, 128], F32), Buf("ut%d" % i)) for i in range(2)])
        vt_rr = RR([(AR.alloc([256], BF16), Buf("vt%d" % i)) for i in range(2)])
        pe_rr = RR([(AR.alloc([1024], BF16), Buf("pexp%d" % i)) for i in range(3)])
        ptb_rr = RR([(AR.alloc([8, 128], BF16), Buf("ptsb%d" % i)) for i in range(3)])
        smx2 = [(AR.alloc([16], F32), Buf("smx%d" % i)) for i in range(2)]
        rinv = AR.alloc([8], F32)
        B_rinv = Buf("rinv")
        ao = AR.alloc([512], BF16)
        B_ao = Buf("ao")
        pooledT = AR.alloc([2, 128], BF16)
        B_pooled = Buf("pooled")
        sgt = AR.alloc([2, 128], F32)
        B_sgt = Buf("sgt")
        oT_rr = RR([(AR.alloc([8, 512], BF16), Buf("oT%d" % i)) for i in range(2)])
        gt_rr = RR([(AR.alloc([4, 3, 128], BF16), Buf("gt%d" % i)) for i in range(2)])
        hc_rr = RR([(AR.alloc([512], F32), Buf("hc%d" % i)) for i in range(5)])
        e_sets = [[(AR.alloc([512], F32), Buf("et%d_%d" % (j_, i))) for i in range(3)] for j_ in range(2)]
        yT = AR.alloc([8, 512], BF16)
        B_yT = Buf("yT")

        def tile_type(t):
            if t >= 24:
                return 5
            j = t - 4
            return {0: 1, 1: 2, 14: 3, 15: 4}.get(j, 0)

        def band_type(t):
            if t == 24:
                return 3
            if t == 25:
                return 4
            j = t - 4
            return {0: 1, 15: 2}.get(j, 0)

        for (lo, hi, st) in segs_for("DE", l):
            n = (hi - lo) * 128
            oT, B_oT = oT_rr.next()
            tiles = list(range(lo, hi))
            TI = {}
            for t in tiles:
                if st == 1:
                    kranges = [(24, 26)]
                else:
                    j = t - 4
                    klo, khi = t - 2, t + 3
                    if j == 0:
                        khi = t + 4
                    if j == 15:
                        klo = t - 3
                    kranges = [(klo, khi), (24, 26)]
                TI[t] = dict(kr=kranges, nk=sum((b - a) for a, b in kranges) * 128,
                             ks=[s_ for a, b in kranges for s_ in range(a, b)], ty=tile_type(t),
                             tc0=(t - lo) * 128, smx=smx2[t % 2])

            def loads(t):
                ti = TI[t]
                ti["qt"] = qt_rr.next()
                qz4 = ti["qt"][0].rearrange("p (c two) n -> p c two n", two=2)
                P.dma("sp", qz4[0:64, :, 0, :], q_scr[t, 0:64], R=[Bq[t]], W=[ti["qt"][1]], key="qta%d" % (qt_rr.i % 2))
                P.dma("sp", qz4[64:128, :, 1, :], q_scr[t, 64:128], R=[Bq[t]], W=[ti["qt"][1]], key="qtb%d" % (qt_rr.i % 2))
                ti["ut"] = ut_rr.next()
                P.dma("sp", ti["ut"][0], u_scr[t], R=[Bu[t]], W=[ti["ut"][1]], key="ut%d" % (ut_rr.i % 2))
                ti["vt"] = vt_rr.next()
                P.dma("sp", ti["vt"][0], vn_scr[t], R=[Bvn[t]], W=[ti["vt"][1]], key="vt%d" % (vt_rr.i % 2))

            def prologue_a(t):
                ti = TI[t]
                bty = band_type(t)
                boff = 0
                if bty != 0:
                    P.dma("pool", bnd[:, 12:24, :], bands[bty], W=[B_bsp], key="bsp")
                    boff = 12
                PP = bank(7).rearrange("p (a b) -> p a b", b=128)
                for g in range(4):
                    c = g // 2
                    srcs = [d_ for d_ in (-1, 0, 1) if not ((t == 24 and d_ == -1) or (t == 25 and d_ == 1))]
                    for ii, d_ in enumerate(srcs):
                        P.mm(PP[:, g, :], p_all[:, t + d_, c * 128:(c + 1) * 128], bnd[:, boff + g * 3 + (d_ + 1), :],
                             ii == 0, ii == len(srcs) - 1, [Bp[t + d_], B_wD, B_bsp], [Bk[7]])
                for g in range(4):
                    gp_ = (g % 2) * 64
                    P.copy("act", pooledT[gp_:gp_ + 64, g // 2, :], PP[gp_:gp_ + 64, g, :], [Bk[7]], [B_pooled])

            def prologue_b(t):
                tc0 = TI[t]["tc0"]
                PY = bank(7)[:, 0:256].rearrange("p (a b) -> p a b", b=128)
                for c in range(2):
                    P.mm(PY[:, c, :], wblk[:, c, :], pooledT[:, c, :], True, True, [B_pooled, B_wD], [Bk[7]])
                for c in range(2):
                    P.act(oT[:, c, tc0:tc0 + 128], PY[:, c, :], AF.Identity, [Bk[7], B_vec], [B_oT],
                          scale=V_ps(l)[:, c:c + 1], bias=0.0)

            def prologue_c(t):
                ti = TI[t]
                tc0 = ti["tc0"]
                ut, But = ti["ut"]
                vt, Bvt = ti["vt"]
                PS_ = bank(7).rearrange("p (a b) -> p a b", b=128)
                for hh in range(4):
                    P.mm(PS_[:, hh, :], vt[:, (hh // 2) * 128:(hh // 2 + 1) * 128], wsT[:, hh, :], True, True,
                         [Bvt, B_wD], [Bk[7]])
                for hh in range(4):
                    hp = (hh % 2) * 64
                    P.tt("dve", sgt[hp:hp + 64, hh // 2, :], PS_[hp:hp + 64, hh, :],
                         sgub[hp:hp + 64, hh // 2, :], ALU.add, [Bk[7], B_wD], [B_sgt])
                P.tt("pool", oT[:, 2:4, tc0:tc0 + 128], sgt, ut, ALU.mult, [B_sgt, But], [B_oT])

            def epilogue_b(t):
                tc0 = TI[t]["tc0"]
                AT = bank(7).bitcast(BF16).rearrange("p (a b) -> p a b", b=128)
                for c in range(4):
                    P.tr(AT[:, c, :], ao[:, c * 128:(c + 1) * 128], identb, [B_ao, B_id], [Bk[7]])
                P.copy("act", oT[:, 4:8, tc0:tc0 + 128], AT[:, 0:4, :], [Bk[7]], [B_oT])

            units = [(t, h) for t in tiles for h in range(8)]
            US = [dict() for _ in units]

            def s1(k):
                t, h = units[k]
                ti = TI[t]
                qt, Bqt = ti["qt"]
                nk = ti["nk"]
                ch, pb = h // 2, (h % 2) * 64
                S = psd[h % 2]
                BS = [Bk[2 * (h % 2)], Bk[2 * (h % 2) + 1]]
                bt, Bbt = bias_rr.next()
                P.dma("sp", bt[:, 0:nk], biasd[l, ti["ty"], h, :, 0:nk], W=[Bbt], key="bias%d" % (bias_rr.i % 3))
                col = 0
                for (a, b) in ti["kr"]:
                    c0 = a * 128
                    rem = (b - a) * 128
                    while rem > 0:
                        w_ = min(rem, 512 - (col % 512))
                        P.mm(S[:, col:col + w_], qt[:, h, :], kT_all[:, ch, c0:c0 + w_],
                             (col % 512) == 0, False, [Bqt] + rng(Bkt, a, b), [BS[col // 512]])
                        col += w_
                        c0 += w_
                        rem -= w_
                for b0_ in range(0, nk, 512):
                    w_ = min(512, nk - b0_)
                    P.mm(S[:, b0_:b0_ + w_], identb, bt[:, b0_:b0_ + w_], False, True, [Bbt, B_id], [BS[b0_ // 512]])
                US[k].update(S=S, BS=BS, bt=bt, Bbt=Bbt)

            def s2(k):
                t, h = units[k]
                ti = TI[t]
                nk = ti["nk"]
                smx, B_smx = ti["smx"]
                u = US[k]
                P.add("dve", lambda e, o=smx[:, h:h + 1], i=u["S"][:, 0:nk]: e.tensor_reduce(
                    out=o, in_=i, axis=AX.X, op=ALU.max, negate=True), u["BS"], [B_smx])
                pex, Bpex = pe_rr.next()
                P.act(pex[:, 0:nk], u["S"][:, 0:nk], AF.Exp, u["BS"] + [B_smx], [Bpex, B_smx], bias=smx[:, h:h + 1],
                      scale=1.0, accum_out=smx[:, 8 + h:9 + h])
                u.update(pex=pex, Bpex=Bpex)

            def s3(k):
                t, h = units[k]
                nkt = TI[t]["nk"] // 128
                u = US[k]
                pb_ = 4 + (k % 2)
                PT = bank(pb_).bitcast(BF16).rearrange("p (a b) -> p a b", b=128)
                for kt in range(nkt):
                    P.tr(PT[:, kt, :], u["pex"][:, kt * 128:(kt + 1) * 128], identb, [u["Bpex"], B_id], [Bk[pb_]])
                ptb, Bptb = ptb_rr.next()
                P.copy("act" if k % 2 == 0 else "dve", ptb[:, 0:nkt, :], PT[:, 0:nkt, :], [Bk[pb_]], [Bptb])
                u.update(ptb=ptb, Bptb=Bptb)

            def s4(k):
                t, h = units[k]
                ti = TI[t]
                nkt = ti["nk"] // 128
                ks = ti["ks"]
                u = US[k]
                for kt in range(nkt):
                    P.mm(bank(6)[:, h * 64:(h + 1) * 64], u["ptb"][:, kt, :], v_all[:, ks[kt], h * 64:(h + 1) * 64],
                         kt == 0, kt == nkt - 1, [u["Bptb"], Bv[ks[kt]]], [Bk[6]])
                if h == 7:
                    smx, B_smx = ti["smx"]
                    tc0 = ti["tc0"]
                    P.add("dve", lambda e, s_=smx: e.reciprocal(out=rinv, in_=s_[:, 8:16]), [B_smx], [B_rinv])
                    P.tt("dve", ao.rearrange("p (h d) -> p h d", d=64), bank(6).rearrange("p (h d) -> p h d", d=64),
                         rinv.unsqueeze(2).broadcast_to([128, 8, 64]), ALU.mult, [Bk[6], B_rinv], [B_ao])

            NU = len(units)
            loads(tiles[0])
            for k in range(NU + 6):
                if k < NU:
                    t, h = units[k]
                    if h == 0:
                        if t + 1 < hi:
                            loads(t + 1)
                        prologue_a(t)
                    elif h == 2:
                        prologue_b(t)
                    elif h == 4:
                        prologue_c(t)
                    s1(k)
                if 0 <= k - 1 < NU:
                    s2(k - 1)
                if 0 <= k - 2 < NU:
                    s3(k - 2)
                if 0 <= k - 3 < NU:
                    s4(k - 3)
                if 0 <= k - 5 < NU and units[k - 5][1] == 7:
                    epilogue_b(units[k - 5][0])
            if debug and l == 0:
                for t in range(lo, hi):
                    P.dma("sp", dbg_o[t], oT[:, :, (t - lo) * 128:(t - lo + 1) * 128], R=[B_oT], W=[Bdbg], key="dbgo")
            for c in range(8):
                gt, Bgt = gt_rr.next()
                P.dma("sp", gt[:, 0:hi - lo], g_scr[lo:hi, c].rearrange("t p b n -> p t b n"), R=rng(Bg, lo, hi), W=[Bgt],
                      key="gt%d" % (gt_rr.i % 2))
                b0 = (c % 2) * 3
                brs = [(wpo, 2, 0), (wso, 2, 2), (wao, 4, 4)]
                for bi_, (wt_, nkk, o0) in enumerate(brs):
                    for k in range(nkk):
                        P.mm(bank(b0 + bi_)[:, 0:n], wt_[:, k, c * 128:(c + 1) * 128], oT[:, o0 + k, 0:n], k == 0, k == nkk - 1,
                             [B_wE, B_oT], [Bk[b0 + bi_]])
                e_t = e_sets[c % 2]
                for bi_ in range(3):
                    et, Bet = e_t[bi_]
                    P.tt("dve", et[:, 0:n].rearrange("p (t n) -> p t n", n=128),
                         bank(b0 + bi_)[:, 0:n].rearrange("p (t n) -> p t n", n=128), gt[:, 0:hi - lo, bi_, :],
                         ALU.mult, [Bk[b0 + bi_], Bgt], [Bet])
                P.tt("pool", e_t[0][0][:, 0:n], e_t[0][0][:, 0:n], e_t[1][0][:, 0:n], ALU.add, [e_t[0][1], e_t[1][1]], [e_t[0][1]])
                P.tt("pool", yT[:, c, 0:n], e_t[0][0][:, 0:n], e_t[2][0][:, 0:n], ALU.add, [e_t[0][1], e_t[2][1]], [B_yT])
            hcs = []

            def ld_hc(c2):
                hc, Bhc = hc_rr.next()
                P.dma("sp", hc[:, 0:n], h_src(l, c2, lo, hi), R=([Bh[t][c2] for t in range(lo, hi)] if l > 0 else []),
                      W=[Bhc], key="hc%d" % (hc_rr.i % 5))
                hcs.append((hc, Bhc, "hc%d" % (hc_rr.i % 5)))

            for c2 in range(3):
                ld_hc(c2)
            for c2 in range(8):
                ob = 6 + (c2 % 2)
                hc, Bhc, hkey = hcs[c2]
                for c in range(8):
                    P.mm(bank(ob)[:, 0:n], wo[:, c, c2 * 128:(c2 + 1) * 128], yT[:, c, 0:n], c == 0, c == 7,
                         [B_wE, B_yT], [Bk[ob]])
                P.stt("dve", hc[:, 0:n], bank(ob)[:, 0:n], MT(l, st, 2)[:, c2:c2 + 1], hc[:, 0:n],
                      ALU.mult, ALU.add, [Bk[ob], Bhc, B_modt], [Bhc])
                if c2 + 3 < 8:
                    ld_hc(c2 + 3)
                P.dma("sp", h_scr[c2, :, lo * 128:hi * 128], hc[:, 0:n], R=[Bhc], W=[Bh[t][c2] for t in range(lo, hi)],
                      key=hkey)
        stop("E", l)

        P.barrier()
        AR.release(m0)
        hF2 = [(AR.alloc([8, 1024], F32), Buf("hF%d" % i)) for i in range(2)]
        aF = AR.alloc([8, 1024], BF16)
        B_aF = Buf("aF")
        hid = AR.alloc([32, 1024], BF16)
        B_hid = Buf("hid")
        rsbF = (AR.alloc([512], F32), Buf("rsF"))
        tmpF_rr = RR([(AR.alloc([512], F32), Buf("tmpF%d" % i)) for i in range(2)])
        w1_rr = RR([(AR.alloc([4, 8, 128], BF16), Buf("w1b%d" % i)) for i in range(2)])
        w2_rr = RR([(AR.alloc([32, 128], BF16), Buf("w2b%d" % i)) for i in range(2)])
        rl_rr = RR([(AR.alloc([512], F32), Buf("rl%d" % i)) for i in range(2)])
        f1_rr = RR([(bank(i), Bk[i]) for i in (1, 2, 3, 4)])
        f2_rr = RR([(bank(i), Bk[i]) for i in (5, 6, 7)])
        groupsF = segs_for("F", l)

        def goffs(grp):
            offs = []
            o_ = 0
            for (lo, hi, st) in grp:
                offs.append(o_)
                o_ += (hi - lo) * 128
            return offs

        def load_group(gi):
            hF, B_hF = hF2[gi % 2]
            grp = groupsF[gi]
            for si_, ((lo, hi, st), o0) in enumerate(zip(grp, goffs(grp))):
                n = (hi - lo) * 128
                P.dma("sp", hF[:, :, o0:o0 + n], h_scr[:, :, lo * 128:hi * 128].rearrange("c p n -> p c n"),
                      R=[b_ for t in range(lo, hi) for b_ in Bh[t]], W=[B_hF], key="hF%d_%d" % (gi % 2, si_))

        load_group(0)
        for gi, grp in enumerate(groupsF):
            hF, B_hF = hF2[gi % 2]
            offs = goffs(grp)
            for (lo, hi, st), o0 in zip(grp, offs):
                n = (hi - lo) * 128
                rms_to_aT(hF[:, :, o0:o0 + n], B_hF, n, MT(l, st, 3), MT(l, st, 4), aF[:, :, o0:o0 + n], [B_aF],
                          rsbF, tmpF_rr, (bank(0), Bk[0]))
            if gi + 1 < len(groupsF):
                load_group(gi + 1)
            for jp in range(8):
                w1t, Bw1 = w1_rr.next()
                P.dma("pool", w1t, w1r[l, jp * 4:jp * 4 + 4].rearrange("j p k m -> p j k m"), W=[Bw1],
                      key="w1b%d" % (w1_rr.i % 2), max_dma_last_dim=4096)
                for jj in range(4):
                    j = jp * 4 + jj
                    for (lo, hi, st), o0 in zip(grp, offs):
                        n = (hi - lo) * 128
                        pf, Bpf = f1_rr.next()
                        for k in range(8):
                            P.mm(pf[:, 0:n], w1t[:, jj, k, :], aF[:, k, o0:o0 + n], k == 0, k == 7, [Bw1, B_aF], [Bpf])
                        rl, Brl = rl_rr.next()
                        P.act(rl[:, 0:n], pf[:, 0:n], AF.Relu, [Bpf], [Brl])
                        P.stt("dve", hid[:, j, o0:o0 + n], pf[:, 0:n], 0.0, rl[:, 0:n], ALU.max, ALU.mult,
                              [Bpf, Brl], [B_hid])
            for c2 in range(8):
                w2t, Bw2 = w2_rr.next()
                P.dma("pool", w2t, w2r[l, c2], W=[Bw2], key="w2b%d" % (w2_rr.i % 2), max_dma_last_dim=4096)
                for (lo, hi, st), o0 in zip(grp, offs):
                    n = (hi - lo) * 128
                    pf, Bpf = f2_rr.next()
                    for j in range(32):
                        P.mm(pf[:, 0:n], w2t[:, j, :], hid[:, j, o0:o0 + n], j == 0, j == 31, [Bw2, B_hid], [Bpf])
                    P.stt("dve", hF[:, c2, o0:o0 + n], pf[:, 0:n], MT(l, st, 5)[:, c2:c2 + 1], hF[:, c2, o0:o0 + n],
                          ALU.mult, ALU.add, [Bpf, B_hF, B_modt], [B_hF])
            for si_, ((lo, hi, st), o0) in enumerate(zip(grp, offs)):
                n = (hi - lo) * 128
                if not last:
                    P.dma("sp", h_scr[:, :, lo * 128:hi * 128].rearrange("c p n -> p c n"), hF[:, :, o0:o0 + n],
                          R=[B_hF], W=[b_ for t in range(lo, hi) for b_ in Bh[t]], key="hFo%d_%d" % (gi % 2, si_))
                else:
                    rs, Brs = rsbF
                    rms_rstd(hF[:, :, o0:o0 + n], B_hF, n, rsbF, (bank(0), Bk[0]))
                    for c in range(8):
                        P.stt("dve", hF[:, c, o0:o0 + n], hF[:, c, o0:o0 + n], V_fg[:, c:c + 1], rs[:, 0:n], ALU.mult, ALU.mult,
                              [B_hF, Brs, B_vec], [B_hF])
                    op = P.dma("sp", outT[:, :, (lo - 4) * 128:(hi - 4) * 128].rearrange("c p n -> p c n"), hF[:, :, o0:o0 + n],
                               R=[B_hF], W=[Bout], key="outst%d_%d" % (gi % 2, si_))
                    out_ops.append(op)
        stop("F", l)

    P.halted = False
    if not out_ops:
        z = AR.t[:, 0:2048]
        out_ops.append(P.dma("sp", outT[0], z, R=[], W=[Bout], key="outst"))
    if debug:
        out_ops.append(P.dma("sp", outT[1, :, 0:8], vec[:, 0:8], R=[Bdbg], W=[Bout], key="dbgfin"))
    P.final_waits = out_ops
    P.emit()
    return nc, P, AR


def _fm(v):
    v = np.asarray(v, np.float32)
    return np.ascontiguousarray(v.reshape(-1, 128).T)


def _rope_perm():
    idx = []
    for h in range(8):
        idx += [h * 64 + 2 * i for i in range(32)] + [h * 64 + 2 * i + 1 for i in range(32)]
    return np.array(idx)


def _shared_inputs(inp):
    w_in = np.asarray(inp["w_in"], np.float32)
    perm = _rope_perm()
    wp = w_in.copy()
    wp[:, :, 768:1280] = w_in[:, :, 768:1280][:, :, perm]
    wp[:, :, 1280:1792] = w_in[:, :, 1280:1792][:, :, perm]
    tm_cols = [np.r_[0:256, 512:768], np.r_[1792:2304]]
    wA_tm = np.stack([np.stack([wp[l][:, cols].reshape(8, 128, 512).transpose(1, 0, 2) for cols in tm_cols])
                      for l in range(2)])
    fm_starts = [256, 384] + [768 + 128 * i for i in range(8)] + [2304 + 128 * i for i in range(24)]
    wA_fm = np.stack([np.stack([wp[l][:, s:s + 128].reshape(8, 128, 128).transpose(1, 0, 2) for s in fm_starts])
                      for l in range(2)])
    w1 = np.asarray(inp["w_ff1"], np.float32)
    w1r = np.stack([np.stack([w1[l][:, j * 128:(j + 1) * 128].reshape(8, 128, 128).transpose(1, 0, 2)
                              for j in range(32)]) for l in range(2)])
    w2 = np.asarray(inp["w_ff2"], np.float32)
    w2r = np.stack([np.stack([w2[l][:, c * 128:(c + 1) * 128].reshape(32, 128, 128).transpose(1, 0, 2)
                              for c in range(8)]) for l in range(2)])
    sgu_w = np.asarray(inp["sgu_w"], np.float32)
    sguw = np.ascontiguousarray(sgu_w.transpose(3, 0, 1, 2))
    sgu_b = np.asarray(inp["sgu_b"], np.float32)
    sgub = np.zeros((128, 2, 2, 128), np.float32)
    for part in range(128):
        for c in range(2):
            sgub[part, :, c, :] = sgu_b[:, 2 * c + part // 64, :]
    w_pool = np.asarray(inp["w_pool"], np.float32)
    wblk = np.zeros((128, 2, 2, 128), np.float32)
    for c in range(2):
        for gl in range(2):
            wblk[gl * 64:(gl + 1) * 64, :, c, gl * 64:(gl + 1) * 64] = w_pool[:, 2 * c + gl].transpose(1, 0, 2)
    consts = np.zeros((128, 3, 128), np.float32)
    consts[:, 0, :] = np.eye(128)
    for pp in range(128):
        partner = pp + 32 if (pp % 64) < 32 else pp - 32
        consts[partner, 1, pp] = 1.0
    consts[:, 2, :] = 1.0
    return dict(w_mod=np.ascontiguousarray(inp["w_mod"], np.float32), wA_tm=np.ascontiguousarray(wA_tm),
                wA_fm=np.ascontiguousarray(wA_fm), sguw=sguw, sgub=sgub, wblk=wblk,
                w_pool_out=np.ascontiguousarray(inp["w_pool_out"], np.float32),
                w_sgu_out=np.ascontiguousarray(inp["w_sgu_out"], np.float32),
                w_attn_out=np.ascontiguousarray(inp["w_attn_out"], np.float32),
                w_o=np.ascontiguousarray(inp["w_o"], np.float32), w1r=np.ascontiguousarray(w1r),
                w2r=np.ascontiguousarray(w2r), consts=consts)


def _band_set(kind):
    out = np.zeros((128, 12, 128), np.float32)
    L = 384
    base = 128
    for g, w in enumerate((2, 4, 8, 16)):
        for tt in range(128):
            pos = base + tt
            lo_b = base if kind == 1 else 0
            hi_b = base + 128 if kind == 2 else L
            lo = min(max(pos - w // 2, lo_b), hi_b)
            hi = min(max(pos + (w - w // 2), lo_b), hi_b)
            cnt = hi - lo
            for s in range(lo, hi):
                d = s // 128
                out[s % 128, g * 3 + d, tt] += 1.0 / cnt
            out[tt, g * 3 + 1, tt] -= 1.0
    return out


def _bias_tables(rpb, core_rows0, n_rows_total=128):
    out = np.full((2, 6, 8, 128, 1024), NEG, np.float32)
    q_i = np.arange(128)
    for ty, j in ((0, 4), (1, 0), (2, 1), (3, 14), (4, 15)):
        klo, khi = j - 2, j + 3
        if j == 0:
            khi = j + 4
        if j == 15:
            klo = j - 3
        nkl = (khi - klo) * 128
        key = np.arange(nkl)
        k_row = core_rows0 + 2 * klo + key // 64
        k_col = key % 64
        q_row = core_rows0 + 2 * j + q_i // 64
        q_col = q_i % 64
        rs = np.clip(q_row - 4, 0, n_rows_total - 8)
        cs = np.clip(q_col - 8, 0, 64 - 16)
        valid = ((k_row[None, :] >= rs[:, None]) & (k_row[None, :] < rs[:, None] + 8) &
                 (k_col[None, :] >= cs[:, None]) & (k_col[None, :] < cs[:, None] + 16) &
                 (k_row[None, :] >= 0) & (k_row[None, :] < n_rows_total))
        dr = np.clip(k_row[None, :] - q_row[:, None] + 7, 0, 14)
        dc = np.clip(k_col[None, :] - q_col[:, None] + 15, 0, 30)
        for l in range(2):
            for h in range(8):
                g = rpb[l, h][dr, dc]
                out[l, ty, h, :, 0:nkl] = np.where(valid, g, NEG)
                out[l, ty, h, :, nkl:nkl + 256] = 0.0
    out[:, 5, :, :, 0:256] = 0.0
    return out.astype(ml_dtypes.bfloat16)


def _core_inputs(inp, core):
    b, blk = core // 4, core % 4
    row0 = 32 * blk
    x = np.asarray(inp["x"], np.float32)[b]
    t0 = (row0 - 8) * 64
    xs = np.zeros((NLAT, D), np.float32)
    lo, hi = max(t0, 0), min(t0 + NLAT, 8192)
    xs[lo - t0:hi - t0] = x[lo:hi]
    xT = np.ascontiguousarray(xs.T.reshape(8, 128, NLAT))
    ctxT = np.ascontiguousarray(np.asarray(inp["ctx"], np.float32)[b].T.reshape(8, 128, 256))
    vecs = np.zeros((128, 156), np.float32)
    vecs[:, 0:8] = _fm(inp["c"][b])
    vecs[:, 8:16] = _fm(inp["c_ctx"])
    for l in range(2):
        vecs[:, 16 + l * 48:16 + (l + 1) * 48] = _fm(inp["b_mod"][l])
        vecs[:, 112 + l * 8:112 + (l + 1) * 8] = _fm(inp["norm1_g"][l])
        vecs[:, 128 + l * 8:128 + (l + 1) * 8] = _fm(inp["norm2_g"][l])
        vecs[:, 152 + l * 2:152 + (l + 1) * 2] = _fm(inp["pool_scale"][l])
    vecs[:, 144:152] = _fm(inp["final_g"])
    tok = np.arange(NLAT)
    row = (row0 - 8 + tok // 64).astype(np.float32)
    col = (tok % 64).astype(np.float32)
    inv_freq = (10000.0 ** (-np.arange(16, dtype=np.float32) / 16)).astype(np.float32)
    ang = np.concatenate([row[:, None] * inv_freq, col[:, None] * inv_freq], axis=-1).astype(np.float32)
    cos, sin = np.cos(ang), np.sin(ang)
    rope = np.zeros((4, 128, NLAT), np.float32)
    for pp in range(128):
        i = pp % 64
        e = i % 32
        rope[0, pp] = cos[:, e]
        rope[1, pp] = -sin[:, e] if i < 32 else sin[:, e]
    rope[2:4] = rope[0:2] * np.float32(0.125)
    first = (blk == 0)
    lastb = (blk == 3)
    gen = _band_set(0)
    bands = np.stack([gen, _band_set(1) if first else gen, _band_set(2) if lastb else gen, _band_set(1), _band_set(2)])
    if first:
        bands[1][:, [0, 3, 6, 9], :] = 0.0
    bias = _bias_tables(np.asarray(inp["na_rpb"], np.float32), row0)
    return dict(xT=xT, ctxT=ctxT, vecs=vecs, rope=rope, bands=np.ascontiguousarray(bands), bias=bias)


_PROG = {}


def kernel(**inputs):
    if "nc" not in _PROG:
        _PROG["nc"] = build_program()[0]
    nc = _PROG["nc"]
    shared = _shared_inputs(inputs)
    in_maps = []
    for core in range(8):
        m = dict(shared)
        m.update(_core_inputs(inputs, core))
        in_maps.append(m)
    res = run_bass_kernel_spmd(nc, in_maps, core_ids=list(range(8)))
    out = np.zeros((2, 8192, D), np.float32)
    for core in range(8):
        b, blk = core // 4, core % 4
        oT = np.asarray(res.results[core]["outT"], np.float32)
        out[b, blk * 2048:(blk + 1) * 2048, :] = oT.reshape(D, 2048).T
    return out
```

```python
import numpy as np
import ml_dtypes
import concourse.bass as bass
import concourse.mybir as mybir
from concourse.bass_utils import run_bass_kernel_spmd

F32 = mybir.dt.float32
F32R = mybir.dt.float32r
BF16 = mybir.dt.bfloat16
AF = mybir.ActivationFunctionType
ALU = mybir.AluOpType
AX = mybir.AxisListType

D = 1024
NLS = 24
NS = 26
NLAT = NLS * 128
TOK = NS * 128
EPS = 1e-6
NEG = -30000.0
DBG_SKIP_LN = False


class Buf:
    __slots__ = ("name", "last_writer", "readers")

    def __init__(self, name):
        self.name = name
        self.last_writer = None
        self.readers = []


class Op:
    __slots__ = ("eng", "fn", "deps", "is_dma", "key", "idx", "signal", "sem", "val", "waits")

    def __init__(self, eng, fn, is_dma, key, idx):
        self.eng = eng
        self.fn = fn
        self.is_dma = is_dma
        self.key = key
        self.idx = idx
        self.deps = []
        self.signal = False
        self.sem = None
        self.val = 0
        self.waits = []


ENGS = ("pe", "act", "dve", "pool", "sp")


class Prog:
    def __init__(self, nc):
        self.nc = nc
        self.ops = []
        self.final_waits = []
        self.phase_buf = Buf("phase")
        self.bar_ap = None
        self.halted = False
        self.dummy = Op("dve", None, False, None, -1)

    def add(self, eng, fn, reads=(), writes=(), dma_key=None):
        if self.halted:
            return self.dummy
        is_dma = dma_key is not None
        op = Op(eng, fn, is_dma, dma_key, len(self.ops))
        deps = {}
        for b in reads:
            w = b.last_writer
            if w is not None:
                deps[w.idx] = [w, True]
        for b in writes:
            w = b.last_writer
            if w is not None and w.idx not in deps:
                deps[w.idx] = [w, False]
            for r in b.readers:
                if r.idx not in deps:
                    deps[r.idx] = [r, False]
        pb = self.phase_buf
        if pb.last_writer is not None:
            deps[pb.last_writer.idx] = [pb.last_writer, True]
        pb.readers.append(op)
        for b in reads:
            b.readers.append(op)
        for b in writes:
            b.last_writer = op
            b.readers = []
        deps.pop(op.idx, None)
        op.deps = list(deps.values())
        self.ops.append(op)
        return op

    def barrier(self):
        if self.halted:
            return self.dummy
        pb = self.phase_buf
        op = Op("dve", lambda e: e.memset(self.bar_ap, 0.0), False, None, len(self.ops))
        deps = {}
        for r in pb.readers:
            deps[r.idx] = [r, True]
        if pb.last_writer is not None:
            deps[pb.last_writer.idx] = [pb.last_writer, True]
        op.deps = list(deps.values())
        pb.last_writer = op
        pb.readers = []
        self.ops.append(op)
        return op

    def dma(self, eng, out, in_, R=(), W=(), key=None, **kw):
        return self.add(eng, lambda e: e.dma_start(out=out, in_=in_, **kw), R, W, dma_key=key)

    def mm(self, out, lhsT, rhs, start, stop, R, W):
        return self.add("pe", lambda e: e.matmul(out, lhsT=lhsT, rhs=rhs, start=start, stop=stop), R, W)

    def tr(self, out, in_, ident, R, W):
        return self.add("pe", lambda e: e.transpose(out=out, in_=in_, identity=ident), R, W)

    def act(self, out, in_, func, R, W, **kw):
        return self.add("act", lambda e: e.activation(out=out, in_=in_, func=func, **kw), R, W)

    def copy(self, eng, out, in_, R, W):
        if eng == "act":
            return self.add("act", lambda e: e.copy(out=out, in_=in_), R, W)
        return self.add(eng, lambda e: e.tensor_copy(out=out, in_=in_), R, W)

    def tt(self, eng, out, in0, in1, op, R, W):
        return self.add(eng, lambda e: e.tensor_tensor(out=out, in0=in0, in1=in1, op=op), R, W)

    def ts(self, eng, out, in0, s1, s2, op0, op1, R, W):
        return self.add(eng, lambda e: e.tensor_scalar(out=out, in0=in0, scalar1=s1, scalar2=s2, op0=op0, op1=op1), R, W)

    def stt(self, eng, out, in0, scalar, in1, op0, op1, R, W):
        return self.add(eng, lambda e: e.scalar_tensor_tensor(out=out, in0=in0, scalar=scalar, in1=in1,
                                                              op0=op0, op1=op1), R, W)

    def emit(self):
        nc = self.nc
        ops = self.ops
        for op in ops:
            need = []
            for d, raw in op.deps:
                if d.is_dma or op.is_dma or d.eng != op.eng or (raw and op.eng != "pe"):
                    need.append(d)
            op.waits = need
            for d in need:
                d.signal = True
        for op in self.final_waits:
            op.signal = True
        for op in ops:
            if op.is_dma:
                op.signal = True
        sems = {}
        counters = {}
        for op in ops:
            if not op.signal:
                continue
            k = ("dma", op.key) if op.is_dma else ("eng", op.eng)
            if k not in sems:
                sems[k] = nc.alloc_semaphore("s%d" % len(sems))
                counters[k] = 0
            op.sem = sems[k]
            counters[k] += 16 if op.is_dma else 1
            op.val = counters[k]
            op.key = k
        self.n_sems = len(sems)
        per_eng = {e: [] for e in ENGS}
        for op in ops:
            per_eng[op.eng].append(op)
        finals = list(self.final_waits)

        def run(eng_name, eng):
            waited = {}
            for op in per_eng[eng_name]:
                req = {}
                for d in op.waits:
                    if req.get(d.key, 0) < d.val:
                        req[d.key] = d.val
                for k, v in req.items():
                    if waited.get(k, 0) >= v:
                        continue
                    waited[k] = v
                    eng.wait_ge(sems[k], v)
                inst = op.fn(eng)
                if op.signal:
                    inst.then_inc(op.sem, 16 if op.is_dma else 1)
            if eng_name == "sp":
                for d in finals:
                    if waited.get(d.key, 0) < d.val:
                        waited[d.key] = d.val
                        eng.wait_ge(sems[d.key], d.val)

        with nc.Block() as block:
            @block.tensor
            def _(e):
                run("pe", e)

            @block.scalar
            def _(e):
                run("act", e)

            @block.vector
            def _(e):
                run("dve", e)

            @block.gpsimd
            def _(e):
                run("pool", e)

            @block.sync
            def _(e):
                run("sp", e)


class Arena:
    def __init__(self, nc, nbytes):
        self.t = nc.alloc_sbuf_tensor("arena", [128, nbytes // 4], F32)
        self.n = nbytes
        self.off = 0
        self.peak = 0

    def alloc(self, shape, dtype):
        esz = 2 if dtype == BF16 else 4
        n = int(np.prod(shape)) * esz
        n4 = (n + 31) // 32 * 32
        assert self.off + n4 <= self.n, ("SBUF arena overflow", self.off, n4, self.n)
        a = self.t[:, self.off // 4:(self.off + n) // 4]
        self.off += n4
        self.peak = max(self.peak, self.off)
        if dtype != F32:
            a = a.bitcast(dtype)
        if len(shape) == 2:
            return a.rearrange("p (a b) -> p a b", a=shape[0])
        if len(shape) == 3:
            return a.rearrange("p (a b c) -> p a b c", a=shape[0], b=shape[1])
        return a

    def mark(self):
        return self.off

    def release(self, m):
        self.off = m


class RR:
    def __init__(self, items):
        self.items = items
        self.i = 0

    def next(self):
        it = self.items[self.i % len(self.items)]
        self.i += 1
        return it


def segs_for(kind, layer):
    if kind == "A":
        if layer == 0:
            s = [(4 * b, 4 * b + 4, 0) for b in range(6)]
        else:
            s = [(2, 4, 0)] + [(4 * b, 4 * b + 4, 0) for b in range(1, 5)] + [(20, 22, 0)]
        return s + [(24, 26, 1)]
    if kind == "DE":
        s = [(4 * b, 4 * b + 4, 0) for b in range(1, 5)]
        if layer == 0:
            s = [(2, 4, 0)] + s + [(20, 22, 0), (24, 26, 1)]
        return s
    if kind == "F":
        g = [[(4, 8, 0), (8, 12, 0)], [(12, 16, 0), (16, 20, 0)]]
        if layer == 0:
            g.append([(2, 4, 0), (20, 22, 0), (24, 26, 1)])
        return g
    raise ValueError(kind)


def build_program(n_layers=2, stop_after=None, debug=False):
    nc = bass.Bass("TRN2", target_bir_lowering=False)
    P = Prog(nc)

    def din(name, shape, dt=F32):
        return nc.dram_tensor(name, list(shape), dt, kind="ExternalInput").ap()

    xT = din("xT", [8, 128, NLAT])
    ctxT = din("ctxT", [8, 128, 256])
    vecs = din("vecs", [128, 156])
    consts = din("consts", [128, 3, 128])
    rope = din("rope", [4, 128, NLAT])
    bands = din("bands", [5, 128, 12, 128])
    biasd = din("bias", [2, 6, 8, 128, 1024], BF16)
    w_mod = din("w_mod", [2, 1024, 6144])
    wA_tm = din("wA_tm", [2, 2, 128, 8, 512])
    wA_fm = din("wA_fm", [2, 34, 128, 8, 128])
    sguw = din("sguw", [128, 2, 4, 128])
    sgub_d = din("sgub", [128, 2, 2, 128])
    wblk_d = din("wblk", [128, 2, 2, 128])
    w_po = din("w_pool_out", [2, 256, 1024])
    w_so = din("w_sgu_out", [2, 256, 1024])
    w_ao = din("w_attn_out", [2, 512, 1024])
    w_o = din("w_o", [2, 1024, 1024])
    w1r = din("w1r", [2, 32, 128, 8, 128])
    w2r = din("w2r", [2, 8, 128, 32, 128])
    outT = nc.dram_tensor("outT", [8, 128, 2048], F32, kind="ExternalOutput").ap()

    skind = "ExternalOutput" if debug else "Internal"

    def dscr(name, shape, dt=F32):
        return nc.dram_tensor(name, list(shape), dt, kind=skind).ap()

    h_scr = dscr("h_scr", [8, 128, TOK])
    q_scr = dscr("q_scr", [NS, 128, 4, 128], BF16)
    u_scr = dscr("u_scr", [NS, 128, 2, 128])
    vn_scr = dscr("vn_scr", [NS, 128, 256], BF16)
    g_scr = dscr("g_scr", [NS, 8, 128, 3, 128], BF16)
    if debug:
        dbg_k = dscr("dbg_k", [128, 4, TOK], BF16)
        dbg_v = dscr("dbg_v", [128, NS, 512], BF16)
        dbg_p = dscr("dbg_p", [128, NS, 256], BF16)
        dbg_a = dscr("dbg_a", [128, 8, TOK], BF16)
        dbg_mod = dscr("dbg_mod", [128, 2, 2, 48])
        dbg_o = dscr("dbg_o", [NS, 128, 8, 128], BF16)

    Bh = [[Buf("h%d_%d" % (t, c)) for c in range(8)] for t in range(NS)]
    Bq = [Buf("q%d" % t) for t in range(NS)]
    Bu = [Buf("u%d" % t) for t in range(NS)]
    Bvn = [Buf("vn%d" % t) for t in range(NS)]
    Bg = [Buf("g%d" % t) for t in range(NS)]
    Bout = Buf("out")
    Bdbg = Buf("dbg")

    psd = [nc.alloc_psum_tensor("psd%d" % i, [128, 1024], F32) for i in range(4)]
    Bk = [Buf("bank%d" % i) for i in range(8)]

    def bank(i):
        return psd[i // 2][:, (i % 2) * 512:(i % 2) * 512 + 512]

    AR = Arena(nc, 197 * 1024)
    bar_t = AR.alloc([8], F32)
    P.bar_ap = bar_t
    cst = AR.alloc([3, 128], F32)
    identb = AR.alloc([128], BF16)
    perm_r = nc.alloc_sbuf_tensor("perm_r", [128, 128], F32R)[:]
    ones_r = nc.alloc_sbuf_tensor("ones_r", [128, 128], F32R)[:]
    sq_rr = RR([(nc.alloc_sbuf_tensor("sq_r%d" % i, [128, 512], F32R)[:], Buf("sq_r%d" % i)) for i in range(2)])
    qf_rr = RR([(nc.alloc_sbuf_tensor("qf_r%d" % i, [128, 512], F32R)[:], Buf("qf_r%d" % i)) for i in range(2)])
    vec = AR.alloc([156], F32)
    modt = AR.alloc([24, 8], F32)
    silu_b = AR.alloc([2, 8], BF16)
    B_c = Buf("consts")
    B_vec = Buf("vec")
    B_modt = Buf("modt")
    B_silu = Buf("silu")

    def V_c(s):
        return vec[:, s * 8:(s + 1) * 8]

    def V_bmod(l):
        return vec[:, 16 + l * 48:16 + (l + 1) * 48]

    def V_n1(l):
        return vec[:, 112 + l * 8:112 + (l + 1) * 8]

    def V_n2(l):
        return vec[:, 128 + l * 8:128 + (l + 1) * 8]

    V_fg = vec[:, 144:152]

    def V_ps(l):
        return vec[:, 152 + l * 2:152 + (l + 1) * 2]

    def MT(l, s, kind):
        i = (l * 2 + s) * 6 + kind
        return modt[:, i, :]

    P.dma("sp", cst, consts, W=[B_c], key="cst")
    P.dma("sp", vec, vecs, W=[B_vec], key="vec")
    B_id = Buf("ident")
    P.copy("dve", identb, cst[:, 0, :], [B_c], [B_id])
    P.copy("dve", perm_r, cst[:, 1, :], [B_c], [B_id])
    P.copy("dve", ones_r, cst[:, 2, :], [B_c], [B_id])
    P.act(silu_b.rearrange("p s k -> p (s k)"), vec[:, 0:16], AF.Silu, [B_vec], [B_silu])

    def stop(name, l=0):
        if stop_after is not None and tuple(stop_after) == (name, l):
            P.halted = True

    stop("S")

    m0 = AR.mark()
    wm = [(AR.alloc([8, 512], BF16), Buf("wm%d" % i)) for i in range(2)]
    wm_rr = RR(wm)
    modraw = AR.alloc([48, 2], F32)
    tmpm = AR.alloc([8, 2], F32)
    B_modraw = Buf("modraw")
    for l in range(n_layers):
        psm = bank(0).rearrange("p (a b) -> p a b", b=2)[:, 0:48, :]
        for pc in range(12):
            wt, wb = wm_rr.next()
            P.dma("pool", wt, w_mod[l, :, pc * 512:(pc + 1) * 512].rearrange("(k p) n -> p k n", p=128),
                  W=[wb], key="wm%d" % (wm_rr.i % 2), max_dma_last_dim=4096)
            for cc in range(4):
                ch = pc * 4 + cc
                for k in range(8):
                    P.mm(psm[:, ch, :], wt[:, k, cc * 128:(cc + 1) * 128], silu_b[:, :, k], k == 0, k == 7,
                         [wb, B_silu], [Bk[0]])
        P.tt("dve", modraw, psm, V_bmod(l).unsqueeze(2).broadcast_to([128, 48, 2]), ALU.add,
             [Bk[0], B_vec], [B_modraw])
        for s in range(2):
            def chunk(i):
                return modraw[:, i * 8:(i + 1) * 8, s]
            P.stt("dve", MT(l, s, 0), chunk(1), 1.0, V_n1(l), ALU.add, ALU.mult, [B_modraw, B_vec], [B_modt])
            P.copy("dve", MT(l, s, 1), chunk(0), [B_modraw], [B_modt])
            P.copy("dve", MT(l, s, 2), chunk(2), [B_modraw], [B_modt])
            P.stt("dve", MT(l, s, 3), chunk(4), 1.0, V_n2(l), ALU.add, ALU.mult, [B_modraw, B_vec], [B_modt])
            P.copy("dve", MT(l, s, 4), chunk(3), [B_modraw], [B_modt])
            P.copy("dve", MT(l, s, 5), chunk(5), [B_modraw], [B_modt])
    if debug:
        P.dma("sp", dbg_mod.rearrange("p l s c -> p (l s c)"), modt.rearrange("p a b -> p (a b)")[:, 0:192],
              R=[B_modt], W=[Bdbg], key="dbgm")
    AR.release(m0)
    stop("M")
    P.barrier()

    kT_all = AR.alloc([4, TOK], BF16)
    v_all = AR.alloc([NS, 512], BF16)
    p_all = AR.alloc([NS, 256], BF16)
    Bkt = [Buf("kT%d" % t) for t in range(NS)]
    Bv = [Buf("v%d" % t) for t in range(NS)]
    Bp = [Buf("p%d" % t) for t in range(NS)]
    res_mark = AR.mark()

    def rng(bufs, lo, hi):
        return [bufs[t] for t in range(lo, hi)]

    def h_src(l, c, lo, hi):
        if l == 0:
            if lo >= 24:
                return ctxT[c, :, (lo - 24) * 128:(hi - 24) * 128]
            return xT[c, :, lo * 128:hi * 128]
        return h_scr[c, :, lo * 128:hi * 128]

    def load_h(dst, l, lo, hi, Bdst, key, first_layer_input):
        n = (hi - lo) * 128
        if first_layer_input:
            src = (ctxT[:, :, (lo - 24) * 128:(hi - 24) * 128] if lo >= 24 else xT[:, :, lo * 128:hi * 128])
            R = []
        else:
            src = h_scr[:, :, lo * 128:hi * 128]
            R = [b_ for t in range(lo, hi) for b_ in Bh[t]]
        P.dma("sp", dst[:, :, 0:n], src.rearrange("c p n -> p c n"), R=R, W=[Bdst], key=key)

    def rms_rstd(hb, Bhb, n, rsb, psb):
        rs, Brs = rsb
        pb, Bpb = psb
        for c in range(8):
            sq, Bsq = sq_rr.next()
            P.act(sq[:, 0:n], hb[:, c, 0:n], AF.Square, [Bhb], [Bsq])
            P.mm(pb[:, 0:n], ones_r, sq[:, 0:n], c == 0, c == 7, [Bsq, B_id], [Bpb])
        P.ts("dve", rs[:, 0:n], pb[:, 0:n], 1.0 / D, EPS, ALU.mult, ALU.add, [Bpb], [Brs])
        P.act(rs[:, 0:n], rs[:, 0:n], AF.Sqrt, [Brs], [Brs])
        P.add("dve", lambda e: e.reciprocal(out=rs[:, 0:n], in_=rs[:, 0:n]), [Brs], [Brs])

    def rms_to_aT(hb, Bhb, n, gs, sh, dst, Wdst, rsb, tmp_rr, psb):
        rms_rstd(hb, Bhb, n, rsb, psb)
        rms_mod(hb, Bhb, n, gs, sh, dst, Wdst, rsb, tmp_rr)

    def rms_mod(hb, Bhb, n, gs, sh, dst, Wdst, rsb, tmp_rr):
        rs, Brs = rsb
        for c in range(8):
            tmp, Btmp = tmp_rr.next()
            P.stt("dve", tmp[:, 0:n], hb[:, c, 0:n], gs[:, c:c + 1], rs[:, 0:n], ALU.mult, ALU.mult,
                  [Bhb, Brs, B_modt], [Btmp])
            P.act(dst[:, c, 0:n], tmp[:, 0:n], AF.Identity, [Btmp, B_modt], Wdst, bias=sh[:, c:c + 1], scale=1.0)

    out_ops = []

    for l in range(n_layers):
        last = (l == n_layers - 1)
        if l > 0:
            P.barrier()
        AR.release(res_mark)
        aT_all = AR.alloc([8, TOK], BF16)
        Ba = [Buf("aT%d" % t) for t in range(NS)]
        a_mark = AR.mark()
        hst = [(AR.alloc([8, 512], F32), Buf("hst%d" % i)) for i in range(3)]
        hst_rr = RR(hst)
        rsb2 = [(AR.alloc([512], F32), Buf("rs%d" % i)) for i in range(2)]
        tmp_rr = RR([(AR.alloc([512], F32), Buf("tmp%d" % i)) for i in range(2)])
        segsA = segs_for("A", l)
        pend = None
        for si, (lo, hi, st) in enumerate(segsA):
            n = (hi - lo) * 128
            hb, Bhb = hst_rr.next()
            load_h(hb, l, lo, hi, Bhb, "hst%d" % (hst_rr.i % 3), l == 0)
            bi = si % 2
            rms_rstd(hb, Bhb, n, rsb2[bi], (bank(bi), Bk[bi]))
            if pend is not None:
                rms_mod(*pend)
            pend = (hb, Bhb, n, MT(l, st, 0), MT(l, st, 1), aT_all[:, :, lo * 128:hi * 128], rng(Ba, lo, hi), rsb2[bi], tmp_rr)
        rms_mod(*pend)
        if debug and l == 0:
            P.dma("sp", dbg_a, aT_all, R=Ba, W=[Bdbg], key="dbga")
        stop("A1", l)
        P.barrier()
        AR.release(a_mark)

        wbuf = [(AR.alloc([8192], BF16), Buf("wbuf%d" % i)) for i in range(2)]
        wb_rr = RR(wbuf)
        ropeb = [(AR.alloc([4, 512], F32), Buf("rope%d" % i)) for i in range(2)]
        rope_rr = RR(ropeb)
        t1_rr = RR([(AR.alloc([512], F32), Buf("t1%d" % i)) for i in range(2)])
        t2_rr = RR([(AR.alloc([512], F32), Buf("t2%d" % i)) for i in range(2)])
        qo_rr = RR([(AR.alloc([512], BF16), Buf("qo%d" % i)) for i in range(2)])
        us_rr = RR([(AR.alloc([512], F32), Buf("us%d" % i)) for i in range(3)])
        gs_rr = RR([(AR.alloc([512], BF16), Buf("gs%d" % i)) for i in range(3)])
        vns_rr = RR([(AR.alloc([256], BF16), Buf("vns%d" % i)) for i in range(2)])
        st_rr = RR([(AR.alloc([8], F32), Buf("st%d" % i)) for i in range(2)])
        vsf_rr = RR([(AR.alloc([256], F32), Buf("vsf%d" % i)) for i in range(4)])
        pj_rr = RR([(bank(i), Bk[i]) for i in range(4)])
        pm_rr = RR([(bank(i), Bk[i]) for i in (4, 5)])
        tm_rr = RR([(bank(i), Bk[i]) for i in (4, 5, 6, 7)])

        def load_w(src, shape):
            wt, wb = wb_rr.next()
            nel = int(np.prod(shape))
            view = wt[:, 0:nel]
            if len(shape) == 2:
                view = view.rearrange("p (a b) -> p a b", a=shape[0])
            else:
                view = view.rearrange("p (a b c) -> p a b c", a=shape[0], b=shape[1])
            P.dma("pool", view, src, W=[wb], key="wbuf%d" % (wb_rr.i % 2), max_dma_last_dim=4096)
            return view, wb

        for piece in range(2):
            wv, wb = load_w(wA_tm[l, piece], [8, 512])
            for (lo, hi, st) in segsA:
                for t in range(lo, hi):
                    pt, Bpt = tm_rr.next()
                    for k in range(8):
                        P.mm(pt, aT_all[:, k, t * 128:(t + 1) * 128], wv[:, k, :], k == 0, k == 7, [Ba[t], wb], [Bpt])
                    if piece == 0:
                        P.copy("act", p_all[:, t, :], pt[:, 0:256], [Bpt], [Bp[t]])
                        if DBG_SKIP_LN:
                            continue
                        stt_, Bst = st_rr.next()
                        vf, Bvf = vsf_rr.next()
                        P.act(vf, pt[:, 256:512], AF.Identity, [Bpt], [Bvf, Bst], accum_out=stt_[:, 0:1])
                        P.ts("dve", stt_[:, 1:2], stt_[:, 0:1], -1.0 / 256, None, ALU.mult, ALU.bypass, [Bst], [Bst])
                        jk, Bjk = vsf_rr.next()
                        P.act(jk, vf, AF.Square, [Bvf, Bst], [Bjk, Bst], bias=stt_[:, 1:2], scale=1.0, accum_out=stt_[:, 2:3])
                        P.ts("dve", stt_[:, 3:4], stt_[:, 2:3], 1.0 / 256, EPS, ALU.mult, ALU.add, [Bst], [Bst])
                        P.act(stt_[:, 3:4], stt_[:, 3:4], AF.Sqrt, [Bst], [Bst])
                        P.add("dve", lambda e, o=stt_[:, 3:4]: e.reciprocal(out=o, in_=o), [Bst], [Bst])
                        vs_, Bvs = vns_rr.next()
                        P.ts("dve", vs_, vf, stt_[:, 1:2], stt_[:, 3:4], ALU.add, ALU.mult, [Bvf, Bst], [Bvs])
                        P.dma("sp", vn_scr[t], vs_, R=[Bvs], W=[Bvn[t]], key="vns%d" % (vns_rr.i % 2))
                    else:
                        P.copy("act", v_all[:, t, :], pt, [Bpt], [Bv[t]])

        stop("A2", l)
        def proj_chunk(wv, wb, lo, hi):
            n = (hi - lo) * 128
            pj, Bpj = pj_rr.next()
            for k in range(8):
                P.mm(pj[:, 0:n], wv[:, k, :], aT_all[:, k, lo * 128:hi * 128], k == 0, k == 7,
                     rng(Ba, lo, hi) + [wb], [Bpj])
            return pj, Bpj, n

        wv, wb = load_w(wA_fm[l, 0:2].rearrange("c p k m -> p c k m"), [2, 8, 128])
        for (lo, hi, st) in segsA:
            for c in range(2):
                pj, Bpj, n = proj_chunk(wv[:, c], wb, lo, hi)
                us, Bus = us_rr.next()
                P.copy("act", us[:, 0:n], pj[:, 0:n], [Bpj], [Bus])
                P.dma("sp", u_scr[lo:hi, :, c, :].rearrange("t p n -> p t n"),
                      us[:, 0:n].rearrange("p (t n) -> p t n", n=128), R=[Bus], W=rng(Bu, lo, hi),
                      key="us%d" % (us_rr.i % 3))
        stop("A3", l)
        wv, wb = load_w(wA_fm[l, 2:10].rearrange("c p k m -> p c k m"), [8, 8, 128])
        for (lo, hi, st) in segsA:
            n = (hi - lo) * 128
            if st == 0:
                rt, Brt = rope_rr.next()
                P.dma("sp", rt[:, :, 0:n], rope[:, :, lo * 128:hi * 128].rearrange("a p n -> p a n"), W=[Brt],
                      key="rope%d" % (rope_rr.i % 2))
            def qk_tail(c, qf, Bqf):
                isq = c < 4
                pm, Bpm = pm_rr.next()
                P.mm(pm[:, 0:n], perm_r, qf[:, 0:n], True, True, [Bqf, B_id], [Bpm])
                t1, Bt1 = t1_rr.next()
                t2, Bt2 = t2_rr.next()
                ro = 2 if isq else 0
                P.tt("pool", t1[:, 0:n], qf[:, 0:n].bitcast(F32), rt[:, ro, 0:n], ALU.mult, [Bqf, Brt], [Bt1])
                P.tt("dve", t2[:, 0:n], pm[:, 0:n], rt[:, ro + 1, 0:n], ALU.mult, [Bpm, Brt], [Bt2])
                if isq:
                    qo, Bqo = qo_rr.next()
                    P.tt("dve", qo[:, 0:n], t1[:, 0:n], t2[:, 0:n], ALU.add, [Bt1, Bt2], [Bqo])
                    P.dma("sp", q_scr[lo:hi, :, c, :].rearrange("t p n -> p t n"),
                          qo[:, 0:n].rearrange("p (t n) -> p t n", n=128), R=[Bqo], W=rng(Bq, lo, hi),
                          key="qo%d" % (qo_rr.i % 2))
                else:
                    P.tt("dve", kT_all[:, c - 4, lo * 128:hi * 128], t1[:, 0:n], t2[:, 0:n], ALU.add,
                         [Bt1, Bt2], rng(Bkt, lo, hi))

            pend = None
            for c in range(8):
                pj, Bpj, n = proj_chunk(wv[:, c], wb, lo, hi)
                isq = c < 4
                if st == 1:
                    if isq:
                        qo, Bqo = qo_rr.next()
                        P.act(qo[:, 0:n], pj[:, 0:n], AF.Identity, [Bpj], [Bqo], scale=0.125, bias=0.0)
                        P.dma("sp", q_scr[lo:hi, :, c, :].rearrange("t p n -> p t n"),
                              qo[:, 0:n].rearrange("p (t n) -> p t n", n=128), R=[Bqo], W=rng(Bq, lo, hi),
                              key="qo%d" % (qo_rr.i % 2))
                    else:
                        P.copy("act", kT_all[:, c - 4, lo * 128:hi * 128], pj[:, 0:n], [Bpj], rng(Bkt, lo, hi))
                else:
                    qf, Bqf = qf_rr.next()
                    P.copy("act", qf[:, 0:n], pj[:, 0:n], [Bpj], [Bqf])
                    if pend is not None:
                        qk_tail(*pend)
                    pend = (c, qf, Bqf)
            if pend is not None:
                qk_tail(*pend)
        stop("A4", l)
        for gp in range(6):
            wv, wb = load_w(wA_fm[l, 10 + gp * 4:10 + gp * 4 + 4].rearrange("c p k m -> p c k m"), [4, 8, 128])
            for (lo, hi, st) in segsA:
                for cc in range(4):
                    gch = gp * 4 + cc
                    br, c = gch // 8, gch % 8
                    pj, Bpj, n = proj_chunk(wv[:, cc], wb, lo, hi)
                    gs_, Bgs = gs_rr.next()
                    P.act(gs_[:, 0:n], pj[:, 0:n], AF.Sigmoid, [Bpj], [Bgs])
                    P.dma("sp", g_scr[lo:hi, c, :, br, :].rearrange("t p n -> p t n"),
                          gs_[:, 0:n].rearrange("p (t n) -> p t n", n=128), R=[Bgs], W=rng(Bg, lo, hi),
                          key="gs%d" % (gs_rr.i % 3))
        if debug and l == 0:
            hh_ = P.halted
            P.halted = False
            P.dma("sp", dbg_k, kT_all, R=Bkt, W=[Bdbg], key="dbgk")
            P.dma("sp", dbg_v, v_all, R=Bv, W=[Bdbg], key="dbgv")
            P.dma("sp", dbg_p, p_all, R=Bp, W=[Bdbg], key="dbgp")
            P.halted = hh_
        stop("A", l)

        P.barrier()
        AR.release(res_mark)
        wsT = AR.alloc([4, 128], BF16)
        sgub = AR.alloc([2, 128], F32)
        wblk = AR.alloc([2, 128], BF16)
        bnd = AR.alloc([2 * 12, 128], BF16)
        B_bsp = Buf("band_special")
        wpo = AR.alloc([2, 1024], BF16)
        wso = AR.alloc([2, 1024], BF16)
        wao = AR.alloc([4, 1024], BF16)
        wo = AR.alloc([8, 1024], BF16)
        B_wD = Buf("wD")
        B_wE = Buf("wE")
        P.dma("pool", wsT, sguw[:, l], W=[B_wD], key="wD0")
        P.dma("sp", sgub, sgub_d[:, l], W=[B_wD], key="wD1")
        P.dma("pool", wblk, wblk_d[:, l], W=[B_wD], key="wD2")
        P.dma("pool", bnd[:, 0:12, :], bands[0], W=[B_wD], key="wD3")
        P.dma("pool", wpo, w_po[l].rearrange("(k p) n -> p k n", p=128), W=[B_wE], key="wD4", max_dma_last_dim=4096)
        P.dma("pool", wso, w_so[l].rearrange("(k p) n -> p k n", p=128), W=[B_wE], key="wD5", max_dma_last_dim=4096)
        P.dma("pool", wao, w_ao[l].rearrange("(k p) n -> p k n", p=128), W=[B_wE], key="wD6", max_dma_last_dim=4096)
        P.dma("pool", wo, w_o[l].rearrange("(k p) n -> p k n", p=128), W=[B_wE], key="wD7", max_dma_last_dim=4096)

        bias_rr = RR([(AR.alloc([1024], BF16), Buf("bias%d" % i)) for i in range(3)])
        qt_rr = RR([(AR.alloc([8, 128], BF16), Buf("qt%d" % i)) for i in range(2)])
        for qz_, Bqz_ in qt_rr.items:
            P.add("pool", lambda e, o=qz_: e.memset(o, 0.0), [], [Bqz_])
        ut_rr = RR([(AR.alloc([2, 128], F32), Buf("ut%d" % i)) for i in range(2)])
        vt_rr = RR([(AR.alloc([256], BF16), Buf("vt%d" % i)) for i in range(2)])
        pe_rr = RR([(AR.alloc([1024], BF16), Buf("pexp%d" % i)) for i in range(4)])
        ptb_rr = RR([(AR.alloc([8, 128], BF16), Buf("ptsb%d" % i)) for i in range(4)])
        smx2 = [(AR.alloc([16], F32), Buf("smx%d" % i)) for i in range(2)]
        rinv = AR.alloc([8], F32)
        B_rinv = Buf("rinv")
        ao = AR.alloc([512], BF16)
        B_ao = Buf("ao")
        pooledT = AR.alloc([2, 128], BF16)
        B_pooled = Buf("pooled")
        sgt = AR.alloc([2, 128], F32)
        B_sgt = Buf("sgt")
        oT_rr = RR([(AR.alloc([8, 512], BF16), Buf("oT%d" % i)) for i in range(2)])
        gt_rr = RR([(AR.alloc([4, 3, 128], BF16), Buf("gt%d" % i)) for i in range(2)])
        hc_rr = RR([(AR.alloc([512], F32), Buf("hc%d" % i)) for i in range(5)])
        e_sets = [[(AR.alloc([512], F32), Buf("et%d_%d" % (j_, i))) for i in range(3)] for j_ in range(2)]
        yT = AR.alloc([8, 512], BF16)
        B_yT = Buf("yT")

        def tile_type(t):
            if t >= 24:
                return 5
            j = t - 4
            return {0: 1, 1: 2, 14: 3, 15: 4}.get(j, 0)

        def band_type(t):
            if t == 24:
                return 3
            if t == 25:
                return 4
            j = t - 4
            return {0: 1, 15: 2}.get(j, 0)

        for (lo, hi, st) in segs_for("DE", l):
            n = (hi - lo) * 128
            oT, B_oT = oT_rr.next()
            tiles = list(range(lo, hi))
            TI = {}
            for t in tiles:
                if st == 1:
                    kranges = [(24, 26)]
                else:
                    j = t - 4
                    klo, khi = t - 2, t + 3
                    if j == 0:
                        khi = t + 4
                    if j == 15:
                        klo = t - 3
                    kranges = [(klo, khi), (24, 26)]
                TI[t] = dict(kr=kranges, nk=sum((b - a) for a, b in kranges) * 128,
                             ks=[s_ for a, b in kranges for s_ in range(a, b)], ty=tile_type(t),
                             tc0=(t - lo) * 128, smx=smx2[t % 2])

            def loads(t):
                ti = TI[t]
                ti["qt"] = qt_rr.next()
                qz4 = ti["qt"][0].rearrange("p (c two) n -> p c two n", two=2)
                P.dma("sp", qz4[0:64, :, 0, :], q_scr[t, 0:64], R=[Bq[t]], W=[ti["qt"][1]], key="qta%d" % (qt_rr.i % 2))
                P.dma("sp", qz4[64:128, :, 1, :], q_scr[t, 64:128], R=[Bq[t]], W=[ti["qt"][1]], key="qtb%d" % (qt_rr.i % 2))
                ti["ut"] = ut_rr.next()
                P.dma("sp", ti["ut"][0], u_scr[t], R=[Bu[t]], W=[ti["ut"][1]], key="ut%d" % (ut_rr.i % 2))
                ti["vt"] = vt_rr.next()
                P.dma("sp", ti["vt"][0], vn_scr[t], R=[Bvn[t]], W=[ti["vt"][1]], key="vt%d" % (vt_rr.i % 2))

            def prologue_a(t):
                ti = TI[t]
                bty = band_type(t)
                boff = 0
                if bty != 0:
                    P.dma("pool", bnd[:, 12:24, :], bands[bty], W=[B_bsp], key="bsp")
                    boff = 12
                PP = bank(7).rearrange("p (a b) -> p a b", b=128)
                for g in range(4):
                    c = g // 2
                    srcs = [d_ for d_ in (-1, 0, 1) if not ((t == 24 and d_ == -1) or (t == 25 and d_ == 1))]
                    for ii, d_ in enumerate(srcs):
                        P.mm(PP[:, g, :], p_all[:, t + d_, c * 128:(c + 1) * 128], bnd[:, boff + g * 3 + (d_ + 1), :],
                             ii == 0, ii == len(srcs) - 1, [Bp[t + d_], B_wD, B_bsp], [Bk[7]])
                for g in range(4):
                    gp_ = (g % 2) * 64
                    P.copy("act", pooledT[gp_:gp_ + 64, g // 2, :], PP[gp_:gp_ + 64, g, :], [Bk[7]], [B_pooled])

            def prologue_b(t):
                tc0 = TI[t]["tc0"]
                PY = bank(7)[:, 0:256].rearrange("p (a b) -> p a b", b=128)
                for c in range(2):
                    P.mm(PY[:, c, :], wblk[:, c, :], pooledT[:, c, :], True, True, [B_pooled, B_wD], [Bk[7]])
                for c in range(2):
                    P.act(oT[:, c, tc0:tc0 + 128], PY[:, c, :], AF.Identity, [Bk[7], B_vec], [B_oT],
                          scale=V_ps(l)[:, c:c + 1], bias=0.0)

            def prologue_c(t):
                ti = TI[t]
                tc0 = ti["tc0"]
                ut, But = ti["ut"]
                vt, Bvt = ti["vt"]
                PS_ = bank(7).rearrange("p (a b) -> p a b", b=128)
                for hh in range(4):
                    P.mm(PS_[:, hh, :], vt[:, (hh // 2) * 128:(hh // 2 + 1) * 128], wsT[:, hh, :], True, True,
                         [Bvt, B_wD], [Bk[7]])
                for hh in range(4):
                    hp = (hh % 2) * 64
                    P.tt("dve", sgt[hp:hp + 64, hh // 2, :], PS_[hp:hp + 64, hh, :],
                         sgub[hp:hp + 64, hh // 2, :], ALU.add, [Bk[7], B_wD], [B_sgt])
                P.tt("pool", oT[:, 2:4, tc0:tc0 + 128], sgt, ut, ALU.mult, [B_sgt, But], [B_oT])

            def epilogue_b(t):
                tc0 = TI[t]["tc0"]
                AT = bank(7).bitcast(BF16).rearrange("p (a b) -> p a b", b=128)
                for c in range(4):
                    P.tr(AT[:, c, :], ao[:, c * 128:(c + 1) * 128], identb, [B_ao, B_id], [Bk[7]])
                P.copy("act", oT[:, 4:8, tc0:tc0 + 128], AT[:, 0:4, :], [Bk[7]], [B_oT])

            units = [(t, h) for t in tiles for h in range(8)]
            US = [dict() for _ in units]

            def s1(k):
                t, h = units[k]
                ti = TI[t]
                qt, Bqt = ti["qt"]
                nk = ti["nk"]
                ch, pb = h // 2, (h % 2) * 64
                S = psd[h % 2]
                BS = [Bk[2 * (h % 2)], Bk[2 * (h % 2) + 1]]
                bt, Bbt = bias_rr.next()
                P.dma("sp", bt[:, 0:nk], biasd[l, ti["ty"], h, :, 0:nk], W=[Bbt], key="bias%d" % (bias_rr.i % 3))
                col = 0
                for (a, b) in ti["kr"]:
                    c0 = a * 128
                    rem = (b - a) * 128
                    while rem > 0:
                        w_ = min(rem, 512 - (col % 512))
                        P.mm(S[:, col:col + w_], qt[:, h, :], kT_all[:, ch, c0:c0 + w_],
                             (col % 512) == 0, False, [Bqt] + rng(Bkt, a, b), [BS[col // 512]])
                        col += w_
                        c0 += w_
                        rem -= w_
                for b0_ in range(0, nk, 512):
                    w_ = min(512, nk - b0_)
                    P.mm(S[:, b0_:b0_ + w_], identb, bt[:, b0_:b0_ + w_], False, True, [Bbt, B_id], [BS[b0_ // 512]])
                US[k].update(S=S, BS=BS, bt=bt, Bbt=Bbt)

            def s2(k):
                t, h = units[k]
                ti = TI[t]
                nk = ti["nk"]
                smx, B_smx = ti["smx"]
                u = US[k]
                P.add("dve", lambda e, o=smx[:, h:h + 1], i=u["S"][:, 0:nk]: e.tensor_reduce(
                    out=o, in_=i, axis=AX.X, op=ALU.max, negate=True), u["BS"], [B_smx])
                pex, Bpex = pe_rr.next()
                P.act(pex[:, 0:nk], u["S"][:, 0:nk], AF.Exp, u["BS"] + [B_smx], [Bpex, B_smx], bias=smx[:, h:h + 1],
                      scale=1.0, accum_out=smx[:, 8 + h:9 + h])
                u.update(pex=pex, Bpex=Bpex)

            def s3(k):
                t, h = units[k]
                nkt = TI[t]["nk"] // 128
                u = US[k]
                pb_ = 4 + (k % 2)
                PT = bank(pb_).bitcast(BF16).rearrange("p (a b) -> p a b", b=128)
                for kt in range(nkt):
                    P.tr(PT[:, kt, :], u["pex"][:, kt * 128:(kt + 1) * 128], identb, [u["Bpex"], B_id], [Bk[pb_]])
                ptb, Bptb = ptb_rr.next()
                P.copy("act" if k % 2 == 0 else "dve", ptb[:, 0:nkt, :], PT[:, 0:nkt, :], [Bk[pb_]], [Bptb])
                u.update(ptb=ptb, Bptb=Bptb)

            def s4(k):
                t, h = units[k]
                ti = TI[t]
                nkt = ti["nk"] // 128
                ks = ti["ks"]
                u = US[k]
                for kt in range(nkt):
                    P.mm(bank(6)[:, h * 64:(h + 1) * 64], u["ptb"][:, kt, :], v_all[:, ks[kt], h * 64:(h + 1) * 64],
                         kt == 0, kt == nkt - 1, [u["Bptb"], Bv[ks[kt]]], [Bk[6]])
                if h == 7:
                    smx, B_smx = ti["smx"]
                    tc0 = ti["tc0"]
                    P.add("dve", lambda e, s_=smx: e.reciprocal(out=rinv, in_=s_[:, 8:16]), [B_smx], [B_rinv])
                    P.tt("dve", ao.rearrange("p (h d) -> p h d", d=64), bank(6).rearrange("p (h d) -> p h d", d=64),
                         rinv.unsqueeze(2).broadcast_to([128, 8, 64]), ALU.mult, [Bk[6], B_rinv], [B_ao])

            NU = len(units)
            loads(tiles[0])
            for k in range(NU + 8):
                if k < NU:
                    t, h = units[k]
                    if h == 0:
                        if t + 1 < hi:
                            loads(t + 1)
                        prologue_a(t)
                    elif h == 2:
                        prologue_b(t)
                    elif h == 4:
                        prologue_c(t)
                    s1(k)
                if 0 <= k - 1 < NU:
                    s2(k - 1)
                if 0 <= k - 3 < NU:
                    s3(k - 3)
                if 0 <= k - 5 < NU:
                    s4(k - 5)
                if 0 <= k - 7 < NU and units[k - 7][1] == 7:
                    epilogue_b(units[k - 7][0])
            if debug and l == 0:
                for t in range(lo, hi):
                    P.dma("sp", dbg_o[t], oT[:, :, (t - lo) * 128:(t - lo + 1) * 128], R=[B_oT], W=[Bdbg], key="dbgo")
            for c in range(8):
                gt, Bgt = gt_rr.next()
                P.dma("sp", gt[:, 0:hi - lo], g_scr[lo:hi, c].rearrange("t p b n -> p t b n"), R=rng(Bg, lo, hi), W=[Bgt],
                      key="gt%d" % (gt_rr.i % 2))
                b0 = (c % 2) * 3
                brs = [(wpo, 2, 0), (wso, 2, 2), (wao, 4, 4)]
                for bi_, (wt_, nkk, o0) in enumerate(brs):
                    for k in range(nkk):
                        P.mm(bank(b0 + bi_)[:, 0:n], wt_[:, k, c * 128:(c + 1) * 128], oT[:, o0 + k, 0:n], k == 0, k == nkk - 1,
                             [B_wE, B_oT], [Bk[b0 + bi_]])
                e_t = e_sets[c % 2]
                for bi_ in range(3):
                    et, Bet = e_t[bi_]
                    P.tt("dve", et[:, 0:n].rearrange("p (t n) -> p t n", n=128),
                         bank(b0 + bi_)[:, 0:n].rearrange("p (t n) -> p t n", n=128), gt[:, 0:hi - lo, bi_, :],
                         ALU.mult, [Bk[b0 + bi_], Bgt], [Bet])
                P.tt("pool", e_t[0][0][:, 0:n], e_t[0][0][:, 0:n], e_t[1][0][:, 0:n], ALU.add, [e_t[0][1], e_t[1][1]], [e_t[0][1]])
                P.tt("pool", yT[:, c, 0:n], e_t[0][0][:, 0:n], e_t[2][0][:, 0:n], ALU.add, [e_t[0][1], e_t[2][1]], [B_yT])
            hcs = []

            def ld_hc(c2):
                hc, Bhc = hc_rr.next()
                P.dma("sp", hc[:, 0:n], h_src(l, c2, lo, hi), R=([Bh[t][c2] for t in range(lo, hi)] if l > 0 else []),
                      W=[Bhc], key="hc%d" % (hc_rr.i % 5))
                hcs.append((hc, Bhc, "hc%d" % (hc_rr.i % 5)))

            for c2 in range(3):
                ld_hc(c2)
            for c2 in range(8):
                ob = 6 + (c2 % 2)
                hc, Bhc, hkey = hcs[c2]
                for c in range(8):
                    P.mm(bank(ob)[:, 0:n], wo[:, c, c2 * 128:(c2 + 1) * 128], yT[:, c, 0:n], c == 0, c == 7,
                         [B_wE, B_yT], [Bk[ob]])
                P.stt("dve", hc[:, 0:n], bank(ob)[:, 0:n], MT(l, st, 2)[:, c2:c2 + 1], hc[:, 0:n],
                      ALU.mult, ALU.add, [Bk[ob], Bhc, B_modt], [Bhc])
                if c2 + 3 < 8:
                    ld_hc(c2 + 3)
                P.dma("sp", h_scr[c2, :, lo * 128:hi * 128], hc[:, 0:n], R=[Bhc], W=[Bh[t][c2] for t in range(lo, hi)],
                      key=hkey)
        stop("E", l)

        P.barrier()
        AR.release(m0)
        hF2 = [(AR.alloc([8, 1024], F32), Buf("hF%d" % i)) for i in range(2)]
        aF = AR.alloc([8, 1024], BF16)
        B_aF = Buf("aF")
        hid = AR.alloc([32, 1024], BF16)
        B_hid = Buf("hid")
        rsbF = (AR.alloc([512], F32), Buf("rsF"))
        tmpF_rr = RR([(AR.alloc([512], F32), Buf("tmpF%d" % i)) for i in range(2)])
        w1_rr = RR([(AR.alloc([4, 8, 128], BF16), Buf("w1b%d" % i)) for i in range(2)])
        w2_rr = RR([(AR.alloc([32, 128], BF16), Buf("w2b%d" % i)) for i in range(2)])
        rl_rr = RR([(AR.alloc([512], F32), Buf("rl%d" % i)) for i in range(2)])
        f1_rr = RR([(bank(i), Bk[i]) for i in (1, 2, 3, 4)])
        f2_rr = RR([(bank(i), Bk[i]) for i in (5, 6, 7)])
        groupsF = segs_for("F", l)

        def goffs(grp):
            offs = []
            o_ = 0
            for (lo, hi, st) in grp:
                offs.append(o_)
                o_ += (hi - lo) * 128
            return offs

        def load_group(gi):
            hF, B_hF = hF2[gi % 2]
            grp = groupsF[gi]
            for si_, ((lo, hi, st), o0) in enumerate(zip(grp, goffs(grp))):
                n = (hi - lo) * 128
                P.dma("sp", hF[:, :, o0:o0 + n], h_scr[:, :, lo * 128:hi * 128].rearrange("c p n -> p c n"),
                      R=[b_ for t in range(lo, hi) for b_ in Bh[t]], W=[B_hF], key="hF%d_%d" % (gi % 2, si_))

        load_group(0)
        for gi, grp in enumerate(groupsF):
            hF, B_hF = hF2[gi % 2]
            offs = goffs(grp)
            for (lo, hi, st), o0 in zip(grp, offs):
                n = (hi - lo) * 128
                rms_to_aT(hF[:, :, o0:o0 + n], B_hF, n, MT(l, st, 3), MT(l, st, 4), aF[:, :, o0:o0 + n], [B_aF],
                          rsbF, tmpF_rr, (bank(0), Bk[0]))
            if gi + 1 < len(groupsF):
                load_group(gi + 1)
            for jp in range(8):
                w1t, Bw1 = w1_rr.next()
                P.dma("pool", w1t, w1r[l, jp * 4:jp * 4 + 4].rearrange("j p k m -> p j k m"), W=[Bw1],
                      key="w1b%d" % (w1_rr.i % 2), max_dma_last_dim=4096)
                for jj in range(4):
                    j = jp * 4 + jj
                    for (lo, hi, st), o0 in zip(grp, offs):
                        n = (hi - lo) * 128
                        pf, Bpf = f1_rr.next()
                        for k in range(8):
                            P.mm(pf[:, 0:n], w1t[:, jj, k, :], aF[:, k, o0:o0 + n], k == 0, k == 7, [Bw1, B_aF], [Bpf])
                        rl, Brl = rl_rr.next()
                        P.act(rl[:, 0:n], pf[:, 0:n], AF.Relu, [Bpf], [Brl])
                        P.stt("dve", hid[:, j, o0:o0 + n], pf[:, 0:n], 0.0, rl[:, 0:n], ALU.max, ALU.mult,
                              [Bpf, Brl], [B_hid])
            for c2 in range(8):
                w2t, Bw2 = w2_rr.next()
                P.dma("pool", w2t, w2r[l, c2], W=[Bw2], key="w2b%d" % (w2_rr.i % 2), max_dma_last_dim=4096)
                for (lo, hi, st), o0 in zip(grp, offs):
                    n = (hi - lo) * 128
                    pf, Bpf = f2_rr.next()
                    for j in range(32):
                        P.mm(pf[:, 0:n], w2t[:, j, :], hid[:, j, o0:o0 + n], j == 0, j == 31, [Bw2, B_hid], [Bpf])
                    P.stt("dve", hF[:, c2, o0:o0 + n], pf[:, 0:n], MT(l, st, 5)[:, c2:c2 + 1], hF[:, c2, o0:o0 + n],
                          ALU.mult, ALU.add, [Bpf, B_hF, B_modt], [B_hF])
            for si_, ((lo, hi, st), o0) in enumerate(zip(grp, offs)):
                n = (hi - lo) * 128
                if not last:
                    P.dma("sp", h_scr[:, :, lo * 128:hi * 128].rearrange("c p n -> p c n"), hF[:, :, o0:o0 + n],
                          R=[B_hF], W=[b_ for t in range(lo, hi) for b_ in Bh[t]], key="hFo%d_%d" % (gi % 2, si_))
                else:
                    rs, Brs = rsbF
                    rms_rstd(hF[:, :, o0:o0 + n], B_hF, n, rsbF, (bank(0), Bk[0]))
                    for c in range(8):
                        P.stt("dve", hF[:, c, o0:o0 + n], hF[:, c, o0:o0 + n], V_fg[:, c:c + 1], rs[:, 0:n], ALU.mult, ALU.mult,
                              [B_hF, Brs, B_vec], [B_hF])
                    op = P.dma("sp", outT[:, :, (lo - 4) * 128:(hi - 4) * 128].rearrange("c p n -> p c n"), hF[:, :, o0:o0 + n],
                               R=[B_hF], W=[Bout], key="outst%d_%d" % (gi % 2, si_))
                    out_ops.append(op)
        stop("F", l)

    P.halted = False
    if not out_ops:
        z = AR.t[:, 0:2048]
        out_ops.append(P.dma("sp", outT[0], z, R=[], W=[Bout], key="outst"))
    if debug:
        out_ops.append(P.dma("sp", outT[1, :, 0:8], vec[:, 0:8], R=[Bdbg], W=[Bout], key="dbgfin"))
    P.final_waits = out_ops
    P.emit()
    return nc, P, AR


def _fm(v):
    v = np.asarray(v, np.float32)
    return np.ascontiguousarray(v.reshape(-1, 128).T)


def _rope_perm():
    idx = []
    for h in range(8):
        idx += [h * 64 + 2 * i for i in range(32)] + [h * 64 + 2 * i + 1 for i in range(32)]
    return np.array(idx)


def _shared_inputs(inp):
    w_in = np.asarray(inp["w_in"], np.float32)
    perm = _rope_perm()
    wp = w_in.copy()
    wp[:, :, 768:1280] = w_in[:, :, 768:1280][:, :, perm]
    wp[:, :, 1280:1792] = w_in[:, :, 1280:1792][:, :, perm]
    tm_cols = [np.r_[0:256, 512:768], np.r_[1792:2304]]
    wA_tm = np.stack([np.stack([wp[l][:, cols].reshape(8, 128, 512).transpose(1, 0, 2) for cols in tm_cols])
                      for l in range(2)])
    fm_starts = [256, 384] + [768 + 128 * i for i in range(8)] + [2304 + 128 * i for i in range(24)]
    wA_fm = np.stack([np.stack([wp[l][:, s:s + 128].reshape(8, 128, 128).transpose(1, 0, 2) for s in fm_starts])
                      for l in range(2)])
    w1 = np.asarray(inp["w_ff1"], np.float32)
    w1r = np.stack([np.stack([w1[l][:, j * 128:(j + 1) * 128].reshape(8, 128, 128).transpose(1, 0, 2)
                              for j in range(32)]) for l in range(2)])
    w2 = np.asarray(inp["w_ff2"], np.float32)
    w2r = np.stack([np.stack([w2[l][:, c * 128:(c + 1) * 128].reshape(32, 128, 128).transpose(1, 0, 2)
                              for c in range(8)]) for l in range(2)])
    sgu_w = np.asarray(inp["sgu_w"], np.float32)
    sguw = np.ascontiguousarray(sgu_w.transpose(3, 0, 1, 2))
    sgu_b = np.asarray(inp["sgu_b"], np.float32)
    sgub = np.zeros((128, 2, 2, 128), np.float32)
    for part in range(128):
        for c in range(2):
            sgub[part, :, c, :] = sgu_b[:, 2 * c + part // 64, :]
    w_pool = np.asarray(inp["w_pool"], np.float32)
    wblk = np.zeros((128, 2, 2, 128), np.float32)
    for c in range(2):
        for gl in range(2):
            wblk[gl * 64:(gl + 1) * 64, :, c, gl * 64:(gl + 1) * 64] = w_pool[:, 2 * c + gl].transpose(1, 0, 2)
    consts = np.zeros((128, 3, 128), np.float32)
    consts[:, 0, :] = np.eye(128)
    for pp in range(128):
        partner = pp + 32 if (pp % 64) < 32 else pp - 32
        consts[partner, 1, pp] = 1.0
    consts[:, 2, :] = 1.0
    return dict(w_mod=np.ascontiguousarray(inp["w_mod"], np.float32), wA_tm=np.ascontiguousarray(wA_tm),
                wA_fm=np.ascontiguousarray(wA_fm), sguw=sguw, sgub=sgub, wblk=wblk,
                w_pool_out=np.ascontiguousarray(inp["w_pool_out"], np.float32),
                w_sgu_out=np.ascontiguousarray(inp["w_sgu_out"], np.float32),
                w_attn_out=np.ascontiguousarray(inp["w_attn_out"], np.float32),
                w_o=np.ascontiguousarray(inp["w_o"], np.float32), w1r=np.ascontiguousarray(w1r),
                w2r=np.ascontiguousarray(w2r), consts=consts)


def _band_set(kind):
    out = np.zeros((128, 12, 128), np.float32)
    L = 384
    base = 128
    for g, w in enumerate((2, 4, 8, 16)):
        for tt in range(128):
            pos = base + tt
            lo_b = base if kind == 1 else 0
            hi_b = base + 128 if kind == 2 else L
            lo = min(max(pos - w // 2, lo_b), hi_b)
            hi = min(max(pos + (w - w // 2), lo_b), hi_b)
            cnt = hi - lo
            for s in range(lo, hi):
                d = s // 128
                out[s % 128, g * 3 + d, tt] += 1.0 / cnt
            out[tt, g * 3 + 1, tt] -= 1.0
    return out


def _bias_tables(rpb, core_rows0, n_rows_total=128):
    out = np.full((2, 6, 8, 128, 1024), NEG, np.float32)
    q_i = np.arange(128)
    for ty, j in ((0, 4), (1, 0), (2, 1), (3, 14), (4, 15)):
        klo, khi = j - 2, j + 3
        if j == 0:
            khi = j + 4
        if j == 15:
            klo = j - 3
        nkl = (khi - klo) * 128
        key = np.arange(nkl)
        k_row = core_rows0 + 2 * klo + key // 64
        k_col = key % 64
        q_row = core_rows0 + 2 * j + q_i // 64
        q_col = q_i % 64
        rs = np.clip(q_row - 4, 0, n_rows_total - 8)
        cs = np.clip(q_col - 8, 0, 64 - 16)
        valid = ((k_row[None, :] >= rs[:, None]) & (k_row[None, :] < rs[:, None] + 8) &
                 (k_col[None, :] >= cs[:, None]) & (k_col[None, :] < cs[:, None] + 16) &
                 (k_row[None, :] >= 0) & (k_row[None, :] < n_rows_total))
        dr = np.clip(k_row[None, :] - q_row[:, None] + 7, 0, 14)
        dc = np.clip(k_col[None, :] - q_col[:, None] + 15, 0, 30)
        for l in range(2):
            for h in range(8):
                g = rpb[l, h][dr, dc]
                out[l, ty, h, :, 0:nkl] = np.where(valid, g, NEG)
                out[l, ty, h, :, nkl:nkl + 256] = 0.0
    out[:, 5, :, :, 0:256] = 0.0
    return out.astype(ml_dtypes.bfloat16)


def _core_inputs(inp, core):
    b, blk = core // 4, core % 4
    row0 = 32 * blk
    x = np.asarray(inp["x"], np.float32)[b]
    t0 = (row0 - 8) * 64
    xs = np.zeros((NLAT, D), np.float32)
    lo, hi = max(t0, 0), min(t0 + NLAT, 8192)
    xs[lo - t0:hi - t0] = x[lo:hi]
    xT = np.ascontiguousarray(xs.T.reshape(8, 128, NLAT))
    ctxT = np.ascontiguousarray(np.asarray(inp["ctx"], np.float32)[b].T.reshape(8, 128, 256))
    vecs = np.zeros((128, 156), np.float32)
    vecs[:, 0:8] = _fm(inp["c"][b])
    vecs[:, 8:16] = _fm(inp["c_ctx"])
    for l in range(2):
        vecs[:, 16 + l * 48:16 + (l + 1) * 48] = _fm(inp["b_mod"][l])
        vecs[:, 112 + l * 8:112 + (l + 1) * 8] = _fm(inp["norm1_g"][l])
        vecs[:, 128 + l * 8:128 + (l + 1) * 8] = _fm(inp["norm2_g"][l])
        vecs[:, 152 + l * 2:152 + (l + 1) * 2] = _fm(inp["pool_scale"][l])
    vecs[:, 144:152] = _fm(inp["final_g"])
    tok = np.arange(NLAT)
    row = (row0 - 8 + tok // 64).astype(np.float32)
    col = (tok % 64).astype(np.float32)
    inv_freq = (10000.0 ** (-np.arange(16, dtype=np.float32) / 16)).astype(np.float32)
    ang = np.concatenate([row[:, None] * inv_freq, col[:, None] * inv_freq], axis=-1).astype(np.float32)
    cos, sin = np.cos(ang), np.sin(ang)
    rope = np.zeros((4, 128, NLAT), np.float32)
    for pp in range(128):
        i = pp % 64
        e = i % 32
        rope[0, pp] = cos[:, e]
        rope[1, pp] = -sin[:, e] if i < 32 else sin[:, e]
    rope[2:4] = rope[0:2] * np.float32(0.125)
    first = (blk == 0)
    lastb = (blk == 3)
    gen = _band_set(0)
    bands = np.stack([gen, _band_set(1) if first else gen, _band_set(2) if lastb else gen, _band_set(1), _band_set(2)])
    if first:
        bands[1][:, [0, 3, 6, 9], :] = 0.0
    bias = _bias_tables(np.asarray(inp["na_rpb"], np.float32), row0)
    return dict(xT=xT, ctxT=ctxT, vecs=vecs, rope=rope, bands=np.ascontiguousarray(bands), bias=bias)


_PROG = {}


def kernel(**inputs):
    if "nc" not in _PROG:
        _PROG["nc"] = build_program()[0]
    nc = _PROG["nc"]
    shared = _shared_inputs(inputs)
    in_maps = []
    for core in range(8):
        m = dict(shared)
        m.update(_core_inputs(inputs, core))
        in_maps.append(m)
    res = run_bass_kernel_spmd(nc, in_maps, core_ids=list(range(8)))
    out = np.zeros((2, 8192, D), np.float32)
    for core in range(8):
        b, blk = core // 4, core % 4
        oT = np.asarray(res.results[core]["outT"], np.float32)
        out[b, blk * 2048:(blk + 1) * 2048, :] = oT.reshape(D, 2048).T
    return out
```

```python
import numpy as np
import ml_dtypes
import concourse.bass as bass
import concourse.mybir as mybir
from concourse.bass_utils import run_bass_kernel_spmd

F32 = mybir.dt.float32
F32R = mybir.dt.float32r
BF16 = mybir.dt.bfloat16
AF = mybir.ActivationFunctionType
ALU = mybir.AluOpType
AX = mybir.AxisListType

D = 1024
NLS = 24
NS = 26
NLAT = NLS * 128
TOK = NS * 128
EPS = 1e-6
NEG = -30000.0
DBG_SKIP_LN = False


class Buf:
    __slots__ = ("name", "last_writer", "readers")

    def __init__(self, name):
        self.name = name
        self.last_writer = None
        self.readers = []


class Op:
    __slots__ = ("eng", "fn", "deps", "is_dma", "key", "idx", "signal", "sem", "val", "waits")

    def __init__(self, eng, fn, is_dma, key, idx):
        self.eng = eng
        self.fn = fn
        self.is_dma = is_dma
        self.key = key
        self.idx = idx
        self.deps = []
        self.signal = False
        self.sem = None
        self.val = 0
        self.waits = []


ENGS = ("pe", "act", "dve", "pool", "sp")


class Prog:
    def __init__(self, nc):
        self.nc = nc
        self.ops = []
        self.final_waits = []
        self.phase_buf = Buf("phase")
        self.bar_ap = None
        self.halted = False
        self.dummy = Op("dve", None, False, None, -1)

    def add(self, eng, fn, reads=(), writes=(), dma_key=None):
        if self.halted:
            return self.dummy
        is_dma = dma_key is not None
        op = Op(eng, fn, is_dma, dma_key, len(self.ops))
        deps = {}
        for b in reads:
            w = b.last_writer
            if w is not None:
                deps[w.idx] = [w, True]
        for b in writes:
            w = b.last_writer
            if w is not None and w.idx not in deps:
                deps[w.idx] = [w, False]
            for r in b.readers:
                if r.idx not in deps:
                    deps[r.idx] = [r, False]
        pb = self.phase_buf
        if pb.last_writer is not None:
            deps[pb.last_writer.idx] = [pb.last_writer, True]
        pb.readers.append(op)
        for b in reads:
            b.readers.append(op)
        for b in writes:
            b.last_writer = op
            b.readers = []
        deps.pop(op.idx, None)
        op.deps = list(deps.values())
        self.ops.append(op)
        return op

    def barrier(self):
        if self.halted:
            return self.dummy
        pb = self.phase_buf
        op = Op("dve", lambda e: e.memset(self.bar_ap, 0.0), False, None, len(self.ops))
        deps = {}
        for r in pb.readers:
            deps[r.idx] = [r, True]
        if pb.last_writer is not None:
            deps[pb.last_writer.idx] = [pb.last_writer, True]
        op.deps = list(deps.values())
        pb.last_writer = op
        pb.readers = []
        self.ops.append(op)
        return op

    def dma(self, eng, out, in_, R=(), W=(), key=None, **kw):
        return self.add(eng, lambda e: e.dma_start(out=out, in_=in_, **kw), R, W, dma_key=key)

    def mm(self, out, lhsT, rhs, start, stop, R, W):
        return self.add("pe", lambda e: e.matmul(out, lhsT=lhsT, rhs=rhs, start=start, stop=stop), R, W)

    def tr(self, out, in_, ident, R, W):
        return self.add("pe", lambda e: e.transpose(out=out, in_=in_, identity=ident), R, W)

    def act(self, out, in_, func, R, W, **kw):
        return self.add("act", lambda e: e.activation(out=out, in_=in_, func=func, **kw), R, W)

    def copy(self, eng, out, in_, R, W):
        if eng == "act":
            return self.add("act", lambda e: e.copy(out=out, in_=in_), R, W)
        return self.add(eng, lambda e: e.tensor_copy(out=out, in_=in_), R, W)

    def tt(self, eng, out, in0, in1, op, R, W):
        return self.add(eng, lambda e: e.tensor_tensor(out=out, in0=in0, in1=in1, op=op), R, W)

    def ts(self, eng, out, in0, s1, s2, op0, op1, R, W):
        return self.add(eng, lambda e: e.tensor_scalar(out=out, in0=in0, scalar1=s1, scalar2=s2, op0=op0, op1=op1), R, W)

    def stt(self, eng, out, in0, scalar, in1, op0, op1, R, W):
        return self.add(eng, lambda e: e.scalar_tensor_tensor(out=out, in0=in0, scalar=scalar, in1=in1,
                                                              op0=op0, op1=op1), R, W)

    def emit(self):
        nc = self.nc
        ops = self.ops
        for op in ops:
            need = []
            for d, raw in op.deps:
                if d.is_dma or op.is_dma or d.eng != op.eng or (raw and op.eng != "pe"):
                    need.append(d)
            op.waits = need
            for d in need:
                d.signal = True
        for op in self.final_waits:
            op.signal = True
        for op in ops:
            if op.is_dma:
                op.signal = True
        sems = {}
        counters = {}
        for op in ops:
            if not op.signal:
                continue
            k = ("dma", op.key) if op.is_dma else ("eng", op.eng)
            if k not in sems:
                sems[k] = nc.alloc_semaphore("s%d" % len(sems))
                counters[k] = 0
            op.sem = sems[k]
            counters[k] += 16 if op.is_dma else 1
            op.val = counters[k]
            op.key = k
        self.n_sems = len(sems)
        per_eng = {e: [] for e in ENGS}
        for op in ops:
            per_eng[op.eng].append(op)
        finals = list(self.final_waits)

        def run(eng_name, eng):
            waited = {}
            for op in per_eng[eng_name]:
                req = {}
                for d in op.waits:
                    if req.get(d.key, 0) < d.val:
                        req[d.key] = d.val
                for k, v in req.items():
                    if waited.get(k, 0) >= v:
                        continue
                    waited[k] = v
                    eng.wait_ge(sems[k], v)
                inst = op.fn(eng)
                if op.signal:
                    inst.then_inc(op.sem, 16 if op.is_dma else 1)
            if eng_name == "sp":
                for d in finals:
                    if waited.get(d.key, 0) < d.val:
                        waited[d.key] = d.val
                        eng.wait_ge(sems[d.key], d.val)

        with nc.Block() as block:
            @block.tensor
            def _(e):
                run("pe", e)

            @block.scalar
            def _(e):
                run("act", e)

            @block.vector
            def _(e):
                run("dve", e)

            @block.gpsimd
            def _(e):
                run("pool", e)

            @block.sync
            def _(e):
                run("sp", e)


class Arena:
    def __init__(self, nc, nbytes):
        self.t = nc.alloc_sbuf_tensor("arena", [128, nbytes // 4], F32)
        self.n = nbytes
        self.off = 0
        self.peak = 0

    def alloc(self, shape, dtype):
        esz = 2 if dtype == BF16 else 4
        n = int(np.prod(shape)) * esz
        n4 = (n + 31) // 32 * 32
        assert self.off + n4 <= self.n, ("SBUF arena overflow", self.off, n4, self.n)
        a = self.t[:, self.off // 4:(self.off + n) // 4]
        self.off += n4
        self.peak = max(self.peak, self.off)
        if dtype != F32:
            a = a.bitcast(dtype)
        if len(shape) == 2:
            return a.rearrange("p (a b) -> p a b", a=shape[0])
        if len(shape) == 3:
            return a.rearrange("p (a b c) -> p a b c", a=shape[0], b=shape[1])
        return a

    def mark(self):
        return self.off

    def release(self, m):
        self.off = m


class RR:
    def __init__(self, items):
        self.items = items
        self.i = 0

    def next(self):
        it = self.items[self.i % len(self.items)]
        self.i += 1
        return it


def segs_for(kind, layer):
    if kind == "A":
        if layer == 0:
            s = [(4 * b, 4 * b + 4, 0) for b in range(6)]
        else:
            s = [(2, 4, 0)] + [(4 * b, 4 * b + 4, 0) for b in range(1, 5)] + [(20, 22, 0)]
        return s + [(24, 26, 1)]
    if kind == "DE":
        s = [(4 * b, 4 * b + 4, 0) for b in range(1, 5)]
        if layer == 0:
            s = [(2, 4, 0)] + s + [(20, 22, 0), (24, 26, 1)]
        return s
    if kind == "F":
        g = [[(4, 8, 0), (8, 12, 0)], [(12, 16, 0), (16, 20, 0)]]
        if layer == 0:
            g.append([(2, 4, 0), (20, 22, 0), (24, 26, 1)])
        return g
    raise ValueError(kind)


def build_program(n_layers=2, stop_after=None, debug=False):
    nc = bass.Bass("TRN2", target_bir_lowering=False)
    P = Prog(nc)

    def din(name, shape, dt=F32):
        return nc.dram_tensor(name, list(shape), dt, kind="ExternalInput").ap()

    xT = din("xT", [8, 128, NLAT])
    ctxT = din("ctxT", [8, 128, 256])
    vecs = din("vecs", [128, 156])
    consts = din("consts", [128, 3, 128])
    rope = din("rope", [4, 128, NLAT])
    bands = din("bands", [5, 128, 12, 128])
    biasd = din("bias", [2, 6, 8, 128, 1024], BF16)
    w_mod = din("w_mod", [2, 1024, 6144])
    wA_tm = din("wA_tm", [2, 2, 128, 8, 512])
    wA_fm = din("wA_fm", [2, 34, 128, 8, 128])
    sguw = din("sguw", [128, 2, 4, 128])
    sgub_d = din("sgub", [128, 2, 2, 128])
    wblk_d = din("wblk", [128, 2, 2, 128])
    w_po = din("w_pool_out", [2, 256, 1024])
    w_so = din("w_sgu_out", [2, 256, 1024])
    w_ao = din("w_attn_out", [2, 512, 1024])
    w_o = din("w_o", [2, 1024, 1024])
    w1r = din("w1r", [2, 32, 128, 8, 128])
    w2r = din("w2r", [2, 8, 128, 32, 128])
    outT = nc.dram_tensor("outT", [8, 128, 2048], F32, kind="ExternalOutput").ap()

    skind = "ExternalOutput" if debug else "Internal"

    def dscr(name, shape, dt=F32):
        return nc.dram_tensor(name, list(shape), dt, kind=skind).ap()

    h_scr = dscr("h_scr", [8, 128, TOK])
    q_scr = dscr("q_scr", [NS, 128, 4, 128], BF16)
    u_scr = dscr("u_scr", [NS, 128, 2, 128])
    vn_scr = dscr("vn_scr", [NS, 128, 256], BF16)
    g_scr = dscr("g_scr", [NS, 8, 128, 3, 128], BF16)
    if debug:
        dbg_k = dscr("dbg_k", [128, 4, TOK], BF16)
        dbg_v = dscr("dbg_v", [128, NS, 512], BF16)
        dbg_p = dscr("dbg_p", [128, NS, 256], BF16)
        dbg_a = dscr("dbg_a", [128, 8, TOK], BF16)
        dbg_mod = dscr("dbg_mod", [128, 2, 2, 48])
        dbg_o = dscr("dbg_o", [NS, 128, 8, 128], BF16)

    Bh = [[Buf("h%d_%d" % (t, c)) for c in range(8)] for t in range(NS)]
    Bq = [Buf("q%d" % t) for t in range(NS)]
    Bu = [Buf("u%d" % t) for t in range(NS)]
    Bvn = [Buf("vn%d" % t) for t in range(NS)]
    Bg = [Buf("g%d" % t) for t in range(NS)]
    Bout = Buf("out")
    Bdbg = Buf("dbg")

    psd = [nc.alloc_psum_tensor("psd%d" % i, [128, 1024], F32) for i in range(4)]
    Bk = [Buf("bank%d" % i) for i in range(8)]

    def bank(i):
        return psd[i // 2][:, (i % 2) * 512:(i % 2) * 512 + 512]

    AR = Arena(nc, 197 * 1024)
    bar_t = AR.alloc([8], F32)
    P.bar_ap = bar_t
    cst = AR.alloc([3, 128], F32)
    identb = AR.alloc([128], BF16)
    perm_r = nc.alloc_sbuf_tensor("perm_r", [128, 128], F32R)[:]
    ones_r = nc.alloc_sbuf_tensor("ones_r", [128, 128], F32R)[:]
    sq_rr = RR([(nc.alloc_sbuf_tensor("sq_r%d" % i, [128, 512], F32R)[:], Buf("sq_r%d" % i)) for i in range(2)])
    qf_rr = RR([(nc.alloc_sbuf_tensor("qf_r%d" % i, [128, 512], F32R)[:], Buf("qf_r%d" % i)) for i in range(2)])
    vec = AR.alloc([156], F32)
    modt = AR.alloc([24, 8], F32)
    silu_b = AR.alloc([2, 8], BF16)
    B_c = Buf("consts")
    B_vec = Buf("vec")
    B_modt = Buf("modt")
    B_silu = Buf("silu")

    def V_c(s):
        return vec[:, s * 8:(s + 1) * 8]

    def V_bmod(l):
        return vec[:, 16 + l * 48:16 + (l + 1) * 48]

    def V_n1(l):
        return vec[:, 112 + l * 8:112 + (l + 1) * 8]

    def V_n2(l):
        return vec[:, 128 + l * 8:128 + (l + 1) * 8]

    V_fg = vec[:, 144:152]

    def V_ps(l):
        return vec[:, 152 + l * 2:152 + (l + 1) * 2]

    def MT(l, s, kind):
        i = (l * 2 + s) * 6 + kind
        return modt[:, i, :]

    P.dma("sp", cst, consts, W=[B_c], key="cst")
    P.dma("sp", vec, vecs, W=[B_vec], key="vec")
    B_id = Buf("ident")
    P.copy("dve", identb, cst[:, 0, :], [B_c], [B_id])
    P.copy("dve", perm_r, cst[:, 1, :], [B_c], [B_id])
    P.copy("dve", ones_r, cst[:, 2, :], [B_c], [B_id])
    P.act(silu_b.rearrange("p s k -> p (s k)"), vec[:, 0:16], AF.Silu, [B_vec], [B_silu])

    def stop(name, l=0):
        if stop_after is not None and tuple(stop_after) == (name, l):
            P.halted = True

    stop("S")

    m0 = AR.mark()
    wm = [(AR.alloc([8, 512], BF16), Buf("wm%d" % i)) for i in range(2)]
    wm_rr = RR(wm)
    modraw = AR.alloc([48, 2], F32)
    tmpm = AR.alloc([8, 2], F32)
    B_modraw = Buf("modraw")
    for l in range(n_layers):
        psm = bank(0).rearrange("p (a b) -> p a b", b=2)[:, 0:48, :]
        for pc in range(12):
            wt, wb = wm_rr.next()
            P.dma("pool", wt, w_mod[l, :, pc * 512:(pc + 1) * 512].rearrange("(k p) n -> p k n", p=128),
                  W=[wb], key="wm%d" % (wm_rr.i % 2), max_dma_last_dim=4096)
            for cc in range(4):
                ch = pc * 4 + cc
                for k in range(8):
                    P.mm(psm[:, ch, :], wt[:, k, cc * 128:(cc + 1) * 128], silu_b[:, :, k], k == 0, k == 7,
                         [wb, B_silu], [Bk[0]])
        P.tt("dve", modraw, psm, V_bmod(l).unsqueeze(2).broadcast_to([128, 48, 2]), ALU.add,
             [Bk[0], B_vec], [B_modraw])
        for s in range(2):
            def chunk(i):
                return modraw[:, i * 8:(i + 1) * 8, s]
            P.stt("dve", MT(l, s, 0), chunk(1), 1.0, V_n1(l), ALU.add, ALU.mult, [B_modraw, B_vec], [B_modt])
            P.copy("dve", MT(l, s, 1), chunk(0), [B_modraw], [B_modt])
            P.copy("dve", MT(l, s, 2), chunk(2), [B_modraw], [B_modt])
            P.stt("dve", MT(l, s, 3), chunk(4), 1.0, V_n2(l), ALU.add, ALU.mult, [B_modraw, B_vec], [B_modt])
            P.copy("dve", MT(l, s, 4), chunk(3), [B_modraw], [B_modt])
            P.copy("dve", MT(l, s, 5), chunk(5), [B_modraw], [B_modt])
    if debug:
        P.dma("sp", dbg_mod.rearrange("p l s c -> p (l s c)"), modt.rearrange("p a b -> p (a b)")[:, 0:192],
              R=[B_modt], W=[Bdbg], key="dbgm")
    AR.release(m0)
    stop("M")
    P.barrier()

    kT_all = AR.alloc([4, TOK], BF16)
    v_all = AR.alloc([NS, 512], BF16)
    p_all = AR.alloc([NS, 256], BF16)
    Bkt = [Buf("kT%d" % t) for t in range(NS)]
    Bv = [Buf("v%d" % t) for t in range(NS)]
    Bp = [Buf("p%d" % t) for t in range(NS)]
    res_mark = AR.mark()

    def rng(bufs, lo, hi):
        return [bufs[t] for t in range(lo, hi)]

    def h_src(l, c, lo, hi):
        if l == 0:
            if lo >= 24:
                return ctxT[c, :, (lo - 24) * 128:(hi - 24) * 128]
            return xT[c, :, lo * 128:hi * 128]
        return h_scr[c, :, lo * 128:hi * 128]

    def load_h(dst, l, lo, hi, Bdst, key, first_layer_input):
        n = (hi - lo) * 128
        if first_layer_input:
            src = (ctxT[:, :, (lo - 24) * 128:(hi - 24) * 128] if lo >= 24 else xT[:, :, lo * 128:hi * 128])
            R = []
        else:
            src = h_scr[:, :, lo * 128:hi * 128]
            R = [b_ for t in range(lo, hi) for b_ in Bh[t]]
        P.dma("sp", dst[:, :, 0:n], src.rearrange("c p n -> p c n"), R=R, W=[Bdst], key=key)

    def rms_rstd(hb, Bhb, n, rsb, psb):
        rs, Brs = rsb
        pb, Bpb = psb
        for c in range(8):
            sq, Bsq = sq_rr.next()
            P.act(sq[:, 0:n], hb[:, c, 0:n], AF.Square, [Bhb], [Bsq])
            P.mm(pb[:, 0:n], ones_r, sq[:, 0:n], c == 0, c == 7, [Bsq, B_id], [Bpb])
        P.ts("dve", rs[:, 0:n], pb[:, 0:n], 1.0 / D, EPS, ALU.mult, ALU.add, [Bpb], [Brs])
        P.act(rs[:, 0:n], rs[:, 0:n], AF.Sqrt, [Brs], [Brs])
        P.add("dve", lambda e: e.reciprocal(out=rs[:, 0:n], in_=rs[:, 0:n]), [Brs], [Brs])

    def rms_to_aT(hb, Bhb, n, gs, sh, dst, Wdst, rsb, tmp_rr, psb):
        rms_rstd(hb, Bhb, n, rsb, psb)
        rms_mod(hb, Bhb, n, gs, sh, dst, Wdst, rsb, tmp_rr)

    def rms_mod(hb, Bhb, n, gs, sh, dst, Wdst, rsb, tmp_rr):
        rs, Brs = rsb
        for c in range(8):
            tmp, Btmp = tmp_rr.next()
            P.stt("dve", tmp[:, 0:n], hb[:, c, 0:n], gs[:, c:c + 1], rs[:, 0:n], ALU.mult, ALU.mult,
                  [Bhb, Brs, B_modt], [Btmp])
            P.act(dst[:, c, 0:n], tmp[:, 0:n], AF.Identity, [Btmp, B_modt], Wdst, bias=sh[:, c:c + 1], scale=1.0)

    out_ops = []

    for l in range(n_layers):
        last = (l == n_layers - 1)
        if l > 0:
            P.barrier()
        AR.release(res_mark)
        aT_all = AR.alloc([8, TOK], BF16)
        Ba = [Buf("aT%d" % t) for t in range(NS)]
        a_mark = AR.mark()
        hst = [(AR.alloc([8, 512], F32), Buf("hst%d" % i)) for i in range(3)]
        hst_rr = RR(hst)
        rsb2 = [(AR.alloc([512], F32), Buf("rs%d" % i)) for i in range(2)]
        tmp_rr = RR([(AR.alloc([512], F32), Buf("tmp%d" % i)) for i in range(2)])
        segsA = segs_for("A", l)
        pend = None
        for si, (lo, hi, st) in enumerate(segsA):
            n = (hi - lo) * 128
            hb, Bhb = hst_rr.next()
            load_h(hb, l, lo, hi, Bhb, "hst%d" % (hst_rr.i % 3), l == 0)
            bi = si % 2
            rms_rstd(hb, Bhb, n, rsb2[bi], (bank(bi), Bk[bi]))
            if pend is not None:
                rms_mod(*pend)
            pend = (hb, Bhb, n, MT(l, st, 0), MT(l, st, 1), aT_all[:, :, lo * 128:hi * 128], rng(Ba, lo, hi), rsb2[bi], tmp_rr)
        rms_mod(*pend)
        if debug and l == 0:
            P.dma("sp", dbg_a, aT_all, R=Ba, W=[Bdbg], key="dbga")
        stop("A1", l)
        P.barrier()
        AR.release(a_mark)

        wbuf = [(AR.alloc([8192], BF16), Buf("wbuf%d" % i)) for i in range(2)]
        wb_rr = RR(wbuf)
        ropeb = [(AR.alloc([4, 512], F32), Buf("rope%d" % i)) for i in range(2)]
        rope_rr = RR(ropeb)
        t1_rr = RR([(AR.alloc([512], F32), Buf("t1%d" % i)) for i in range(2)])
        t2_rr = RR([(AR.alloc([512], F32), Buf("t2%d" % i)) for i in range(2)])
        qo_rr = RR([(AR.alloc([512], BF16), Buf("qo%d" % i)) for i in range(2)])
        us_rr = RR([(AR.alloc([512], F32), Buf("us%d" % i)) for i in range(3)])
        gs_rr = RR([(AR.alloc([512], BF16), Buf("gs%d" % i)) for i in range(3)])
        vns_rr = RR([(AR.alloc([256], BF16), Buf("vns%d" % i)) for i in range(2)])
        st_rr = RR([(AR.alloc([8], F32), Buf("st%d" % i)) for i in range(2)])
        vsf_rr = RR([(AR.alloc([256], F32), Buf("vsf%d" % i)) for i in range(4)])
        pj_rr = RR([(bank(i), Bk[i]) for i in range(4)])
        pm_rr = RR([(bank(i), Bk[i]) for i in (4, 5)])
        tm_rr = RR([(bank(i), Bk[i]) for i in (4, 5, 6, 7)])

        def load_w(src, shape):
            wt, wb = wb_rr.next()
            nel = int(np.prod(shape))
            view = wt[:, 0:nel]
            if len(shape) == 2:
                view = view.rearrange("p (a b) -> p a b", a=shape[0])
            else:
                view = view.rearrange("p (a b c) -> p a b c", a=shape[0], b=shape[1])
            P.dma("pool", view, src, W=[wb], key="wbuf%d" % (wb_rr.i % 2), max_dma_last_dim=4096)
            return view, wb

        for piece in range(2):
            wv, wb = load_w(wA_tm[l, piece], [8, 512])
            for (lo, hi, st) in segsA:
                for t in range(lo, hi):
                    pt, Bpt = tm_rr.next()
                    for k in range(8):
                        P.mm(pt, aT_all[:, k, t * 128:(t + 1) * 128], wv[:, k, :], k == 0, k == 7, [Ba[t], wb], [Bpt])
                    if piece == 0:
                        P.copy("act", p_all[:, t, :], pt[:, 0:256], [Bpt], [Bp[t]])
                        if DBG_SKIP_LN:
                            continue
                        stt_, Bst = st_rr.next()
                        vf, Bvf = vsf_rr.next()
                        P.act(vf, pt[:, 256:512], AF.Identity, [Bpt], [Bvf, Bst], accum_out=stt_[:, 0:1])
                        P.ts("dve", stt_[:, 1:2], stt_[:, 0:1], -1.0 / 256, None, ALU.mult, ALU.bypass, [Bst], [Bst])
                        jk, Bjk = vsf_rr.next()
                        P.act(jk, vf, AF.Square, [Bvf, Bst], [Bjk, Bst], bias=stt_[:, 1:2], scale=1.0, accum_out=stt_[:, 2:3])
                        P.ts("dve", stt_[:, 3:4], stt_[:, 2:3], 1.0 / 256, EPS, ALU.mult, ALU.add, [Bst], [Bst])
                        P.act(stt_[:, 3:4], stt_[:, 3:4], AF.Sqrt, [Bst], [Bst])
                        P.add("dve", lambda e, o=stt_[:, 3:4]: e.reciprocal(out=o, in_=o), [Bst], [Bst])
                        vs_, Bvs = vns_rr.next()
                        P.ts("dve", vs_, vf, stt_[:, 1:2], stt_[:, 3:4], ALU.add, ALU.mult, [Bvf, Bst], [Bvs])
                        P.dma("sp", vn_scr[t], vs_, R=[Bvs], W=[Bvn[t]], key="vns%d" % (vns_rr.i % 2))
                    else:
                        P.copy("act", v_all[:, t, :], pt, [Bpt], [Bv[t]])

        stop("A2", l)
        def proj_chunk(wv, wb, lo, hi):
            n = (hi - lo) * 128
            pj, Bpj = pj_rr.next()
            for k in range(8):
                P.mm(pj[:, 0:n], wv[:, k, :], aT_all[:, k, lo * 128:hi * 128], k == 0, k == 7,
                     rng(Ba, lo, hi) + [wb], [Bpj])
            return pj, Bpj, n

        wv, wb = load_w(wA_fm[l, 0:2].rearrange("c p k m -> p c k m"), [2, 8, 128])
        for (lo, hi, st) in segsA:
            for c in range(2):
                pj, Bpj, n = proj_chunk(wv[:, c], wb, lo, hi)
                us, Bus = us_rr.next()
                P.copy("act", us[:, 0:n], pj[:, 0:n], [Bpj], [Bus])
                P.dma("sp", u_scr[lo:hi, :, c, :].rearrange("t p n -> p t n"),
                      us[:, 0:n].rearrange("p (t n) -> p t n", n=128), R=[Bus], W=rng(Bu, lo, hi),
                      key="us%d" % (us_rr.i % 3))
        stop("A3", l)
        wv, wb = load_w(wA_fm[l, 2:10].rearrange("c p k m -> p c k m"), [8, 8, 128])
        for (lo, hi, st) in segsA:
            n = (hi - lo) * 128
            if st == 0:
                rt, Brt = rope_rr.next()
                P.dma("sp", rt[:, :, 0:n], rope[:, :, lo * 128:hi * 128].rearrange("a p n -> p a n"), W=[Brt],
                      key="rope%d" % (rope_rr.i % 2))
            def qk_tail(c, qf, Bqf):
                isq = c < 4
                pm, Bpm = pm_rr.next()
                P.mm(pm[:, 0:n], perm_r, qf[:, 0:n], True, True, [Bqf, B_id], [Bpm])
                t1, Bt1 = t1_rr.next()
                t2, Bt2 = t2_rr.next()
                ro = 2 if isq else 0
                P.tt("pool", t1[:, 0:n], qf[:, 0:n].bitcast(F32), rt[:, ro, 0:n], ALU.mult, [Bqf, Brt], [Bt1])
                P.tt("dve", t2[:, 0:n], pm[:, 0:n], rt[:, ro + 1, 0:n], ALU.mult, [Bpm, Brt], [Bt2])
                if isq:
                    qo, Bqo = qo_rr.next()
                    P.tt("dve", qo[:, 0:n], t1[:, 0:n], t2[:, 0:n], ALU.add, [Bt1, Bt2], [Bqo])
                    P.dma("sp", q_scr[lo:hi, :, c, :].rearrange("t p n -> p t n"),
                          qo[:, 0:n].rearrange("p (t n) -> p t n", n=128), R=[Bqo], W=rng(Bq, lo, hi),
                          key="qo%d" % (qo_rr.i % 2))
                else:
                    P.tt("dve", kT_all[:, c - 4, lo * 128:hi * 128], t1[:, 0:n], t2[:, 0:n], ALU.add,
                         [Bt1, Bt2], rng(Bkt, lo, hi))

            pend = None
            for c in range(8):
                pj, Bpj, n = proj_chunk(wv[:, c], wb, lo, hi)
                isq = c < 4
                if st == 1:
                    if isq:
                        qo, Bqo = qo_rr.next()
                        P.act(qo[:, 0:n], pj[:, 0:n], AF.Identity, [Bpj], [Bqo], scale=0.125, bias=0.0)
                        P.dma("sp", q_scr[lo:hi, :, c, :].rearrange("t p n -> p t n"),
                              qo[:, 0:n].rearrange("p (t n) -> p t n", n=128), R=[Bqo], W=rng(Bq, lo, hi),
                              key="qo%d" % (qo_rr.i % 2))
                    else:
                        P.copy("act", kT_all[:, c - 4, lo * 128:hi * 128], pj[:, 0:n], [Bpj], rng(Bkt, lo, hi))
                else:
                    qf, Bqf = qf_rr.next()
                    P.copy("act", qf[:, 0:n], pj[:, 0:n], [Bpj], [Bqf])
                    if pend is not None:
                        qk_tail(*pend)
                    pend = (c, qf, Bqf)
            if pend is not None:
                qk_tail(*pend)
        stop("A4", l)
        for gp in range(6):
            wv, wb = load_w(wA_fm[l, 10 + gp * 4:10 + gp * 4 + 4].rearrange("c p k m -> p c k m"), [4, 8, 128])
            for (lo, hi, st) in segsA:
                for cc in range(4):
                    gch = gp * 4 + cc
                    br, c = gch // 8, gch % 8
                    pj, Bpj, n = proj_chunk(wv[:, cc], wb, lo, hi)
                    gs_, Bgs = gs_rr.next()
                    P.act(gs_[:, 0:n], pj[:, 0:n], AF.Sigmoid, [Bpj], [Bgs])
                    P.dma("sp", g_scr[lo:hi, c, :, br, :].rearrange("t p n -> p t n"),
                          gs_[:, 0:n].rearrange("p (t n) -> p t n", n=128), R=[Bgs], W=rng(Bg, lo, hi),
                          key="gs%d" % (gs_rr.i % 3))
        if debug and l == 0:
            hh_ = P.halted
            P.halted = False
            P.dma("sp", dbg_k, kT_all, R=Bkt, W=[Bdbg], key="dbgk")
            P.dma("sp", dbg_v, v_all, R=Bv, W=[Bdbg], key="dbgv")
            P.dma("sp", dbg_p, p_all, R=Bp, W=[Bdbg], key="dbgp")
            P.halted = hh_
        stop("A", l)

        P.barrier()
        AR.release(res_mark)
        wsT = AR.alloc([4, 128], BF16)
        sgub = AR.alloc([2, 128], F32)
        wblk = AR.alloc([2, 128], BF16)
        bnd = AR.alloc([2 * 12, 128], BF16)
        B_bsp = Buf("band_special")
        wpo = AR.alloc([2, 1024], BF16)
        wso = AR.alloc([2, 1024], BF16)
        wao = AR.alloc([4, 1024], BF16)
        wo = AR.alloc([8, 1024], BF16)
        B_wD = Buf("wD")
        B_wE = Buf("wE")
        P.dma("pool", wsT, sguw[:, l], W=[B_wD], key="wD0")
        P.dma("sp", sgub, sgub_d[:, l], W=[B_wD], key="wD1")
        P.dma("pool", wblk, wblk_d[:, l], W=[B_wD], key="wD2")
        P.dma("pool", bnd[:, 0:12, :], bands[0], W=[B_wD], key="wD3")
        P.dma("pool", wpo, w_po[l].rearrange("(k p) n -> p k n", p=128), W=[B_wE], key="wD4", max_dma_last_dim=4096)
        P.dma("pool", wso, w_so[l].rearrange("(k p) n -> p k n", p=128), W=[B_wE], key="wD5", max_dma_last_dim=4096)
        P.dma("pool", wao, w_ao[l].rearrange("(k p) n -> p k n", p=128), W=[B_wE], key="wD6", max_dma_last_dim=4096)
        P.dma("pool", wo, w_o[l].rearrange("(k p) n -> p k n", p=128), W=[B_wE], key="wD7", max_dma_last_dim=4096)

        bias_rr = RR([(AR.alloc([1024], BF16), Buf("bias%d" % i)) for i in range(3)])
        qt_rr = RR([(AR.alloc([8, 128], BF16), Buf("qt%d" % i)) for i in range(2)])
        for qz_, Bqz_ in qt_rr.items:
            P.add("pool", lambda e, o=qz_: e.memset(o, 0.0), [], [Bqz_])
        ut_rr = RR([(AR.alloc([2, 128], F32), Buf("ut%d" % i)) for i in range(2)])
        vt_rr = RR([(AR.alloc([256], BF16), Buf("vt%d" % i)) for i in range(2)])
        pe_rr = RR([(AR.alloc([1024], BF16), Buf("pexp%d" % i)) for i in range(3)])
        ptb_rr = RR([(AR.alloc([8, 128], BF16), Buf("ptsb%d" % i)) for i in range(3)])
        smx2 = [(AR.alloc([16], F32), Buf("smx%d" % i)) for i in range(2)]
        rinv = AR.alloc([8], F32)
        B_rinv = Buf("rinv")
        ao = AR.alloc([512], BF16)
        B_ao = Buf("ao")
        pooledT = AR.alloc([2, 128], BF16)
        B_pooled = Buf("pooled")
        sgt = AR.alloc([2, 128], F32)
        B_sgt = Buf("sgt")
        oT_rr = RR([(AR.alloc([8, 512], BF16), Buf("oT%d" % i)) for i in range(2)])
        gt_rr = RR([(AR.alloc([4, 3, 128], BF16), Buf("gt%d" % i)) for i in range(2)])
        hc_rr = RR([(AR.alloc([512], F32), Buf("hc%d" % i)) for i in range(5)])
        e_sets = [[(AR.alloc([512], F32), Buf("et%d_%d" % (j_, i))) for i in range(3)] for j_ in range(2)]
        yT = AR.alloc([8, 512], BF16)
        B_yT = Buf("yT")

        def tile_type(t):
            if t >= 24:
                return 5
            j = t - 4
            return {0: 1, 1: 2, 14: 3, 15: 4}.get(j, 0)

        def band_type(t):
            if t == 24:
                return 3
            if t == 25:
                return 4
            j = t - 4
            return {0: 1, 15: 2}.get(j, 0)

        for (lo, hi, st) in segs_for("DE", l):
            n = (hi - lo) * 128
            oT, B_oT = oT_rr.next()
            tiles = list(range(lo, hi))
            TI = {}
            for t in tiles:
                if st == 1:
                    kranges = [(24, 26)]
                else:
                    j = t - 4
                    klo, khi = t - 2, t + 3
                    if j == 0:
                        khi = t + 4
                    if j == 15:
                        klo = t - 3
                    kranges = [(klo, khi), (24, 26)]
                TI[t] = dict(kr=kranges, nk=sum((b - a) for a, b in kranges) * 128,
                             ks=[s_ for a, b in kranges for s_ in range(a, b)], ty=tile_type(t),
                             tc0=(t - lo) * 128, smx=smx2[t % 2])

            def loads(t):
                ti = TI[t]
                ti["qt"] = qt_rr.next()
                qz4 = ti["qt"][0].rearrange("p (c two) n -> p c two n", two=2)
                P.dma("sp", qz4[0:64, :, 0, :], q_scr[t, 0:64], R=[Bq[t]], W=[ti["qt"][1]], key="qta%d" % (qt_rr.i % 2))
                P.dma("sp", qz4[64:128, :, 1, :], q_scr[t, 64:128], R=[Bq[t]], W=[ti["qt"][1]], key="qtb%d" % (qt_rr.i % 2))
                ti["ut"] = ut_rr.next()
                P.dma("sp", ti["ut"][0], u_scr[t], R=[Bu[t]], W=[ti["ut"][1]], key="ut%d" % (ut_rr.i % 2))
                ti["vt"] = vt_rr.next()
                P.dma("sp", ti["vt"][0], vn_scr[t], R=[Bvn[t]], W=[ti["vt"][1]], key="vt%d" % (vt_rr.i % 2))

            def prologue_a(t):
                ti = TI[t]
                bty = band_type(t)
                boff = 0
                if bty != 0:
                    P.dma("pool", bnd[:, 12:24, :], bands[bty], W=[B_bsp], key="bsp")
                    boff = 12
                PP = bank(7).rearrange("p (a b) -> p a b", b=128)
                for g in range(4):
                    c = g // 2
                    srcs = [d_ for d_ in (-1, 0, 1) if not ((t == 24 and d_ == -1) or (t == 25 and d_ == 1))]
                    for ii, d_ in enumerate(srcs):
                        P.mm(PP[:, g, :], p_all[:, t + d_, c * 128:(c + 1) * 128], bnd[:, boff + g * 3 + (d_ + 1), :],
                             ii == 0, ii == len(srcs) - 1, [Bp[t + d_], B_wD, B_bsp], [Bk[7]])
                for g in range(4):
                    gp_ = (g % 2) * 64
                    P.copy("act", pooledT[gp_:gp_ + 64, g // 2, :], PP[gp_:gp_ + 64, g, :], [Bk[7]], [B_pooled])

            def prologue_b(t):
                tc0 = TI[t]["tc0"]
                PY = bank(7)[:, 0:256].rearrange("p (a b) -> p a b", b=128)
                for c in range(2):
                    P.mm(PY[:, c, :], wblk[:, c, :], pooledT[:, c, :], True, True, [B_pooled, B_wD], [Bk[7]])
                for c in range(2):
                    P.act(oT[:, c, tc0:tc0 + 128], PY[:, c, :], AF.Identity, [Bk[7], B_vec], [B_oT],
                          scale=V_ps(l)[:, c:c + 1], bias=0.0)

            def prologue_c(t):
                ti = TI[t]
                tc0 = ti["tc0"]
                ut, But = ti["ut"]
                vt, Bvt = ti["vt"]
                PS_ = bank(7).rearrange("p (a b) -> p a b", b=128)
                for hh in range(4):
                    P.mm(PS_[:, hh, :], vt[:, (hh // 2) * 128:(hh // 2 + 1) * 128], wsT[:, hh, :], True, True,
                         [Bvt, B_wD], [Bk[7]])
                for hh in range(4):
                    hp = (hh % 2) * 64
                    P.tt("dve", sgt[hp:hp + 64, hh // 2, :], PS_[hp:hp + 64, hh, :],
                         sgub[hp:hp + 64, hh // 2, :], ALU.add, [Bk[7], B_wD], [B_sgt])
                P.tt("pool", oT[:, 2:4, tc0:tc0 + 128], sgt, ut, ALU.mult, [B_sgt, But], [B_oT])

            def epilogue_b(t):
                tc0 = TI[t]["tc0"]
                AT = bank(7).bitcast(BF16).rearrange("p (a b) -> p a b", b=128)
                for c in range(4):
                    P.tr(AT[:, c, :], ao[:, c * 128:(c + 1) * 128], identb, [B_ao, B_id], [Bk[7]])
                P.copy("act", oT[:, 4:8, tc0:tc0 + 128], AT[:, 0:4, :], [Bk[7]], [B_oT])

            units = [(t, h) for t in tiles for h in range(8)]
            US = [dict() for _ in units]

            def s1(k):
                t, h = units[k]
                ti = TI[t]
                qt, Bqt = ti["qt"]
                nk = ti["nk"]
                ch, pb = h // 2, (h % 2) * 64
                S = psd[h % 2]
                BS = [Bk[2 * (h % 2)], Bk[2 * (h % 2) + 1]]
                bt, Bbt = bias_rr.next()
                P.dma("sp", bt[:, 0:nk], biasd[l, ti["ty"], h, :, 0:nk], W=[Bbt], key="bias%d" % (bias_rr.i % 3))
                col = 0
                for (a, b) in ti["kr"]:
                    c0 = a * 128
                    rem = (b - a) * 128
                    while rem > 0:
                        w_ = min(rem, 512 - (col % 512))
                        P.mm(S[:, col:col + w_], qt[:, h, :], kT_all[:, ch, c0:c0 + w_],
                             (col % 512) == 0, False, [Bqt] + rng(Bkt, a, b), [BS[col // 512]])
                        col += w_
                        c0 += w_
                        rem -= w_
                for b0_ in range(0, nk, 512):
                    w_ = min(512, nk - b0_)
                    P.mm(S[:, b0_:b0_ + w_], identb, bt[:, b0_:b0_ + w_], False, True, [Bbt, B_id], [BS[b0_ // 512]])
                US[k].update(S=S, BS=BS, bt=bt, Bbt=Bbt)

            def s2(k):
                t, h = units[k]
                ti = TI[t]
                nk = ti["nk"]
                smx, B_smx = ti["smx"]
                u = US[k]
                P.add("dve", lambda e, o=smx[:, h:h + 1], i=u["S"][:, 0:nk]: e.tensor_reduce(
                    out=o, in_=i, axis=AX.X, op=ALU.max, negate=True), u["BS"], [B_smx])
                pex, Bpex = pe_rr.next()
                P.act(pex[:, 0:nk], u["S"][:, 0:nk], AF.Exp, u["BS"] + [B_smx], [Bpex, B_smx], bias=smx[:, h:h + 1],
                      scale=1.0, accum_out=smx[:, 8 + h:9 + h])
                u.update(pex=pex, Bpex=Bpex)

            def s3(k):
                t, h = units[k]
                nkt = TI[t]["nk"] // 128
                u = US[k]
                pb_ = 4 + (k % 2)
                PT = bank(pb_).bitcast(BF16).rearrange("p (a b) -> p a b", b=128)
                for kt in range(nkt):
                    P.tr(PT[:, kt, :], u["pex"][:, kt * 128:(kt + 1) * 128], identb, [u["Bpex"], B_id], [Bk[pb_]])
                ptb, Bptb = ptb_rr.next()
                P.copy("act" if k % 2 == 0 else "dve", ptb[:, 0:nkt, :], PT[:, 0:nkt, :], [Bk[pb_]], [Bptb])
                u.update(ptb=ptb, Bptb=Bptb)

            def s4(k):
                t, h = units[k]
                ti = TI[t]
                nkt = ti["nk"] // 128
                ks = ti["ks"]
                u = US[k]
                for kt in range(nkt):
                    P.mm(bank(6)[:, h * 64:(h + 1) * 64], u["ptb"][:, kt, :], v_all[:, ks[kt], h * 64:(h + 1) * 64],
                         kt == 0, kt == nkt - 1, [u["Bptb"], Bv[ks[kt]]], [Bk[6]])
                if h == 7:
                    smx, B_smx = ti["smx"]
                    tc0 = ti["tc0"]
                    P.add("dve", lambda e, s_=smx: e.reciprocal(out=rinv, in_=s_[:, 8:16]), [B_smx], [B_rinv])
                    P.tt("dve", ao.rearrange("p (h d) -> p h d", d=64), bank(6).rearrange("p (h d) -> p h d", d=64),
                         rinv.unsqueeze(2).broadcast_to([128, 8, 64]), ALU.mult, [Bk[6], B_rinv], [B_ao])

            NU = len(units)
            loads(tiles[0])
            for k in range(NU + 6):
                if k < NU:
                    t, h = units[k]
                    if h == 0:
                        if t + 1 < hi:
                            loads(t + 1)
                        prologue_a(t)
                    elif h == 2:
                        prologue_b(t)
                    elif h == 4:
                        prologue_c(t)
                    s1(k)
                if 0 <= k - 1 < NU:
                    s2(k - 1)
                if 0 <= k - 2 < NU:
                    s3(k - 2)
                if 0 <= k - 3 < NU:
                    s4(k - 3)
                if 0 <= k - 5 < NU and units[k - 5][1] == 7:
                    epilogue_b(units[k - 5][0])
            if debug and l == 0:
                for t in range(lo, hi):
                    P.dma("sp", dbg_o[t], oT[:, :, (t - lo) * 128:(t - lo + 1) * 128], R=[B_oT], W=[Bdbg], key="dbgo")
            for c in range(8):
                gt, Bgt = gt_rr.next()
                P.dma("sp", gt[:, 0:hi - lo], g_scr[lo:hi, c].rearrange("t p b n -> p t b n"), R=rng(Bg, lo, hi), W=[Bgt],
                      key="gt%d" % (gt_rr.i % 2))
                b0 = (c % 2) * 3
                brs = [(wpo, 2, 0), (wso, 2, 2), (wao, 4, 4)]
                for bi_, (wt_, nkk, o0) in enumerate(brs):
                    for k in range(nkk):
                        P.mm(bank(b0 + bi_)[:, 0:n], wt_[:, k, c * 128:(c + 1) * 128], oT[:, o0 + k, 0:n], k == 0, k == nkk - 1,
                             [B_wE, B_oT], [Bk[b0 + bi_]])
                e_t = e_sets[c % 2]
                for bi_ in range(3):
                    et, Bet = e_t[bi_]
                    P.tt("dve", et[:, 0:n].rearrange("p (t n) -> p t n", n=128),
                         bank(b0 + bi_)[:, 0:n].rearrange("p (t n) -> p t n", n=128), gt[:, 0:hi - lo, bi_, :],
                         ALU.mult, [Bk[b0 + bi_], Bgt], [Bet])
                P.tt("pool", e_t[0][0][:, 0:n], e_t[0][0][:, 0:n], e_t[1][0][:, 0:n], ALU.add, [e_t[0][1], e_t[1][1]], [e_t[0][1]])
                P.tt("pool", yT[:, c, 0:n], e_t[0][0][:, 0:n], e_t[2][0][:, 0:n], ALU.add, [e_t[0][1], e_t[2][1]], [B_yT])
            hcs = []

            def ld_hc(c2):
                hc, Bhc = hc_rr.next()
                P.dma("sp", hc[:, 0:n], h_src(l, c2, lo, hi), R=([Bh[t][c2] for t in range(lo, hi)] if l > 0 else []),
                      W=[Bhc], key="hc%d" % (hc_rr.i % 5))
                hcs.append((hc, Bhc, "hc%d" % (hc_rr.i % 5)))

            for c2 in range(3):
                ld_hc(c2)
            for c2 in range(8):
                ob = 6 + (c2 % 2)
                hc, Bhc, hkey = hcs[c2]
                for c in range(8):
                    P.mm(bank(ob)[:, 0:n], wo[:, c, c2 * 128:(c2 + 1) * 128], yT[:, c, 0:n], c == 0, c == 7,
                         [B_wE, B_yT], [Bk[ob]])
                P.stt("dve", hc[:, 0:n], bank(ob)[:, 0:n], MT(l, st, 2)[:, c2:c2 + 1], hc[:, 0:n],
                      ALU.mult, ALU.add, [Bk[ob], Bhc, B_modt], [Bhc])
                if c2 + 3 < 8:
                    ld_hc(c2 + 3)
                P.dma("sp", h_scr[c2, :, lo * 128:hi * 128], hc[:, 0:n], R=[Bhc], W=[Bh[t][c2] for t in range(lo, hi)],
                      key=hkey)
        stop("E", l)

        P.barrier()
        AR.release(m0)
        hF2 = [(AR.alloc([8, 1024], F32), Buf("hF%d" % i)) for i in range(2)]
        aF = AR.alloc([8, 1024], BF16)
        B_aF = Buf("aF")
        hid = AR.alloc([32, 1024], BF16)
        B_hid = Buf("hid")
        rsbF = (AR.alloc([512], F32), Buf("rsF"))
        rsF3 = [rsbF] + [(AR.alloc([512], F32), Buf("rsF%d" % i)) for i in (1, 2)]
        tmpF_rr = RR([(AR.alloc([512], F32), Buf("tmpF%d" % i)) for i in range(2)])
        w1_rr = RR([(AR.alloc([4, 8, 128], BF16), Buf("w1b%d" % i)) for i in range(2)])
        w2_rr = RR([(AR.alloc([32, 128], BF16), Buf("w2b%d" % i)) for i in range(2)])
        rl_rr = RR([(AR.alloc([512], F32), Buf("rl%d" % i)) for i in range(2)])
        f1_rr = RR([(bank(i), Bk[i]) for i in (1, 2, 3, 4)])
        f2_rr = RR([(bank(i), Bk[i]) for i in (5, 6, 7)])
        groupsF = segs_for("F", l)

        def goffs(grp):
            offs = []
            o_ = 0
            for (lo, hi, st) in grp:
                offs.append(o_)
                o_ += (hi - lo) * 128
            return offs

        def load_group(gi):
            hF, B_hF = hF2[gi % 2]
            grp = groupsF[gi]
            for si_, ((lo, hi, st), o0) in enumerate(zip(grp, goffs(grp))):
                n = (hi - lo) * 128
                P.dma("sp", hF[:, :, o0:o0 + n], h_scr[:, :, lo * 128:hi * 128].rearrange("c p n -> p c n"),
                      R=[b_ for t in range(lo, hi) for b_ in Bh[t]], W=[B_hF], key="hF%d_%d" % (gi % 2, si_))

        def norm_group(gi):
            hF_, B_hF_ = hF2[gi % 2]
            grp_ = groupsF[gi]
            for si_, ((lo, hi, st), o0) in enumerate(zip(grp_, goffs(grp_))):
                n = (hi - lo) * 128
                rms_to_aT(hF_[:, :, o0:o0 + n], B_hF_, n, MT(l, st, 3), MT(l, st, 4), aF[:, :, o0:o0 + n], [B_aF],
                          rsF3[si_], tmpF_rr, (bank(0), Bk[0]))

        load_group(0)
        norm_group(0)
        for gi, grp in enumerate(groupsF):
            hF, B_hF = hF2[gi % 2]
            offs = goffs(grp)
            if gi + 1 < len(groupsF):
                load_group(gi + 1)
            for jp in range(8):
                w1t, Bw1 = w1_rr.next()
                P.dma("pool", w1t, w1r[l, jp * 4:jp * 4 + 4].rearrange("j p k m -> p j k m"), W=[Bw1],
                      key="w1b%d" % (w1_rr.i % 2), max_dma_last_dim=4096)
                for jj in range(4):
                    j = jp * 4 + jj
                    for (lo, hi, st), o0 in zip(grp, offs):
                        n = (hi - lo) * 128
                        pf, Bpf = f1_rr.next()
                        for k in range(8):
                            P.mm(pf[:, 0:n], w1t[:, jj, k, :], aF[:, k, o0:o0 + n], k == 0, k == 7, [Bw1, B_aF], [Bpf])
                        rl, Brl = rl_rr.next()
                        P.act(rl[:, 0:n], pf[:, 0:n], AF.Relu, [Bpf], [Brl])
                        P.stt("dve", hid[:, j, o0:o0 + n], pf[:, 0:n], 0.0, rl[:, 0:n], ALU.max, ALU.mult,
                              [Bpf, Brl], [B_hid])
            if gi + 1 < len(groupsF):
                norm_group(gi + 1)
            for c2 in range(8):
                w2t, Bw2 = w2_rr.next()
                P.dma("pool", w2t, w2r[l, c2], W=[Bw2], key="w2b%d" % (w2_rr.i % 2), max_dma_last_dim=4096)
                for (lo, hi, st), o0 in zip(grp, offs):
                    n = (hi - lo) * 128
                    pf, Bpf = f2_rr.next()
                    for j in range(32):
                        P.mm(pf[:, 0:n], w2t[:, j, :], hid[:, j, o0:o0 + n], j == 0, j == 31, [Bw2, B_hid], [Bpf])
                    P.stt("dve", hF[:, c2, o0:o0 + n], pf[:, 0:n], MT(l, st, 5)[:, c2:c2 + 1], hF[:, c2, o0:o0 + n],
                          ALU.mult, ALU.add, [Bpf, B_hF, B_modt], [B_hF])
            for si_, ((lo, hi, st), o0) in enumerate(zip(grp, offs)):
                n = (hi - lo) * 128
                if not last:
                    P.dma("sp", h_scr[:, :, lo * 128:hi * 128].rearrange("c p n -> p c n"), hF[:, :, o0:o0 + n],
                          R=[B_hF], W=[b_ for t in range(lo, hi) for b_ in Bh[t]], key="hFo%d_%d" % (gi % 2, si_))
                else:
                    rs, Brs = rsbF
                    rms_rstd(hF[:, :, o0:o0 + n], B_hF, n, rsbF, (bank(0), Bk[0]))
                    for c in range(8):
                        P.stt("dve", hF[:, c, o0:o0 + n], hF[:, c, o0:o0 + n], V_fg[:, c:c + 1], rs[:, 0:n], ALU.mult, ALU.mult,
                              [B_hF, Brs, B_vec], [B_hF])
                    op = P.dma("sp", outT[:, :, (lo - 4) * 128:(hi - 4) * 128].rearrange("c p n -> p c n"), hF[:, :, o0:o0 + n],
                               R=[B_hF], W=[Bout], key="outst%d_%d" % (gi % 2, si_))
                    out_ops.append(op)
        stop("F", l)

    P.halted = False
    if not out_ops:
        z = AR.t[:, 0:2048]
        out_ops.append(P.dma("sp", outT[0], z, R=[], W=[Bout], key="outst"))
    if debug:
        out_ops.append(P.dma("sp", outT[1, :, 0:8], vec[:, 0:8], R=[Bdbg], W=[Bout], key="dbgfin"))
    P.final_waits = out_ops
    P.emit()
    return nc, P, AR


def _fm(v):
    v = np.asarray(v, np.float32)
    return np.ascontiguousarray(v.reshape(-1, 128).T)


def _rope_perm():
    idx = []
    for h in range(8):
        idx += [h * 64 + 2 * i for i in range(32)] + [h * 64 + 2 * i + 1 for i in range(32)]
    return np.array(idx)


def _shared_inputs(inp):
    w_in = np.asarray(inp["w_in"], np.float32)
    perm = _rope_perm()
    wp = w_in.copy()
    wp[:, :, 768:1280] = w_in[:, :, 768:1280][:, :, perm]
    wp[:, :, 1280:1792] = w_in[:, :, 1280:1792][:, :, perm]
    tm_cols = [np.r_[0:256, 512:768], np.r_[1792:2304]]
    wA_tm = np.stack([np.stack([wp[l][:, cols].reshape(8, 128, 512).transpose(1, 0, 2) for cols in tm_cols])
                      for l in range(2)])
    fm_starts = [256, 384] + [768 + 128 * i for i in range(8)] + [2304 + 128 * i for i in range(24)]
    wA_fm = np.stack([np.stack([wp[l][:, s:s + 128].reshape(8, 128, 128).transpose(1, 0, 2) for s in fm_starts])
                      for l in range(2)])
    w1 = np.asarray(inp["w_ff1"], np.float32)
    w1r = np.stack([np.stack([w1[l][:, j * 128:(j + 1) * 128].reshape(8, 128, 128).transpose(1, 0, 2)
                              for j in range(32)]) for l in range(2)])
    w2 = np.asarray(inp["w_ff2"], np.float32)
    w2r = np.stack([np.stack([w2[l][:, c * 128:(c + 1) * 128].reshape(32, 128, 128).transpose(1, 0, 2)
                              for c in range(8)]) for l in range(2)])
    sgu_w = np.asarray(inp["sgu_w"], np.float32)
    sguw = np.ascontiguousarray(sgu_w.transpose(3, 0, 1, 2))
    sgu_b = np.asarray(inp["sgu_b"], np.float32)
    sgub = np.zeros((128, 2, 2, 128), np.float32)
    for part in range(128):
        for c in range(2):
            sgub[part, :, c, :] = sgu_b[:, 2 * c + part // 64, :]
    w_pool = np.asarray(inp["w_pool"], np.float32)
    wblk = np.zeros((128, 2, 2, 128), np.float32)
    for c in range(2):
        for gl in range(2):
            wblk[gl * 64:(gl + 1) * 64, :, c, gl * 64:(gl + 1) * 64] = w_pool[:, 2 * c + gl].transpose(1, 0, 2)
    consts = np.zeros((128, 3, 128), np.float32)
    consts[:, 0, :] = np.eye(128)
    for pp in range(128):
        partner = pp + 32 if (pp % 64) < 32 else pp - 32
        consts[partner, 1, pp] = 1.0
    consts[:, 2, :] = 1.0
    return dict(w_mod=np.ascontiguousarray(inp["w_mod"], np.float32), wA_tm=np.ascontiguousarray(wA_tm),
                wA_fm=np.ascontiguousarray(wA_fm), sguw=sguw, sgub=sgub, wblk=wblk,
                w_pool_out=np.ascontiguousarray(inp["w_pool_out"], np.float32),
                w_sgu_out=np.ascontiguousarray(inp["w_sgu_out"], np.float32),
                w_attn_out=np.ascontiguousarray(inp["w_attn_out"], np.float32),
                w_o=np.ascontiguousarray(inp["w_o"], np.float32), w1r=np.ascontiguousarray(w1r),
                w2r=np.ascontiguousarray(w2r), consts=consts)


def _band_set(kind):
    out = np.zeros((128, 12, 128), np.float32)
    L = 384
    base = 128
    for g, w in enumerate((2, 4, 8, 16)):
        for tt in range(128):
            pos = base + tt
            lo_b = base if kind == 1 else 0
            hi_b = base + 128 if kind == 2 else L
            lo = min(max(pos - w // 2, lo_b), hi_b)
            hi = min(max(pos + (w - w // 2), lo_b), hi_b)
            cnt = hi - lo
            for s in range(lo, hi):
                d = s // 128
                out[s % 128, g * 3 + d, tt] += 1.0 / cnt
            out[tt, g * 3 + 1, tt] -= 1.0
    return out


def _bias_tables(rpb, core_rows0, n_rows_total=128):
    out = np.full((2, 6, 8, 128, 1024), NEG, np.float32)
    q_i = np.arange(128)
    for ty, j in ((0, 4), (1, 0), (2, 1), (3, 14), (4, 15)):
        klo, khi = j - 2, j + 3
        if j == 0:
            khi = j + 4
        if j == 15:
            klo = j - 3
        nkl = (khi - klo) * 128
        key = np.arange(nkl)
        k_row = core_rows0 + 2 * klo + key // 64
        k_col = key % 64
        q_row = core_rows0 + 2 * j + q_i // 64
        q_col = q_i % 64
        rs = np.clip(q_row - 4, 0, n_rows_total - 8)
        cs = np.clip(q_col - 8, 0, 64 - 16)
        valid = ((k_row[None, :] >= rs[:, None]) & (k_row[None, :] < rs[:, None] + 8) &
                 (k_col[None, :] >= cs[:, None]) & (k_col[None, :] < cs[:, None] + 16) &
                 (k_row[None, :] >= 0) & (k_row[None, :] < n_rows_total))
        dr = np.clip(k_row[None, :] - q_row[:, None] + 7, 0, 14)
        dc = np.clip(k_col[None, :] - q_col[:, None] + 15, 0, 30)
        for l in range(2):
            for h in range(8):
                g = rpb[l, h][dr, dc]
                out[l, ty, h, :, 0:nkl] = np.where(valid, g, NEG)
                out[l, ty, h, :, nkl:nkl + 256] = 0.0
    out[:, 5, :, :, 0:256] = 0.0
    return out.astype(ml_dtypes.bfloat16)


def _core_inputs(inp, core):
    b, blk = core // 4, core % 4
    row0 = 32 * blk
    x = np.asarray(inp["x"], np.float32)[b]
    t0 = (row0 - 8) * 64
    xs = np.zeros((NLAT, D), np.float32)
    lo, hi = max(t0, 0), min(t0 + NLAT, 8192)
    xs[lo - t0:hi - t0] = x[lo:hi]
    xT = np.ascontiguousarray(xs.T.reshape(8, 128, NLAT))
    ctxT = np.ascontiguousarray(np.asarray(inp["ctx"], np.float32)[b].T.reshape(8, 128, 256))
    vecs = np.zeros((128, 156), np.float32)
    vecs[:, 0:8] = _fm(inp["c"][b])
    vecs[:, 8:16] = _fm(inp["c_ctx"])
    for l in range(2):
        vecs[:, 16 + l * 48:16 + (l + 1) * 48] = _fm(inp["b_mod"][l])
        vecs[:, 112 + l * 8:112 + (l + 1) * 8] = _fm(inp["norm1_g"][l])
        vecs[:, 128 + l * 8:128 + (l + 1) * 8] = _fm(inp["norm2_g"][l])
        vecs[:, 152 + l * 2:152 + (l + 1) * 2] = _fm(inp["pool_scale"][l])
    vecs[:, 144:152] = _fm(inp["final_g"])
    tok = np.arange(NLAT)
    row = (row0 - 8 + tok // 64).astype(np.float32)
    col = (tok % 64).astype(np.float32)
    inv_freq = (10000.0 ** (-np.arange(16, dtype=np.float32) / 16)).astype(np.float32)
    ang = np.concatenate([row[:, None] * inv_freq, col[:, None] * inv_freq], axis=-1).astype(np.float32)
    cos, sin = np.cos(ang), np.sin(ang)
    rope = np.zeros((4, 128, NLAT), np.float32)
    for pp in range(128):
        i = pp % 64
        e = i % 32
        rope[0, pp] = cos[:, e]
        rope[1, pp] = -sin[:, e] if i < 32 else sin[:, e]
    rope[2:4] = rope[0:2] * np.float32(0.125)
    first = (blk == 0)
    lastb = (blk == 3)
    gen = _band_set(0)
    bands = np.stack([gen, _band_set(1) if first else gen, _band_set(2) if lastb else gen, _band_set(1), _band_set(2)])
    if first:
        bands[1][:, [0, 3, 6, 9], :] = 0.0
    bias = _bias_tables(np.asarray(inp["na_rpb"], np.float32), row0)
    return dict(xT=xT, ctxT=ctxT, vecs=vecs, rope=rope, bands=np.ascontiguousarray(bands), bias=bias)


_PROG = {}


def kernel(**inputs):
    if "nc" not in _PROG:
        _PROG["nc"] = build_program()[0]
    nc = _PROG["nc"]
    shared = _shared_inputs(inputs)
    in_maps = []
    for core in range(8):
        m = dict(shared)
        m.update(_core_inputs(inputs, core))
        in_maps.append(m)
    res = run_bass_kernel_spmd(nc, in_maps, core_ids=list(range(8)))
    out = np.zeros((2, 8192, D), np.float32)
    for core in range(8):
        b, blk = core // 4, core % 4
        oT = np.asarray(res.results[core]["outT"], np.float32)
        out[b, blk * 2048:(blk + 1) * 2048, :] = oT.reshape(D, 2048).T
    return out
```
